# Optimizing a Trainium2 kernel written in Bass

```python
import functools
import math
import jax
import jax.numpy as jnp
from jax import lax
import numpy as np

D_MODEL = 1024
BATCH = 2
SEQ = 8192
DEPTH = 4

GRID_W = 64
CTX_LEN = 256
N_EVEN = (DEPTH + 1) // 2
N_ODD = DEPTH // 2
EPS = 1e-6
ROPE_BASE = 10000.0
F32 = jnp.float32

S5_WIDTH = D_MODEL // 2
S5_GROUP = 16
S5_GROUPS = S5_WIDTH // S5_GROUP
S5_STATE = 64

MLA_HEADS = 8
MLA_NOPE = 64
MLA_ROPE = 32
MLA_V = 64
MLA_Q_RANK = D_MODEL // 4
MLA_KV_RANK = D_MODEL // 8
ATT_BLOCK = 128

EVEN_SIZES = (S5_WIDTH, MLA_Q_RANK, MLA_KV_RANK, MLA_ROPE)
EVEN_IN = sum(EVEN_SIZES)
EVEN_MIX = S5_WIDTH + MLA_HEADS * MLA_V

RET_HEADS = 4
RET_DK = 128
RET_DV = 128
RET_CHUNK = 128
RET_QK = RET_HEADS * RET_DK
RET_VW = RET_HEADS * RET_DV
RET_DECAY_OFFSETS = (0.0, 0.5)

HG_HEADS = 4
HG_DK = 128
HG_DV = 128
HG_CHUNK = 64
HG_QK = HG_HEADS * HG_DK
HG_VW = HG_HEADS * HG_DV

ODD_SIZES = (RET_QK, RET_QK, RET_VW, RET_VW, HG_QK, HG_QK, HG_QK, HG_VW, HG_VW)
ODD_IN = sum(ODD_SIZES)
ODD_MIX = RET_VW + HG_VW

D_FF = -(-8 * D_MODEL // (3 * 256)) * 256

kernel_name = 'hybrid_s5_mla_retention_hgrn2_diffusion_trunk'


def rms_norm(x, g):
    xf = x.astype(F32)
    y = xf * lax.rsqrt(jnp.mean(xf * xf, axis=-1, keepdims=True) + EPS)
    return (y * g.astype(F32)).astype(x.dtype)


def split_heads(t, n_heads):
    b, l, w = t.shape
    return t.reshape(b, l, n_heads, w // n_heads).transpose(0, 2, 1, 3)


def merge_heads(t):
    b, h, l, d = t.shape
    return t.transpose(0, 2, 1, 3).reshape(b, l, h * d)


def head_norm(o, g, center):
    of = o.astype(F32)
    if center:
        of = of - jnp.mean(of, axis=-1, keepdims=True)
    of = of * lax.rsqrt(jnp.mean(of * of, axis=-1, keepdims=True) + EPS)
    return (merge_heads(of) * g.astype(F32)).astype(o.dtype)


def split_cols(t, sizes):
    idx = []
    s = 0
    for n in sizes[:-1]:
        s += n
        idx.append(s)
    return jnp.split(t, idx, axis=-1)


def axial_rope(rows, dim):
    r, col = jnp.meshgrid(jnp.arange(rows, dtype=F32), jnp.arange(GRID_W, dtype=F32), indexing='ij')
    quarter = dim // 4
    inv = ROPE_BASE ** (-jnp.arange(quarter, dtype=F32) / quarter)
    ang = jnp.concatenate([r.reshape(-1, 1) * inv, col.reshape(-1, 1) * inv], axis=-1)
    return jnp.cos(ang), jnp.sin(ang)


def apply_rope(x, cos, sin):
    xp = x.reshape(x.shape[:-1] + (x.shape[-1] // 2, 2))
    x1, x2 = xp[..., 0], xp[..., 1]
    cos = cos.astype(x.dtype)
    sin = sin.astype(x.dtype)
    return jnp.stack([x1 * cos - x2 * sin, x1 * sin + x2 * cos], axis=-1).reshape(x.shape)


def swiglu(h, w1, w3, w2):
    return (jax.nn.silu(h @ w1) * (h @ w3)) @ w2


def context_then_latent(dir_fn, ctx_in, lat_in, s0, axis, reverse):
    flip = (lambda t: jnp.flip(t, axis)) if reverse else (lambda t: t)
    o_c, s_c = dir_fn(*[flip(t) for t in ctx_in], s0)
    o_x, _ = dir_fn(*[flip(t) for t in lat_in], s_c)
    return flip(o_x), flip(o_c)


def block_attention(q, k, v):
    b, h, lq, d = q.shape
    nb = lq // ATT_BLOCK
    qb = q.reshape(b, h, nb, ATT_BLOCK, d).transpose(2, 0, 1, 3, 4)
    scale = d ** -0.5

    def attend(qblk):
        s = jnp.einsum('bhqd,bhkd->bhqk', qblk, k).astype(F32) * scale
        p = jax.nn.softmax(s, axis=-1).astype(v.dtype)
        return jnp.einsum('bhqk,bhkd->bhqd', p, v)

    o = lax.map(attend, qb)
    return o.transpose(1, 2, 0, 3, 4).reshape(b, h, lq, v.shape[-1])


def s5_discretize(a_re, a_im, log_dt, b_re, b_im):
    dt = jnp.exp(log_dt)[:, None]
    mag = jnp.exp(a_re * dt)
    ab_re = mag * jnp.cos(a_im * dt)
    ab_im = mag * jnp.sin(a_im * dt)
    nr, ni = ab_re - 1.0, ab_im
    den = a_re * a_re + a_im * a_im
    fr = (nr * a_re + ni * a_im) / den
    fi = (ni * a_re - nr * a_im) / den
    bb_re = fr[..., None] * b_re - fi[..., None] * b_im
    bb_im = fr[..., None] * b_im + fi[..., None] * b_re
    return ab_re, ab_im, bb_re, bb_im


def _complex_affine_combine(e1, e2):
    a1r, a1i, b1r, b1i = e1
    a2r, a2i, b2r, b2i = e2
    return (a2r * a1r - a2i * a1i, a2r * a1i + a2i * a1r,
            a2r * b1r - a2i * b1i + b2r, a2r * b1i + a2i * b1r + b2i)


def s5_direction(u, s0, ab_re, ab_im, bb_re, bb_im, c_re, c_im):
    bu_re = jnp.einsum('gpk,blgk->blgp', bb_re, u)
    bu_im = jnp.einsum('gpk,blgk->blgp', bb_im, u)
    h0r, h0i = s0
    bu_re = bu_re.at[:, 0].add(ab_re * h0r - ab_im * h0i)
    bu_im = bu_im.at[:, 0].add(ab_re * h0i + ab_im * h0r)
    ar = jnp.broadcast_to(ab_re.astype(bu_re.dtype), bu_re.shape)
    ai = jnp.broadcast_to(ab_im.astype(bu_re.dtype), bu_re.shape)
    _, _, hr, hi = lax.associative_scan(_complex_affine_combine, (ar, ai, bu_re, bu_im), axis=1)
    y = jnp.einsum('gkp,blgp->blgk', c_re, hr) - jnp.einsum('gkp,blgp->blgk', c_im, hi)
    return y, (hr[:, -1], hi[:, -1])


def s5_mixer(ux, uc, a_re, a_im, log_dt, b_re, b_im, c_re, c_im, d, w_glu, ctx_out):
    b, lx, _ = ux.shape
    lc = uc.shape[1]
    gx = ux.reshape(b, lx, S5_GROUPS, S5_GROUP)
    gc = uc.reshape(b, lc, S5_GROUPS, S5_GROUP)
    zero = jnp.zeros((b, S5_GROUPS, S5_STATE), ux.dtype)
    yx = d * ux
    yc = d * uc
    for r in range(2):
        ab_re, ab_im, bb_re, bb_im = s5_discretize(a_re[r], a_im[r], log_dt[r], b_re[r], b_im[r])
        fn = functools.partial(s5_direction, ab_re=ab_re, ab_im=ab_im, bb_re=bb_re, bb_im=bb_im,
                               c_re=c_re[r], c_im=c_im[r])
        ox, oc = context_then_latent(fn, (gc,), (gx,), (zero, zero), 1, r == 1)
        yx = yx + ox.reshape(b, lx, S5_WIDTH)
        yc = yc + oc.reshape(b, lc, S5_WIDTH)

    def glu(y):
        z = jax.nn.gelu(y)
        return z * jax.nn.sigmoid(z @ w_glu)

    return glu(yx), (glu(yc) if ctx_out else None)


def mla_queries(cq, q_norm, w_uq, rope):
    b, l, _ = cq.shape
    q = (rms_norm(cq, q_norm) @ w_uq).reshape(b, l, MLA_HEADS, MLA_NOPE + MLA_ROPE).transpose(0, 2, 1, 3)
    if rope is not None:
        q = jnp.concatenate([q[..., :MLA_NOPE], apply_rope(q[..., MLA_NOPE:], *rope)], axis=-1)
    return q


def mla_keys_values(ckv, kr, kv_norm, w_ukv, rope):
    b, l, _ = ckv.shape
    kv = (rms_norm(ckv, kv_norm) @ w_ukv).reshape(b, l, MLA_HEADS, MLA_NOPE + MLA_V).transpose(0, 2, 1, 3)
    kr = kr[:, None]
    if rope is not None:
        kr = apply_rope(kr, *rope)
    k = jnp.concatenate([kv[..., :MLA_NOPE], jnp.broadcast_to(kr, (b, MLA_HEADS, l, MLA_ROPE))], axis=-1)
    return k, kv[..., MLA_NOPE:]


def even_mixer(hx, hc, w_in, a_re, a_im, log_dt, b_re, b_im, c_re, c_im, d, w_glu,
               q_norm, w_uq, kv_norm, w_ukv, rope, ctx_out):
    ux, cqx, ckvx, krx = split_cols(hx @ w_in, EVEN_SIZES)
    uc, cqc, ckvc, krc = split_cols(hc @ w_in, EVEN_SIZES)
    yx, yc = s5_mixer(ux, uc, a_re, a_im, log_dt, b_re, b_im, c_re, c_im, d, w_glu, ctx_out)
    kx, vx = mla_keys_values(ckvx, krx, kv_norm, w_ukv, rope)
    kc, vc = mla_keys_values(ckvc, krc, kv_norm, w_ukv, None)
    qx = mla_queries(cqx, q_norm, w_uq, rope)
    ox = block_attention(qx, jnp.concatenate([kc, kx], axis=2), jnp.concatenate([vc, vx], axis=2))
    out_x = jnp.concatenate([yx, merge_heads(ox)], axis=-1)
    if not ctx_out:
        return out_x, None
    oc = block_attention(mla_queries(cqc, q_norm, w_uq, None), kc, vc)
    return out_x, jnp.concatenate([yc, merge_heads(oc)], axis=-1)


def retention_direction(q, k, v, s0, log_gamma):
    b, h, l, dk = q.shape
    dv = v.shape[-1]
    c = RET_CHUNK
    n = l // c
    dt = q.dtype
    pos = jnp.arange(c, dtype=F32)
    lg = log_gamma[:, None]
    rel = pos[:, None] - pos[None, :]
    dmask = jnp.where(rel >= 0, jnp.exp(lg[:, :, None] * jnp.maximum(rel, 0.0)), 0.0).astype(dt)
    k_w = jnp.exp(lg * (c - 1 - pos)).astype(dt)
    q_w = jnp.exp(lg * (pos + 1)).astype(dt)
    g_c = jnp.exp(log_gamma * c).astype(dt)[None, :, None, None]
    qc = q.reshape(b, h, n, c, dk)
    kc = k.reshape(b, h, n, c, dk)
    vc = v.reshape(b, h, n, c, dv)
    inner = jnp.einsum('bhntd,bhnsd->bhnts', qc, kc) * dmask[:, None]
    o_intra = jnp.einsum('bhnts,bhnse->bhnte', inner, vc)
    chunk_kv = jnp.einsum('bhnsd,hs,bhnse->nbhde', kc, k_w, vc)

    def step(s, kv):
        return (g_c * s + kv).astype(s.dtype), s

    s_fin, s_prev = lax.scan(step, s0, chunk_kv)
    o_cross = jnp.einsum('bhntd,ht,nbhde->bhnte', qc, q_w, s_prev)
    return (o_intra + o_cross).reshape(b, h, l, dv), s_fin


def retention_mixer(parts_x, parts_c, gn, rope, ctx_out):
    q_x, k_x, v_x, g_x = parts_x
    q_c, k_c, v_c, g_c = parts_c
    scale = RET_DK ** -0.5
    qxh = apply_rope(split_heads(q_x, RET_HEADS), *rope)
    kxh = apply_rope(split_heads(k_x, RET_HEADS), *rope) * scale
    vxh = split_heads(v_x, RET_HEADS)
    qch = split_heads(q_c, RET_HEADS)
    kch = split_heads(k_c, RET_HEADS) * scale
    vch = split_heads(v_c, RET_HEADS)
    s0 = jnp.zeros((q_x.shape[0], RET_HEADS, RET_DK, RET_DV), q_x.dtype)
    ox = jnp.zeros_like(vxh)
    oc = jnp.zeros_like(vch)
    for r, offset in enumerate(RET_DECAY_OFFSETS):
        log_gamma = jnp.log1p(-jnp.exp2(-(5.0 + offset) - jnp.arange(RET_HEADS, dtype=F32)))
        fn = functools.partial(retention_direction, log_gamma=log_gamma)
        o_x, o_c = context_then_latent(fn, (qch, kch, vch), (qxh, kxh, vxh), s0, 2, r == 1)
        ox = ox + o_x
        oc = oc + o_c
    out_x = head_norm(ox, gn, True) * jax.nn.silu(g_x)
    return out_x, (head_norm(oc, gn, True) * jax.nn.silu(g_c) if ctx_out else None)


def hgrn2_direction(q, k, v, logf, s0):
    b, h, l, dk = q.shape
    dv = v.shape[-1]
    n = l // HG_CHUNK
    causal = jnp.tril(jnp.ones((HG_CHUNK, HG_CHUNK), bool))[:, :, None]

    def chunks(t):
        return t.reshape(b, h, n, HG_CHUNK, t.shape[-1]).transpose(2, 0, 1, 3, 4)

    def step(s, blk):
        qb, kb, vb, lf = blk
        cum = jnp.cumsum(lf.astype(F32), axis=2)
        pair = jnp.where(causal, jnp.exp(jnp.minimum(cum[:, :, :, None] - cum[:, :, None], 0.0)), 0.0)
        att = jnp.einsum('bhtd,bhsd,bhtsd->bhts', qb, kb, pair.astype(qb.dtype))
        o = att @ vb + jnp.einsum('bhtd,bhde->bhte', qb * jnp.exp(cum).astype(qb.dtype), s)
        end = cum[:, :, -1:]
        s_new = (jnp.exp(end[:, :, 0])[..., None].astype(s.dtype) * s
                 + jnp.einsum('bhsd,bhse->bhde', kb * jnp.exp(end - cum).astype(kb.dtype), vb))
        return s_new.astype(s.dtype), o

    s_fin, o = lax.scan(step, s0, (chunks(q), chunks(k), chunks(v), chunks(logf)))
    return o.transpose(1, 2, 0, 3, 4).reshape(b, h, l, dv), s_fin


def hgrn2_mixer(parts_x, parts_c, lb, gn, ctx_out):
    lbh = lb.reshape(HG_HEADS, 1, HG_DK)

    def prep(q, ff, fb, i):
        gates = []
        for fpre in (ff, fb):
            f = lbh + (1.0 - lbh) * jax.nn.sigmoid(split_heads(fpre, HG_HEADS).astype(F32))
            gates.append(((1.0 - f).astype(q.dtype), jnp.log(f).astype(q.dtype)))
        return split_heads(q, HG_HEADS), split_heads(i, HG_HEADS), gates

    q_x, ff_x, fb_x, i_x, g_x = parts_x
    q_c, ff_c, fb_c, i_c, g_c = parts_c
    qxh, ixh, gates_x = prep(q_x, ff_x, fb_x, i_x)
    qch, ich, gates_c = prep(q_c, ff_c, fb_c, i_c)
    s0 = jnp.zeros((q_x.shape[0], HG_HEADS, HG_DK, HG_DV), q_x.dtype)
    ox = jnp.zeros_like(ixh)
    oc = jnp.zeros_like(ich)
    for r in range(2):
        o_x, o_c = context_then_latent(hgrn2_direction,
                                       (qch, gates_c[r][0], ich, gates_c[r][1]),
                                       (qxh, gates_x[r][0], ixh, gates_x[r][1]), s0, 2, r == 1)
        ox = ox + o_x
        oc = oc + o_c
    out_x = head_norm(ox, gn, False) * jax.nn.silu(g_x)
    return out_x, (head_norm(oc, gn, False) * jax.nn.silu(g_c) if ctx_out else None)


def odd_mixer(hx, hc, w_in, ret_gn, lb, hg_gn, rope, ctx_out):
    px = split_cols(hx @ w_in, ODD_SIZES)
    pc = split_cols(hc @ w_in, ODD_SIZES)
    rx, rc = retention_mixer(px[:4], pc[:4], ret_gn, rope, ctx_out)
    gx, gc = hgrn2_mixer(px[4:], pc[4:], lb, hg_gn, ctx_out)
    out_x = jnp.concatenate([rx, gx], axis=-1)
    return out_x, (jnp.concatenate([rc, gc], axis=-1) if ctx_out else None)


def setup_inputs(seed: int = 0) -> dict:
    key = jax.random.key(seed)
    ks = iter(jax.random.split(key, 48))

    def nrm(shape, scale):
        return jax.random.normal(next(ks), shape, F32) * scale

    D = D_MODEL
    G, P, K = S5_GROUPS, S5_STATE, S5_GROUP
    return {
        'x': nrm((BATCH, SEQ, D), 1.0),
        'c': nrm((BATCH, D), 1.0),
        'ctx': nrm((BATCH, CTX_LEN, D), 1.0),
        'c_ctx': nrm((D,), 1.0),
        'w_mod': nrm((DEPTH, D, 6 * D), 0.5 * D ** -0.5),
        'b_mod': nrm((DEPTH, 6 * D), 0.01),
        'norm1_g': 1.0 + nrm((DEPTH, D), 0.01),
        'norm2_g': 1.0 + nrm((DEPTH, D), 0.01),
        'ffn_w1': nrm((DEPTH, D, D_FF), D ** -0.5),
        'ffn_w3': nrm((DEPTH, D, D_FF), D ** -0.5),
        'ffn_w2': nrm((DEPTH, D_FF, D), D_FF ** -0.5),
        'w_in_even': nrm((N_EVEN, D, EVEN_IN), D ** -0.5),
        'w_out_even': nrm((N_EVEN, EVEN_MIX, D), EVEN_MIX ** -0.5),
        's5_a_re': -0.5 + nrm((N_EVEN, 2, G, P), 0.01),
        's5_a_im': math.pi * jnp.arange(P, dtype=F32) + nrm((N_EVEN, 2, G, P), 0.01),
        's5_log_dt': jax.random.uniform(next(ks), (N_EVEN, 2, G), F32, math.log(1e-3), math.log(1e-1)),
        's5_b_re': nrm((N_EVEN, 2, G, P, K), (2 * K) ** -0.5),
        's5_b_im': nrm((N_EVEN, 2, G, P, K), (2 * K) ** -0.5),
        's5_c_re': nrm((N_EVEN, 2, G, K, P), (2 * P) ** -0.5),
        's5_c_im': nrm((N_EVEN, 2, G, K, P), (2 * P) ** -0.5),
        's5_d': nrm((N_EVEN, S5_WIDTH), 1.0),
        's5_w_glu': nrm((N_EVEN, S5_WIDTH, S5_WIDTH), S5_WIDTH ** -0.5),
        'mla_q_norm': 1.0 + nrm((N_EVEN, MLA_Q_RANK), 0.01),
        'mla_w_uq': nrm((N_EVEN, MLA_Q_RANK, MLA_HEADS * (MLA_NOPE + MLA_ROPE)), MLA_Q_RANK ** -0.5),
        'mla_kv_norm': 1.0 + nrm((N_EVEN, MLA_KV_RANK), 0.01),
        'mla_w_ukv': nrm((N_EVEN, MLA_KV_RANK, MLA_HEADS * (MLA_NOPE + MLA_V)), MLA_KV_RANK ** -0.5),
        'w_in_odd': nrm((N_ODD, D, ODD_IN), D ** -0.5),
        'w_out_odd': nrm((N_ODD, ODD_MIX, D), ODD_MIX ** -0.5),
        'ret_gn': 1.0 + nrm((N_ODD, RET_VW), 0.01),
        'hg_lb_logits': nrm((N_ODD + 1, HG_QK), 0.1),
        'hg_gn': 1.0 + nrm((N_ODD, HG_VW), 0.01),
        'final_norm': 1.0 + nrm((D,), 0.01),
    }


def reference(x, c, ctx, c_ctx, w_mod, b_mod, norm1_g, norm2_g, ffn_w1, ffn_w3, ffn_w2,
              w_in_even, w_out_even, s5_a_re, s5_a_im, s5_log_dt, s5_b_re, s5_b_im, s5_c_re, s5_c_im,
              s5_d, s5_w_glu, mla_q_norm, mla_w_uq, mla_kv_norm, mla_w_ukv,
              w_in_odd, w_out_odd, ret_gn, hg_lb_logits, hg_gn, final_norm):
    b, l, _ = x.shape
    rows = l // GRID_W
    rope_mla = axial_rope(rows, MLA_ROPE)
    rope_ret = axial_rope(rows, RET_DK)
    hg_lb = jnp.cumsum(jax.nn.softmax(hg_lb_logits.astype(F32), axis=0), axis=0)[:N_ODD]
    for li in range(DEPTH):
        last = li == DEPTH - 1
        mx = (jax.nn.silu(c) @ w_mod[li] + b_mod[li])[:, None, :]
        mc = (jax.nn.silu(c_ctx) @ w_mod[li] + b_mod[li])[None, None, :]
        sx1, ax1, gx1, sx2, ax2, gx2 = jnp.split(mx, 6, axis=-1)
        sc1, ac1, gc1, sc2, ac2, gc2 = jnp.split(mc, 6, axis=-1)
        hx = rms_norm(x, norm1_g[li]) * (1.0 + ax1) + sx1
        hc = rms_norm(ctx, norm1_g[li]) * (1.0 + ac1) + sc1
        j = li // 2
        if li % 2 == 0:
            mix_x, mix_c = even_mixer(hx, hc, w_in_even[j], s5_a_re[j], s5_a_im[j], s5_log_dt[j],
                                      s5_b_re[j], s5_b_im[j], s5_c_re[j], s5_c_im[j], s5_d[j], s5_w_glu[j],
                                      mla_q_norm[j], mla_w_uq[j], mla_kv_norm[j], mla_w_ukv[j],
                                      rope_mla, not last)
            w_out = w_out_even[j]
        else:
            mix_x, mix_c = odd_mixer(hx, hc, w_in_odd[j], ret_gn[j], hg_lb[j], hg_gn[j], rope_ret, not last)
            w_out = w_out_odd[j]
        x = x + gx1 * (mix_x @ w_out)
        x = x + gx2 * swiglu(rms_norm(x, norm2_g[li]) * (1.0 + ax2) + sx2, ffn_w1[li], ffn_w3[li], ffn_w2[li])
        if not last:
            ctx = ctx + gc1 * (mix_c @ w_out)
            ctx = ctx + gc2 * swiglu(rms_norm(ctx, norm2_g[li]) * (1.0 + ac2) + sc2,
                                     ffn_w1[li], ffn_w3[li], ffn_w2[li])
    return rms_norm(x, final_norm)
```

```python
import numpy as np
from contextlib import ExitStack
import concourse.bass as bass
import concourse.mybir as mybir
from concourse.bass_utils import run_bass_kernel_spmd

F32 = mybir.dt.float32
BF16 = mybir.dt.bfloat16
I32 = mybir.dt.int32
AF = mybir.ActivationFunctionType
ALU = mybir.AluOpType
AX = mybir.AxisListType


class Buf:
    __slots__ = ("t", "lw", "rd", "name")

    def __init__(self, t, name=""):
        self.t = t
        self.lw = None
        self.rd = []
        self.name = name


class Emit:
    NDS = 24

    def __init__(self, nc):
        self.nc = nc
        self.es = ExitStack()
        self.eng = {"pe": nc.tensor, "act": nc.scalar, "dve": nc.vector, "pool": nc.gpsimd, "sp": nc.sync}
        self.sem = {e: self.es.enter_context(nc.semaphore("s_" + e)) for e in ("pe", "act", "dve", "pool")}
        self.cnt = {e: 0 for e in self.sem}
        self.dsem = [self.es.enter_context(nc.semaphore("d%d" % i)) for i in range(self.NDS)]
        self.dcnt = [0] * self.NDS
        self.dnext = 0
        self.seen = {e: {} for e in self.eng}
        self.outs = []
        self.psb = [self.ps("psb%d" % i, [128, 512], F32) for i in range(6)]
        self.psx = [self.ps("psx%d" % i, [128, 512], F32) for i in range(2)]
        self.psi = 0
        self.n_inst = 0

    def dram_in(self, name, shape, dt):
        return Buf(self.nc.dram_tensor(name, list(shape), dt, kind="ExternalInput"), name)

    def dram_out(self, name, shape, dt):
        return Buf(self.nc.dram_tensor(name, list(shape), dt, kind="ExternalOutput"), name)

    def dram_tmp(self, name, shape, dt):
        return Buf(self.nc.dram_tensor(name, list(shape), dt, kind="Internal"), name)

    def sb(self, name, shape, dt):
        self.n_sb = getattr(self, "n_sb", 0) + 1
        return Buf(self.es.enter_context(self.nc.sbuf_tensor("sb%d_%s" % (self.n_sb, name), list(shape), dt)), name)

    def ps(self, name, shape, dt):
        return Buf(self.es.enter_context(self.nc.psum_tensor(name, list(shape), dt)), name)

    def ps_get(self):
        b = self.psb[self.psi % len(self.psb)]
        self.psi += 1
        return b

    def _wait(self, e, key, val):
        if self.seen[e].get(key, 0) >= val:
            return
        self.seen[e][key] = val
        s = self.sem[key] if isinstance(key, str) else self.dsem[key]
        self.eng[e].wait_ge(s, val)

    def _deps(self, e, R, W):
        deps = {}
        for r in R:
            if r.lw is not None:
                k, v = r.lw
                deps[k] = max(deps.get(k, 0), v)
        for w in W:
            if w.lw is not None:
                k, v = w.lw
                deps[k] = max(deps.get(k, 0), v)
            for k, v in w.rd:
                deps[k] = max(deps.get(k, 0), v)
        for k, v in deps.items():
            if k == "pe" and e == "pe":
                continue
            self._wait(e, k, v)

    def _record(self, ev, R, W):
        for w in W:
            w.lw = ev
            w.rd = []
        for r in R:
            if r in W:
                continue
            r.rd.append(ev)
            if len(r.rd) > 12:
                m = {}
                for k, v in r.rd:
                    m[k] = max(m.get(k, 0), v)
                r.rd = list(m.items())

    def op(self, e, fn, R=(), W=()):
        self._deps(e, R, W)
        inst = fn(self.eng[e])
        self.cnt[e] += 1
        inst.then_inc(self.sem[e], 1)
        self._record((e, self.cnt[e]), R, W)
        self.n_inst += 1
        return inst

    def dma(self, q, out, in_, R=(), W=(), **kw):
        i = self.dnext
        self.dnext = (self.dnext + 1) % self.NDS
        if self.dcnt[i] > 0:
            self._wait(q, i, 16 * self.dcnt[i])
        self._deps(q, R, W)
        inst = self.eng[q].dma_start(out=out, in_=in_, **kw)
        self.dcnt[i] += 1
        inst.then_inc(self.dsem[i], 16)
        ev = (i, 16 * self.dcnt[i])
        self._record(ev, R, W)
        self.n_inst += 1
        return ev

    def barrier(self):
        for e in ("pe", "act", "dve", "pool", "sp"):
            for i in range(self.NDS):
                if self.dcnt[i] > 0:
                    self._wait(e, i, 16 * self.dcnt[i])
            for e2 in ("pe", "act", "dve", "pool"):
                if e2 != e and self.cnt[e2] > 0:
                    self._wait(e, e2, self.cnt[e2])

    def scope(self):
        K = self

        class _S:
            def __enter__(s2):
                s2.old = K.es
                K.es = ExitStack()
                return s2

            def __exit__(s2, *a):
                K.barrier()
                K.es.close()
                K.es = s2.old
                return False
        return _S()

    def finish(self):
        for i in range(self.NDS):
            if self.dcnt[i] > 0:
                self._wait("sp", i, 16 * self.dcnt[i])
        for e in ("pe", "act", "dve", "pool"):
            if self.cnt[e] > 0:
                self._wait("sp", e, self.cnt[e])
        self.es.close()


D = 1024
NCH = 8
NT = 2112
SLABS = [(0, 64, 1)] + [(64 + 512 * i, 512, 0) for i in range(4)]
DFF = 2816
EPS = 1e-6


def pslice(ap, lo, hi):
    return ap[lo:hi]


class Tok:
    def __init__(self, K):
        self.K = K
        nc = K.nc
        self.ones = K.sb("ones_f", [128, 128], F32)
        K.op("dve", lambda e: e.memset(self.ones.t[:, :], 1.0), W=[self.ones])
        self.onesb = K.sb("ones_b", [128, 128], BF16)
        K.op("dve", lambda e: e.memset(self.onesb.t[:, :], 1.0), W=[self.onesb])
        self.eps = K.sb("eps_c", [128, 1], F32)
        K.op("dve", lambda e: e.memset(self.eps.t[:, :], EPS), W=[self.eps])
        self.sq = [K.sb("sq%d" % i, [128, 512], BF16) for i in range(3)]
        self.sqi = 0
        self.rstd = [K.sb("rstd%d" % i, [128, 512], F32) for i in range(2)]
        self.rsi = 0
        self.tmp = [K.sb("ttmp%d" % i, [128, 512], F32) for i in range(3)]
        self.tmi = 0

    def gettmp(self):
        self.tmi += 1
        return self.tmp[self.tmi % len(self.tmp)]

    def load(self, name, dram, shape, dt=F32, q="sp", src=None):
        b = self.K.sb(name, shape, dt)
        sl = tuple(slice(None) for _ in shape)
        self.K.dma(q, b.t[sl], src if src is not None else dram.t.ap(), R=[dram], W=[b])
        return b

    def rms_rstd(self, src, nchunks, c0, w, nfeat, srcchunk0=0):
        K = self.K
        ps = K.ps_get()
        for c in range(nchunks):
            sq = self.sq[self.sqi % 3]
            self.sqi += 1
            K.op("act", lambda e: e.activation(sq.t[:, 0:w], src.t[:, srcchunk0 + c, c0:c0 + w], AF.Square), R=[src], W=[sq])
            K.op("pe", lambda e: e.matmul(ps.t[:, 0:w], self.onesb.t[:, :], sq.t[:, 0:w], start=(c == 0), stop=(c == nchunks - 1)), R=[sq, self.onesb], W=[ps])
        r = self.rstd[self.rsi % 2]
        self.rsi += 1
        K.op("act", lambda e: e.activation(r.t[:, 0:w], ps.t[:, 0:w], AF.Ln, bias=self.eps.t[:, 0:1], scale=1.0 / nfeat), R=[ps, self.eps], W=[r])
        K.op("act", lambda e: e.activation(r.t[:, 0:w], r.t[:, 0:w], AF.Exp, scale=-0.5), R=[r], W=[r])
        return r

    def modvec(self, cT, w_mod, b_modT):
        K = self.K
        sc = K.sb("silc", [128, 8, 2], F32)
        K.op("act", lambda e: e.activation(sc.t[:, :, :], cT.t[:, :, :], AF.Silu), R=[cT], W=[sc])
        mod = K.sb("modv", [128, 2, 48], F32)
        wv = w_mod.t.ap().rearrange("(c p) n -> p c n", p=128)
        wbufs = [K.sb("wmod%d" % i, [128, 8, 512], F32) for i in range(2)]
        for g in range(12):
            wb = wbufs[g % 2]
            K.dma("sp" if g % 2 == 0 else "act", wb.t[:, :, :], wv[:, :, g * 512:(g + 1) * 512], R=[w_mod], W=[wb])
            ps = K.ps_get()
            for oc in range(4):
                for kc in range(8):
                    K.op("pe", lambda e: e.matmul(ps.t[:, oc * 2:oc * 2 + 2], wb.t[:, kc, oc * 128:(oc + 1) * 128], sc.t[:, kc, :], start=(kc == 0), stop=(kc == 7)), R=[wb, sc], W=[ps])
            for oc in range(4):
                o = g * 4 + oc
                K.op("dve", lambda e: e.tensor_scalar(mod.t[:, :, o], ps.t[:, oc * 2:oc * 2 + 2], b_modT.t[:, o:o + 1], None, op0=ALU.add), R=[ps, b_modT], W=[mod])
        return mod

    def norm_mod(self, x, dst, mod, gT, jshift, jscale, tag):
        K = self.K
        sc = K.sb("nsc" + tag, [128, 2, 8], F32)
        for w in range(2):
            K.op("dve", lambda e: e.scalar_tensor_tensor(sc.t[:, w, :], mod.t[:, w, jscale * 8:jscale * 8 + 8], 1.0, gT.t[:, :], op0=ALU.add, op1=ALU.mult), R=[mod, gT], W=[sc])
        for (c0, w, isctx) in SLABS:
            r = self.rms_rstd(x, 8, c0, w, D)
            for c in range(8):
                t = self.gettmp()
                K.op("dve", lambda e: e.tensor_tensor(t.t[:, 0:w], x.t[:, c, c0:c0 + w], r.t[:, 0:w], op=ALU.mult), R=[x, r], W=[t])
                K.op("act", lambda e: e.activation(dst.t[:, c, c0:c0 + w], t.t[:, 0:w], AF.Identity, bias=mod.t[:, isctx, jshift * 8 + c:jshift * 8 + c + 1], scale=sc.t[:, isctx, c:c + 1]), R=[t, mod, sc], W=[dst])

    def rope(self, si, c0, w, x1, x2, col, P, cosd, sind, put, dst1, dst2):
        K = self.K
        if not hasattr(self, "rc"):
            self.rc = [K.sb("ropec%d" % i, [128, 2, 512], F32) for i in range(2)]
            self.rta = K.sb("ropea", [128, 512], F32)
            self.rtb = K.sb("ropeb", [128, 512], F32)
            self.rci = 0
        c = self.rc[self.rci % 2]
        self.rci += 1
        ta, tb = self.rta, self.rtb
        K.dma("act", c.t[0:P, 0, 0:w], cosd.t.ap()[0:P, c0:c0 + w], R=[cosd], W=[c])
        K.dma("act", c.t[0:P, 1, 0:w], sind.t.ap()[0:P, c0:c0 + w], R=[sind], W=[c])
        K.op("dve", lambda e: e.tensor_tensor(ta.t[0:P, 0:w], x1.t[0:P, col:col + w], c.t[0:P, 0, 0:w], op=ALU.mult), R=[c, x1], W=[ta])
        K.op("dve", lambda e: e.tensor_tensor(tb.t[0:P, 0:w], x2.t[0:P, col:col + w], c.t[0:P, 1, 0:w], op=ALU.mult), R=[c, x2], W=[tb])
        put("dve", P, w, lambda b: K.op("dve", lambda e: e.tensor_tensor(b.t[0:P, 0:w], ta.t[0:P, 0:w], tb.t[0:P, 0:w], op=ALU.subtract), R=[ta, tb], W=[b]), dst1)
        K.op("dve", lambda e: e.tensor_tensor(ta.t[0:P, 0:w], x1.t[0:P, col:col + w], c.t[0:P, 1, 0:w], op=ALU.mult), R=[c, x1], W=[ta])
        K.op("dve", lambda e: e.tensor_tensor(tb.t[0:P, 0:w], x2.t[0:P, col:col + w], c.t[0:P, 0, 0:w], op=ALU.mult), R=[c, x2], W=[tb])
        put("dve", P, w, lambda b: K.op("dve", lambda e: e.tensor_tensor(b.t[0:P, 0:w], ta.t[0:P, 0:w], tb.t[0:P, 0:w], op=ALU.add), R=[ta, tb], W=[b]), dst2)

    def proj(self, src, nk, w_dram, col0, ncols, epi, group=512, tag="w", krows=128):
        K = self.K
        wv = w_dram.t.ap().rearrange("(c p) n -> p c n", p=krows)
        ngroups = (ncols + group - 1) // group
        if not hasattr(self, "wb_" + tag):
            setattr(self, "wb_" + tag, [K.sb("wb_%s%d" % (tag, i), [128, nk, group], BF16) for i in range(2)])
            setattr(self, "wbi_" + tag, 0)
        bufs = getattr(self, "wb_" + tag)
        oc = 0
        for g in range(ngroups):
            gi = getattr(self, "wbi_" + tag)
            setattr(self, "wbi_" + tag, gi + 1)
            wb = bufs[gi % 2]
            gc0 = col0 + g * group
            gw = min(group, ncols - g * group)
            K.dma("pool", wb.t[0:krows, :, 0:gw], wv[:, :, gc0:gc0 + gw], R=[w_dram], W=[wb])
            nm = (gw + 127) // 128
            for si, (c0, w, isctx) in enumerate(SLABS):
                for mi in range(nm):
                    m = min(128, gw - mi * 128)
                    ps = K.ps_get()
                    for kc in range(nk):
                        K.op("pe", lambda e: e.matmul(ps.t[0:m, 0:w], wb.t[0:krows, kc, mi * 128:mi * 128 + m], src.t[0:krows, kc, c0:c0 + w], start=(kc == 0), stop=(kc == nk - 1)), R=[wb, src], W=[ps])
                    epi(oc + mi, m, si, c0, w, isctx, ps)
            oc += nm


def stage_out(K, T, dram, dst_ap_fn, dt):
    bufs = [K.sb("stg_%s%d" % (dram.name, i), [128, 512], dt) for i in range(3)]
    st = {"i": 0}

    def put(eng, m, w, ps_or_fn, dst_ap):
        b = bufs[st["i"] % 3]
        st["i"] += 1
        ps_or_fn(b)
        K.dma("sp", dst_ap, b.t[0:m, 0:w], R=[b], W=[dram])
    return put


def ta_even(K, T, hT, io):
    uT, qT, kT, vT, krT = io["uT"], io["qT"], io["kT"], io["vT"], io["krT"]
    w_in, w_uq, w_ukv = io["w_in_e"], io["w_uq"], io["w_ukv"]
    qn = T.load("qn", io["qnT"], [128, 2])
    kvn = T.load("kvn", io["kvnT"], [128, 1])
    cq = K.sb("cq", [128, 3, NT], F32)
    kr = K.sb("kr12", [16, 2, NT], F32)
    put_u = stage_out(K, T, uT, None, F32)
    put_b = stage_out(K, T, qT, None, BF16)

    def epi_in(oc, m, si, c0, w, isctx, ps):
        if oc < 4:
            put_u("act", 128, w, lambda b: K.op("act", lambda e: e.copy(b.t[:, 0:w], ps.t[:, 0:w]), R=[ps], W=[b]), uT.t.ap()[:, oc, c0:c0 + w])
        else:
            K.op("dve", lambda e: e.tensor_copy(cq.t[:, oc - 4, c0:c0 + w], ps.t[:, 0:w]), R=[ps], W=[cq])
    T.proj(hT, 8, w_in, 0, 896, epi_in, tag="win")

    def epi_kr(j):
        def f(oc, m, si, c0, w, isctx, ps):
            K.op("dve", lambda e: e.tensor_copy(kr.t[:, j, c0:c0 + w], ps.t[0:16, 0:w]), R=[ps], W=[kr])
        return f
    T.proj(hT, 8, w_in, 896, 16, epi_kr(0), tag="win")
    T.proj(hT, 8, w_in, 912, 16, epi_kr(1), tag="win")

    cqn = K.sb("cqn", [128, 3, NT], BF16)
    for (c0, w, isctx) in SLABS:
        r = T.rms_rstd(cq, 2, c0, w, 256)
        for c in range(2):
            K.op("dve", lambda e: e.scalar_tensor_tensor(cqn.t[:, c, c0:c0 + w], cq.t[:, c, c0:c0 + w], qn.t[:, c:c + 1], r.t[:, 0:w], op0=ALU.mult, op1=ALU.mult), R=[cq, qn, r], W=[cqn])
        r = T.rms_rstd(cq, 1, c0, w, 128, srcchunk0=2)
        K.op("dve", lambda e: e.scalar_tensor_tensor(cqn.t[:, 2, c0:c0 + w], cq.t[:, 2, c0:c0 + w], kvn.t[:, 0:1], r.t[:, 0:w], op0=ALU.mult, op1=ALU.mult), R=[cq, kvn, r], W=[cqn])

    q1 = [K.sb("q1t%d" % i, [128, 512], F32) for i in range(2)]
    q2 = [K.sb("q2t%d" % i, [128, 512], F32) for i in range(2)]

    def epi_q(oc, m, si, c0, w, isctx, ps):
        if oc < 4:
            put_b("act", 128, w, lambda b: K.op("act", lambda e: e.copy(b.t[:, 0:w], ps.t[:, 0:w]), R=[ps], W=[b]), qT.t.ap()[:, oc, c0:c0 + w])
        elif oc == 4:
            K.op("act", lambda e: e.copy(q1[si % 2].t[:, 0:w], ps.t[:, 0:w]), R=[ps], W=[q1[si % 2]])
        else:
            K.op("act", lambda e: e.copy(q2[si % 2].t[:, 0:w], ps.t[:, 0:w]), R=[ps], W=[q2[si % 2]])
            T.rope(si, c0, w, q1[si % 2], q2[si % 2], 0, 128, io["cosq"], io["sinq"], put_b, qT.t.ap()[:, 4, c0:c0 + w], qT.t.ap()[:, 5, c0:c0 + w])
    T.proj(cqn, 2, w_uq, 0, 768, epi_q, tag="wuq")

    def epi_kv(oc, m, si, c0, w, isctx, ps):
        dst = kT if oc < 4 else vT
        put_b("act", 128, w, lambda b: K.op("act", lambda e: e.copy(b.t[:, 0:w], ps.t[:, 0:w]), R=[ps], W=[b]), dst.t.ap()[:, oc % 4, c0:c0 + w])
    T.proj(Buf3(cqn, 2), 1, w_ukv, 0, 1024, epi_kv, tag="wukv")
    k1 = K.sb("kr1s", [16, 512], F32)
    k2 = K.sb("kr2s", [16, 512], F32)
    for si, (c0, w, isctx) in enumerate(SLABS):
        K.op("act", lambda e: e.copy(k1.t[:, 0:w], kr.t[:, 0, c0:c0 + w]), R=[kr], W=[k1])
        K.op("act", lambda e: e.copy(k2.t[:, 0:w], kr.t[:, 1, c0:c0 + w]), R=[kr], W=[k2])
        T.rope(si, c0, w, k1, k2, 0, 16, io["cosq"], io["sinq"], put_b, krT.t.ap()[:, 0, c0:c0 + w], krT.t.ap()[:, 1, c0:c0 + w])


class _ChunkView:
    def __init__(self, t, off):
        self._t = t
        self._off = off

    def __getitem__(self, idx):
        p, c, n = idx
        return self._t[p, c + self._off, n]


def Buf3(buf, off):
    b = Buf(_ChunkView(buf.t, off), buf.name)
    return _Alias(buf, b.t)


class _Alias:
    def __init__(self, parent, t):
        object.__setattr__(self, "_p", parent)
        object.__setattr__(self, "t", t)

    def __getattr__(self, k):
        return getattr(object.__getattribute__(self, "_p"), k)

    def __setattr__(self, k, v):
        if k == "t":
            object.__setattr__(self, k, v)
        else:
            setattr(object.__getattribute__(self, "_p"), k, v)


NKEY = 8448
NQ = 8448
ATT_SCALE = 96 ** -0.5


def h_even_attn(K, io, n_units=2):
    QTd, KTd, Vd, OTd = io["QT"], io["KT"], io["V"], io["OT"]
    onesb = K.sb("a_ones", [128, 1], BF16)
    K.op("dve", lambda e: e.memset(onesb.t[:, :], 1.0), W=[onesb])
    QT = K.sb("a_QT", [128, NQ], BF16)
    KT = K.sb("a_KT", [128, NKEY], BF16)
    V = K.sb("a_V", [128, 66, 128], BF16)
    sq = [K.sb("a_sq%d" % i, [128, 512], BF16) for i in range(2)]
    qsq = K.sb("a_qsq", [1, NQ], F32)
    ksq = K.sb("a_ksq", [1, NKEY], F32)
    kmax = K.sb("a_kmax", [1, 1], F32)
    negm = K.sb("a_negm", [1, NQ], BF16)
    PT = [K.sb("a_PT%d" % i, [128, 512], BF16) for i in range(3)]
    stg = [K.sb("a_stg%d" % i, [128, 512], F32) for i in range(2)]
    for u in range(n_units):
        K.op("dve", lambda e: e.memset(KT.t[:, :], 1.0), W=[KT])
        K.op("pool", lambda e: e.memset(V.t[:, :, :], 1.0), W=[V])
        K.dma("sp", QT.t[0:96, :], QTd.t.ap()[u, 0:96, :], R=[QTd], W=[QT])
        K.dma("act", KT.t[0:96, :], KTd.t.ap()[u, 0:96, :], R=[KTd], W=[KT])
        K.dma("sp", V.t[:, :, 0:64], Vd.t.ap()[u], R=[Vd], W=[V])
        for (src, dst, n) in ((QT, qsq, NQ), (KT, ksq, NKEY)):
            i = 0
            for c0 in range(0, n, 512):
                w = min(512, n - c0)
                s = sq[i % 2]
                i += 1
                K.op("act", lambda e: e.activation(s.t[0:96, 0:w], src.t[0:96, c0:c0 + w], AF.Square), R=[src], W=[s])
                ps = K.ps_get()
                K.op("pe", lambda e: e.matmul(ps.t[0:1, 0:w], onesb.t[0:96, 0:1], s.t[0:96, 0:w], start=True, stop=True), R=[onesb, s], W=[ps])
                K.op("dve", lambda e: e.tensor_copy(dst.t[0:1, c0:c0 + w], ps.t[0:1, 0:w]), R=[ps], W=[dst])
        K.op("dve", lambda e: e.tensor_reduce(kmax.t[0:1, 0:1], ksq.t[0:1, :], axis=AX.X, op=ALU.max), R=[ksq], W=[kmax])
        K.op("dve", lambda e: e.tensor_scalar(qsq.t[0:1, :], qsq.t[0:1, :], kmax.t[0:1, 0:1], None, op0=ALU.mult), R=[qsq, kmax], W=[qsq])
        K.op("act", lambda e: e.activation(qsq.t[0:1, :], qsq.t[0:1, :], AF.Sqrt), R=[qsq], W=[qsq])
        K.op("dve", lambda e: e.tensor_scalar(negm.t[0:1, :], qsq.t[0:1, :], -1.0, None, op0=ALU.mult), R=[qsq], W=[negm])
        K.dma("sp", QT.t[96:97, :], negm.t[0:1, :], R=[negm], W=[QT])
        pi = 0
        slabs = [(512 * i, 512, 66) for i in range(16)] + [(8192, 256, 2)]
        for si, (c0, w, nkt) in enumerate(slabs):
            psO = K.psx[si % 2]
            for kt in range(nkt):
                ps1 = K.ps_get()
                K.op("pe", lambda e: e.matmul(ps1.t[:, 0:w], KT.t[0:97, kt * 128:(kt + 1) * 128], QT.t[0:97, c0:c0 + w], start=True, stop=True), R=[KT, QT], W=[ps1])
                p = PT[pi % 3]
                pi += 1
                K.op("act", lambda e: e.activation(p.t[:, 0:w], ps1.t[:, 0:w], AF.Exp, scale=ATT_SCALE), R=[ps1], W=[p])
                K.op("pe", lambda e: e.matmul(psO.t[:, 0:w], V.t[:, kt, :], p.t[:, 0:w], start=(kt == 0), stop=(kt == nkt - 1)), R=[V, p], W=[psO])
            s = stg[si % 2]
            K.op("dve", lambda e: e.tensor_copy(s.t[:, 0:w], psO.t[:, 0:w]), R=[psO], W=[s])
            K.dma("sp", OTd.t.ap()[u, :, c0:c0 + w], s.t[:, 0:w], R=[s], W=[OTd])


def decl_ta_even(K):
    io = {}
    io["w_in_e"] = K.dram_in("w_in_e", [1024, 928], F32)
    io["qnT"] = K.dram_in("qnT", [128, 2], F32)
    io["kvnT"] = K.dram_in("kvnT", [128, 1], F32)
    io["w_uq"] = K.dram_in("w_uq", [256, 768], F32)
    io["w_ukv"] = K.dram_in("w_ukv", [128, 1024], F32)
    io["cosq"] = K.dram_in("cosq", [128, NT], F32)
    io["sinq"] = K.dram_in("sinq", [128, NT], F32)
    io["uT"] = K.dram_out("uT", [128, 4, NT], F32)
    io["qT"] = K.dram_out("qT", [128, 6, NT], BF16)
    io["kT"] = K.dram_out("kT", [128, 4, NT], BF16)
    io["vT"] = K.dram_out("vT", [128, 4, NT], BF16)
    io["krT"] = K.dram_out("krT", [16, 2, NT], BF16)
    return io


def decl_mod(K, tag):
    io = {}
    io["cT"] = K.dram_in("cT" + tag, [128, 8, 2], F32)
    io["w_mod"] = K.dram_in("w_mod" + tag, [1024, 6144], F32)
    io["b_modT"] = K.dram_in("b_modT" + tag, [128, 48], F32)
    io["n1gT"] = K.dram_in("n1gT" + tag, [128, 8], F32)
    io["n2gT"] = K.dram_in("n2gT" + tag, [128, 8], F32)
    return io


def build_tok(post, pre, final=False):
    nc = bass.Bass("TRN2", target_bir_lowering=False)
    K = Emit(nc)
    T = Tok(K)
    xin = K.dram_in("xT_in", [128, 8, NT], F32)
    hT = K.sb("hT", [128, 8, NT], BF16)
    with K.scope():
        x = K.sb("xT", [128, 8, NT], F32)
        for c in range(8):
            K.dma("sp" if c % 2 == 0 else "act", x.t[:, c, :], xin.t.ap()[:, c, :], R=[xin], W=[x])
        if post is not None:
            iom = decl_mod(K, "_a")
            iop = decl_tb(K, post)
            with K.scope():
                cT = T.load("cT_a", iom["cT"], [128, 8, 2])
                bm = T.load("bm_a", iom["b_modT"], [128, 48])
                mod_a = K.sb("mod_a", [128, 2, 48], F32)
                with K.scope():
                    m = T.modvec(cT, iom["w_mod"], bm)
                    K.op("dve", lambda e: e.tensor_copy(mod_a.t[:, :, :], m.t[:, :, :]), R=[m], W=[mod_a])
                n2g = T.load("n2g_a", iom["n2gT"], [128, 8])
                tb_phase(K, T, x, hT, mod_a, n2g, iop, post)
            xout = K.dram_out("xT_out", [128, 8, NT], F32)
            for c in range(8):
                K.dma("sp", xout.t.ap()[:, c, :], x.t[:, c, :], R=[x], W=[xout])
            if final:
                fio = K.dram_in("fnT", [128, 8], F32)
                fo = K.dram_out("yT", [128, 8, NT], F32)
                fn = T.load("fn", fio, [128, 8])
                stgs = [K.sb("fstg%d" % i, [128, 512], F32) for i in range(3)]
                i = 0
                for (c0, w, isctx) in SLABS:
                    r = T.rms_rstd(x, 8, c0, w, D)
                    for c in range(8):
                        s = stgs[i % 3]
                        i += 1
                        K.op("dve", lambda e: e.scalar_tensor_tensor(s.t[:, 0:w], x.t[:, c, c0:c0 + w], fn.t[:, c:c + 1], r.t[:, 0:w], op0=ALU.mult, op1=ALU.mult), R=[x, fn, r], W=[s])
                        K.dma("sp", fo.t.ap()[:, c, c0:c0 + w], s.t[:, 0:w], R=[s], W=[fo])
        if pre is not None:
            iom2 = decl_mod(K, "_b")
            with K.scope():
                cT = T.load("cT_b", iom2["cT"], [128, 8, 2])
                bm = T.load("bm_b", iom2["b_modT"], [128, 48])
                mod_b = K.sb("mod_b", [128, 2, 48], F32)
                with K.scope():
                    m = T.modvec(cT, iom2["w_mod"], bm)
                    K.op("dve", lambda e: e.tensor_copy(mod_b.t[:, :, :], m.t[:, :, :]), R=[m], W=[mod_b])
                n1g = T.load("n1g_b", iom2["n1gT"], [128, 8])
                T.norm_mod(x, hT, mod_b, n1g, 0, 1, "1")
    if pre == "even":
        io = decl_ta_even(K)
        with K.scope():
            ta_even(K, T, hT, io)
    elif pre == "odd":
        io = decl_ta_odd(K)
        with K.scope():
            ta_odd(K, T, hT, io)
    K.finish()
    return nc


def build_h_even():
    nc = bass.Bass("TRN2", target_bir_lowering=False)
    K = Emit(nc)
    io = {}
    io["QT"] = K.dram_in("QT", [2, 96, NQ], BF16)
    io["KT"] = K.dram_in("KT", [2, 96, NKEY], BF16)
    io["V"] = K.dram_in("V", [2, 128, 66, 64], BF16)
    io["OT"] = K.dram_out("OT", [2, 128, NQ], F32)
    with K.scope():
        h_even_attn(K, io)
    ios = decl_s5(K)
    with K.scope():
        h_even_s5(K, ios)
    K.finish()
    return nc


def to_fm(a):
    n, f = a.shape
    return np.ascontiguousarray(a.reshape(n, f // 128, 128).transpose(2, 1, 0))


def from_fm(t):
    p, c, n = t.shape
    return np.ascontiguousarray(t.transpose(2, 1, 0).reshape(n, c * p))


def vec_fm(v):
    return np.ascontiguousarray(v.reshape(-1, 128).T)


def rope_tables(dim):
    l = np.arange(8192)
    r = (l // 64).astype(np.float32)
    col = (l % 64).astype(np.float32)
    quarter = dim // 4
    inv = (np.float32(10000.0) ** (-np.arange(quarter, dtype=np.float32) / np.float32(quarter))).astype(np.float32)
    ang = np.concatenate([r[:, None] * inv, col[:, None] * inv], axis=-1).astype(np.float32)
    return np.cos(ang).astype(np.float32), np.sin(ang).astype(np.float32)


def core_rope(dim, reps, r):
    cos, sin = rope_tables(dim)
    h = dim // 2
    c = np.ones((reps * h, NT), np.float32)
    s = np.zeros((reps * h, NT), np.float32)
    c[:, 64:] = np.tile(cos[2048 * r:2048 * (r + 1)].T, (reps, 1))
    s[:, 64:] = np.tile(sin[2048 * r:2048 * (r + 1)].T, (reps, 1))
    return c, s


def perm_even(w_in, w_uq, w_ukv):
    ci = np.concatenate([np.arange(896), 896 + 2 * np.arange(16), 897 + 2 * np.arange(16)])
    nope = np.concatenate([96 * h + np.arange(64) for h in range(8)])
    r1 = np.concatenate([96 * h + 64 + 2 * np.arange(16) for h in range(8)])
    r2 = r1 + 1
    kn = np.concatenate([128 * h + np.arange(64) for h in range(8)])
    vv = kn + 64
    return (np.ascontiguousarray(w_in[:, ci]), np.ascontiguousarray(w_uq[:, np.concatenate([nope, r1, r2])]),
            np.ascontiguousarray(w_ukv[:, np.concatenate([kn, vv])]))


def mod_inputs(inp, li, b, tag):
    cT = np.stack([vec_fm(inp["c"][b]), vec_fm(inp["c_ctx"])], axis=-1)
    return {"cT" + tag: np.ascontiguousarray(cT), "w_mod" + tag: inp["w_mod"][li],
            "b_modT" + tag: vec_fm(inp["b_mod"][li]), "n1gT" + tag: vec_fm(inp["norm1_g"][li]),
            "n2gT" + tag: vec_fm(inp["norm2_g"][li])}


def ta_even_inputs(inp, j, r):
    w_in, w_uq, w_ukv = perm_even(inp["w_in_even"][j], inp["mla_w_uq"][j], inp["mla_w_ukv"][j])
    c, s = core_rope(32, 8, r)
    return {"w_in_e": w_in, "w_uq": w_uq, "w_ukv": w_ukv, "qnT": vec_fm(inp["mla_q_norm"][j]),
            "kvnT": vec_fm(inp["mla_kv_norm"][j]), "cosq": c, "sinq": s}


def attn_inputs(res, last=False):
    import ml_dtypes
    bf = ml_dtypes.bfloat16
    maps = []
    for core in range(8):
        QT = np.zeros((2, 96, NQ), bf)
        KT = np.zeros((2, 96, NKEY), bf)
        V = np.zeros((2, 128, 66, 64), bf)
        for ui in range(2):
            unit = core * 2 + ui
            b, h = unit // 8, unit % 8
            ch, ro = h // 2, (h % 2) * 64
            q = np.concatenate([np.concatenate([res[4 * b + r]["qT"][ro:ro + 64, ch, :],
                                                res[4 * b + r]["qT"][h * 16:h * 16 + 16, 4, :],
                                                res[4 * b + r]["qT"][h * 16:h * 16 + 16, 5, :]], axis=0) for r in range(4)], axis=1)
            k = np.concatenate([np.concatenate([res[4 * b + r]["kT"][ro:ro + 64, ch, :],
                                                res[4 * b + r]["krT"][:, 0, :], res[4 * b + r]["krT"][:, 1, :]], axis=0) for r in range(4)], axis=1)
            v = np.concatenate([res[4 * b + r]["vT"][ro:ro + 64, ch, :] for r in range(4)], axis=1)
            q = q.reshape(96, 4, NT)
            k = k.reshape(96, 4, NT)
            v = v.reshape(64, 4, NT)
            QT[ui] = np.concatenate([q[:, :, 64:].reshape(96, 8192), q[:, :, :64].reshape(96, 256)], axis=1)
            KT[ui] = np.concatenate([k[:, :, :64].reshape(96, 256), k[:, :, 64:].reshape(96, 8192)], axis=1)
            vv = np.concatenate([v[:, :, :64].reshape(64, 256), v[:, :, 64:].reshape(64, 8192)], axis=1)
            V[ui] = vv.T.reshape(66, 128, 64).transpose(1, 0, 2)
        maps.append({"QT": QT, "KT": KT, "V": V})
    return maps


NC8 = 1056
TWO_PI = 6.283185307179586


def decl_s5(K):
    io = {}
    for n in ("are", "aim", "ldt"):
        io[n] = K.dram_in("s5_" + n, [64, 8], F32)
    for n in ("bre", "bim", "creT", "cimT"):
        io[n] = K.dram_in("s5_" + n, [64, 8, 16], F32)
    io["U8"] = K.dram_in("s5_U8", [8, 128, 2 * NC8], F32)
    io["mask8"] = K.dram_in("s5_mask8", [128, 128], F32)
    io["ident"] = K.dram_in("s5_ident", [64, 64], F32)
    io["Y8"] = K.dram_out("s5_Y8", [8, 128, 2 * NC8], F32)
    return io


def h_even_s5(K, io):
    NU = 8
    N = 2 * NC8

    def ld(name, shape):
        b = K.sb("s5" + name, shape, F32)
        sl = tuple(slice(None) for _ in shape)
        K.dma("sp", b.t[sl], io[name].t.ap(), R=[io[name]], W=[b])
        return b
    are, aim, ldt = ld("are", [64, 8]), ld("aim", [64, 8]), ld("ldt", [64, 8])
    bre, bim, creT, cimT = ld("bre", [64, 8, 16]), ld("bim", [64, 8, 16]), ld("creT", [64, 8, 16]), ld("cimT", [64, 8, 16])
    mask8, ident = ld("mask8", [128, 128]), ld("ident", [64, 64])
    cnt = {"i": 0}

    def sm(shape=(64, 8), dt=F32):
        cnt["i"] += 1
        return K.sb("s5t%d" % cnt["i"], list(shape), dt)

    def tt(out, a, b, op, R, W, eng="dve"):
        K.op(eng, lambda e: e.tensor_tensor(out, a, b, op=op), R=R, W=W)

    dt_, lr, li, mag = sm(), sm(), sm(), sm()
    K.op("act", lambda e: e.activation(dt_.t[:, :], ldt.t[:, :], AF.Exp), R=[ldt], W=[dt_])
    tt(lr.t[:, :], are.t[:, :], dt_.t[:, :], ALU.mult, [are, dt_], [lr])
    tt(li.t[:, :], aim.t[:, :], dt_.t[:, :], ALU.mult, [aim, dt_], [li])
    K.op("act", lambda e: e.activation(mag.t[:, :], lr.t[:, :], AF.Exp), R=[lr], W=[mag])
    rho, irho = sm(), sm()
    K.op("act", lambda e: e.activation(rho.t[:, :], lr.t[:, :], AF.Exp, scale=8.0), R=[lr], W=[rho])
    K.op("act", lambda e: e.activation(irho.t[:, :], lr.t[:, :], AF.Exp, scale=-8.0), R=[lr], W=[irho])
    kf, ki, r0, rs, rc, s1, c1 = sm(), sm((64, 8), I32), sm(), sm(), sm(), sm(), sm()
    K.op("dve", lambda e: e.tensor_scalar(kf.t[:, :], li.t[:, :], 1.0 / TWO_PI, None, op0=ALU.mult), R=[li], W=[kf])
    K.op("dve", lambda e: e.tensor_copy(ki.t[:, :], kf.t[:, :]), R=[kf], W=[ki])
    K.op("dve", lambda e: e.tensor_copy(kf.t[:, :], ki.t[:, :]), R=[ki], W=[kf])
    K.op("dve", lambda e: e.scalar_tensor_tensor(r0.t[:, :], kf.t[:, :], -TWO_PI, li.t[:, :], op0=ALU.mult, op1=ALU.add), R=[kf, li], W=[r0])
    wa, wb2, wy = sm(), sm(), sm()

    def wrap(dst, shift):
        K.op("dve", lambda e: e.tensor_scalar(wy.t[:, :], r0.t[:, :], float(shift), None, op0=ALU.add), R=[r0], W=[wy])
        K.op("dve", lambda e: e.tensor_scalar(wa.t[:, :], wy.t[:, :], float(np.pi), -TWO_PI, op0=ALU.is_gt, op1=ALU.mult), R=[wy], W=[wa])
        K.op("dve", lambda e: e.tensor_scalar(wb2.t[:, :], wy.t[:, :], -float(np.pi), TWO_PI, op0=ALU.is_lt, op1=ALU.mult), R=[wy], W=[wb2])
        K.op("dve", lambda e: e.tensor_tensor(wa.t[:, :], wa.t[:, :], wb2.t[:, :], op=ALU.add), R=[wa, wb2], W=[wa])
        K.op("dve", lambda e: e.tensor_tensor(dst.t[:, :], wy.t[:, :], wa.t[:, :], op=ALU.add), R=[wy, wa], W=[dst])
    wrap(rs, 0.0)
    wrap(rc, np.pi / 2)
    K.op("act", lambda e: e.activation(s1.t[:, :], rs.t[:, :], AF.Sin), R=[rs], W=[s1])
    K.op("act", lambda e: e.activation(c1.t[:, :], rc.t[:, :], AF.Sin), R=[rc], W=[c1])
    abr, abi = sm(), sm()
    tt(abr.t[:, :], mag.t[:, :], c1.t[:, :], ALU.mult, [mag, c1], [abr])
    tt(abi.t[:, :], mag.t[:, :], s1.t[:, :], ALU.mult, [mag, s1], [abi])
    nr, den, fr, fi, t1, t2 = sm(), sm(), sm(), sm(), sm(), sm()
    K.op("dve", lambda e: e.tensor_scalar(nr.t[:, :], abr.t[:, :], -1.0, None, op0=ALU.add), R=[abr], W=[nr])
    tt(t1.t[:, :], are.t[:, :], are.t[:, :], ALU.mult, [are], [t1])
    tt(t2.t[:, :], aim.t[:, :], aim.t[:, :], ALU.mult, [aim], [t2])
    tt(den.t[:, :], t1.t[:, :], t2.t[:, :], ALU.add, [t1, t2], [den])
    K.op("dve", lambda e: e.reciprocal(den.t[:, :], den.t[:, :]), R=[den], W=[den])
    tt(t1.t[:, :], nr.t[:, :], are.t[:, :], ALU.mult, [nr, are], [t1])
    tt(t2.t[:, :], abi.t[:, :], aim.t[:, :], ALU.mult, [abi, aim], [t2])
    tt(t1.t[:, :], t1.t[:, :], t2.t[:, :], ALU.add, [t1, t2], [t1])
    tt(fr.t[:, :], t1.t[:, :], den.t[:, :], ALU.mult, [t1, den], [fr])
    tt(t1.t[:, :], abi.t[:, :], are.t[:, :], ALU.mult, [abi, are], [t1])
    tt(t2.t[:, :], nr.t[:, :], aim.t[:, :], ALU.mult, [nr, aim], [t2])
    tt(t1.t[:, :], t1.t[:, :], t2.t[:, :], ALU.subtract, [t1, t2], [t1])
    tt(fi.t[:, :], t1.t[:, :], den.t[:, :], ALU.mult, [t1, den], [fi])

    Apr, Api = sm((64, 8, 9)), sm((64, 8, 9))
    Qr, Qi = sm((64, 8, 8)), sm((64, 8, 8))
    K.op("dve", lambda e: e.memset(Apr.t[:, :, 0], 1.0), W=[Apr])
    K.op("dve", lambda e: e.memset(Api.t[:, :, 0], 0.0), W=[Api])
    K.op("dve", lambda e: e.tensor_copy(Qr.t[:, :, 0], fr.t[:, :]), R=[fr], W=[Qr])
    K.op("dve", lambda e: e.tensor_copy(Qi.t[:, :, 0], fi.t[:, :]), R=[fi], W=[Qi])

    def cstep(Tr, Ti, n):
        for tau in range(1, n):
            tt(t1.t[:, :], Tr.t[:, :, tau - 1], abr.t[:, :], ALU.mult, [Tr, abr], [t1])
            tt(t2.t[:, :], Ti.t[:, :, tau - 1], abi.t[:, :], ALU.mult, [Ti, abi], [t2])
            tt(Tr.t[:, :, tau], t1.t[:, :], t2.t[:, :], ALU.subtract, [t1, t2], [Tr])
            tt(t1.t[:, :], Tr.t[:, :, tau - 1], abi.t[:, :], ALU.mult, [Tr, abi], [t1])
            tt(t2.t[:, :], Ti.t[:, :, tau - 1], abr.t[:, :], ALU.mult, [Ti, abr], [t2])
            tt(Ti.t[:, :, tau], t1.t[:, :], t2.t[:, :], ALU.add, [t1, t2], [Ti])
    cstep(Apr, Api, 9)
    cstep(Qr, Qi, 8)
    nApi, nQi = sm((64, 8, 9)), sm((64, 8, 8))
    nApr = sm((64, 8, 9))
    K.op("dve", lambda e: e.tensor_scalar(nApr.t[:, :, :], Apr.t[:, :, :], -1.0, None, op0=ALU.mult), R=[Apr], W=[nApr])
    K.op("dve", lambda e: e.tensor_scalar(nApi.t[:, :, :], Api.t[:, :, :], -1.0, None, op0=ALU.mult), R=[Api], W=[nApi])
    K.op("dve", lambda e: e.tensor_scalar(nQi.t[:, :, :], Qi.t[:, :, :], -1.0, None, op0=ALU.mult), R=[Qi], W=[nQi])
    n2, i8r, i8i, ni8i, phr, phi = sm(), sm(), sm(), sm(), sm(), sm()
    tt(t1.t[:, :], Apr.t[:, :, 8], Apr.t[:, :, 8], ALU.mult, [Apr], [t1])
    tt(t2.t[:, :], Api.t[:, :, 8], Api.t[:, :, 8], ALU.mult, [Api], [t2])
    tt(n2.t[:, :], t1.t[:, :], t2.t[:, :], ALU.add, [t1, t2], [n2])
    K.op("dve", lambda e: e.reciprocal(n2.t[:, :], n2.t[:, :]), R=[n2], W=[n2])
    tt(i8r.t[:, :], Apr.t[:, :, 8], n2.t[:, :], ALU.mult, [Apr, n2], [i8r])
    tt(ni8i.t[:, :], Api.t[:, :, 8], n2.t[:, :], ALU.mult, [Api, n2], [ni8i])
    K.op("dve", lambda e: e.tensor_scalar(i8i.t[:, :], ni8i.t[:, :], -1.0, None, op0=ALU.mult), R=[ni8i], W=[i8i])
    tt(phr.t[:, :], Apr.t[:, :, 8], irho.t[:, :], ALU.mult, [Apr, irho], [phr])
    tt(phi.t[:, :], Api.t[:, :, 8], irho.t[:, :], ALU.mult, [Api, irho], [phi])
    pwr, pwi, npwi = sm((64, 8, 11)), sm((64, 8, 11)), sm((64, 8, 11))
    K.op("dve", lambda e: e.tensor_copy(pwr.t[:, :, 0], phr.t[:, :]), R=[phr], W=[pwr])
    K.op("dve", lambda e: e.tensor_copy(pwi.t[:, :, 0], phi.t[:, :]), R=[phi], W=[pwi])
    for j in range(1, 11):
        tt(t1.t[:, :], pwr.t[:, :, j - 1], pwr.t[:, :, j - 1], ALU.mult, [pwr], [t1])
        tt(t2.t[:, :], pwi.t[:, :, j - 1], pwi.t[:, :, j - 1], ALU.mult, [pwi], [t2])
        tt(pwr.t[:, :, j], t1.t[:, :], t2.t[:, :], ALU.subtract, [t1, t2], [pwr])
        tt(t1.t[:, :], pwr.t[:, :, j - 1], pwi.t[:, :, j - 1], ALU.mult, [pwr, pwi], [t1])
        K.op("dve", lambda e: e.tensor_scalar(pwi.t[:, :, j], t1.t[:, :], 2.0, None, op0=ALU.mult), R=[t1], W=[pwi])
    K.op("dve", lambda e: e.tensor_scalar(npwi.t[:, :, :], pwi.t[:, :, :], -1.0, None, op0=ALU.mult), R=[pwi], W=[npwi])

    if "dbg" in io:
        for i, tb_ in enumerate((dt_, lr, li, mag, r0, rs, rc, s1, c1, abr, abi, fr, fi, i8r, i8i, phr, phi, rho)):
            K.dma("sp", io["dbg"].t.ap()[:, i, :], tb_.t[:, :], R=[tb_], W=[io["dbg"]])
    Wre, Wim = sm((64, 8, 16)), sm((64, 8, 16))
    Wpr, Wpi = sm((64, 128)), sm((64, 128))
    RrT, RiT = sm((64, 8, 16)), sm((64, 8, 16))
    RrTb, RiTb = sm((64, 128), BF16), sm((64, 128), BF16)
    tmp16 = sm((64, 16))
    tmp128 = sm((64, 128))
    MTb = sm((128, 128), BF16)
    WreTb, WimTb = sm((128, 64), BF16), sm((128, 64), BF16)
    Phr, Phi = sm((64, NC8)), sm((64, NC8))
    ptmp = sm((64, 512))
    rmask = sm((64, N))
    U8f = [sm((128, N)) for _ in range(2)]
    U8b = sm((128, N), BF16)
    Sre, Sim = sm((64, N)), sm((64, N))
    Gr, Gi = sm((64, N)), sm((64, N))
    Gr2, Gi2 = sm((64, N)), sm((64, N))
    ta, tb = sm((64, NC8)), sm((64, NC8))
    Hpr, Hpi = sm((64, N), BF16), sm((64, N), BF16)
    ystg = [sm((128, 512)) for _ in range(2)]
    K.op("dve", lambda e: e.memset(Hpr.t[:, :], 0.0), W=[Hpr])
    K.op("dve", lambda e: e.memset(Hpi.t[:, :], 0.0), W=[Hpi])
    slabs = [(c0, min(512, N - c0)) for c0 in range(0, N, 512)]

    for u in range(NU):
        K.dma("act", U8f[u % 2].t[:, :], io["U8"].t.ap()[u], R=[io["U8"]], W=[U8f[u % 2]])
        for s in range(8):
            q = 7 - s
            K.op("dve", lambda e: e.tensor_scalar(tmp16.t[:, :], bre.t[:, u, :], Qr.t[:, u, q:q + 1], None, op0=ALU.mult), R=[bre, Qr], W=[tmp16])
            K.op("dve", lambda e: e.scalar_tensor_tensor(Wre.t[:, s, :], bim.t[:, u, :], nQi.t[:, u, q:q + 1], tmp16.t[:, :], op0=ALU.mult, op1=ALU.add), R=[bim, nQi, tmp16], W=[Wre])
            K.op("dve", lambda e: e.tensor_scalar(tmp16.t[:, :], bim.t[:, u, :], Qr.t[:, u, q:q + 1], None, op0=ALU.mult), R=[bim, Qr], W=[tmp16])
            K.op("dve", lambda e: e.scalar_tensor_tensor(Wim.t[:, s, :], bre.t[:, u, :], Qi.t[:, u, q:q + 1], tmp16.t[:, :], op0=ALU.mult, op1=ALU.add), R=[bre, Qi, tmp16], W=[Wim])
        for t in range(8):
            K.op("dve", lambda e: e.tensor_scalar(tmp16.t[:, :], creT.t[:, u, :], Apr.t[:, u, t + 1:t + 2], None, op0=ALU.mult), R=[creT, Apr], W=[tmp16])
            K.op("dve", lambda e: e.scalar_tensor_tensor(RrT.t[:, t, :], cimT.t[:, u, :], nApi.t[:, u, t + 1:t + 2], tmp16.t[:, :], op0=ALU.mult, op1=ALU.add), R=[cimT, nApi, tmp16], W=[RrT])
            K.op("dve", lambda e: e.tensor_scalar(tmp16.t[:, :], creT.t[:, u, :], nApi.t[:, u, t + 1:t + 2], None, op0=ALU.mult), R=[creT, nApi], W=[tmp16])
            K.op("dve", lambda e: e.scalar_tensor_tensor(RiT.t[:, t, :], cimT.t[:, u, :], nApr.t[:, u, t + 1:t + 2], tmp16.t[:, :], op0=ALU.mult, op1=ALU.add), R=[cimT, nApr, tmp16], W=[RiT])
        Wre2 = Wre.t[:, :, :].rearrange("p s k -> p (s k)")
        Wim2 = Wim.t[:, :, :].rearrange("p s k -> p (s k)")
        Rr2 = RrT.t[:, :, :].rearrange("p s k -> p (s k)")
        Ri2 = RiT.t[:, :, :].rearrange("p s k -> p (s k)")
        K.op("dve", lambda e: e.tensor_scalar(tmp128.t[:, :], Wre2, i8r.t[:, u:u + 1], None, op0=ALU.mult), R=[Wre, i8r], W=[tmp128])
        K.op("dve", lambda e: e.scalar_tensor_tensor(Wpr.t[:, :], Wim2, ni8i.t[:, u:u + 1], tmp128.t[:, :], op0=ALU.mult, op1=ALU.add), R=[Wim, ni8i, tmp128], W=[Wpr])
        K.op("dve", lambda e: e.tensor_scalar(tmp128.t[:, :], Wim2, i8r.t[:, u:u + 1], None, op0=ALU.mult), R=[Wim, i8r], W=[tmp128])
        K.op("dve", lambda e: e.scalar_tensor_tensor(Wpi.t[:, :], Wre2, i8i.t[:, u:u + 1], tmp128.t[:, :], op0=ALU.mult, op1=ALU.add), R=[Wre, i8i, tmp128], W=[Wpi])
        ps = K.ps_get()
        K.op("pe", lambda e: e.matmul(ps.t[:, 0:128], Wpr.t[:, :], Rr2, start=True, stop=False), R=[Wpr, RrT], W=[ps])
        K.op("pe", lambda e: e.matmul(ps.t[:, 0:128], Wpi.t[:, :], Ri2, start=False, stop=True), R=[Wpi, RiT], W=[ps])
        K.op("dve", lambda e: e.tensor_tensor(MTb.t[:, :], ps.t[:, 0:128], mask8.t[:, :], op=ALU.mult), R=[ps, mask8], W=[MTb])
        ps = K.ps_get()
        K.op("pe", lambda e: e.matmul(ps.t[:, 0:64], Wre2, ident.t[:, :], start=True, stop=True), R=[Wre, ident], W=[ps])
        K.op("act", lambda e: e.copy(WreTb.t[:, :], ps.t[:, 0:64]), R=[ps], W=[WreTb])
        ps = K.ps_get()
        K.op("pe", lambda e: e.matmul(ps.t[:, 0:64], Wim2, ident.t[:, :], start=True, stop=True), R=[Wim, ident], W=[ps])
        K.op("act", lambda e: e.copy(WimTb.t[:, :], ps.t[:, 0:64]), R=[ps], W=[WimTb])
        K.op("act", lambda e: e.copy(RrTb.t[:, :], Rr2), R=[RrT], W=[RrTb])
        K.op("act", lambda e: e.copy(RiTb.t[:, :], Ri2), R=[RiT], W=[RiTb])
        K.op("pool", lambda e: e.memset(Phr.t[:, 0:1], 1.0), W=[Phr])
        K.op("pool", lambda e: e.memset(Phi.t[:, 0:1], 0.0), W=[Phi])
        n = 1
        j = 0
        while n < NC8:
            m = min(n, NC8 - n)
            for c0 in range(0, m, 512):
                w = min(512, m - c0)
                K.op("dve", lambda e: e.tensor_scalar(ptmp.t[:, 0:w], Phr.t[:, c0:c0 + w], pwr.t[:, u, j:j + 1], None, op0=ALU.mult), R=[Phr, pwr], W=[ptmp])
                K.op("dve", lambda e: e.scalar_tensor_tensor(Phr.t[:, n + c0:n + c0 + w], Phi.t[:, c0:c0 + w], npwi.t[:, u, j:j + 1], ptmp.t[:, 0:w], op0=ALU.mult, op1=ALU.add), R=[Phi, npwi, ptmp, Phr], W=[Phr])
                K.op("dve", lambda e: e.tensor_scalar(ptmp.t[:, 0:w], Phr.t[:, c0:c0 + w], pwi.t[:, u, j:j + 1], None, op0=ALU.mult), R=[Phr, pwi], W=[ptmp])
                K.op("dve", lambda e: e.scalar_tensor_tensor(Phi.t[:, n + c0:n + c0 + w], Phi.t[:, c0:c0 + w], pwr.t[:, u, j:j + 1], ptmp.t[:, 0:w], op0=ALU.mult, op1=ALU.add), R=[Phi, pwr, ptmp], W=[Phi])
            n *= 2
            j += 1
        K.op("pool", lambda e: e.memset(rmask.t[:, :], 1.0), W=[rmask])
        K.op("pool", lambda e: e.tensor_scalar(rmask.t[:, :], rmask.t[:, :], rho.t[:, u:u + 1], None, op0=ALU.mult), R=[rho, rmask], W=[rmask])
        K.op("pool", lambda e: e.memset(rmask.t[:, 0:1], 0.0), W=[rmask])
        K.op("pool", lambda e: e.memset(rmask.t[:, NC8:NC8 + 1], 0.0), W=[rmask])
        Uf = U8f[u % 2]
        K.op("act", lambda e: e.copy(U8b.t[:, :], Uf.t[:, :]), R=[Uf], W=[U8b])
        for (c0, w) in slabs:
            for (WT, S) in ((WreTb, Sre), (WimTb, Sim)):
                ps = K.ps_get()
                K.op("pe", lambda e: e.matmul(ps.t[0:64, 0:w], WT.t[:, :], U8b.t[:, c0:c0 + w], start=True, stop=True), R=[WT, U8b], W=[ps])
                K.op("act", lambda e: e.copy(S.t[:, c0:c0 + w], ps.t[0:64, 0:w]), R=[ps], W=[S])
        for hb in range(2):
            sl = slice(hb * NC8, (hb + 1) * NC8)
            tt(ta.t[:, :], Phr.t[:, :], Sre.t[:, sl], ALU.mult, [Phr, Sre], [ta])
            tt(tb.t[:, :], Phi.t[:, :], Sim.t[:, sl], ALU.mult, [Phi, Sim], [tb], eng="pool")
            tt(Gr.t[:, sl], ta.t[:, :], tb.t[:, :], ALU.add, [ta, tb], [Gr])
            tt(ta.t[:, :], Phr.t[:, :], Sim.t[:, sl], ALU.mult, [Phr, Sim], [ta])
            tt(tb.t[:, :], Phi.t[:, :], Sre.t[:, sl], ALU.mult, [Phi, Sre], [tb], eng="pool")
            tt(Gi.t[:, sl], ta.t[:, :], tb.t[:, :], ALU.subtract, [ta, tb], [Gi])
        K.op("dve", lambda e: e.tensor_tensor_scan(Gr2.t[:, :], rmask.t[:, :], Gr.t[:, :], 0.0, op0=ALU.mult, op1=ALU.add), R=[rmask, Gr], W=[Gr2])
        K.op("dve", lambda e: e.tensor_tensor_scan(Gi2.t[:, :], rmask.t[:, :], Gi.t[:, :], 0.0, op0=ALU.mult, op1=ALU.add), R=[rmask, Gi], W=[Gi2])
        for hb in range(2):
            o = hb * NC8
            n1 = NC8 - 1
            tt(ta.t[:, 0:n1], Phr.t[:, 0:n1], Gr2.t[:, o:o + n1], ALU.mult, [Phr, Gr2], [ta])
            tt(tb.t[:, 0:n1], Phi.t[:, 0:n1], Gi2.t[:, o:o + n1], ALU.mult, [Phi, Gi2], [tb], eng="pool")
            tt(Hpr.t[:, o + 1:o + 1 + n1], ta.t[:, 0:n1], tb.t[:, 0:n1], ALU.subtract, [ta, tb], [Hpr])
            tt(ta.t[:, 0:n1], Phr.t[:, 0:n1], Gi2.t[:, o:o + n1], ALU.mult, [Phr, Gi2], [ta])
            tt(tb.t[:, 0:n1], Phi.t[:, 0:n1], Gr2.t[:, o:o + n1], ALU.mult, [Phi, Gr2], [tb], eng="pool")
            tt(Hpi.t[:, o + 1:o + 1 + n1], ta.t[:, 0:n1], tb.t[:, 0:n1], ALU.add, [ta, tb], [Hpi])
        for si, (c0, w) in enumerate(slabs):
            ps = K.ps_get()
            K.op("pe", lambda e: e.matmul(ps.t[:, 0:w], MTb.t[:, :], U8b.t[:, c0:c0 + w], start=True, stop=False), R=[MTb, U8b], W=[ps])
            K.op("pe", lambda e: e.matmul(ps.t[:, 0:w], RrTb.t[:, :], Hpr.t[:, c0:c0 + w], start=False, stop=False), R=[RrTb, Hpr], W=[ps])
            K.op("pe", lambda e: e.matmul(ps.t[:, 0:w], RiTb.t[:, :], Hpi.t[:, c0:c0 + w], start=False, stop=True), R=[RiTb, Hpi], W=[ps])
            s = ystg[si % 2]
            K.op("act", lambda e: e.copy(s.t[:, 0:w], ps.t[:, 0:w]), R=[ps], W=[s])
            K.dma("sp", io["Y8"].t.ap()[u, :, c0:c0 + w], s.t[:, 0:w], R=[s], W=[io["Y8"]])


def s5_inputs(inp, j, uT_list):
    useq = []
    for b in range(2):
        parts = [from_fm(uT_list[4 * b + r]) for r in range(4)]
        ctx = np.concatenate([p[:64] for p in parts], 0)
        lat = np.concatenate([p[64:] for p in parts], 0)
        useq.append((ctx, lat))
    mask8 = (np.arange(128)[:, None] // 16 <= np.arange(128)[None, :] // 16).astype(np.float32)
    ident = np.eye(64, dtype=np.float32)
    maps = []
    for core in range(8):
        m = {"s5_mask8": mask8, "s5_ident": ident}
        U8 = np.zeros((8, 128, 2 * NC8), np.float32)
        sel = lambda a: np.stack([a[j, d, 4 * core + gl] for d in range(2) for gl in range(4)], axis=-1)
        m["s5_are"] = np.ascontiguousarray(sel(inp["s5_a_re"]))
        m["s5_aim"] = np.ascontiguousarray(sel(inp["s5_a_im"]))
        m["s5_ldt"] = np.ascontiguousarray(np.broadcast_to(np.stack([inp["s5_log_dt"][j, d, 4 * core + gl] for d in range(2) for gl in range(4)])[None, :], (64, 8)))
        m["s5_bre"] = np.ascontiguousarray(np.stack([inp["s5_b_re"][j, d, 4 * core + gl] for d in range(2) for gl in range(4)], axis=1))
        m["s5_bim"] = np.ascontiguousarray(np.stack([inp["s5_b_im"][j, d, 4 * core + gl] for d in range(2) for gl in range(4)], axis=1))
        m["s5_creT"] = np.ascontiguousarray(np.stack([inp["s5_c_re"][j, d, 4 * core + gl].T for d in range(2) for gl in range(4)], axis=1))
        m["s5_cimT"] = np.ascontiguousarray(np.stack([inp["s5_c_im"][j, d, 4 * core + gl].T for d in range(2) for gl in range(4)], axis=1))
        for d in range(2):
            for gl in range(4):
                g = 4 * core + gl
                for b in range(2):
                    ctx, lat = useq[b]
                    if d == 0:
                        seq = np.concatenate([ctx[:, 16 * g:16 * g + 16], lat[:, 16 * g:16 * g + 16]], 0)
                    else:
                        seq = np.concatenate([ctx[::-1, 16 * g:16 * g + 16], lat[::-1, 16 * g:16 * g + 16]], 0)
                    U8[d * 4 + gl, :, b * NC8:(b + 1) * NC8] = seq.reshape(NC8, 128).T
        m["s5_U8"] = U8
        maps.append(m)
    return maps


def s5_outputs(Y8_list):
    yf = np.zeros((2, 8448, 512), np.float32)
    yb = np.zeros((2, 8448, 512), np.float32)
    for core in range(8):
        Y8 = Y8_list[core]
        for d in range(2):
            for gl in range(4):
                g = 4 * core + gl
                for b in range(2):
                    seq = Y8[d * 4 + gl, :, b * NC8:(b + 1) * NC8].T.reshape(8448, 16)
                    if d == 0:
                        yf[b, :, 16 * g:16 * g + 16] = seq
                    else:
                        yb[b, :256, 16 * g:16 * g + 16] = seq[:256][::-1]
                        yb[b, 256:, 16 * g:16 * g + 16] = seq[256:][::-1]
    return yf, yb


GELU_C = 2.0 * 0.7978845608028654


def decl_tb(K, parity):
    io = {}
    if parity == "even":
        for n in ("uTi", "yfT", "ybT", "OTn", "denT"):
            io[n] = K.dram_in(n, [128, 4, NT], F32)
        io["dT"] = K.dram_in("dT", [128, 4], F32)
        io["w_glu"] = K.dram_in("w_glu", [512, 512], F32)
    else:
        for n in ("of", "ob", "gT"):
            io[n] = K.dram_in(n, [128, 8, NT], F32)
        io["gnT"] = K.dram_in("gnT", [128, 8], F32)
    io["w_out"] = K.dram_in("w_out", [1024, 1024], F32)
    io["w1"] = K.dram_in("ffn_w1", [1024, DFF], F32)
    io["w3"] = K.dram_in("ffn_w3", [1024, DFF], F32)
    io["w2"] = K.dram_in("ffn_w2", [DFF, 1024], F32)
    return io


def tb_phase(K, T, x, hT, mod, n2g, io, parity):
    with K.scope():
        stg = [K.sb("tbs%d" % i, [128, 512], F32) for i in range(6)]
        st = {"i": 0}

        def ldslab(d, c, c0, w, q="sp"):
            b = stg[st["i"] % 6]
            st["i"] += 1
            K.dma(q, b.t[:, 0:w], d.t.ap()[:, c, c0:c0 + w], R=[d], W=[b])
            return b

        if parity == "even":
            dT = T.load("dT", io["dT"], [128, 4])
            z = K.sb("zT", [128, 4, NT], F32)
            zb = K.sb("zbT", [128, 4, NT], BF16)
            for (c0, w, isctx) in SLABS:
                for c in range(4):
                    u = ldslab(io["uTi"], c, c0, w)
                    yf = ldslab(io["yfT"], c, c0, w, "act")
                    yb = ldslab(io["ybT"], c, c0, w)
                    t = T.gettmp()
                    t2 = T.gettmp()
                    K.op("dve", lambda e: e.scalar_tensor_tensor(t.t[:, 0:w], u.t[:, 0:w], dT.t[:, c:c + 1], yf.t[:, 0:w], op0=ALU.mult, op1=ALU.add), R=[u, dT, yf], W=[t])
                    K.op("dve", lambda e: e.tensor_tensor(t.t[:, 0:w], t.t[:, 0:w], yb.t[:, 0:w], op=ALU.add), R=[t, yb], W=[t])
                    K.op("dve", lambda e: e.tensor_tensor(t2.t[:, 0:w], t.t[:, 0:w], t.t[:, 0:w], op=ALU.mult), R=[t], W=[t2])
                    K.op("dve", lambda e: e.tensor_scalar(t2.t[:, 0:w], t2.t[:, 0:w], 0.044715, 1.0, op0=ALU.mult, op1=ALU.add), R=[t2], W=[t2])
                    K.op("dve", lambda e: e.tensor_tensor(t2.t[:, 0:w], t2.t[:, 0:w], t.t[:, 0:w], op=ALU.mult), R=[t2, t], W=[t2])
                    K.op("act", lambda e: e.activation(t2.t[:, 0:w], t2.t[:, 0:w], AF.Sigmoid, scale=GELU_C), R=[t2], W=[t2])
                    K.op("dve", lambda e: e.tensor_tensor(z.t[:, c, c0:c0 + w], t.t[:, 0:w], t2.t[:, 0:w], op=ALU.mult), R=[t, t2], W=[z])
                    K.op("act", lambda e: e.copy(zb.t[:, c, c0:c0 + w], z.t[:, c, c0:c0 + w]), R=[z], W=[zb])
                    o = ldslab(io["OTn"], c, c0, w, "act")
                    dn = ldslab(io["denT"], c, c0, w)
                    K.op("dve", lambda e: e.reciprocal(dn.t[:, 0:w], dn.t[:, 0:w]), R=[dn], W=[dn])
                    K.op("dve", lambda e: e.tensor_tensor(hT.t[:, 4 + c, c0:c0 + w], o.t[:, 0:w], dn.t[:, 0:w], op=ALU.mult), R=[o, dn], W=[hT])

            def epi_glu(oc, m, si, c0, w, isctx, ps):
                t = T.gettmp()
                K.op("act", lambda e: e.activation(t.t[:, 0:w], ps.t[:, 0:w], AF.Sigmoid), R=[ps], W=[t])
                K.op("dve", lambda e: e.tensor_tensor(hT.t[:, oc, c0:c0 + w], z.t[:, oc, c0:c0 + w], t.t[:, 0:w], op=ALU.mult), R=[z, t], W=[hT])
            T.proj(zb, 4, io["w_glu"], 0, 512, epi_glu, tag="wglu")
        else:
            gn = T.load("gnT", io["gnT"], [128, 8])
            for (c0, w, isctx) in SLABS:
                for c in range(8):
                    a = ldslab(io["of"], c, c0, w)
                    b = ldslab(io["ob"], c, c0, w, "act")
                    g = ldslab(io["gT"], c, c0, w)
                    o = T.gettmp()
                    K.op("dve", lambda e: e.tensor_tensor(o.t[:, 0:w], a.t[:, 0:w], b.t[:, 0:w], op=ALU.add), R=[a, b], W=[o])
                    if c < 4:
                        ps = K.ps_get()
                        K.op("pe", lambda e: e.matmul(ps.t[:, 0:w], T.ones.t[:, :], o.t[:, 0:w], start=True, stop=True), R=[T.ones, o], W=[ps])
                        K.op("dve", lambda e: e.scalar_tensor_tensor(o.t[:, 0:w], ps.t[:, 0:w], -1.0 / 128, o.t[:, 0:w], op0=ALU.mult, op1=ALU.add), R=[ps, o], W=[o])
                    sq = T.gettmp()
                    K.op("act", lambda e: e.activation(sq.t[:, 0:w], o.t[:, 0:w], AF.Square), R=[o], W=[sq])
                    ps = K.ps_get()
                    K.op("pe", lambda e: e.matmul(ps.t[:, 0:w], T.ones.t[:, :], sq.t[:, 0:w], start=True, stop=True), R=[T.ones, sq], W=[ps])
                    K.op("act", lambda e: e.activation(sq.t[:, 0:w], ps.t[:, 0:w], AF.Ln, bias=T.eps.t[:, 0:1], scale=1.0 / 128), R=[ps, T.eps], W=[sq])
                    K.op("act", lambda e: e.activation(sq.t[:, 0:w], sq.t[:, 0:w], AF.Exp, scale=-0.5), R=[sq], W=[sq])
                    K.op("dve", lambda e: e.scalar_tensor_tensor(o.t[:, 0:w], o.t[:, 0:w], gn.t[:, c:c + 1], sq.t[:, 0:w], op0=ALU.mult, op1=ALU.mult), R=[o, gn, sq], W=[o])
                    K.op("act", lambda e: e.activation(g.t[:, 0:w], g.t[:, 0:w], AF.Silu), R=[g], W=[g])
                    K.op("dve", lambda e: e.tensor_tensor(hT.t[:, c, c0:c0 + w], o.t[:, 0:w], g.t[:, 0:w], op=ALU.mult), R=[o, g], W=[hT])

    def epi_out(oc, m, si, c0, w, isctx, ps):
        K.op("dve", lambda e: e.scalar_tensor_tensor(x.t[:, oc, c0:c0 + w], ps.t[:, 0:w], mod.t[:, isctx, 16 + oc:17 + oc], x.t[:, oc, c0:c0 + w], op0=ALU.mult, op1=ALU.add), R=[ps, mod, x], W=[x])
    T.proj(hT, 8, io["w_out"], 0, 1024, epi_out, tag="wout")
    T.norm_mod(x, hT, mod, n2g, 3, 4, "2")
    FG = 2
    w1v = io["w1"].t.ap().rearrange("(c p) n -> p c n", p=128)
    w3v = io["w3"].t.ap().rearrange("(c p) n -> p c n", p=128)
    w2v = io["w2"].t.ap().rearrange("(c p) n -> p c n", p=128)
    wb1 = [K.sb("f1_%d" % i, [128, 8, 128 * FG], BF16) for i in range(2)]
    wb3 = [K.sb("f3_%d" % i, [128, 8, 128 * FG], BF16) for i in range(2)]
    wb2 = [K.sb("f2_%d" % i, [128, FG, 1024], BF16) for i in range(2)]
    aT = [K.sb("faT%d" % i, [128, FG, 512], BF16) for i in range(2)]
    sl = [K.sb("fsl%d" % i, [128, 512], F32) for i in range(2)]
    ai = 0
    for fg in range(22 // FG):
        b1, b3, b2 = wb1[fg % 2], wb3[fg % 2], wb2[fg % 2]
        K.dma("pool", b1.t[:, :, :], w1v[:, :, fg * 128 * FG:(fg + 1) * 128 * FG], R=[io["w1"]], W=[b1])
        K.dma("pool", b3.t[:, :, :], w3v[:, :, fg * 128 * FG:(fg + 1) * 128 * FG], R=[io["w3"]], W=[b3])
        K.dma("pool", b2.t[:, :, :], w2v[:, fg * FG:(fg + 1) * FG, :], R=[io["w2"]], W=[b2])
        for (c0, w, isctx) in SLABS:
            a = aT[ai % 2]
            ai += 1
            for fc in range(FG):
                ps1 = K.ps_get()
                for kc in range(8):
                    K.op("pe", lambda e: e.matmul(ps1.t[:, 0:w], b1.t[:, kc, fc * 128:(fc + 1) * 128], hT.t[:, kc, c0:c0 + w], start=(kc == 0), stop=(kc == 7)), R=[b1, hT], W=[ps1])
                ps3 = K.ps_get()
                for kc in range(8):
                    K.op("pe", lambda e: e.matmul(ps3.t[:, 0:w], b3.t[:, kc, fc * 128:(fc + 1) * 128], hT.t[:, kc, c0:c0 + w], start=(kc == 0), stop=(kc == 7)), R=[b3, hT], W=[ps3])
                s_ = sl[fc % 2]
                K.op("act", lambda e: e.activation(s_.t[:, 0:w], ps1.t[:, 0:w], AF.Silu), R=[ps1], W=[s_])
                K.op("dve", lambda e: e.tensor_tensor(a.t[:, fc, 0:w], s_.t[:, 0:w], ps3.t[:, 0:w], op=ALU.mult), R=[s_, ps3], W=[a])
            for oc in range(8):
                ps = K.ps_get()
                for fc in range(FG):
                    K.op("pe", lambda e: e.matmul(ps.t[:, 0:w], b2.t[:, fc, oc * 128:(oc + 1) * 128], a.t[:, fc, 0:w], start=(fc == 0), stop=(fc == FG - 1)), R=[b2, a], W=[ps])
                K.op("dve", lambda e: e.scalar_tensor_tensor(x.t[:, oc, c0:c0 + w], ps.t[:, 0:w], mod.t[:, isctx, 40 + oc:41 + oc], x.t[:, oc, c0:c0 + w], op0=ALU.mult, op1=ALU.add), R=[ps, mod, x], W=[x])


def decl_ta_odd(K):
    io = {}
    io["w_in_o"] = K.dram_in("w_in_o", [1024, 4608], F32)
    io["cosr"] = K.dram_in("cosr", [128, NT], F32)
    io["sinr"] = K.dram_in("sinr", [128, NT], F32)
    io["pb"] = K.dram_out("pb", [128, 20, NT], BF16)
    io["pf"] = K.dram_out("pf", [128, 16, NT], F32)
    return io


def ta_odd(K, T, hT, io):
    pb, pf = io["pb"], io["pf"]
    put_b = stage_out(K, T, pb, None, BF16)
    put_f = stage_out(K, T, pf, None, F32)
    x1 = [K.sb("ox1_%d" % i, [128, 512], F32) for i in range(2)]
    x2 = [K.sb("ox2_%d" % i, [128, 512], F32) for i in range(2)]
    RSC = 128 ** -0.5

    def epi(oc, m, si, c0, w, isctx, ps):
        if oc < 8:
            sc = 1.0 if oc < 4 else RSC
            if oc % 2 == 0:
                K.op("act", lambda e: e.mul(x1[si % 2].t[:, 0:w], ps.t[:, 0:w], sc), R=[ps], W=[x1[si % 2]])
            else:
                K.op("act", lambda e: e.mul(x2[si % 2].t[:, 0:w], ps.t[:, 0:w], sc), R=[ps], W=[x2[si % 2]])
                T.rope(si, c0, w, x1[si % 2], x2[si % 2], 0, 128, io["cosr"], io["sinr"], put_b, pb.t.ap()[:, oc - 1, c0:c0 + w], pb.t.ap()[:, oc, c0:c0 + w])
        else:
            sec = (oc - 8) // 4
            j = (oc - 8) % 4
            if sec in (0, 2, 5):
                di = {0: 8, 2: 12, 5: 16}[sec] + j
                put_b("act", 128, w, lambda b: K.op("act", lambda e: e.copy(b.t[:, 0:w], ps.t[:, 0:w]), R=[ps], W=[b]), pb.t.ap()[:, di, c0:c0 + w])
            else:
                di = {1: 0, 3: 4, 4: 8, 6: 12}[sec] + j
                put_f("act", 128, w, lambda b: K.op("act", lambda e: e.copy(b.t[:, 0:w], ps.t[:, 0:w]), R=[ps], W=[b]), pf.t.ap()[:, di, c0:c0 + w])
    T.proj(hT, 8, io["w_in_o"], 0, 4608, epi, tag="wino")


def perm_odd(w_in):
    cols = []
    for sec in range(2):
        base = sec * 512
        for pair in range(2):
            hs = (2 * pair, 2 * pair + 1)
            cols.append(np.concatenate([base + h * 128 + 2 * np.arange(64) for h in hs]))
            cols.append(np.concatenate([base + h * 128 + 2 * np.arange(64) + 1 for h in hs]))
    cols.append(np.arange(1024, 4608))
    return np.ascontiguousarray(w_in[:, np.concatenate(cols)])


def core_rows(seq, r):
    return np.concatenate([seq[64 * r:64 * r + 64], seq[256 + 2048 * r:256 + 2048 * (r + 1)]], axis=0)


def tb_common_inputs(inp, li):
    return {"ffn_w1": inp["ffn_w1"][li], "ffn_w3": inp["ffn_w3"][li], "ffn_w2": inp["ffn_w2"][li]}


def tb_even_inputs(inp, li, uT_list, OT_list, yf, yb):
    j = li // 2
    maps = []
    for core in range(8):
        b, r = core // 4, core % 4
        m = tb_common_inputs(inp, li)
        m["uTi"] = uT_list[core]
        m["yfT"] = to_fm(core_rows(yf[b], r))
        m["ybT"] = to_fm(core_rows(yb[b], r))
        On = np.zeros((128, 4, NT), np.float32)
        Dn = np.zeros((128, 4, NT), np.float32)
        for h in range(8):
            unit = b * 8 + h
            OT = OT_list[unit // 2][unit % 2]
            cols = np.concatenate([8192 + 64 * r + np.arange(64), 2048 * r + np.arange(2048)])
            ro = (h % 2) * 64
            On[ro:ro + 64, h // 2, :] = OT[0:64][:, cols]
            Dn[ro:ro + 64, h // 2, :] = OT[64:128][:, cols]
        m["OTn"] = On
        m["denT"] = Dn
        m["dT"] = vec_fm(inp["s5_d"][j])
        m["w_glu"] = inp["s5_w_glu"][j]
        m["w_out"] = inp["w_out_even"][j]
        maps.append(m)
    return maps


def ta_odd_inputs(inp, j, r):
    c, s = core_rope(128, 2, r)
    return {"w_in_o": perm_odd(inp["w_in_odd"][j]), "cosr": c, "sinr": s}


LSEQ = 8448
NCK = 132


def decl_h_odd(K):
    io = {}
    io["GqT"] = K.dram_in("GqT", [4, 128, LSEQ], BF16)
    io["GkT"] = K.dram_in("GkT", [2, 128, LSEQ], BF16)
    io["Gf"] = K.dram_in("Gf", [2, 128, LSEQ], F32)
    io["Gv"] = K.dram_in("Gv", [4, 64, NCK, 128], BF16)
    io["lgam"] = K.dram_in("lgam", [128, 2], F32)
    io["lbl"] = K.dram_in("lbl", [128, 3], F32)
    io["lbsel"] = K.dram_in("lbsel", [128, 3], F32)
    io["cmask"] = K.dram_in("cmask", [64, 64], F32)
    io["identb"] = K.dram_in("identb", [128, 128], BF16)
    io["Go"] = K.dram_out("Go", [4, 64, NCK, 128], F32)
    return io


def h_odd(K, io):
    def ld(name, shape, dt=F32):
        b = K.sb("g_" + name, shape, dt)
        sl = tuple(slice(None) for _ in shape)
        K.dma("sp", b.t[sl], io[name].t.ap(), R=[io[name]], W=[b])
        return b
    lgam, lbl, lbsel = ld("lgam", [128, 2]), ld("lbl", [128, 3]), ld("lbsel", [128, 3])
    cmask, identb = ld("cmask", [64, 64]), ld("identb", [128, 128], BF16)
    e3, lb, oml, tot = K.sb("g_e3", [128, 3], F32), K.sb("g_lb", [128, 1], F32), K.sb("g_oml", [128, 1], F32), K.sb("g_tot", [128, 1], F32)
    K.op("act", lambda e: e.activation(e3.t[:, :], lbl.t[:, :], AF.Exp), R=[lbl], W=[e3])
    K.op("dve", lambda e: e.tensor_reduce(tot.t[:, :], e3.t[:, :], axis=AX.X, op=ALU.add), R=[e3], W=[tot])
    K.op("dve", lambda e: e.tensor_tensor(e3.t[:, :], e3.t[:, :], lbsel.t[:, :], op=ALU.mult), R=[e3, lbsel], W=[e3])
    K.op("dve", lambda e: e.tensor_reduce(lb.t[:, :], e3.t[:, :], axis=AX.X, op=ALU.add), R=[e3], W=[lb])
    K.op("dve", lambda e: e.reciprocal(tot.t[:, :], tot.t[:, :]), R=[tot], W=[tot])
    K.op("dve", lambda e: e.tensor_tensor(lb.t[:, :], lb.t[:, :], tot.t[:, :], op=ALU.mult), R=[lb, tot], W=[lb])
    K.op("dve", lambda e: e.tensor_scalar(oml.t[:, :], lb.t[:, :], -1.0, 1.0, op0=ALU.mult, op1=ALU.add), R=[lb], W=[oml])

    A = K.sb("g_A", [128, LSEQ], F32)
    Bc = K.sb("g_B", [128, LSEQ], F32)
    rmask = K.sb("g_rm", [128, LSEQ], BF16)
    qt = K.sb("g_q", [128, LSEQ], BF16)
    kt = K.sb("g_k", [128, LSEQ], BF16)
    v = K.sb("g_v", [64, NCK, 128], BF16)
    S = K.sb("g_S", [128, 128], F32)
    Sb = K.sb("g_Sb", [128, 128], BF16)
    tmpS = K.sb("g_tS", [128, 128], F32)
    ktok = [K.sb("g_kt%d" % i, [64, 128], BF16) for i in range(2)]
    attm = [K.sb("g_at%d" % i, [64, 64], BF16) for i in range(2)]
    ostg = [K.sb("g_os%d" % i, [64, 8, 128], F32) for i in range(2)]
    K.op("pool", lambda e: e.memset(rmask.t[:, :], 1.0), W=[rmask])
    K.op("pool", lambda e: e.memset(rmask.t[:, :].rearrange("p (c t) -> p c t", t=64)[:, :, 0], 0.0), W=[rmask])
    Av = A.t[:, :].rearrange("p (c t) -> p c t", t=64)
    HW = LSEQ // 2
    for u in range(4):
        hg = u >= 2
        K.dma("sp", qt.t[:, :], io["GqT"].t.ap()[u], R=[io["GqT"]], W=[qt])
        K.dma("act", v.t[:, :, :], io["Gv"].t.ap()[u], R=[io["Gv"]], W=[v])
        if not hg:
            K.dma("sp", kt.t[:, :], io["GkT"].t.ap()[u], R=[io["GkT"]], W=[kt])
            K.op("pool", lambda e: e.memset(A.t[:, :], 1.0), W=[A])
            K.op("dve", lambda e: e.tensor_scalar(A.t[:, :], A.t[:, :], lgam.t[:, u:u + 1], None, op0=ALU.mult), R=[A, lgam], W=[A])
        else:
            K.dma("sp", A.t[:, :], io["Gf"].t.ap()[u - 2], R=[io["Gf"]], W=[A])
            for hf in range(2):
                sl = slice(hf * HW, (hf + 1) * HW)
                K.op("act", lambda e: e.activation(A.t[:, sl], A.t[:, sl], AF.Sigmoid), R=[A], W=[A])
                K.op("dve", lambda e: e.tensor_scalar(A.t[:, sl], A.t[:, sl], oml.t[:, 0:1], lb.t[:, 0:1], op0=ALU.mult, op1=ALU.add), R=[A, oml, lb], W=[A])
                K.op("dve", lambda e: e.tensor_scalar(kt.t[:, sl], A.t[:, sl], -1.0, 1.0, op0=ALU.mult, op1=ALU.add), R=[A], W=[kt])
                K.op("act", lambda e: e.activation(A.t[:, sl], A.t[:, sl], AF.Ln), R=[A], W=[A])
        K.op("dve", lambda e: e.tensor_tensor_scan(Bc.t[:, :], rmask.t[:, :], A.t[:, :], 0.0, op0=ALU.mult, op1=ALU.add), R=[rmask, A], W=[Bc])
        for hf in range(2):
            sl = slice(hf * HW, (hf + 1) * HW)
            K.op("act", lambda e: e.activation(A.t[:, sl], Bc.t[:, sl], AF.Exp), R=[Bc], W=[A])
            K.op("dve", lambda e: e.tensor_tensor(qt.t[:, sl], qt.t[:, sl], A.t[:, sl], op=ALU.mult), R=[qt, A], W=[qt])
            K.op("act", lambda e: e.activation(Bc.t[:, sl], Bc.t[:, sl], AF.Exp, scale=-1.0), R=[Bc], W=[Bc])
            K.op("pool", lambda e: e.tensor_tensor(kt.t[:, sl], kt.t[:, sl], Bc.t[:, sl], op=ALU.mult), R=[kt, Bc], W=[kt])
        K.op("dve", lambda e: e.memset(S.t[:, :], 0.0), W=[S])
        K.op("dve", lambda e: e.memset(Sb.t[:, :], 0.0), W=[Sb])
        for c in range(NCK):
            cs = slice(c * 64, (c + 1) * 64)
            pst = K.ps_get()
            K.op("pe", lambda e: e.matmul(pst.t[0:64, 0:128], kt.t[:, cs], identb.t[:, :], start=True, stop=True), R=[kt, identb], W=[pst])
            psa = K.ps_get()
            K.op("pe", lambda e: e.matmul(psa.t[0:64, 0:64], kt.t[:, cs], qt.t[:, cs], start=True, stop=True), R=[kt, qt], W=[psa])
            kk_ = ktok[c % 2]
            am = attm[c % 2]
            K.op("act", lambda e: e.copy(kk_.t[:, :], pst.t[0:64, 0:128]), R=[pst], W=[kk_])
            K.op("dve", lambda e: e.tensor_tensor(am.t[:, :], psa.t[0:64, 0:64], cmask.t[:, :], op=ALU.mult), R=[psa, cmask], W=[am])
            pso = K.ps_get()
            K.op("pe", lambda e: e.matmul(pso.t[0:64, 0:128], am.t[:, :], v.t[:, c, :], start=True, stop=False), R=[am, v], W=[pso])
            K.op("pe", lambda e: e.matmul(pso.t[0:64, 0:128], qt.t[:, cs], Sb.t[:, :], start=False, stop=True), R=[qt, Sb], W=[pso])
            psk = K.ps_get()
            K.op("pe", lambda e: e.matmul(psk.t[:, 0:128], kk_.t[:, :], v.t[:, c, :], start=True, stop=True), R=[kk_, v], W=[psk])
            K.op("dve", lambda e: e.tensor_tensor(tmpS.t[:, :], psk.t[:, 0:128], S.t[:, :], op=ALU.add), R=[psk, S], W=[tmpS])
            K.op("dve", lambda e: e.tensor_scalar(S.t[:, :], tmpS.t[:, :], Av[:, c, 63:64], None, op0=ALU.mult), R=[tmpS, A], W=[S])
            K.op("act", lambda e: e.copy(Sb.t[:, :], S.t[:, :]), R=[S], W=[Sb])
            og = ostg[(c // 8) % 2]
            K.op("act", lambda e: e.copy(og.t[:, c % 8, :], pso.t[0:64, 0:128]), R=[pso], W=[og])
            if c % 8 == 7 or c == NCK - 1:
                c0 = (c // 8) * 8
                n = c - c0 + 1
                K.dma("sp", io["Go"].t.ap()[u, :, c0:c0 + n, :], og.t[:, 0:n, :], R=[og], W=[io["Go"]])


def build_h_odd():
    nc = bass.Bass("TRN2", target_bir_lowering=False)
    K = Emit(nc)
    io = decl_h_odd(K)
    h_odd(K, io)
    K.finish()
    return nc


def flipseq(a, rev):
    if not rev:
        return a
    return np.concatenate([a[:256][::-1], a[256:][::-1]], axis=0)


def gather_seq(core_arrays, b):
    parts = [core_arrays[4 * b + r] for r in range(4)]
    ctx = np.concatenate([p[:, :64] for p in parts], axis=1)
    lat = np.concatenate([p[:, 64:] for p in parts], axis=1)
    return np.concatenate([ctx, lat], axis=1).T


def h_odd_inputs(inp, j, pb_list, pf_list):
    import ml_dtypes
    bf = ml_dtypes.bfloat16
    cmask = (np.arange(64)[:, None] <= np.arange(64)[None, :]).astype(np.float32)
    identb = np.eye(128, dtype=np.float32).astype(bf)
    sel = np.zeros((128, 3), np.float32)
    sel[:, :j + 1] = 1.0
    maps = []
    for core in range(8):
        b, h = core // 4, core % 4
        ch, ro = 2 * (h // 2), (h % 2) * 64
        rq = gather_seq([np.concatenate([p[ro:ro + 64, ch], p[ro:ro + 64, ch + 1]], 0) for p in pb_list], b)
        rk = gather_seq([np.concatenate([p[ro:ro + 64, 4 + ch], p[ro:ro + 64, 5 + ch]], 0) for p in pb_list], b)
        rv = gather_seq([p[:, 8 + h] for p in pb_list], b)
        hq = gather_seq([p[:, 12 + h] for p in pb_list], b)
        hi = gather_seq([p[:, 16 + h] for p in pb_list], b)
        hff = gather_seq([p[:, 4 + h] for p in pf_list], b)
        hfb = gather_seq([p[:, 8 + h] for p in pf_list], b)
        GqT = np.stack([flipseq(rq, 0).T, flipseq(rq, 1).T, flipseq(hq, 0).T, flipseq(hq, 1).T])
        GkT = np.stack([flipseq(rk, 0).T, flipseq(rk, 1).T])
        Gf = np.stack([flipseq(hff, 0).T, flipseq(hfb, 1).T])
        tok = lambda a: a.reshape(NCK, 64, 128).transpose(1, 0, 2)
        Gv = np.stack([tok(flipseq(rv, 0)), tok(flipseq(rv, 1)), tok(flipseq(hi, 0)), tok(flipseq(hi, 1))])
        lg = np.array([np.log1p(-np.exp2(np.float32(-(5.0 + off) - h))) for off in (0.0, 0.5)], np.float32)
        maps.append({"GqT": np.ascontiguousarray(GqT), "GkT": np.ascontiguousarray(GkT), "Gf": np.ascontiguousarray(Gf),
                     "Gv": np.ascontiguousarray(Gv), "lgam": np.ascontiguousarray(np.broadcast_to(lg[None, :], (128, 2))),
                     "lbl": np.ascontiguousarray(inp["hg_lb_logits"][:, h * 128:(h + 1) * 128].T),
                     "lbsel": sel, "cmask": cmask, "identb": identb})
    return maps


def tb_odd_inputs(inp, li, Go_list, pf_list):
    j = li // 2
    nat = {}
    for core in range(8):
        b, h = core // 4, core % 4
        Go = Go_list[core]
        for u in range(4):
            seq = Go[u].transpose(1, 0, 2).reshape(LSEQ, 128)
            nat[(b, h, u)] = flipseq(seq, u % 2)
    maps = []
    for core in range(8):
        b, r = core // 4, core % 4
        m = tb_common_inputs(inp, li)
        of = np.concatenate([core_rows(nat[(b, h, 0)], r) for h in range(4)] + [core_rows(nat[(b, h, 2)], r) for h in range(4)], axis=1)
        ob = np.concatenate([core_rows(nat[(b, h, 1)], r) for h in range(4)] + [core_rows(nat[(b, h, 3)], r) for h in range(4)], axis=1)
        m["of"] = to_fm(of)
        m["ob"] = to_fm(ob)
        pf = pf_list[core]
        m["gT"] = np.ascontiguousarray(np.concatenate([pf[:, 0:4], pf[:, 12:16]], axis=1))
        m["gnT"] = vec_fm(np.concatenate([inp["ret_gn"][j], inp["hg_gn"][j]]))
        m["w_out"] = inp["w_out_odd"][j]
        maps.append(m)
    return maps


_PROGS = {}


def _prog(key, fn):
    if key not in _PROGS:
        _PROGS[key] = fn()
    return _PROGS[key]


def _run(nc, maps):
    import sys, time
    t0 = time.time()
    res = run_bass_kernel_spmd(nc, maps, core_ids=list(range(8))).results
    out = [{k: np.asarray(v) for k, v in r.items()} for r in res]
    print("[kernel] launch done in %.1fs" % (time.time() - t0), file=sys.stderr, flush=True)
    return out


def kernel(**inp):
    inp = {k: np.asarray(v) for k, v in inp.items()}
    maps = []
    for core in range(8):
        b, r = core // 4, core % 4
        X = np.concatenate([inp["ctx"][b, 64 * r:64 * r + 64], inp["x"][b, 2048 * r:2048 * (r + 1)]], axis=0)
        m = {"xT_in": to_fm(X)}
        m.update(mod_inputs(inp, 0, b, "_b"))
        m.update(ta_even_inputs(inp, 0, r))
        maps.append(m)
    ta = _run(_prog("A", lambda: build_tok(None, "even")), maps)
    xT = None
    for li in range(4):
        j = li // 2
        last = li == 3
        if li % 2 == 0:
            am = attn_inputs(ta)
            sm_ = s5_inputs(inp, j, [r["uT"] for r in ta])
            for core in range(8):
                am[core].update(sm_[core])
            ho = _run(_prog("HE", build_h_even), am)
            yf, yb = s5_outputs([o["s5_Y8"] for o in ho])
            maps = tb_even_inputs(inp, li, [r["uT"] for r in ta], [o["OT"] for o in ho], yf, yb)
        else:
            hm = h_odd_inputs(inp, j, [r["pb"] for r in ta], [r["pf"] for r in ta])
            ho = _run(_prog("HO", build_h_odd), hm)
            maps = tb_odd_inputs(inp, li, [o["Go"] for o in ho], [r["pf"] for r in ta])
        for core in range(8):
            b, r = core // 4, core % 4
            if li == 0:
                X = np.concatenate([inp["ctx"][b, 64 * r:64 * r + 64], inp["x"][b, 2048 * r:2048 * (r + 1)]], axis=0)
                maps[core]["xT_in"] = to_fm(X)
            else:
                maps[core]["xT_in"] = xT[core]
            maps[core].update(mod_inputs(inp, li, b, "_a"))
            if not last:
                maps[core].update(mod_inputs(inp, li + 1, b, "_b"))
                if li % 2 == 0:
                    maps[core].update(ta_odd_inputs(inp, (li + 1) // 2, r))
                else:
                    maps[core].update(ta_even_inputs(inp, (li + 1) // 2, r))
            else:
                maps[core]["fnT"] = vec_fm(inp["final_norm"])
        if last:
            ta = _run(_prog("D", lambda: build_tok("odd", None, final=True)), maps)
        elif li % 2 == 0:
            ta = _run(_prog("B", lambda: build_tok("even", "odd")), maps)
        else:
            ta = _run(_prog("C", lambda: build_tok("odd", "even")), maps)
        xT = [r["xT_out"] for r in ta]
    out = np.zeros((2, 8192, 1024), np.float32)
    for core in range(8):
        b, r = core // 4, core % 4
        out[b, 2048 * r:2048 * (r + 1)] = from_fm(ta[core]["yT"])[64:]
    return out
```

```python
import numpy as np
from contextlib import ExitStack
import concourse.bass as bass
import concourse.mybir as mybir
from concourse.bass_utils import run_bass_kernel_spmd

F32 = mybir.dt.float32
BF16 = mybir.dt.bfloat16
I32 = mybir.dt.int32
AF = mybir.ActivationFunctionType
ALU = mybir.AluOpType
AX = mybir.AxisListType


class Buf:
    __slots__ = ("t", "lw", "rd", "name")

    def __init__(self, t, name=""):
        self.t = t
        self.lw = None
        self.rd = []
        self.name = name


class Emit:
    NDS = 24

    def __init__(self, nc):
        self.nc = nc
        self.es = ExitStack()
        self.eng = {"pe": nc.tensor, "act": nc.scalar, "dve": nc.vector, "pool": nc.gpsimd, "sp": nc.sync}
        self.sem = {e: self.es.enter_context(nc.semaphore("s_" + e)) for e in ("pe", "act", "dve", "pool")}
        self.cnt = {e: 0 for e in self.sem}
        self.dsem = [self.es.enter_context(nc.semaphore("d%d" % i)) for i in range(self.NDS)]
        self.dcnt = [0] * self.NDS
        self.dnext = 0
        self.seen = {e: {} for e in self.eng}
        self.outs = []
        self.psb = [self.ps("psb%d" % i, [128, 512], F32) for i in range(6)]
        self.psx = [self.ps("psx%d" % i, [128, 512], F32) for i in range(2)]
        self.psi = 0
        self.n_inst = 0

    def dram_in(self, name, shape, dt):
        return Buf(self.nc.dram_tensor(name, list(shape), dt, kind="ExternalInput"), name)

    def dram_out(self, name, shape, dt):
        return Buf(self.nc.dram_tensor(name, list(shape), dt, kind="ExternalOutput"), name)

    def dram_tmp(self, name, shape, dt):
        return Buf(self.nc.dram_tensor(name, list(shape), dt, kind="Internal"), name)

    def sb(self, name, shape, dt):
        self.n_sb = getattr(self, "n_sb", 0) + 1
        return Buf(self.es.enter_context(self.nc.sbuf_tensor("sb%d_%s" % (self.n_sb, name), list(shape), dt)), name)

    def ps(self, name, shape, dt):
        return Buf(self.es.enter_context(self.nc.psum_tensor(name, list(shape), dt)), name)

    def ps_get(self):
        b = self.psb[self.psi % len(self.psb)]
        self.psi += 1
        return b

    def _wait(self, e, key, val):
        if self.seen[e].get(key, 0) >= val:
            return
        self.seen[e][key] = val
        s = self.sem[key] if isinstance(key, str) else self.dsem[key]
        self.eng[e].wait_ge(s, val)

    def _deps(self, e, R, W):
        deps = {}
        for r in R:
            if r.lw is not None:
                k, v = r.lw
                deps[k] = max(deps.get(k, 0), v)
        for w in W:
            if w.lw is not None:
                k, v = w.lw
                deps[k] = max(deps.get(k, 0), v)
            for k, v in w.rd:
                deps[k] = max(deps.get(k, 0), v)
        for k, v in deps.items():
            if k == "pe" and e == "pe":
                continue
            self._wait(e, k, v)

    def _record(self, ev, R, W):
        for w in W:
            w.lw = ev
            w.rd = []
        for r in R:
            if r in W:
                continue
            r.rd.append(ev)
            if len(r.rd) > 12:
                m = {}
                for k, v in r.rd:
                    m[k] = max(m.get(k, 0), v)
                r.rd = list(m.items())

    def op(self, e, fn, R=(), W=()):
        self._deps(e, R, W)
        inst = fn(self.eng[e])
        self.cnt[e] += 1
        inst.then_inc(self.sem[e], 1)
        self._record((e, self.cnt[e]), R, W)
        self.n_inst += 1
        return inst

    def dma(self, q, out, in_, R=(), W=(), **kw):
        i = self.dnext
        self.dnext = (self.dnext + 1) % self.NDS
        if self.dcnt[i] > 0:
            self._wait(q, i, 16 * self.dcnt[i])
        self._deps(q, R, W)
        inst = self.eng[q].dma_start(out=out, in_=in_, **kw)
        self.dcnt[i] += 1
        inst.then_inc(self.dsem[i], 16)
        ev = (i, 16 * self.dcnt[i])
        self._record(ev, R, W)
        self.n_inst += 1
        return ev

    def barrier(self):
        for e in ("pe", "act", "dve", "pool", "sp"):
            for i in range(self.NDS):
                if self.dcnt[i] > 0:
                    self._wait(e, i, 16 * self.dcnt[i])
            for e2 in ("pe", "act", "dve", "pool"):
                if e2 != e and self.cnt[e2] > 0:
                    self._wait(e, e2, self.cnt[e2])

    def scope(self):
        K = self

        class _S:
            def __enter__(s2):
                s2.old = K.es
                K.es = ExitStack()
                return s2

            def __exit__(s2, *a):
                K.barrier()
                K.es.close()
                K.es = s2.old
                return False
        return _S()

    def finish(self):
        for i in range(self.NDS):
            if self.dcnt[i] > 0:
                self._wait("sp", i, 16 * self.dcnt[i])
        for e in ("pe", "act", "dve", "pool"):
            if self.cnt[e] > 0:
                self._wait("sp", e, self.cnt[e])
        self.es.close()


D = 1024
NCH = 8
NT = 2112
SLABS = [(0, 64, 1)] + [(64 + 512 * i, 512, 0) for i in range(4)]
DFF = 2816
EPS = 1e-6


def pslice(ap, lo, hi):
    return ap[lo:hi]


class Tok:
    def __init__(self, K):
        self.K = K
        nc = K.nc
        self.ones = K.sb("ones_f", [128, 128], F32)
        K.op("dve", lambda e: e.memset(self.ones.t[:, :], 1.0), W=[self.ones])
        self.onesb = K.sb("ones_b", [128, 128], BF16)
        K.op("dve", lambda e: e.memset(self.onesb.t[:, :], 1.0), W=[self.onesb])
        self.eps = K.sb("eps_c", [128, 1], F32)
        K.op("dve", lambda e: e.memset(self.eps.t[:, :], EPS), W=[self.eps])
        self.sq = [K.sb("sq%d" % i, [128, 512], BF16) for i in range(3)]
        self.sqi = 0
        self.rstd = [K.sb("rstd%d" % i, [128, 512], F32) for i in range(2)]
        self.rsi = 0
        self.tmp = [K.sb("ttmp%d" % i, [128, 512], F32) for i in range(3)]
        self.tmi = 0

    def gettmp(self):
        self.tmi += 1
        return self.tmp[self.tmi % len(self.tmp)]

    def load(self, name, dram, shape, dt=F32, q="sp", src=None):
        b = self.K.sb(name, shape, dt)
        sl = tuple(slice(None) for _ in shape)
        self.K.dma(q, b.t[sl], src if src is not None else dram.t.ap(), R=[dram], W=[b])
        return b

    def rms_rstd(self, src, nchunks, c0, w, nfeat, srcchunk0=0):
        K = self.K
        ps = K.ps_get()
        for c in range(nchunks):
            sq = self.sq[self.sqi % 3]
            self.sqi += 1
            K.op("act", lambda e: e.activation(sq.t[:, 0:w], src.t[:, srcchunk0 + c, c0:c0 + w], AF.Square), R=[src], W=[sq])
            K.op("pe", lambda e: e.matmul(ps.t[:, 0:w], self.onesb.t[:, :], sq.t[:, 0:w], start=(c == 0), stop=(c == nchunks - 1)), R=[sq, self.onesb], W=[ps])
        r = self.rstd[self.rsi % 2]
        self.rsi += 1
        K.op("act", lambda e: e.activation(r.t[:, 0:w], ps.t[:, 0:w], AF.Ln, bias=self.eps.t[:, 0:1], scale=1.0 / nfeat), R=[ps, self.eps], W=[r])
        K.op("act", lambda e: e.activation(r.t[:, 0:w], r.t[:, 0:w], AF.Exp, scale=-0.5), R=[r], W=[r])
        return r

    def modvec(self, cT, w_mod, b_modT, tag="", groups=range(12)):
        K = self.K
        sc = K.sb("silc" + tag, [128, 8, 2], F32)
        K.op("act", lambda e: e.activation(sc.t[:, :, :], cT.t[:, :, :], AF.Silu), R=[cT], W=[sc])
        mod = K.sb("modv" + tag, [128, 2, 48], F32)
        wv = w_mod.t.ap().rearrange("(c p) n -> p c n", p=128)
        if not hasattr(self, "wmodb"):
            self.wmodb = [K.sb("wmod%d" % i, [128, 8, 512], F32) for i in range(2)]
        wbufs = self.wmodb
        K.op("dve", lambda e: e.memset(mod.t[:, :, :], 0.0), W=[mod])
        for gi, g in enumerate(groups):
            wb = wbufs[gi % 2]
            K.dma("sp" if gi % 2 == 0 else "act", wb.t[:, :, :], wv[:, :, g * 512:(g + 1) * 512], R=[w_mod], W=[wb])
            ps = K.ps_get()
            for oc in range(4):
                for kc in range(8):
                    K.op("pe", lambda e: e.matmul(ps.t[:, oc * 2:oc * 2 + 2], wb.t[:, kc, oc * 128:(oc + 1) * 128], sc.t[:, kc, :], start=(kc == 0), stop=(kc == 7)), R=[wb, sc], W=[ps])
            for oc in range(4):
                o = g * 4 + oc
                K.op("dve", lambda e: e.tensor_scalar(mod.t[:, :, o], ps.t[:, oc * 2:oc * 2 + 2], b_modT.t[:, o:o + 1], None, op0=ALU.add), R=[ps, b_modT], W=[mod])
        return mod

    def norm_mod(self, x, dst, mod, gT, jshift, jscale, tag):
        K = self.K
        sc = K.sb("nsc" + tag, [128, 2, 8], F32)
        for w in range(2):
            K.op("dve", lambda e: e.scalar_tensor_tensor(sc.t[:, w, :], mod.t[:, w, jscale * 8:jscale * 8 + 8], 1.0, gT.t[:, :], op0=ALU.add, op1=ALU.mult), R=[mod, gT], W=[sc])
        for (c0, w, isctx) in SLABS:
            r = self.rms_rstd(x, 8, c0, w, D)
            for c in range(8):
                t = self.gettmp()
                K.op("dve", lambda e: e.tensor_tensor(t.t[:, 0:w], x.t[:, c, c0:c0 + w], r.t[:, 0:w], op=ALU.mult), R=[x, r], W=[t])
                K.op("act", lambda e: e.activation(dst.t[:, c, c0:c0 + w], t.t[:, 0:w], AF.Identity, bias=mod.t[:, isctx, jshift * 8 + c:jshift * 8 + c + 1], scale=sc.t[:, isctx, c:c + 1]), R=[t, mod, sc], W=[dst])

    def rope(self, si, c0, w, x1, x2, col, P, cosd, sind, put, dst1, dst2):
        K = self.K
        if not hasattr(self, "rc"):
            self.rc = [K.sb("ropec%d" % i, [128, 2, 512], F32) for i in range(2)]
            self.rta = K.sb("ropea", [128, 512], F32)
            self.rtb = K.sb("ropeb", [128, 512], F32)
            self.rci = 0
        c = self.rc[self.rci % 2]
        self.rci += 1
        ta, tb = self.rta, self.rtb
        K.dma("act", c.t[0:P, 0, 0:w], cosd.t.ap()[0:P, c0:c0 + w], R=[cosd], W=[c])
        K.dma("act", c.t[0:P, 1, 0:w], sind.t.ap()[0:P, c0:c0 + w], R=[sind], W=[c])
        K.op("dve", lambda e: e.tensor_tensor(ta.t[0:P, 0:w], x1.t[0:P, col:col + w], c.t[0:P, 0, 0:w], op=ALU.mult), R=[c, x1], W=[ta])
        K.op("dve", lambda e: e.tensor_tensor(tb.t[0:P, 0:w], x2.t[0:P, col:col + w], c.t[0:P, 1, 0:w], op=ALU.mult), R=[c, x2], W=[tb])
        put("dve", P, w, lambda b: K.op("dve", lambda e: e.tensor_tensor(b.t[0:P, 0:w], ta.t[0:P, 0:w], tb.t[0:P, 0:w], op=ALU.subtract), R=[ta, tb], W=[b]), dst1)
        K.op("dve", lambda e: e.tensor_tensor(ta.t[0:P, 0:w], x1.t[0:P, col:col + w], c.t[0:P, 1, 0:w], op=ALU.mult), R=[c, x1], W=[ta])
        K.op("dve", lambda e: e.tensor_tensor(tb.t[0:P, 0:w], x2.t[0:P, col:col + w], c.t[0:P, 0, 0:w], op=ALU.mult), R=[c, x2], W=[tb])
        put("dve", P, w, lambda b: K.op("dve", lambda e: e.tensor_tensor(b.t[0:P, 0:w], ta.t[0:P, 0:w], tb.t[0:P, 0:w], op=ALU.add), R=[ta, tb], W=[b]), dst2)

    def proj(self, src, nk, w_dram, col0, ncols, epi, group=512, tag="w", krows=128):
        K = self.K
        wv = w_dram.t.ap().rearrange("(c p) n -> p c n", p=krows)
        ngroups = (ncols + group - 1) // group
        if not hasattr(self, "wb_" + tag):
            setattr(self, "wb_" + tag, [K.sb("wb_%s%d" % (tag, i), [128, nk, group], BF16) for i in range(2)])
            setattr(self, "wbi_" + tag, 0)
        bufs = getattr(self, "wb_" + tag)
        oc = 0
        for g in range(ngroups):
            gi = getattr(self, "wbi_" + tag)
            setattr(self, "wbi_" + tag, gi + 1)
            wb = bufs[gi % 2]
            gc0 = col0 + g * group
            gw = min(group, ncols - g * group)
            K.dma("pool", wb.t[0:krows, :, 0:gw], wv[:, :, gc0:gc0 + gw], R=[w_dram], W=[wb])
            nm = (gw + 127) // 128
            for si, (c0, w, isctx) in enumerate(SLABS):
                for mi in range(nm):
                    m = min(128, gw - mi * 128)
                    ps = K.ps_get()
                    for kc in range(nk):
                        K.op("pe", lambda e: e.matmul(ps.t[0:m, 0:w], wb.t[0:krows, kc, mi * 128:mi * 128 + m], src.t[0:krows, kc, c0:c0 + w], start=(kc == 0), stop=(kc == nk - 1)), R=[wb, src], W=[ps])
                    epi(oc + mi, m, si, c0, w, isctx, ps)
            oc += nm


def stage_out(K, T, dram, dst_ap_fn, dt):
    bufs = [K.sb("stg_%s%d" % (dram.name, i), [128, 512], dt) for i in range(3)]
    st = {"i": 0}

    def put(eng, m, w, ps_or_fn, dst_ap):
        b = bufs[st["i"] % 3]
        st["i"] += 1
        ps_or_fn(b)
        K.dma("sp", dst_ap, b.t[0:m, 0:w], R=[b], W=[dram])
    return put


def ta_even(K, T, hT, io):
    uT, qT, kT, vT, krT = io["uT"], io["qT"], io["kT"], io["vT"], io["krT"]
    w_in, w_uq, w_ukv = io["w_in_e"], io["w_uq"], io["w_ukv"]
    qn = T.load("qn", io["qnT"], [128, 2])
    kvn = T.load("kvn", io["kvnT"], [128, 1])
    cq = K.sb("cq", [128, 3, NT], F32)
    kr = K.sb("kr12", [16, 2, NT], F32)
    put_u = stage_out(K, T, uT, None, F32)
    put_b = stage_out(K, T, qT, None, BF16)

    def epi_in(oc, m, si, c0, w, isctx, ps):
        if oc < 4:
            put_u("act", 128, w, lambda b: K.op("act", lambda e: e.copy(b.t[:, 0:w], ps.t[:, 0:w]), R=[ps], W=[b]), uT.t.ap()[:, oc, c0:c0 + w])
        else:
            K.op("dve", lambda e: e.tensor_copy(cq.t[:, oc - 4, c0:c0 + w], ps.t[:, 0:w]), R=[ps], W=[cq])
    T.proj(hT, 8, w_in, 0, 896, epi_in, tag="win")

    def epi_kr(j):
        def f(oc, m, si, c0, w, isctx, ps):
            K.op("dve", lambda e: e.tensor_copy(kr.t[:, j, c0:c0 + w], ps.t[0:16, 0:w]), R=[ps], W=[kr])
        return f
    T.proj(hT, 8, w_in, 896, 16, epi_kr(0), tag="win")
    T.proj(hT, 8, w_in, 912, 16, epi_kr(1), tag="win")

    cqn = K.sb("cqn", [128, 3, NT], BF16)
    for (c0, w, isctx) in SLABS:
        r = T.rms_rstd(cq, 2, c0, w, 256)
        for c in range(2):
            K.op("dve", lambda e: e.scalar_tensor_tensor(cqn.t[:, c, c0:c0 + w], cq.t[:, c, c0:c0 + w], qn.t[:, c:c + 1], r.t[:, 0:w], op0=ALU.mult, op1=ALU.mult), R=[cq, qn, r], W=[cqn])
        r = T.rms_rstd(cq, 1, c0, w, 128, srcchunk0=2)
        K.op("dve", lambda e: e.scalar_tensor_tensor(cqn.t[:, 2, c0:c0 + w], cq.t[:, 2, c0:c0 + w], kvn.t[:, 0:1], r.t[:, 0:w], op0=ALU.mult, op1=ALU.mult), R=[cq, kvn, r], W=[cqn])

    q1 = [K.sb("q1t%d" % i, [128, 512], F32) for i in range(2)]
    q2 = [K.sb("q2t%d" % i, [128, 512], F32) for i in range(2)]

    def epi_q(oc, m, si, c0, w, isctx, ps):
        if oc < 4:
            put_b("act", 128, w, lambda b: K.op("act", lambda e: e.copy(b.t[:, 0:w], ps.t[:, 0:w]), R=[ps], W=[b]), qT.t.ap()[:, oc, c0:c0 + w])
        elif oc == 4:
            K.op("act", lambda e: e.copy(q1[si % 2].t[:, 0:w], ps.t[:, 0:w]), R=[ps], W=[q1[si % 2]])
        else:
            K.op("act", lambda e: e.copy(q2[si % 2].t[:, 0:w], ps.t[:, 0:w]), R=[ps], W=[q2[si % 2]])
            T.rope(si, c0, w, q1[si % 2], q2[si % 2], 0, 128, io["cosq"], io["sinq"], put_b, qT.t.ap()[:, 4, c0:c0 + w], qT.t.ap()[:, 5, c0:c0 + w])
    T.proj(cqn, 2, w_uq, 0, 768, epi_q, tag="wuq")

    def epi_kv(oc, m, si, c0, w, isctx, ps):
        dst = kT if oc < 4 else vT
        put_b("act", 128, w, lambda b: K.op("act", lambda e: e.copy(b.t[:, 0:w], ps.t[:, 0:w]), R=[ps], W=[b]), dst.t.ap()[:, oc % 4, c0:c0 + w])
    T.proj(Buf3(cqn, 2), 1, w_ukv, 0, 1024, epi_kv, tag="wukv")
    k1 = K.sb("kr1s", [16, 512], F32)
    k2 = K.sb("kr2s", [16, 512], F32)
    for si, (c0, w, isctx) in enumerate(SLABS):
        K.op("act", lambda e: e.copy(k1.t[:, 0:w], kr.t[:, 0, c0:c0 + w]), R=[kr], W=[k1])
        K.op("act", lambda e: e.copy(k2.t[:, 0:w], kr.t[:, 1, c0:c0 + w]), R=[kr], W=[k2])
        T.rope(si, c0, w, k1, k2, 0, 16, io["cosq"], io["sinq"], put_b, krT.t.ap()[:, 0, c0:c0 + w], krT.t.ap()[:, 1, c0:c0 + w])


class _ChunkView:
    def __init__(self, t, off):
        self._t = t
        self._off = off

    def __getitem__(self, idx):
        p, c, n = idx
        return self._t[p, c + self._off, n]


def Buf3(buf, off):
    b = Buf(_ChunkView(buf.t, off), buf.name)
    return _Alias(buf, b.t)


class _Alias:
    def __init__(self, parent, t):
        object.__setattr__(self, "_p", parent)
        object.__setattr__(self, "t", t)

    def __getattr__(self, k):
        return getattr(object.__getattribute__(self, "_p"), k)

    def __setattr__(self, k, v):
        if k == "t":
            object.__setattr__(self, k, v)
        else:
            setattr(object.__getattribute__(self, "_p"), k, v)


NKEY = 8448
NQ = 8448
ATT_SCALE = 96 ** -0.5


def h_even_attn(K, io, n_units=2):
    QTd, KTd, Vd, OTd = io["QT"], io["KT"], io["V"], io["OT"]
    onesb = K.sb("a_ones", [128, 1], BF16)
    K.op("dve", lambda e: e.memset(onesb.t[:, :], 1.0), W=[onesb])
    QT = K.sb("a_QT", [128, NQ], BF16)
    KT = K.sb("a_KT", [128, NKEY], BF16)
    V = K.sb("a_V", [128, 66, 128], BF16)
    sq = [K.sb("a_sq%d" % i, [128, 512], BF16) for i in range(2)]
    qsq = K.sb("a_qsq", [1, NQ], F32)
    ksq = K.sb("a_ksq", [1, NKEY], F32)
    kmax = K.sb("a_kmax", [1, 1], F32)
    negm = K.sb("a_negm", [1, NQ], BF16)
    PT = [K.sb("a_PT%d" % i, [128, 512], BF16) for i in range(3)]
    stg = [K.sb("a_stg%d" % i, [128, 512], F32) for i in range(2)]
    for u in range(n_units):
        K.op("dve", lambda e: e.memset(KT.t[:, :], 1.0), W=[KT])
        K.op("pool", lambda e: e.memset(V.t[:, :, :], 1.0), W=[V])
        K.dma("sp", QT.t[0:96, :], QTd.t.ap()[u, 0:96, :], R=[QTd], W=[QT])
        K.dma("act", KT.t[0:96, :], KTd.t.ap()[u, 0:96, :], R=[KTd], W=[KT])
        K.dma("sp", V.t[:, :, 0:64], Vd.t.ap()[u], R=[Vd], W=[V])
        for (src, dst, n) in ((QT, qsq, NQ), (KT, ksq, NKEY)):
            i = 0
            for c0 in range(0, n, 512):
                w = min(512, n - c0)
                s = sq[i % 2]
                i += 1
                K.op("act", lambda e: e.activation(s.t[0:96, 0:w], src.t[0:96, c0:c0 + w], AF.Square), R=[src], W=[s])
                ps = K.ps_get()
                K.op("pe", lambda e: e.matmul(ps.t[0:1, 0:w], onesb.t[0:96, 0:1], s.t[0:96, 0:w], start=True, stop=True), R=[onesb, s], W=[ps])
                K.op("dve", lambda e: e.tensor_copy(dst.t[0:1, c0:c0 + w], ps.t[0:1, 0:w]), R=[ps], W=[dst])
        K.op("dve", lambda e: e.tensor_reduce(kmax.t[0:1, 0:1], ksq.t[0:1, :], axis=AX.X, op=ALU.max), R=[ksq], W=[kmax])
        K.op("dve", lambda e: e.tensor_scalar(qsq.t[0:1, :], qsq.t[0:1, :], kmax.t[0:1, 0:1], None, op0=ALU.mult), R=[qsq, kmax], W=[qsq])
        K.op("act", lambda e: e.activation(qsq.t[0:1, :], qsq.t[0:1, :], AF.Sqrt), R=[qsq], W=[qsq])
        K.op("dve", lambda e: e.tensor_scalar(negm.t[0:1, :], qsq.t[0:1, :], -1.0, None, op0=ALU.mult), R=[qsq], W=[negm])
        K.dma("sp", QT.t[96:97, :], negm.t[0:1, :], R=[negm], W=[QT])
        pi = 0
        slabs = [(512 * i, 512, 66) for i in range(16)] + [(8192, 256, 2)]
        for si, (c0, w, nkt) in enumerate(slabs):
            psO = K.psx[si % 2]
            nxt = None
            for kt in range(nkt):
                if nxt is None:
                    ps1 = K.ps_get()
                    K.op("pe", lambda e: e.matmul(ps1.t[:, 0:w], KT.t[0:97, kt * 128:(kt + 1) * 128], QT.t[0:97, c0:c0 + w], start=True, stop=True), R=[KT, QT], W=[ps1])
                else:
                    ps1 = nxt
                if kt + 1 < nkt:
                    nxt = K.ps_get()
                    K.op("pe", lambda e: e.matmul(nxt.t[:, 0:w], KT.t[0:97, (kt + 1) * 128:(kt + 2) * 128], QT.t[0:97, c0:c0 + w], start=True, stop=True), R=[KT, QT], W=[nxt])
                else:
                    nxt = None
                p = PT[pi % 3]
                pi += 1
                K.op("act", lambda e: e.activation(p.t[:, 0:w], ps1.t[:, 0:w], AF.Exp, scale=ATT_SCALE), R=[ps1], W=[p])
                K.op("pe", lambda e: e.matmul(psO.t[:, 0:w], V.t[:, kt, :], p.t[:, 0:w], start=(kt == 0), stop=(kt == nkt - 1)), R=[V, p], W=[psO])
            s = stg[si % 2]
            K.op("dve", lambda e: e.tensor_copy(s.t[:, 0:w], psO.t[:, 0:w]), R=[psO], W=[s])
            K.dma("sp", OTd.t.ap()[u, :, c0:c0 + w], s.t[:, 0:w], R=[s], W=[OTd])


def decl_ta_even(K):
    io = {}
    io["w_in_e"] = K.dram_in("w_in_e", [1024, 928], F32)
    io["qnT"] = K.dram_in("qnT", [128, 2], F32)
    io["kvnT"] = K.dram_in("kvnT", [128, 1], F32)
    io["w_uq"] = K.dram_in("w_uq", [256, 768], F32)
    io["w_ukv"] = K.dram_in("w_ukv", [128, 1024], F32)
    io["cosq"] = K.dram_in("cosq", [128, NT], F32)
    io["sinq"] = K.dram_in("sinq", [128, NT], F32)
    io["uT"] = K.dram_out("uT", [128, 4, NT], F32)
    io["qT"] = K.dram_out("qT", [128, 6, NT], BF16)
    io["kT"] = K.dram_out("kT", [128, 4, NT], BF16)
    io["vT"] = K.dram_out("vT", [128, 4, NT], BF16)
    io["krT"] = K.dram_out("krT", [16, 2, NT], BF16)
    return io


def decl_mod(K, tag):
    io = {}
    io["cT"] = K.dram_in("cT" + tag, [128, 8, 2], F32)
    io["w_mod"] = K.dram_in("w_mod" + tag, [1024, 6144], F32)
    io["b_modT"] = K.dram_in("b_modT" + tag, [128, 48], F32)
    io["n1gT"] = K.dram_in("n1gT" + tag, [128, 8], F32)
    io["n2gT"] = K.dram_in("n2gT" + tag, [128, 8], F32)
    return io


def build_tok(post, pre, final=False):
    nc = bass.Bass("TRN2", target_bir_lowering=False)
    K = Emit(nc)
    T = Tok(K)
    xin = K.dram_in("xT_in", [128, 8, NT], F32)
    hT = K.sb("hT", [128, 8, NT], BF16)
    mod_a = mod_b = None
    if post is not None:
        iom = decl_mod(K, "_a")
        cTa = T.load("cT_a", iom["cT"], [128, 8, 2])
        bma = T.load("bm_a", iom["b_modT"], [128, 48])
        with nc.named_scope("modvec_a"):
            mod_a = T.modvec(cTa, iom["w_mod"], bma, "a", groups=range(4, 12))
    if pre is not None:
        iom2 = decl_mod(K, "_b")
        cTb = T.load("cT_b", iom2["cT"], [128, 8, 2])
        bmb = T.load("bm_b", iom2["b_modT"], [128, 48])
        with nc.named_scope("modvec_b"):
            mod_b = T.modvec(cTb, iom2["w_mod"], bmb, "b", groups=range(0, 4))
    with K.scope():
        x = K.sb("xT", [128, 8, NT], F32)
        for c in range(8):
            K.dma("sp" if c % 2 == 0 else "act", x.t[:, c, :], xin.t.ap()[:, c, :], R=[xin], W=[x])
        if post is not None:
            iop = decl_tb(K, post)
            with K.scope():
                n2g = T.load("n2g_a", iom["n2gT"], [128, 8])
                tb_phase(K, T, x, hT, mod_a, n2g, iop, post)
            xout = K.dram_out("xT_out", [128, 8, NT], F32)
            for c in range(8):
                K.dma("sp", xout.t.ap()[:, c, :], x.t[:, c, :], R=[x], W=[xout])
            if final:
                fio = K.dram_in("fnT", [128, 8], F32)
                fo = K.dram_out("yT", [128, 8, NT], F32)
                fn = T.load("fn", fio, [128, 8])
                stgs = [K.sb("fstg%d" % i, [128, 512], F32) for i in range(3)]
                i = 0
                for (c0, w, isctx) in SLABS:
                    r = T.rms_rstd(x, 8, c0, w, D)
                    for c in range(8):
                        s = stgs[i % 3]
                        i += 1
                        K.op("dve", lambda e: e.scalar_tensor_tensor(s.t[:, 0:w], x.t[:, c, c0:c0 + w], fn.t[:, c:c + 1], r.t[:, 0:w], op0=ALU.mult, op1=ALU.mult), R=[x, fn, r], W=[s])
                        K.dma("sp", fo.t.ap()[:, c, c0:c0 + w], s.t[:, 0:w], R=[s], W=[fo])
        if pre is not None:
            with K.scope():
                n1g = T.load("n1g_b", iom2["n1gT"], [128, 8])
                with nc.named_scope("norm1"):
                    T.norm_mod(x, hT, mod_b, n1g, 0, 1, "1")
    if pre == "even":
        io = decl_ta_even(K)
        with K.scope(), nc.named_scope("ta_even"):
            ta_even(K, T, hT, io)
    elif pre == "odd":
        io = decl_ta_odd(K)
        with K.scope(), nc.named_scope("ta_odd"):
            ta_odd(K, T, hT, io)
    K.finish()
    return nc


def build_h_even():
    nc = bass.Bass("TRN2", target_bir_lowering=False)
    K = Emit(nc)
    io = {}
    io["QT"] = K.dram_in("QT", [2, 96, NQ], BF16)
    io["KT"] = K.dram_in("KT", [2, 96, NKEY], BF16)
    io["V"] = K.dram_in("V", [2, 128, 66, 64], BF16)
    io["OT"] = K.dram_out("OT", [2, 128, NQ], F32)
    with K.scope():
        h_even_attn(K, io)
    ios = decl_s5(K)
    with K.scope():
        h_even_s5(K, ios)
    K.finish()
    return nc


def to_fm(a):
    n, f = a.shape
    return np.ascontiguousarray(a.reshape(n, f // 128, 128).transpose(2, 1, 0))


def from_fm(t):
    p, c, n = t.shape
    return np.ascontiguousarray(t.transpose(2, 1, 0).reshape(n, c * p))


def vec_fm(v):
    return np.ascontiguousarray(v.reshape(-1, 128).T)


def rope_tables(dim):
    l = np.arange(8192)
    r = (l // 64).astype(np.float32)
    col = (l % 64).astype(np.float32)
    quarter = dim // 4
    inv = (np.float32(10000.0) ** (-np.arange(quarter, dtype=np.float32) / np.float32(quarter))).astype(np.float32)
    ang = np.concatenate([r[:, None] * inv, col[:, None] * inv], axis=-1).astype(np.float32)
    return np.cos(ang).astype(np.float32), np.sin(ang).astype(np.float32)


def core_rope(dim, reps, r):
    cos, sin = rope_tables(dim)
    h = dim // 2
    c = np.ones((reps * h, NT), np.float32)
    s = np.zeros((reps * h, NT), np.float32)
    c[:, 64:] = np.tile(cos[2048 * r:2048 * (r + 1)].T, (reps, 1))
    s[:, 64:] = np.tile(sin[2048 * r:2048 * (r + 1)].T, (reps, 1))
    return c, s


def perm_even(w_in, w_uq, w_ukv):
    ci = np.concatenate([np.arange(896), 896 + 2 * np.arange(16), 897 + 2 * np.arange(16)])
    nope = np.concatenate([96 * h + np.arange(64) for h in range(8)])
    r1 = np.concatenate([96 * h + 64 + 2 * np.arange(16) for h in range(8)])
    r2 = r1 + 1
    kn = np.concatenate([128 * h + np.arange(64) for h in range(8)])
    vv = kn + 64
    return (np.ascontiguousarray(w_in[:, ci]), np.ascontiguousarray(w_uq[:, np.concatenate([nope, r1, r2])]),
            np.ascontiguousarray(w_ukv[:, np.concatenate([kn, vv])]))


def mod_inputs(inp, li, b, tag):
    cT = np.stack([vec_fm(inp["c"][b]), vec_fm(inp["c_ctx"])], axis=-1)
    return {"cT" + tag: np.ascontiguousarray(cT), "w_mod" + tag: inp["w_mod"][li],
            "b_modT" + tag: vec_fm(inp["b_mod"][li]), "n1gT" + tag: vec_fm(inp["norm1_g"][li]),
            "n2gT" + tag: vec_fm(inp["norm2_g"][li])}


def ta_even_inputs(inp, j, r):
    w_in, w_uq, w_ukv = perm_even(inp["w_in_even"][j], inp["mla_w_uq"][j], inp["mla_w_ukv"][j])
    c, s = core_rope(32, 8, r)
    return {"w_in_e": w_in, "w_uq": w_uq, "w_ukv": w_ukv, "qnT": vec_fm(inp["mla_q_norm"][j]),
            "kvnT": vec_fm(inp["mla_kv_norm"][j]), "cosq": c, "sinq": s}


def attn_inputs(res, last=False):
    import ml_dtypes
    bf = ml_dtypes.bfloat16
    maps = []
    for core in range(8):
        QT = np.zeros((2, 96, NQ), bf)
        KT = np.zeros((2, 96, NKEY), bf)
        V = np.zeros((2, 128, 66, 64), bf)
        for ui in range(2):
            unit = core * 2 + ui
            b, h = unit // 8, unit % 8
            ch, ro = h // 2, (h % 2) * 64
            q = np.concatenate([np.concatenate([res[4 * b + r]["qT"][ro:ro + 64, ch, :],
                                                res[4 * b + r]["qT"][h * 16:h * 16 + 16, 4, :],
                                                res[4 * b + r]["qT"][h * 16:h * 16 + 16, 5, :]], axis=0) for r in range(4)], axis=1)
            k = np.concatenate([np.concatenate([res[4 * b + r]["kT"][ro:ro + 64, ch, :],
                                                res[4 * b + r]["krT"][:, 0, :], res[4 * b + r]["krT"][:, 1, :]], axis=0) for r in range(4)], axis=1)
            v = np.concatenate([res[4 * b + r]["vT"][ro:ro + 64, ch, :] for r in range(4)], axis=1)
            q = q.reshape(96, 4, NT)
            k = k.reshape(96, 4, NT)
            v = v.reshape(64, 4, NT)
            QT[ui] = np.concatenate([q[:, :, 64:].reshape(96, 8192), q[:, :, :64].reshape(96, 256)], axis=1)
            KT[ui] = np.concatenate([k[:, :, :64].reshape(96, 256), k[:, :, 64:].reshape(96, 8192)], axis=1)
            vv = np.concatenate([v[:, :, :64].reshape(64, 256), v[:, :, 64:].reshape(64, 8192)], axis=1)
            V[ui] = vv.T.reshape(66, 128, 64).transpose(1, 0, 2)
        maps.append({"QT": QT, "KT": KT, "V": V})
    return maps


NC8 = 1056
TWO_PI = 6.283185307179586


def decl_s5(K):
    io = {}
    for n in ("are", "aim", "ldt"):
        io[n] = K.dram_in("s5_" + n, [64, 8], F32)
    for n in ("bre", "bim", "creT", "cimT"):
        io[n] = K.dram_in("s5_" + n, [64, 8, 16], F32)
    io["U8"] = K.dram_in("s5_U8", [8, 128, 2 * NC8], F32)
    io["mask8"] = K.dram_in("s5_mask8", [128, 128], F32)
    io["ident"] = K.dram_in("s5_ident", [64, 64], F32)
    io["Y8"] = K.dram_out("s5_Y8", [8, 128, 2 * NC8], F32)
    return io


def h_even_s5(K, io):
    NU = 8
    N = 2 * NC8

    def ld(name, shape):
        b = K.sb("s5" + name, shape, F32)
        sl = tuple(slice(None) for _ in shape)
        K.dma("sp", b.t[sl], io[name].t.ap(), R=[io[name]], W=[b])
        return b
    are, aim, ldt = ld("are", [64, 8]), ld("aim", [64, 8]), ld("ldt", [64, 8])
    bre, bim, creT, cimT = ld("bre", [64, 8, 16]), ld("bim", [64, 8, 16]), ld("creT", [64, 8, 16]), ld("cimT", [64, 8, 16])
    mask8, ident = ld("mask8", [128, 128]), ld("ident", [64, 64])
    cnt = {"i": 0}

    def sm(shape=(64, 8), dt=F32):
        cnt["i"] += 1
        return K.sb("s5t%d" % cnt["i"], list(shape), dt)

    def tt(out, a, b, op, R, W, eng="dve"):
        K.op(eng, lambda e: e.tensor_tensor(out, a, b, op=op), R=R, W=W)

    dt_, lr, li, mag = sm(), sm(), sm(), sm()
    K.op("act", lambda e: e.activation(dt_.t[:, :], ldt.t[:, :], AF.Exp), R=[ldt], W=[dt_])
    tt(lr.t[:, :], are.t[:, :], dt_.t[:, :], ALU.mult, [are, dt_], [lr])
    tt(li.t[:, :], aim.t[:, :], dt_.t[:, :], ALU.mult, [aim, dt_], [li])
    K.op("act", lambda e: e.activation(mag.t[:, :], lr.t[:, :], AF.Exp), R=[lr], W=[mag])
    rho, irho = sm(), sm()
    K.op("act", lambda e: e.activation(rho.t[:, :], lr.t[:, :], AF.Exp, scale=8.0), R=[lr], W=[rho])
    K.op("act", lambda e: e.activation(irho.t[:, :], lr.t[:, :], AF.Exp, scale=-8.0), R=[lr], W=[irho])
    kf, ki, r0, rs, rc, s1, c1 = sm(), sm((64, 8), I32), sm(), sm(), sm(), sm(), sm()
    K.op("dve", lambda e: e.tensor_scalar(kf.t[:, :], li.t[:, :], 1.0 / TWO_PI, None, op0=ALU.mult), R=[li], W=[kf])
    K.op("dve", lambda e: e.tensor_copy(ki.t[:, :], kf.t[:, :]), R=[kf], W=[ki])
    K.op("dve", lambda e: e.tensor_copy(kf.t[:, :], ki.t[:, :]), R=[ki], W=[kf])
    K.op("dve", lambda e: e.scalar_tensor_tensor(r0.t[:, :], kf.t[:, :], -TWO_PI, li.t[:, :], op0=ALU.mult, op1=ALU.add), R=[kf, li], W=[r0])
    wa, wb2, wy = sm(), sm(), sm()

    def wrap(dst, shift):
        K.op("dve", lambda e: e.tensor_scalar(wy.t[:, :], r0.t[:, :], float(shift), None, op0=ALU.add), R=[r0], W=[wy])
        K.op("dve", lambda e: e.tensor_scalar(wa.t[:, :], wy.t[:, :], float(np.pi), -TWO_PI, op0=ALU.is_gt, op1=ALU.mult), R=[wy], W=[wa])
        K.op("dve", lambda e: e.tensor_scalar(wb2.t[:, :], wy.t[:, :], -float(np.pi), TWO_PI, op0=ALU.is_lt, op1=ALU.mult), R=[wy], W=[wb2])
        K.op("dve", lambda e: e.tensor_tensor(wa.t[:, :], wa.t[:, :], wb2.t[:, :], op=ALU.add), R=[wa, wb2], W=[wa])
        K.op("dve", lambda e: e.tensor_tensor(dst.t[:, :], wy.t[:, :], wa.t[:, :], op=ALU.add), R=[wy, wa], W=[dst])
    wrap(rs, 0.0)
    wrap(rc, np.pi / 2)
    K.op("act", lambda e: e.activation(s1.t[:, :], rs.t[:, :], AF.Sin), R=[rs], W=[s1])
    K.op("act", lambda e: e.activation(c1.t[:, :], rc.t[:, :], AF.Sin), R=[rc], W=[c1])
    abr, abi = sm(), sm()
    tt(abr.t[:, :], mag.t[:, :], c1.t[:, :], ALU.mult, [mag, c1], [abr])
    tt(abi.t[:, :], mag.t[:, :], s1.t[:, :], ALU.mult, [mag, s1], [abi])
    nr, den, fr, fi, t1, t2 = sm(), sm(), sm(), sm(), sm(), sm()
    K.op("dve", lambda e: e.tensor_scalar(nr.t[:, :], abr.t[:, :], -1.0, None, op0=ALU.add), R=[abr], W=[nr])
    tt(t1.t[:, :], are.t[:, :], are.t[:, :], ALU.mult, [are], [t1])
    tt(t2.t[:, :], aim.t[:, :], aim.t[:, :], ALU.mult, [aim], [t2])
    tt(den.t[:, :], t1.t[:, :], t2.t[:, :], ALU.add, [t1, t2], [den])
    K.op("dve", lambda e: e.reciprocal(den.t[:, :], den.t[:, :]), R=[den], W=[den])
    tt(t1.t[:, :], nr.t[:, :], are.t[:, :], ALU.mult, [nr, are], [t1])
    tt(t2.t[:, :], abi.t[:, :], aim.t[:, :], ALU.mult, [abi, aim], [t2])
    tt(t1.t[:, :], t1.t[:, :], t2.t[:, :], ALU.add, [t1, t2], [t1])
    tt(fr.t[:, :], t1.t[:, :], den.t[:, :], ALU.mult, [t1, den], [fr])
    tt(t1.t[:, :], abi.t[:, :], are.t[:, :], ALU.mult, [abi, are], [t1])
    tt(t2.t[:, :], nr.t[:, :], aim.t[:, :], ALU.mult, [nr, aim], [t2])
    tt(t1.t[:, :], t1.t[:, :], t2.t[:, :], ALU.subtract, [t1, t2], [t1])
    tt(fi.t[:, :], t1.t[:, :], den.t[:, :], ALU.mult, [t1, den], [fi])

    Apr, Api = sm((64, 8, 9)), sm((64, 8, 9))
    Qr, Qi = sm((64, 8, 8)), sm((64, 8, 8))
    K.op("dve", lambda e: e.memset(Apr.t[:, :, 0], 1.0), W=[Apr])
    K.op("dve", lambda e: e.memset(Api.t[:, :, 0], 0.0), W=[Api])
    K.op("dve", lambda e: e.tensor_copy(Qr.t[:, :, 0], fr.t[:, :]), R=[fr], W=[Qr])
    K.op("dve", lambda e: e.tensor_copy(Qi.t[:, :, 0], fi.t[:, :]), R=[fi], W=[Qi])

    def cstep(Tr, Ti, n):
        for tau in range(1, n):
            tt(t1.t[:, :], Tr.t[:, :, tau - 1], abr.t[:, :], ALU.mult, [Tr, abr], [t1])
            tt(t2.t[:, :], Ti.t[:, :, tau - 1], abi.t[:, :], ALU.mult, [Ti, abi], [t2])
            tt(Tr.t[:, :, tau], t1.t[:, :], t2.t[:, :], ALU.subtract, [t1, t2], [Tr])
            tt(t1.t[:, :], Tr.t[:, :, tau - 1], abi.t[:, :], ALU.mult, [Tr, abi], [t1])
            tt(t2.t[:, :], Ti.t[:, :, tau - 1], abr.t[:, :], ALU.mult, [Ti, abr], [t2])
            tt(Ti.t[:, :, tau], t1.t[:, :], t2.t[:, :], ALU.add, [t1, t2], [Ti])
    cstep(Apr, Api, 9)
    cstep(Qr, Qi, 8)
    nApi, nQi = sm((64, 8, 9)), sm((64, 8, 8))
    nApr = sm((64, 8, 9))
    K.op("dve", lambda e: e.tensor_scalar(nApr.t[:, :, :], Apr.t[:, :, :], -1.0, None, op0=ALU.mult), R=[Apr], W=[nApr])
    K.op("dve", lambda e: e.tensor_scalar(nApi.t[:, :, :], Api.t[:, :, :], -1.0, None, op0=ALU.mult), R=[Api], W=[nApi])
    K.op("dve", lambda e: e.tensor_scalar(nQi.t[:, :, :], Qi.t[:, :, :], -1.0, None, op0=ALU.mult), R=[Qi], W=[nQi])
    n2, i8r, i8i, ni8i, phr, phi = sm(), sm(), sm(), sm(), sm(), sm()
    tt(t1.t[:, :], Apr.t[:, :, 8], Apr.t[:, :, 8], ALU.mult, [Apr], [t1])
    tt(t2.t[:, :], Api.t[:, :, 8], Api.t[:, :, 8], ALU.mult, [Api], [t2])
    tt(n2.t[:, :], t1.t[:, :], t2.t[:, :], ALU.add, [t1, t2], [n2])
    K.op("dve", lambda e: e.reciprocal(n2.t[:, :], n2.t[:, :]), R=[n2], W=[n2])
    tt(i8r.t[:, :], Apr.t[:, :, 8], n2.t[:, :], ALU.mult, [Apr, n2], [i8r])
    tt(ni8i.t[:, :], Api.t[:, :, 8], n2.t[:, :], ALU.mult, [Api, n2], [ni8i])
    K.op("dve", lambda e: e.tensor_scalar(i8i.t[:, :], ni8i.t[:, :], -1.0, None, op0=ALU.mult), R=[ni8i], W=[i8i])
    tt(phr.t[:, :], Apr.t[:, :, 8], irho.t[:, :], ALU.mult, [Apr, irho], [phr])
    tt(phi.t[:, :], Api.t[:, :, 8], irho.t[:, :], ALU.mult, [Api, irho], [phi])
    pwr, pwi, npwi = sm((64, 8, 11)), sm((64, 8, 11)), sm((64, 8, 11))
    K.op("dve", lambda e: e.tensor_copy(pwr.t[:, :, 0], phr.t[:, :]), R=[phr], W=[pwr])
    K.op("dve", lambda e: e.tensor_copy(pwi.t[:, :, 0], phi.t[:, :]), R=[phi], W=[pwi])
    for j in range(1, 11):
        tt(t1.t[:, :], pwr.t[:, :, j - 1], pwr.t[:, :, j - 1], ALU.mult, [pwr], [t1])
        tt(t2.t[:, :], pwi.t[:, :, j - 1], pwi.t[:, :, j - 1], ALU.mult, [pwi], [t2])
        tt(pwr.t[:, :, j], t1.t[:, :], t2.t[:, :], ALU.subtract, [t1, t2], [pwr])
        tt(t1.t[:, :], pwr.t[:, :, j - 1], pwi.t[:, :, j - 1], ALU.mult, [pwr, pwi], [t1])
        K.op("dve", lambda e: e.tensor_scalar(pwi.t[:, :, j], t1.t[:, :], 2.0, None, op0=ALU.mult), R=[t1], W=[pwi])
    K.op("dve", lambda e: e.tensor_scalar(npwi.t[:, :, :], pwi.t[:, :, :], -1.0, None, op0=ALU.mult), R=[pwi], W=[npwi])

    if "dbg" in io:
        for i, tb_ in enumerate((dt_, lr, li, mag, r0, rs, rc, s1, c1, abr, abi, fr, fi, i8r, i8i, phr, phi, rho)):
            K.dma("sp", io["dbg"].t.ap()[:, i, :], tb_.t[:, :], R=[tb_], W=[io["dbg"]])
    Wre, Wim = sm((64, 8, 16)), sm((64, 8, 16))
    Wpr, Wpi = sm((64, 128)), sm((64, 128))
    RrT, RiT = sm((64, 8, 16)), sm((64, 8, 16))
    RrTb, RiTb = sm((64, 128), BF16), sm((64, 128), BF16)
    tmp16 = sm((64, 16))
    tmp128 = sm((64, 128))
    MTb = sm((128, 128), BF16)
    WreTb, WimTb = sm((128, 64), BF16), sm((128, 64), BF16)
    Phr, Phi = sm((64, NC8)), sm((64, NC8))
    ptmp = sm((64, 512))
    rmask = sm((64, N))
    U8f = [sm((128, N)) for _ in range(2)]
    U8b = sm((128, N), BF16)
    Sre, Sim = sm((64, N)), sm((64, N))
    Gr, Gi = sm((64, N)), sm((64, N))
    Gr2, Gi2 = sm((64, N)), sm((64, N))
    ta, tb = sm((64, NC8)), sm((64, NC8))
    Hpr, Hpi = sm((64, N), BF16), sm((64, N), BF16)
    ystg = [sm((128, 512)) for _ in range(2)]
    K.op("dve", lambda e: e.memset(Hpr.t[:, :], 0.0), W=[Hpr])
    K.op("dve", lambda e: e.memset(Hpi.t[:, :], 0.0), W=[Hpi])
    slabs = [(c0, min(512, N - c0)) for c0 in range(0, N, 512)]

    for u in range(NU):
        K.dma("act", U8f[u % 2].t[:, :], io["U8"].t.ap()[u], R=[io["U8"]], W=[U8f[u % 2]])
        for s in range(8):
            q = 7 - s
            K.op("dve", lambda e: e.tensor_scalar(tmp16.t[:, :], bre.t[:, u, :], Qr.t[:, u, q:q + 1], None, op0=ALU.mult), R=[bre, Qr], W=[tmp16])
            K.op("dve", lambda e: e.scalar_tensor_tensor(Wre.t[:, s, :], bim.t[:, u, :], nQi.t[:, u, q:q + 1], tmp16.t[:, :], op0=ALU.mult, op1=ALU.add), R=[bim, nQi, tmp16], W=[Wre])
            K.op("dve", lambda e: e.tensor_scalar(tmp16.t[:, :], bim.t[:, u, :], Qr.t[:, u, q:q + 1], None, op0=ALU.mult), R=[bim, Qr], W=[tmp16])
            K.op("dve", lambda e: e.scalar_tensor_tensor(Wim.t[:, s, :], bre.t[:, u, :], Qi.t[:, u, q:q + 1], tmp16.t[:, :], op0=ALU.mult, op1=ALU.add), R=[bre, Qi, tmp16], W=[Wim])
        for t in range(8):
            K.op("dve", lambda e: e.tensor_scalar(tmp16.t[:, :], creT.t[:, u, :], Apr.t[:, u, t + 1:t + 2], None, op0=ALU.mult), R=[creT, Apr], W=[tmp16])
            K.op("dve", lambda e: e.scalar_tensor_tensor(RrT.t[:, t, :], cimT.t[:, u, :], nApi.t[:, u, t + 1:t + 2], tmp16.t[:, :], op0=ALU.mult, op1=ALU.add), R=[cimT, nApi, tmp16], W=[RrT])
            K.op("dve", lambda e: e.tensor_scalar(tmp16.t[:, :], creT.t[:, u, :], nApi.t[:, u, t + 1:t + 2], None, op0=ALU.mult), R=[creT, nApi], W=[tmp16])
            K.op("dve", lambda e: e.scalar_tensor_tensor(RiT.t[:, t, :], cimT.t[:, u, :], nApr.t[:, u, t + 1:t + 2], tmp16.t[:, :], op0=ALU.mult, op1=ALU.add), R=[cimT, nApr, tmp16], W=[RiT])
        Wre2 = Wre.t[:, :, :].rearrange("p s k -> p (s k)")
        Wim2 = Wim.t[:, :, :].rearrange("p s k -> p (s k)")
        Rr2 = RrT.t[:, :, :].rearrange("p s k -> p (s k)")
        Ri2 = RiT.t[:, :, :].rearrange("p s k -> p (s k)")
        K.op("dve", lambda e: e.tensor_scalar(tmp128.t[:, :], Wre2, i8r.t[:, u:u + 1], None, op0=ALU.mult), R=[Wre, i8r], W=[tmp128])
        K.op("dve", lambda e: e.scalar_tensor_tensor(Wpr.t[:, :], Wim2, ni8i.t[:, u:u + 1], tmp128.t[:, :], op0=ALU.mult, op1=ALU.add), R=[Wim, ni8i, tmp128], W=[Wpr])
        K.op("dve", lambda e: e.tensor_scalar(tmp128.t[:, :], Wim2, i8r.t[:, u:u + 1], None, op0=ALU.mult), R=[Wim, i8r], W=[tmp128])
        K.op("dve", lambda e: e.scalar_tensor_tensor(Wpi.t[:, :], Wre2, i8i.t[:, u:u + 1], tmp128.t[:, :], op0=ALU.mult, op1=ALU.add), R=[Wre, i8i, tmp128], W=[Wpi])
        ps = K.ps_get()
        K.op("pe", lambda e: e.matmul(ps.t[:, 0:128], Wpr.t[:, :], Rr2, start=True, stop=False), R=[Wpr, RrT], W=[ps])
        K.op("pe", lambda e: e.matmul(ps.t[:, 0:128], Wpi.t[:, :], Ri2, start=False, stop=True), R=[Wpi, RiT], W=[ps])
        K.op("dve", lambda e: e.tensor_tensor(MTb.t[:, :], ps.t[:, 0:128], mask8.t[:, :], op=ALU.mult), R=[ps, mask8], W=[MTb])
        ps = K.ps_get()
        K.op("pe", lambda e: e.matmul(ps.t[:, 0:64], Wre2, ident.t[:, :], start=True, stop=True), R=[Wre, ident], W=[ps])
        K.op("act", lambda e: e.copy(WreTb.t[:, :], ps.t[:, 0:64]), R=[ps], W=[WreTb])
        ps = K.ps_get()
        K.op("pe", lambda e: e.matmul(ps.t[:, 0:64], Wim2, ident.t[:, :], start=True, stop=True), R=[Wim, ident], W=[ps])
        K.op("act", lambda e: e.copy(WimTb.t[:, :], ps.t[:, 0:64]), R=[ps], W=[WimTb])
        K.op("act", lambda e: e.copy(RrTb.t[:, :], Rr2), R=[RrT], W=[RrTb])
        K.op("act", lambda e: e.copy(RiTb.t[:, :], Ri2), R=[RiT], W=[RiTb])
        K.op("pool", lambda e: e.memset(Phr.t[:, 0:1], 1.0), W=[Phr])
        K.op("pool", lambda e: e.memset(Phi.t[:, 0:1], 0.0), W=[Phi])
        n = 1
        j = 0
        while n < NC8:
            m = min(n, NC8 - n)
            for c0 in range(0, m, 512):
                w = min(512, m - c0)
                K.op("dve", lambda e: e.tensor_scalar(ptmp.t[:, 0:w], Phr.t[:, c0:c0 + w], pwr.t[:, u, j:j + 1], None, op0=ALU.mult), R=[Phr, pwr], W=[ptmp])
                K.op("dve", lambda e: e.scalar_tensor_tensor(Phr.t[:, n + c0:n + c0 + w], Phi.t[:, c0:c0 + w], npwi.t[:, u, j:j + 1], ptmp.t[:, 0:w], op0=ALU.mult, op1=ALU.add), R=[Phi, npwi, ptmp, Phr], W=[Phr])
                K.op("dve", lambda e: e.tensor_scalar(ptmp.t[:, 0:w], Phr.t[:, c0:c0 + w], pwi.t[:, u, j:j + 1], None, op0=ALU.mult), R=[Phr, pwi], W=[ptmp])
                K.op("dve", lambda e: e.scalar_tensor_tensor(Phi.t[:, n + c0:n + c0 + w], Phi.t[:, c0:c0 + w], pwr.t[:, u, j:j + 1], ptmp.t[:, 0:w], op0=ALU.mult, op1=ALU.add), R=[Phi, pwr, ptmp], W=[Phi])
            n *= 2
            j += 1
        K.op("pool", lambda e: e.memset(rmask.t[:, :], 1.0), W=[rmask])
        K.op("pool", lambda e: e.tensor_scalar(rmask.t[:, :], rmask.t[:, :], rho.t[:, u:u + 1], None, op0=ALU.mult), R=[rho, rmask], W=[rmask])
        K.op("pool", lambda e: e.memset(rmask.t[:, 0:1], 0.0), W=[rmask])
        K.op("pool", lambda e: e.memset(rmask.t[:, NC8:NC8 + 1], 0.0), W=[rmask])
        Uf = U8f[u % 2]
        K.op("act", lambda e: e.copy(U8b.t[:, :], Uf.t[:, :]), R=[Uf], W=[U8b])
        for (c0, w) in slabs:
            for (WT, S) in ((WreTb, Sre), (WimTb, Sim)):
                ps = K.ps_get()
                K.op("pe", lambda e: e.matmul(ps.t[0:64, 0:w], WT.t[:, :], U8b.t[:, c0:c0 + w], start=True, stop=True), R=[WT, U8b], W=[ps])
                K.op("act", lambda e: e.copy(S.t[:, c0:c0 + w], ps.t[0:64, 0:w]), R=[ps], W=[S])
        for hb in range(2):
            sl = slice(hb * NC8, (hb + 1) * NC8)
            tt(ta.t[:, :], Phr.t[:, :], Sre.t[:, sl], ALU.mult, [Phr, Sre], [ta])
            tt(tb.t[:, :], Phi.t[:, :], Sim.t[:, sl], ALU.mult, [Phi, Sim], [tb], eng="pool")
            tt(Gr.t[:, sl], ta.t[:, :], tb.t[:, :], ALU.add, [ta, tb], [Gr])
            tt(ta.t[:, :], Phr.t[:, :], Sim.t[:, sl], ALU.mult, [Phr, Sim], [ta])
            tt(tb.t[:, :], Phi.t[:, :], Sre.t[:, sl], ALU.mult, [Phi, Sre], [tb], eng="pool")
            tt(Gi.t[:, sl], ta.t[:, :], tb.t[:, :], ALU.subtract, [ta, tb], [Gi])
        K.op("dve", lambda e: e.tensor_tensor_scan(Gr2.t[:, :], rmask.t[:, :], Gr.t[:, :], 0.0, op0=ALU.mult, op1=ALU.add), R=[rmask, Gr], W=[Gr2])
        K.op("dve", lambda e: e.tensor_tensor_scan(Gi2.t[:, :], rmask.t[:, :], Gi.t[:, :], 0.0, op0=ALU.mult, op1=ALU.add), R=[rmask, Gi], W=[Gi2])
        for hb in range(2):
            o = hb * NC8
            n1 = NC8 - 1
            tt(ta.t[:, 0:n1], Phr.t[:, 0:n1], Gr2.t[:, o:o + n1], ALU.mult, [Phr, Gr2], [ta])
            tt(tb.t[:, 0:n1], Phi.t[:, 0:n1], Gi2.t[:, o:o + n1], ALU.mult, [Phi, Gi2], [tb], eng="pool")
            tt(Hpr.t[:, o + 1:o + 1 + n1], ta.t[:, 0:n1], tb.t[:, 0:n1], ALU.subtract, [ta, tb], [Hpr])
            tt(ta.t[:, 0:n1], Phr.t[:, 0:n1], Gi2.t[:, o:o + n1], ALU.mult, [Phr, Gi2], [ta])
            tt(tb.t[:, 0:n1], Phi.t[:, 0:n1], Gr2.t[:, o:o + n1], ALU.mult, [Phi, Gr2], [tb], eng="pool")
            tt(Hpi.t[:, o + 1:o + 1 + n1], ta.t[:, 0:n1], tb.t[:, 0:n1], ALU.add, [ta, tb], [Hpi])
        for si, (c0, w) in enumerate(slabs):
            ps = K.ps_get()
            K.op("pe", lambda e: e.matmul(ps.t[:, 0:w], MTb.t[:, :], U8b.t[:, c0:c0 + w], start=True, stop=False), R=[MTb, U8b], W=[ps])
            K.op("pe", lambda e: e.matmul(ps.t[:, 0:w], RrTb.t[:, :], Hpr.t[:, c0:c0 + w], start=False, stop=False), R=[RrTb, Hpr], W=[ps])
            K.op("pe", lambda e: e.matmul(ps.t[:, 0:w], RiTb.t[:, :], Hpi.t[:, c0:c0 + w], start=False, stop=True), R=[RiTb, Hpi], W=[ps])
            s = ystg[si % 2]
            K.op("act", lambda e: e.copy(s.t[:, 0:w], ps.t[:, 0:w]), R=[ps], W=[s])
            K.dma("sp", io["Y8"].t.ap()[u, :, c0:c0 + w], s.t[:, 0:w], R=[s], W=[io["Y8"]])


def s5_inputs(inp, j, uT_list):
    useq = []
    for b in range(2):
        parts = [from_fm(uT_list[4 * b + r]) for r in range(4)]
        ctx = np.concatenate([p[:64] for p in parts], 0)
        lat = np.concatenate([p[64:] for p in parts], 0)
        useq.append((ctx, lat))
    mask8 = (np.arange(128)[:, None] // 16 <= np.arange(128)[None, :] // 16).astype(np.float32)
    ident = np.eye(64, dtype=np.float32)
    maps = []
    for core in range(8):
        m = {"s5_mask8": mask8, "s5_ident": ident}
        U8 = np.zeros((8, 128, 2 * NC8), np.float32)
        sel = lambda a: np.stack([a[j, d, 4 * core + gl] for d in range(2) for gl in range(4)], axis=-1)
        m["s5_are"] = np.ascontiguousarray(sel(inp["s5_a_re"]))
        m["s5_aim"] = np.ascontiguousarray(sel(inp["s5_a_im"]))
        m["s5_ldt"] = np.ascontiguousarray(np.broadcast_to(np.stack([inp["s5_log_dt"][j, d, 4 * core + gl] for d in range(2) for gl in range(4)])[None, :], (64, 8)))
        m["s5_bre"] = np.ascontiguousarray(np.stack([inp["s5_b_re"][j, d, 4 * core + gl] for d in range(2) for gl in range(4)], axis=1))
        m["s5_bim"] = np.ascontiguousarray(np.stack([inp["s5_b_im"][j, d, 4 * core + gl] for d in range(2) for gl in range(4)], axis=1))
        m["s5_creT"] = np.ascontiguousarray(np.stack([inp["s5_c_re"][j, d, 4 * core + gl].T for d in range(2) for gl in range(4)], axis=1))
        m["s5_cimT"] = np.ascontiguousarray(np.stack([inp["s5_c_im"][j, d, 4 * core + gl].T for d in range(2) for gl in range(4)], axis=1))
        for d in range(2):
            for gl in range(4):
                g = 4 * core + gl
                for b in range(2):
                    ctx, lat = useq[b]
                    if d == 0:
                        seq = np.concatenate([ctx[:, 16 * g:16 * g + 16], lat[:, 16 * g:16 * g + 16]], 0)
                    else:
                        seq = np.concatenate([ctx[::-1, 16 * g:16 * g + 16], lat[::-1, 16 * g:16 * g + 16]], 0)
                    U8[d * 4 + gl, :, b * NC8:(b + 1) * NC8] = seq.reshape(NC8, 128).T
        m["s5_U8"] = U8
        maps.append(m)
    return maps


def s5_outputs(Y8_list):
    yf = np.zeros((2, 8448, 512), np.float32)
    yb = np.zeros((2, 8448, 512), np.float32)
    for core in range(8):
        Y8 = Y8_list[core]
        for d in range(2):
            for gl in range(4):
                g = 4 * core + gl
                for b in range(2):
                    seq = Y8[d * 4 + gl, :, b * NC8:(b + 1) * NC8].T.reshape(8448, 16)
                    if d == 0:
                        yf[b, :, 16 * g:16 * g + 16] = seq
                    else:
                        yb[b, :256, 16 * g:16 * g + 16] = seq[:256][::-1]
                        yb[b, 256:, 16 * g:16 * g + 16] = seq[256:][::-1]
    return yf, yb


GELU_C = 2.0 * 0.7978845608028654


def decl_tb(K, parity):
    io = {}
    if parity == "even":
        for n in ("uTi", "yfT", "ybT", "OTn", "denT"):
            io[n] = K.dram_in(n, [128, 4, NT], F32)
        io["dT"] = K.dram_in("dT", [128, 4], F32)
        io["w_glu"] = K.dram_in("w_glu", [512, 512], F32)
    else:
        for n in ("of", "ob", "gT"):
            io[n] = K.dram_in(n, [128, 8, NT], F32)
        io["gnT"] = K.dram_in("gnT", [128, 8], F32)
    io["w_out"] = K.dram_in("w_out", [1024, 1024], F32)
    io["w1"] = K.dram_in("ffn_w1", [1024, DFF], F32)
    io["w3"] = K.dram_in("ffn_w3", [1024, DFF], F32)
    io["w2"] = K.dram_in("ffn_w2", [DFF, 1024], F32)
    return io


def tb_phase(K, T, x, hT, mod, n2g, io, parity):
    with K.scope(), K.nc.named_scope("mix_post"):
        stg = [K.sb("tbs%d" % i, [128, 512], F32) for i in range(6)]
        st = {"i": 0}

        def ldslab(d, c, c0, w, q="sp"):
            b = stg[st["i"] % 6]
            st["i"] += 1
            K.dma(q, b.t[:, 0:w], d.t.ap()[:, c, c0:c0 + w], R=[d], W=[b])
            return b

        if parity == "even":
            dT = T.load("dT", io["dT"], [128, 4])
            zb = K.sb("zbT", [128, 4, NT], BF16)
            z = zb
            for (c0, w, isctx) in SLABS:
                for c in range(4):
                    u = ldslab(io["uTi"], c, c0, w)
                    yf = ldslab(io["yfT"], c, c0, w, "act")
                    yb = ldslab(io["ybT"], c, c0, w)
                    t = T.gettmp()
                    t2 = T.gettmp()
                    K.op("dve", lambda e: e.scalar_tensor_tensor(t.t[:, 0:w], u.t[:, 0:w], dT.t[:, c:c + 1], yf.t[:, 0:w], op0=ALU.mult, op1=ALU.add), R=[u, dT, yf], W=[t])
                    K.op("dve", lambda e: e.tensor_tensor(t.t[:, 0:w], t.t[:, 0:w], yb.t[:, 0:w], op=ALU.add), R=[t, yb], W=[t])
                    K.op("dve", lambda e: e.tensor_tensor(t2.t[:, 0:w], t.t[:, 0:w], t.t[:, 0:w], op=ALU.mult), R=[t], W=[t2])
                    K.op("dve", lambda e: e.tensor_scalar(t2.t[:, 0:w], t2.t[:, 0:w], 0.044715, 1.0, op0=ALU.mult, op1=ALU.add), R=[t2], W=[t2])
                    K.op("dve", lambda e: e.tensor_tensor(t2.t[:, 0:w], t2.t[:, 0:w], t.t[:, 0:w], op=ALU.mult), R=[t2, t], W=[t2])
                    K.op("act", lambda e: e.activation(t2.t[:, 0:w], t2.t[:, 0:w], AF.Sigmoid, scale=GELU_C), R=[t2], W=[t2])
                    K.op("dve", lambda e: e.tensor_tensor(zb.t[:, c, c0:c0 + w], t.t[:, 0:w], t2.t[:, 0:w], op=ALU.mult), R=[t, t2], W=[zb])
                    o = ldslab(io["OTn"], c, c0, w, "act")
                    dn = ldslab(io["denT"], c, c0, w)
                    K.op("dve", lambda e: e.reciprocal(dn.t[:, 0:w], dn.t[:, 0:w]), R=[dn], W=[dn])
                    K.op("dve", lambda e: e.tensor_tensor(hT.t[:, 4 + c, c0:c0 + w], o.t[:, 0:w], dn.t[:, 0:w], op=ALU.mult), R=[o, dn], W=[hT])

            def epi_glu(oc, m, si, c0, w, isctx, ps):
                t = T.gettmp()
                K.op("act", lambda e: e.activation(t.t[:, 0:w], ps.t[:, 0:w], AF.Sigmoid), R=[ps], W=[t])
                K.op("dve", lambda e: e.tensor_tensor(hT.t[:, oc, c0:c0 + w], z.t[:, oc, c0:c0 + w], t.t[:, 0:w], op=ALU.mult), R=[z, t], W=[hT])
            T.proj(zb, 4, io["w_glu"], 0, 512, epi_glu, tag="wglu")
        else:
            gn = T.load("gnT", io["gnT"], [128, 8])
            for (c0, w, isctx) in SLABS:
                for c in range(8):
                    a = ldslab(io["of"], c, c0, w)
                    b = ldslab(io["ob"], c, c0, w, "act")
                    g = ldslab(io["gT"], c, c0, w)
                    o = T.gettmp()
                    K.op("dve", lambda e: e.tensor_tensor(o.t[:, 0:w], a.t[:, 0:w], b.t[:, 0:w], op=ALU.add), R=[a, b], W=[o])
                    if c < 4:
                        ps = K.ps_get()
                        K.op("pe", lambda e: e.matmul(ps.t[:, 0:w], T.ones.t[:, :], o.t[:, 0:w], start=True, stop=True), R=[T.ones, o], W=[ps])
                        K.op("dve", lambda e: e.scalar_tensor_tensor(o.t[:, 0:w], ps.t[:, 0:w], -1.0 / 128, o.t[:, 0:w], op0=ALU.mult, op1=ALU.add), R=[ps, o], W=[o])
                    sq = T.gettmp()
                    K.op("act", lambda e: e.activation(sq.t[:, 0:w], o.t[:, 0:w], AF.Square), R=[o], W=[sq])
                    ps = K.ps_get()
                    K.op("pe", lambda e: e.matmul(ps.t[:, 0:w], T.ones.t[:, :], sq.t[:, 0:w], start=True, stop=True), R=[T.ones, sq], W=[ps])
                    K.op("act", lambda e: e.activation(sq.t[:, 0:w], ps.t[:, 0:w], AF.Ln, bias=T.eps.t[:, 0:1], scale=1.0 / 128), R=[ps, T.eps], W=[sq])
                    K.op("act", lambda e: e.activation(sq.t[:, 0:w], sq.t[:, 0:w], AF.Exp, scale=-0.5), R=[sq], W=[sq])
                    K.op("dve", lambda e: e.scalar_tensor_tensor(o.t[:, 0:w], o.t[:, 0:w], gn.t[:, c:c + 1], sq.t[:, 0:w], op0=ALU.mult, op1=ALU.mult), R=[o, gn, sq], W=[o])
                    K.op("act", lambda e: e.activation(g.t[:, 0:w], g.t[:, 0:w], AF.Silu), R=[g], W=[g])
                    K.op("dve", lambda e: e.tensor_tensor(hT.t[:, c, c0:c0 + w], o.t[:, 0:w], g.t[:, 0:w], op=ALU.mult), R=[o, g], W=[hT])

    def epi_out(oc, m, si, c0, w, isctx, ps):
        K.op("dve", lambda e: e.scalar_tensor_tensor(x.t[:, oc, c0:c0 + w], ps.t[:, 0:w], mod.t[:, isctx, 16 + oc:17 + oc], x.t[:, oc, c0:c0 + w], op0=ALU.mult, op1=ALU.add), R=[ps, mod, x], W=[x])
    with K.nc.named_scope("w_out"):
        T.proj(hT, 8, io["w_out"], 0, 1024, epi_out, tag="wout")
    with K.nc.named_scope("norm2"):
        T.norm_mod(x, hT, mod, n2g, 3, 4, "2")
    _ffn_sid = K.nc.enter_named_scope("ffn", False)[0]
    FG = 2
    w1v = io["w1"].t.ap().rearrange("(c p) n -> p c n", p=128)
    w3v = io["w3"].t.ap().rearrange("(c p) n -> p c n", p=128)
    w2v = io["w2"].t.ap().rearrange("(c p) n -> p c n", p=128)
    wb1 = [K.sb("f1_%d" % i, [128, 8, 128 * FG], BF16) for i in range(2)]
    wb3 = [K.sb("f3_%d" % i, [128, 8, 128 * FG], BF16) for i in range(2)]
    wb2 = [K.sb("f2_%d" % i, [128, FG, 1024], BF16) for i in range(2)]
    aT = [K.sb("faT%d" % i, [128, FG, 512], BF16) for i in range(2)]
    sl = [K.sb("fsl%d" % i, [128, 512], F32) for i in range(2)]
    ai = 0
    for fg in range(22 // FG):
        b1, b3, b2 = wb1[fg % 2], wb3[fg % 2], wb2[fg % 2]
        K.dma("pool", b1.t[:, :, :], w1v[:, :, fg * 128 * FG:(fg + 1) * 128 * FG], R=[io["w1"]], W=[b1])
        K.dma("pool", b3.t[:, :, :], w3v[:, :, fg * 128 * FG:(fg + 1) * 128 * FG], R=[io["w3"]], W=[b3])
        K.dma("pool", b2.t[:, :, :], w2v[:, fg * FG:(fg + 1) * FG, :], R=[io["w2"]], W=[b2])
        for (c0, w, isctx) in SLABS:
            a = aT[ai % 2]
            ai += 1
            for fc in range(FG):
                ps1 = K.ps_get()
                for kc in range(8):
                    K.op("pe", lambda e: e.matmul(ps1.t[:, 0:w], b1.t[:, kc, fc * 128:(fc + 1) * 128], hT.t[:, kc, c0:c0 + w], start=(kc == 0), stop=(kc == 7)), R=[b1, hT], W=[ps1])
                ps3 = K.ps_get()
                for kc in range(8):
                    K.op("pe", lambda e: e.matmul(ps3.t[:, 0:w], b3.t[:, kc, fc * 128:(fc + 1) * 128], hT.t[:, kc, c0:c0 + w], start=(kc == 0), stop=(kc == 7)), R=[b3, hT], W=[ps3])
                s_ = sl[fc % 2]
                K.op("act", lambda e: e.activation(s_.t[:, 0:w], ps1.t[:, 0:w], AF.Silu), R=[ps1], W=[s_])
                K.op("dve", lambda e: e.tensor_tensor(a.t[:, fc, 0:w], s_.t[:, 0:w], ps3.t[:, 0:w], op=ALU.mult), R=[s_, ps3], W=[a])
            for oc in range(8):
                ps = K.ps_get()
                for fc in range(FG):
                    K.op("pe", lambda e: e.matmul(ps.t[:, 0:w], b2.t[:, fc, oc * 128:(oc + 1) * 128], a.t[:, fc, 0:w], start=(fc == 0), stop=(fc == FG - 1)), R=[b2, a], W=[ps])
                K.op("dve", lambda e: e.scalar_tensor_tensor(x.t[:, oc, c0:c0 + w], ps.t[:, 0:w], mod.t[:, isctx, 40 + oc:41 + oc], x.t[:, oc, c0:c0 + w], op0=ALU.mult, op1=ALU.add), R=[ps, mod, x], W=[x])
    K.nc.leave_named_scope("ffn", _ffn_sid, False)


def decl_ta_odd(K):
    io = {}
    io["w_in_o"] = K.dram_in("w_in_o", [1024, 4608], F32)
    io["cosr"] = K.dram_in("cosr", [128, NT], F32)
    io["sinr"] = K.dram_in("sinr", [128, NT], F32)
    io["pb"] = K.dram_out("pb", [128, 20, NT], BF16)
    io["pf"] = K.dram_out("pf", [128, 16, NT], F32)
    return io


def ta_odd(K, T, hT, io):
    pb, pf = io["pb"], io["pf"]
    put_b = stage_out(K, T, pb, None, BF16)
    put_f = stage_out(K, T, pf, None, F32)
    x1 = [K.sb("ox1_%d" % i, [128, 512], F32) for i in range(2)]
    x2 = [K.sb("ox2_%d" % i, [128, 512], F32) for i in range(2)]
    RSC = 128 ** -0.5

    def epi(oc, m, si, c0, w, isctx, ps):
        if oc < 8:
            sc = 1.0 if oc < 4 else RSC
            if oc % 2 == 0:
                K.op("act", lambda e: e.mul(x1[si % 2].t[:, 0:w], ps.t[:, 0:w], sc), R=[ps], W=[x1[si % 2]])
            else:
                K.op("act", lambda e: e.mul(x2[si % 2].t[:, 0:w], ps.t[:, 0:w], sc), R=[ps], W=[x2[si % 2]])
                T.rope(si, c0, w, x1[si % 2], x2[si % 2], 0, 128, io["cosr"], io["sinr"], put_b, pb.t.ap()[:, oc - 1, c0:c0 + w], pb.t.ap()[:, oc, c0:c0 + w])
        else:
            sec = (oc - 8) // 4
            j = (oc - 8) % 4
            if sec in (0, 2, 5):
                di = {0: 8, 2: 12, 5: 16}[sec] + j
                put_b("act", 128, w, lambda b: K.op("act", lambda e: e.copy(b.t[:, 0:w], ps.t[:, 0:w]), R=[ps], W=[b]), pb.t.ap()[:, di, c0:c0 + w])
            else:
                di = {1: 0, 3: 4, 4: 8, 6: 12}[sec] + j
                put_f("act", 128, w, lambda b: K.op("act", lambda e: e.copy(b.t[:, 0:w], ps.t[:, 0:w]), R=[ps], W=[b]), pf.t.ap()[:, di, c0:c0 + w])
    T.proj(hT, 8, io["w_in_o"], 0, 4608, epi, tag="wino")


def perm_odd(w_in):
    cols = []
    for sec in range(2):
        base = sec * 512
        for pair in range(2):
            hs = (2 * pair, 2 * pair + 1)
            cols.append(np.concatenate([base + h * 128 + 2 * np.arange(64) for h in hs]))
            cols.append(np.concatenate([base + h * 128 + 2 * np.arange(64) + 1 for h in hs]))
    cols.append(np.arange(1024, 4608))
    return np.ascontiguousarray(w_in[:, np.concatenate(cols)])


def core_rows(seq, r):
    return np.concatenate([seq[64 * r:64 * r + 64], seq[256 + 2048 * r:256 + 2048 * (r + 1)]], axis=0)


def tb_common_inputs(inp, li):
    return {"ffn_w1": inp["ffn_w1"][li], "ffn_w3": inp["ffn_w3"][li], "ffn_w2": inp["ffn_w2"][li]}


def tb_even_inputs(inp, li, uT_list, OT_list, yf, yb):
    j = li // 2
    maps = []
    for core in range(8):
        b, r = core // 4, core % 4
        m = tb_common_inputs(inp, li)
        m["uTi"] = uT_list[core]
        m["yfT"] = to_fm(core_rows(yf[b], r))
        m["ybT"] = to_fm(core_rows(yb[b], r))
        On = np.zeros((128, 4, NT), np.float32)
        Dn = np.zeros((128, 4, NT), np.float32)
        for h in range(8):
            unit = b * 8 + h
            OT = OT_list[unit // 2][unit % 2]
            cols = np.concatenate([8192 + 64 * r + np.arange(64), 2048 * r + np.arange(2048)])
            ro = (h % 2) * 64
            On[ro:ro + 64, h // 2, :] = OT[0:64][:, cols]
            Dn[ro:ro + 64, h // 2, :] = OT[64:128][:, cols]
        m["OTn"] = On
        m["denT"] = Dn
        m["dT"] = vec_fm(inp["s5_d"][j])
        m["w_glu"] = inp["s5_w_glu"][j]
        m["w_out"] = inp["w_out_even"][j]
        maps.append(m)
    return maps


def ta_odd_inputs(inp, j, r):
    c, s = core_rope(128, 2, r)
    return {"w_in_o": perm_odd(inp["w_in_odd"][j]), "cosr": c, "sinr": s}


LSEQ = 8448
NCK = 132


def decl_h_odd(K):
    io = {}
    io["GqT"] = K.dram_in("GqT", [4, 128, LSEQ], BF16)
    io["GkT"] = K.dram_in("GkT", [2, 128, LSEQ], BF16)
    io["Gf"] = K.dram_in("Gf", [2, 128, LSEQ], F32)
    io["Gv"] = K.dram_in("Gv", [4, 64, NCK, 128], BF16)
    io["lgam"] = K.dram_in("lgam", [128, 2], F32)
    io["lbl"] = K.dram_in("lbl", [128, 3], F32)
    io["lbsel"] = K.dram_in("lbsel", [128, 3], F32)
    io["cmask"] = K.dram_in("cmask", [64, 64], F32)
    io["identb"] = K.dram_in("identb", [128, 128], BF16)
    io["Go"] = K.dram_out("Go", [4, 64, NCK, 128], F32)
    return io


def h_odd(K, io):
    def ld(name, shape, dt=F32):
        b = K.sb("g_" + name, shape, dt)
        sl = tuple(slice(None) for _ in shape)
        K.dma("sp", b.t[sl], io[name].t.ap(), R=[io[name]], W=[b])
        return b
    lgam, lbl, lbsel = ld("lgam", [128, 2]), ld("lbl", [128, 3]), ld("lbsel", [128, 3])
    cmask, identb = ld("cmask", [64, 64]), ld("identb", [128, 128], BF16)
    e3, lb, oml, tot = K.sb("g_e3", [128, 3], F32), K.sb("g_lb", [128, 1], F32), K.sb("g_oml", [128, 1], F32), K.sb("g_tot", [128, 1], F32)
    K.op("act", lambda e: e.activation(e3.t[:, :], lbl.t[:, :], AF.Exp), R=[lbl], W=[e3])
    K.op("dve", lambda e: e.tensor_reduce(tot.t[:, :], e3.t[:, :], axis=AX.X, op=ALU.add), R=[e3], W=[tot])
    K.op("dve", lambda e: e.tensor_tensor(e3.t[:, :], e3.t[:, :], lbsel.t[:, :], op=ALU.mult), R=[e3, lbsel], W=[e3])
    K.op("dve", lambda e: e.tensor_reduce(lb.t[:, :], e3.t[:, :], axis=AX.X, op=ALU.add), R=[e3], W=[lb])
    K.op("dve", lambda e: e.reciprocal(tot.t[:, :], tot.t[:, :]), R=[tot], W=[tot])
    K.op("dve", lambda e: e.tensor_tensor(lb.t[:, :], lb.t[:, :], tot.t[:, :], op=ALU.mult), R=[lb, tot], W=[lb])
    K.op("dve", lambda e: e.tensor_scalar(oml.t[:, :], lb.t[:, :], -1.0, 1.0, op0=ALU.mult, op1=ALU.add), R=[lb], W=[oml])

    HW = LSEQ // 2
    A = K.sb("g_A", [128, HW], F32)
    Bc = K.sb("g_B", [128, HW], F32)
    rmask = K.sb("g_rm", [128, HW], BF16)
    K.op("pool", lambda e: e.memset(rmask.t[:, :], 1.0), W=[rmask])
    K.op("pool", lambda e: e.memset(rmask.t[:, :].rearrange("p (c t) -> p c t", t=64)[:, :, 0], 0.0), W=[rmask])
    Av = A.t[:, :].rearrange("p (c t) -> p c t", t=64)
    NH = NCK // 2
    U = []
    for i in range(2):
        U.append(dict(
            qt=K.sb("g_q%d" % i, [128, LSEQ], BF16), kt=K.sb("g_k%d" % i, [128, LSEQ], BF16),
            v=K.sb("g_v%d" % i, [64, NCK, 128], BF16), dec=K.sb("g_dec%d" % i, [128, NCK], F32),
            S=K.sb("g_S%d" % i, [128, 128], F32), Sb=K.sb("g_Sb%d" % i, [128, 128], BF16),
            tmpS=K.sb("g_tS%d" % i, [128, 128], F32),
            ktok=[K.sb("g_kt%d_%d" % (i, j), [64, 128], BF16) for j in range(2)],
            attm=[K.sb("g_at%d_%d" % (i, j), [64, 64], BF16) for j in range(2)],
            ostg=[K.sb("g_os%d_%d" % (i, j), [64, 8, 128], F32) for j in range(2)]))
    for pair in range(2):
        hg = pair == 1
        for i in range(2):
            u = pair * 2 + i
            d = U[i]
            qt, kt, v = d["qt"], d["kt"], d["v"]
            K.dma("sp", qt.t[:, :], io["GqT"].t.ap()[u], R=[io["GqT"]], W=[qt])
            K.dma("act", v.t[:, :, :], io["Gv"].t.ap()[u], R=[io["Gv"]], W=[v])
            if not hg:
                K.dma("sp", kt.t[:, :], io["GkT"].t.ap()[u], R=[io["GkT"]], W=[kt])
            for hf in range(2):
                sl = slice(hf * HW, (hf + 1) * HW)
                if not hg:
                    K.op("pool", lambda e: e.memset(A.t[:, :], 1.0), W=[A])
                    K.op("dve", lambda e: e.tensor_scalar(A.t[:, :], A.t[:, :], lgam.t[:, u:u + 1], None, op0=ALU.mult), R=[A, lgam], W=[A])
                else:
                    K.dma("sp", A.t[:, :], io["Gf"].t.ap()[u - 2, :, sl], R=[io["Gf"]], W=[A])
                    K.op("act", lambda e: e.activation(A.t[:, :], A.t[:, :], AF.Sigmoid), R=[A], W=[A])
                    K.op("dve", lambda e: e.tensor_scalar(A.t[:, :], A.t[:, :], oml.t[:, 0:1], lb.t[:, 0:1], op0=ALU.mult, op1=ALU.add), R=[A, oml, lb], W=[A])
                    K.op("dve", lambda e: e.tensor_scalar(kt.t[:, sl], A.t[:, :], -1.0, 1.0, op0=ALU.mult, op1=ALU.add), R=[A], W=[kt])
                    K.op("act", lambda e: e.activation(A.t[:, :], A.t[:, :], AF.Ln), R=[A], W=[A])
                K.op("dve", lambda e: e.tensor_tensor_scan(Bc.t[:, :], rmask.t[:, :], A.t[:, :], 0.0, op0=ALU.mult, op1=ALU.add), R=[rmask, A], W=[Bc])
                K.op("act", lambda e: e.activation(A.t[:, :], Bc.t[:, :], AF.Exp), R=[Bc], W=[A])
                K.op("dve", lambda e: e.tensor_tensor(qt.t[:, sl], qt.t[:, sl], A.t[:, :], op=ALU.mult), R=[qt, A], W=[qt])
                K.op("pool", lambda e: e.tensor_copy(d["dec"].t[:, hf * NH:(hf + 1) * NH], Av[:, :, 63]), R=[A], W=[d["dec"]])
                K.op("act", lambda e: e.activation(Bc.t[:, :], Bc.t[:, :], AF.Exp, scale=-1.0), R=[Bc], W=[Bc])
                K.op("pool", lambda e: e.tensor_tensor(kt.t[:, sl], kt.t[:, sl], Bc.t[:, :], op=ALU.mult), R=[kt, Bc], W=[kt])
            K.op("dve", lambda e: e.memset(d["S"].t[:, :], 0.0), W=[d["S"]])
            K.op("dve", lambda e: e.memset(d["Sb"].t[:, :], 0.0), W=[d["Sb"]])
        for c in range(NCK):
            cs = slice(c * 64, (c + 1) * 64)
            for i in range(2):
                u = pair * 2 + i
                d = U[i]
                qt, kt, v, S, Sb, tmpS = d["qt"], d["kt"], d["v"], d["S"], d["Sb"], d["tmpS"]
                pst = K.ps_get()
                K.op("pe", lambda e: e.matmul(pst.t[0:64, 0:128], kt.t[:, cs], identb.t[:, :], start=True, stop=True), R=[kt, identb], W=[pst])
                pst2 = K.ps_get()
                K.op("pe", lambda e: e.matmul(pst2.t[0:64, 0:64], kt.t[:, cs], qt.t[:, cs], start=True, stop=True), R=[kt, qt], W=[pst2])
                kk_ = d["ktok"][c % 2]
                am = d["attm"][c % 2]
                K.op("act", lambda e: e.copy(kk_.t[:, :], pst.t[0:64, 0:128]), R=[pst], W=[kk_])
                K.op("dve", lambda e: e.tensor_tensor(am.t[:, :], pst2.t[0:64, 0:64], cmask.t[:, :], op=ALU.mult), R=[pst2, cmask], W=[am])
                pso = K.ps_get()
                K.op("pe", lambda e: e.matmul(pso.t[0:64, 0:128], am.t[:, :], v.t[:, c, :], start=True, stop=False), R=[am, v], W=[pso])
                K.op("pe", lambda e: e.matmul(pso.t[0:64, 0:128], qt.t[:, cs], Sb.t[:, :], start=False, stop=True), R=[qt, Sb], W=[pso])
                psk = K.ps_get()
                K.op("pe", lambda e: e.matmul(psk.t[:, 0:128], kk_.t[:, :], v.t[:, c, :], start=True, stop=True), R=[kk_, v], W=[psk])
                K.op("dve", lambda e: e.tensor_tensor(tmpS.t[:, :], psk.t[:, 0:128], S.t[:, :], op=ALU.add), R=[psk, S], W=[tmpS])
                K.op("dve", lambda e: e.tensor_scalar(S.t[:, :], tmpS.t[:, :], d["dec"].t[:, c:c + 1], None, op0=ALU.mult), R=[tmpS, d["dec"]], W=[S])
                K.op("act", lambda e: e.copy(Sb.t[:, :], S.t[:, :]), R=[S], W=[Sb])
                og = d["ostg"][(c // 8) % 2]
                K.op("act", lambda e: e.copy(og.t[:, c % 8, :], pso.t[0:64, 0:128]), R=[pso], W=[og])
                if c % 8 == 7 or c == NCK - 1:
                    c0 = (c // 8) * 8
                    n = c - c0 + 1
                    K.dma("sp", io["Go"].t.ap()[u, :, c0:c0 + n, :], og.t[:, 0:n, :], R=[og], W=[io["Go"]])


def build_h_odd():
    nc = bass.Bass("TRN2", target_bir_lowering=False)
    K = Emit(nc)
    io = decl_h_odd(K)
    h_odd(K, io)
    K.finish()
    return nc


def flipseq(a, rev):
    if not rev:
        return a
    return np.concatenate([a[:256][::-1], a[256:][::-1]], axis=0)


def gather_seq(core_arrays, b):
    parts = [core_arrays[4 * b + r] for r in range(4)]
    ctx = np.concatenate([p[:, :64] for p in parts], axis=1)
    lat = np.concatenate([p[:, 64:] for p in parts], axis=1)
    return np.concatenate([ctx, lat], axis=1).T


def h_odd_inputs(inp, j, pb_list, pf_list):
    import ml_dtypes
    bf = ml_dtypes.bfloat16
    cmask = (np.arange(64)[:, None] <= np.arange(64)[None, :]).astype(np.float32)
    identb = np.eye(128, dtype=np.float32).astype(bf)
    sel = np.zeros((128, 3), np.float32)
    sel[:, :j + 1] = 1.0
    maps = []
    for core in range(8):
        b, h = core // 4, core % 4
        ch, ro = 2 * (h // 2), (h % 2) * 64
        rq = gather_seq([np.concatenate([p[ro:ro + 64, ch], p[ro:ro + 64, ch + 1]], 0) for p in pb_list], b)
        rk = gather_seq([np.concatenate([p[ro:ro + 64, 4 + ch], p[ro:ro + 64, 5 + ch]], 0) for p in pb_list], b)
        rv = gather_seq([p[:, 8 + h] for p in pb_list], b)
        hq = gather_seq([p[:, 12 + h] for p in pb_list], b)
        hi = gather_seq([p[:, 16 + h] for p in pb_list], b)
        hff = gather_seq([p[:, 4 + h] for p in pf_list], b)
        hfb = gather_seq([p[:, 8 + h] for p in pf_list], b)
        GqT = np.stack([flipseq(rq, 0).T, flipseq(rq, 1).T, flipseq(hq, 0).T, flipseq(hq, 1).T])
        GkT = np.stack([flipseq(rk, 0).T, flipseq(rk, 1).T])
        Gf = np.stack([flipseq(hff, 0).T, flipseq(hfb, 1).T])
        tok = lambda a: a.reshape(NCK, 64, 128).transpose(1, 0, 2)
        Gv = np.stack([tok(flipseq(rv, 0)), tok(flipseq(rv, 1)), tok(flipseq(hi, 0)), tok(flipseq(hi, 1))])
        lg = np.array([np.log1p(-np.exp2(np.float32(-(5.0 + off) - h))) for off in (0.0, 0.5)], np.float32)
        maps.append({"GqT": np.ascontiguousarray(GqT), "GkT": np.ascontiguousarray(GkT), "Gf": np.ascontiguousarray(Gf),
                     "Gv": np.ascontiguousarray(Gv), "lgam": np.ascontiguousarray(np.broadcast_to(lg[None, :], (128, 2))),
                     "lbl": np.ascontiguousarray(inp["hg_lb_logits"][:, h * 128:(h + 1) * 128].T),
                     "lbsel": sel, "cmask": cmask, "identb": identb})
    return maps


def tb_odd_inputs(inp, li, Go_list, pf_list):
    j = li // 2
    nat = {}
    for core in range(8):
        b, h = core // 4, core % 4
        Go = Go_list[core]
        for u in range(4):
            seq = Go[u].transpose(1, 0, 2).reshape(LSEQ, 128)
            nat[(b, h, u)] = flipseq(seq, u % 2)
    maps = []
    for core in range(8):
        b, r = core // 4, core % 4
        m = tb_common_inputs(inp, li)
        of = np.concatenate([core_rows(nat[(b, h, 0)], r) for h in range(4)] + [core_rows(nat[(b, h, 2)], r) for h in range(4)], axis=1)
        ob = np.concatenate([core_rows(nat[(b, h, 1)], r) for h in range(4)] + [core_rows(nat[(b, h, 3)], r) for h in range(4)], axis=1)
        m["of"] = to_fm(of)
        m["ob"] = to_fm(ob)
        pf = pf_list[core]
        m["gT"] = np.ascontiguousarray(np.concatenate([pf[:, 0:4], pf[:, 12:16]], axis=1))
        m["gnT"] = vec_fm(np.concatenate([inp["ret_gn"][j], inp["hg_gn"][j]]))
        m["w_out"] = inp["w_out_odd"][j]
        maps.append(m)
    return maps


_PROGS = {}


def _prog(key, fn):
    if key not in _PROGS:
        _PROGS[key] = fn()
    return _PROGS[key]


def _run(nc, maps):
    import sys, time
    t0 = time.time()
    res = run_bass_kernel_spmd(nc, maps, core_ids=list(range(8))).results
    out = [{k: np.asarray(v) for k, v in r.items()} for r in res]
    print("[kernel] launch done in %.1fs" % (time.time() - t0), file=sys.stderr, flush=True)
    return out


def kernel(**inp):
    inp = {k: np.asarray(v) for k, v in inp.items()}
    maps = []
    for core in range(8):
        b, r = core // 4, core % 4
        X = np.concatenate([inp["ctx"][b, 64 * r:64 * r + 64], inp["x"][b, 2048 * r:2048 * (r + 1)]], axis=0)
        m = {"xT_in": to_fm(X)}
        m.update(mod_inputs(inp, 0, b, "_b"))
        m.update(ta_even_inputs(inp, 0, r))
        maps.append(m)
    ta = _run(_prog("A", lambda: build_tok(None, "even")), maps)
    xT = None
    for li in range(4):
        j = li // 2
        last = li == 3
        if li % 2 == 0:
            am = attn_inputs(ta)
            sm_ = s5_inputs(inp, j, [r["uT"] for r in ta])
            for core in range(8):
                am[core].update(sm_[core])
            ho = _run(_prog("HE", build_h_even), am)
            yf, yb = s5_outputs([o["s5_Y8"] for o in ho])
            maps = tb_even_inputs(inp, li, [r["uT"] for r in ta], [o["OT"] for o in ho], yf, yb)
        else:
            hm = h_odd_inputs(inp, j, [r["pb"] for r in ta], [r["pf"] for r in ta])
            ho = _run(_prog("HO", build_h_odd), hm)
            maps = tb_odd_inputs(inp, li, [o["Go"] for o in ho], [r["pf"] for r in ta])
        for core in range(8):
            b, r = core // 4, core % 4
            if li == 0:
                X = np.concatenate([inp["ctx"][b, 64 * r:64 * r + 64], inp["x"][b, 2048 * r:2048 * (r + 1)]], axis=0)
                maps[core]["xT_in"] = to_fm(X)
            else:
                maps[core]["xT_in"] = xT[core]
            maps[core].update(mod_inputs(inp, li, b, "_a"))
            if not last:
                maps[core].update(mod_inputs(inp, li + 1, b, "_b"))
                if li % 2 == 0:
                    maps[core].update(ta_odd_inputs(inp, (li + 1) // 2, r))
                else:
                    maps[core].update(ta_even_inputs(inp, (li + 1) // 2, r))
            else:
                maps[core]["fnT"] = vec_fm(inp["final_norm"])
        if last:
            ta = _run(_prog("D", lambda: build_tok("odd", None, final=True)), maps)
        elif li % 2 == 0:
            ta = _run(_prog("B", lambda: build_tok("even", "odd")), maps)
        else:
            ta = _run(_prog("C", lambda: build_tok("odd", "even")), maps)
        xT = [r["xT_out"] for r in ta]
    out = np.zeros((2, 8192, 1024), np.float32)
    for core in range(8):
        b, r = core // 4, core % 4
        out[b, 2048 * r:2048 * (r + 1)] = from_fm(ta[core]["yT"])[64:]
    return out
```

```python
import numpy as np
from contextlib import ExitStack
import concourse.bass as bass
import concourse.mybir as mybir
from concourse.bass_utils import run_bass_kernel_spmd

F32 = mybir.dt.float32
BF16 = mybir.dt.bfloat16
I32 = mybir.dt.int32
AF = mybir.ActivationFunctionType
ALU = mybir.AluOpType
AX = mybir.AxisListType


class Buf:
    __slots__ = ("t", "lw", "rd", "name")

    def __init__(self, t, name=""):
        self.t = t
        self.lw = None
        self.rd = []
        self.name = name


class Emit:
    NDS = 24

    def __init__(self, nc):
        self.nc = nc
        self.es = ExitStack()
        self.eng = {"pe": nc.tensor, "act": nc.scalar, "dve": nc.vector, "pool": nc.gpsimd, "sp": nc.sync}
        self.sem = {e: self.es.enter_context(nc.semaphore("s_" + e)) for e in ("pe", "act", "dve", "pool")}
        self.cnt = {e: 0 for e in self.sem}
        self.dsem = [self.es.enter_context(nc.semaphore("d%d" % i)) for i in range(self.NDS)]
        self.dcnt = [0] * self.NDS
        self.dnext = 0
        self.seen = {e: {} for e in self.eng}
        self.outs = []
        self.psb = [self.ps("psb%d" % i, [128, 512], F32) for i in range(6)]
        self.psx = [self.ps("psx%d" % i, [128, 512], F32) for i in range(2)]
        self.psi = 0
        self.n_inst = 0

    def dram_in(self, name, shape, dt):
        return Buf(self.nc.dram_tensor(name, list(shape), dt, kind="ExternalInput"), name)

    def dram_out(self, name, shape, dt):
        return Buf(self.nc.dram_tensor(name, list(shape), dt, kind="ExternalOutput"), name)

    def dram_tmp(self, name, shape, dt):
        return Buf(self.nc.dram_tensor(name, list(shape), dt, kind="Internal"), name)

    def sb(self, name, shape, dt):
        self.n_sb = getattr(self, "n_sb", 0) + 1
        return Buf(self.es.enter_context(self.nc.sbuf_tensor("sb%d_%s" % (self.n_sb, name), list(shape), dt)), name)

    def ps(self, name, shape, dt):
        return Buf(self.es.enter_context(self.nc.psum_tensor(name, list(shape), dt)), name)

    def ps_get(self):
        b = self.psb[self.psi % len(self.psb)]
        self.psi += 1
        return b

    def _wait(self, e, key, val):
        if self.seen[e].get(key, 0) >= val:
            return
        self.seen[e][key] = val
        s = self.sem[key] if isinstance(key, str) else self.dsem[key]
        self.eng[e].wait_ge(s, val)

    def _deps(self, e, R, W):
        deps = {}
        for r in R:
            if r.lw is not None:
                k, v = r.lw
                deps[k] = max(deps.get(k, 0), v)
        for w in W:
            if w.lw is not None:
                k, v = w.lw
                deps[k] = max(deps.get(k, 0), v)
            for k, v in w.rd:
                deps[k] = max(deps.get(k, 0), v)
        for k, v in deps.items():
            if k == "pe" and e == "pe":
                continue
            self._wait(e, k, v)

    def _record(self, ev, R, W):
        for w in W:
            w.lw = ev
            w.rd = []
        for r in R:
            if r in W:
                continue
            r.rd.append(ev)
            if len(r.rd) > 12:
                m = {}
                for k, v in r.rd:
                    m[k] = max(m.get(k, 0), v)
                r.rd = list(m.items())

    def op(self, e, fn, R=(), W=()):
        self._deps(e, R, W)
        inst = fn(self.eng[e])
        self.cnt[e] += 1
        inst.then_inc(self.sem[e], 1)
        self._record((e, self.cnt[e]), R, W)
        self.n_inst += 1
        return inst

    def dma(self, q, out, in_, R=(), W=(), **kw):
        i = self.dnext
        self.dnext = (self.dnext + 1) % self.NDS
        if self.dcnt[i] > 0:
            self._wait(q, i, 16 * self.dcnt[i])
        self._deps(q, R, W)
        inst = self.eng[q].dma_start(out=out, in_=in_, **kw)
        self.dcnt[i] += 1
        inst.then_inc(self.dsem[i], 16)
        ev = (i, 16 * self.dcnt[i])
        self._record(ev, R, W)
        self.n_inst += 1
        return ev

    def barrier(self):
        for e in ("pe", "act", "dve", "pool", "sp"):
            for i in range(self.NDS):
                if self.dcnt[i] > 0:
                    self._wait(e, i, 16 * self.dcnt[i])
            for e2 in ("pe", "act", "dve", "pool"):
                if e2 != e and self.cnt[e2] > 0:
                    self._wait(e, e2, self.cnt[e2])

    def scope(self):
        K = self

        class _S:
            def __enter__(s2):
                s2.old = K.es
                K.es = ExitStack()
                return s2

            def __exit__(s2, *a):
                K.barrier()
                K.es.close()
                K.es = s2.old
                return False
        return _S()

    def finish(self):
        for i in range(self.NDS):
            if self.dcnt[i] > 0:
                self._wait("sp", i, 16 * self.dcnt[i])
        for e in ("pe", "act", "dve", "pool"):
            if self.cnt[e] > 0:
                self._wait("sp", e, self.cnt[e])
        self.es.close()


D = 1024
NCH = 8
NT = 2112
SLABS = [(0, 64, 1)] + [(64 + 512 * i, 512, 0) for i in range(4)]
DFF = 2816
EPS = 1e-6


def pslice(ap, lo, hi):
    return ap[lo:hi]


class Tok:
    def __init__(self, K):
        self.K = K
        nc = K.nc
        self.ones = K.sb("ones_f", [128, 128], F32)
        K.op("dve", lambda e: e.memset(self.ones.t[:, :], 1.0), W=[self.ones])
        self.onesb = K.sb("ones_b", [128, 128], BF16)
        K.op("dve", lambda e: e.memset(self.onesb.t[:, :], 1.0), W=[self.onesb])
        self.eps = K.sb("eps_c", [128, 1], F32)
        K.op("dve", lambda e: e.memset(self.eps.t[:, :], EPS), W=[self.eps])
        self.sq = [K.sb("sq%d" % i, [128, 512], BF16) for i in range(3)]
        self.sqi = 0
        self.rstd = [K.sb("rstd%d" % i, [128, 512], F32) for i in range(2)]
        self.rsi = 0
        self.tmp = [K.sb("ttmp%d" % i, [128, 512], F32) for i in range(3)]
        self.tmi = 0

    def gettmp(self):
        self.tmi += 1
        return self.tmp[self.tmi % len(self.tmp)]

    def load(self, name, dram, shape, dt=F32, q="sp", src=None):
        b = self.K.sb(name, shape, dt)
        sl = tuple(slice(None) for _ in shape)
        self.K.dma(q, b.t[sl], src if src is not None else dram.t.ap(), R=[dram], W=[b])
        return b

    def rms_rstd(self, src, nchunks, c0, w, nfeat, srcchunk0=0):
        K = self.K
        ps = K.ps_get()
        for c in range(nchunks):
            sq = self.sq[self.sqi % 3]
            self.sqi += 1
            K.op("act", lambda e: e.activation(sq.t[:, 0:w], src.t[:, srcchunk0 + c, c0:c0 + w], AF.Square), R=[src], W=[sq])
            K.op("pe", lambda e: e.matmul(ps.t[:, 0:w], self.onesb.t[:, :], sq.t[:, 0:w], start=(c == 0), stop=(c == nchunks - 1)), R=[sq, self.onesb], W=[ps])
        r = self.rstd[self.rsi % 2]
        self.rsi += 1
        K.op("act", lambda e: e.activation(r.t[:, 0:w], ps.t[:, 0:w], AF.Ln, bias=self.eps.t[:, 0:1], scale=1.0 / nfeat), R=[ps, self.eps], W=[r])
        K.op("act", lambda e: e.activation(r.t[:, 0:w], r.t[:, 0:w], AF.Exp, scale=-0.5), R=[r], W=[r])
        return r

    def modvec(self, cT, w_mod, b_modT, tag="", groups=range(12)):
        K = self.K
        sc = K.sb("silc" + tag, [128, 8, 2], F32)
        K.op("act", lambda e: e.activation(sc.t[:, :, :], cT.t[:, :, :], AF.Silu), R=[cT], W=[sc])
        mod = K.sb("modv" + tag, [128, 2, 48], F32)
        wv = w_mod.t.ap().rearrange("(c p) n -> p c n", p=128)
        if not hasattr(self, "wmodb"):
            self.wmodb = [K.sb("wmod%d" % i, [128, 8, 512], F32) for i in range(2)]
        wbufs = self.wmodb
        K.op("dve", lambda e: e.memset(mod.t[:, :, :], 0.0), W=[mod])
        for gi, g in enumerate(groups):
            wb = wbufs[gi % 2]
            K.dma("sp" if gi % 2 == 0 else "act", wb.t[:, :, :], wv[:, :, g * 512:(g + 1) * 512], R=[w_mod], W=[wb])
            ps = K.ps_get()
            for oc in range(4):
                for kc in range(8):
                    K.op("pe", lambda e: e.matmul(ps.t[:, oc * 2:oc * 2 + 2], wb.t[:, kc, oc * 128:(oc + 1) * 128], sc.t[:, kc, :], start=(kc == 0), stop=(kc == 7)), R=[wb, sc], W=[ps])
            for oc in range(4):
                o = g * 4 + oc
                K.op("dve", lambda e: e.tensor_scalar(mod.t[:, :, o], ps.t[:, oc * 2:oc * 2 + 2], b_modT.t[:, o:o + 1], None, op0=ALU.add), R=[ps, b_modT], W=[mod])
        return mod

    def norm_mod(self, x, dst, mod, gT, jshift, jscale, tag):
        K = self.K
        sc = K.sb("nsc" + tag, [128, 2, 8], F32)
        for w in range(2):
            K.op("dve", lambda e: e.scalar_tensor_tensor(sc.t[:, w, :], mod.t[:, w, jscale * 8:jscale * 8 + 8], 1.0, gT.t[:, :], op0=ALU.add, op1=ALU.mult), R=[mod, gT], W=[sc])
        for (c0, w, isctx) in SLABS:
            r = self.rms_rstd(x, 8, c0, w, D)
            for c in range(8):
                t = self.gettmp()
                K.op("dve", lambda e: e.tensor_tensor(t.t[:, 0:w], x.t[:, c, c0:c0 + w], r.t[:, 0:w], op=ALU.mult), R=[x, r], W=[t])
                K.op("act", lambda e: e.activation(dst.t[:, c, c0:c0 + w], t.t[:, 0:w], AF.Identity, bias=mod.t[:, isctx, jshift * 8 + c:jshift * 8 + c + 1], scale=sc.t[:, isctx, c:c + 1]), R=[t, mod, sc], W=[dst])

    def rope(self, si, c0, w, x1, x2, col, P, cosd, sind, put, dst1, dst2):
        K = self.K
        if not hasattr(self, "rc"):
            self.rc = [K.sb("ropec%d" % i, [128, 2, 512], F32) for i in range(2)]
            self.rta = K.sb("ropea", [128, 512], F32)
            self.rtb = K.sb("ropeb", [128, 512], F32)
            self.rci = 0
        c = self.rc[self.rci % 2]
        self.rci += 1
        ta, tb = self.rta, self.rtb
        K.dma("act", c.t[0:P, 0, 0:w], cosd.t.ap()[0:P, c0:c0 + w], R=[cosd], W=[c])
        K.dma("act", c.t[0:P, 1, 0:w], sind.t.ap()[0:P, c0:c0 + w], R=[sind], W=[c])
        K.op("dve", lambda e: e.tensor_tensor(ta.t[0:P, 0:w], x1.t[0:P, col:col + w], c.t[0:P, 0, 0:w], op=ALU.mult), R=[c, x1], W=[ta])
        K.op("dve", lambda e: e.tensor_tensor(tb.t[0:P, 0:w], x2.t[0:P, col:col + w], c.t[0:P, 1, 0:w], op=ALU.mult), R=[c, x2], W=[tb])
        put("dve", P, w, lambda b: K.op("dve", lambda e: e.tensor_tensor(b.t[0:P, 0:w], ta.t[0:P, 0:w], tb.t[0:P, 0:w], op=ALU.subtract), R=[ta, tb], W=[b]), dst1)
        K.op("dve", lambda e: e.tensor_tensor(ta.t[0:P, 0:w], x1.t[0:P, col:col + w], c.t[0:P, 1, 0:w], op=ALU.mult), R=[c, x1], W=[ta])
        K.op("dve", lambda e: e.tensor_tensor(tb.t[0:P, 0:w], x2.t[0:P, col:col + w], c.t[0:P, 0, 0:w], op=ALU.mult), R=[c, x2], W=[tb])
        put("dve", P, w, lambda b: K.op("dve", lambda e: e.tensor_tensor(b.t[0:P, 0:w], ta.t[0:P, 0:w], tb.t[0:P, 0:w], op=ALU.add), R=[ta, tb], W=[b]), dst2)

    def proj(self, src, nk, w_dram, col0, ncols, epi, group=512, tag="w", krows=128):
        K = self.K
        wv = w_dram.t.ap().rearrange("(c p) n -> p c n", p=krows)
        ngroups = (ncols + group - 1) // group
        if not hasattr(self, "wb_" + tag):
            setattr(self, "wb_" + tag, [K.sb("wb_%s%d" % (tag, i), [128, nk, group], BF16) for i in range(2)])
            setattr(self, "wbi_" + tag, 0)
        bufs = getattr(self, "wb_" + tag)
        oc = 0
        for g in range(ngroups):
            gi = getattr(self, "wbi_" + tag)
            setattr(self, "wbi_" + tag, gi + 1)
            wb = bufs[gi % 2]
            gc0 = col0 + g * group
            gw = min(group, ncols - g * group)
            K.dma("pool", wb.t[0:krows, :, 0:gw], wv[:, :, gc0:gc0 + gw], R=[w_dram], W=[wb])
            nm = (gw + 127) // 128
            for si, (c0, w, isctx) in enumerate(SLABS):
                for mi in range(nm):
                    m = min(128, gw - mi * 128)
                    ps = K.ps_get()
                    for kc in range(nk):
                        K.op("pe", lambda e: e.matmul(ps.t[0:m, 0:w], wb.t[0:krows, kc, mi * 128:mi * 128 + m], src.t[0:krows, kc, c0:c0 + w], start=(kc == 0), stop=(kc == nk - 1)), R=[wb, src], W=[ps])
                    epi(oc + mi, m, si, c0, w, isctx, ps)
            oc += nm


def stage_out(K, T, dram, dst_ap_fn, dt):
    bufs = [K.sb("stg_%s%d" % (dram.name, i), [128, 512], dt) for i in range(3)]
    st = {"i": 0}

    def put(eng, m, w, ps_or_fn, dst_ap):
        b = bufs[st["i"] % 3]
        st["i"] += 1
        ps_or_fn(b)
        K.dma("sp", dst_ap, b.t[0:m, 0:w], R=[b], W=[dram])
    return put


def ta_even(K, T, hT, io):
    uT, qT, kT, vT, krT = io["uT"], io["qT"], io["kT"], io["vT"], io["krT"]
    w_in, w_uq, w_ukv = io["w_in_e"], io["w_uq"], io["w_ukv"]
    qn = T.load("qn", io["qnT"], [128, 2])
    kvn = T.load("kvn", io["kvnT"], [128, 1])
    cq = K.sb("cq", [128, 3, NT], F32)
    kr = K.sb("kr12", [16, 2, NT], F32)
    put_u = stage_out(K, T, uT, None, F32)
    put_b = stage_out(K, T, qT, None, BF16)

    def epi_in(oc, m, si, c0, w, isctx, ps):
        if oc < 4:
            put_u("act", 128, w, lambda b: K.op("act", lambda e: e.copy(b.t[:, 0:w], ps.t[:, 0:w]), R=[ps], W=[b]), uT.t.ap()[:, oc, c0:c0 + w])
        else:
            K.op("dve", lambda e: e.tensor_copy(cq.t[:, oc - 4, c0:c0 + w], ps.t[:, 0:w]), R=[ps], W=[cq])
    T.proj(hT, 8, w_in, 0, 896, epi_in, tag="win")

    def epi_kr(j):
        def f(oc, m, si, c0, w, isctx, ps):
            K.op("dve", lambda e: e.tensor_copy(kr.t[:, j, c0:c0 + w], ps.t[0:16, 0:w]), R=[ps], W=[kr])
        return f
    T.proj(hT, 8, w_in, 896, 16, epi_kr(0), tag="win")
    T.proj(hT, 8, w_in, 912, 16, epi_kr(1), tag="win")

    cqn = K.sb("cqn", [128, 3, NT], BF16)
    for (c0, w, isctx) in SLABS:
        r = T.rms_rstd(cq, 2, c0, w, 256)
        for c in range(2):
            K.op("dve", lambda e: e.scalar_tensor_tensor(cqn.t[:, c, c0:c0 + w], cq.t[:, c, c0:c0 + w], qn.t[:, c:c + 1], r.t[:, 0:w], op0=ALU.mult, op1=ALU.mult), R=[cq, qn, r], W=[cqn])
        r = T.rms_rstd(cq, 1, c0, w, 128, srcchunk0=2)
        K.op("dve", lambda e: e.scalar_tensor_tensor(cqn.t[:, 2, c0:c0 + w], cq.t[:, 2, c0:c0 + w], kvn.t[:, 0:1], r.t[:, 0:w], op0=ALU.mult, op1=ALU.mult), R=[cq, kvn, r], W=[cqn])

    q1 = [K.sb("q1t%d" % i, [128, 512], F32) for i in range(2)]
    q2 = [K.sb("q2t%d" % i, [128, 512], F32) for i in range(2)]

    def epi_q(oc, m, si, c0, w, isctx, ps):
        if oc < 4:
            put_b("act", 128, w, lambda b: K.op("act", lambda e: e.copy(b.t[:, 0:w], ps.t[:, 0:w]), R=[ps], W=[b]), qT.t.ap()[:, oc, c0:c0 + w])
        elif oc == 4:
            K.op("act", lambda e: e.copy(q1[si % 2].t[:, 0:w], ps.t[:, 0:w]), R=[ps], W=[q1[si % 2]])
        else:
            K.op("act", lambda e: e.copy(q2[si % 2].t[:, 0:w], ps.t[:, 0:w]), R=[ps], W=[q2[si % 2]])
            T.rope(si, c0, w, q1[si % 2], q2[si % 2], 0, 128, io["cosq"], io["sinq"], put_b, qT.t.ap()[:, 4, c0:c0 + w], qT.t.ap()[:, 5, c0:c0 + w])
    T.proj(cqn, 2, w_uq, 0, 768, epi_q, tag="wuq")

    def epi_kv(oc, m, si, c0, w, isctx, ps):
        dst = kT if oc < 4 else vT
        put_b("act", 128, w, lambda b: K.op("act", lambda e: e.copy(b.t[:, 0:w], ps.t[:, 0:w]), R=[ps], W=[b]), dst.t.ap()[:, oc % 4, c0:c0 + w])
    T.proj(Buf3(cqn, 2), 1, w_ukv, 0, 1024, epi_kv, tag="wukv")
    k1 = K.sb("kr1s", [16, 512], F32)
    k2 = K.sb("kr2s", [16, 512], F32)
    for si, (c0, w, isctx) in enumerate(SLABS):
        K.op("act", lambda e: e.copy(k1.t[:, 0:w], kr.t[:, 0, c0:c0 + w]), R=[kr], W=[k1])
        K.op("act", lambda e: e.copy(k2.t[:, 0:w], kr.t[:, 1, c0:c0 + w]), R=[kr], W=[k2])
        T.rope(si, c0, w, k1, k2, 0, 16, io["cosq"], io["sinq"], put_b, krT.t.ap()[:, 0, c0:c0 + w], krT.t.ap()[:, 1, c0:c0 + w])


class _ChunkView:
    def __init__(self, t, off):
        self._t = t
        self._off = off

    def __getitem__(self, idx):
        p, c, n = idx
        return self._t[p, c + self._off, n]


def Buf3(buf, off):
    b = Buf(_ChunkView(buf.t, off), buf.name)
    return _Alias(buf, b.t)


class _Alias:
    def __init__(self, parent, t):
        object.__setattr__(self, "_p", parent)
        object.__setattr__(self, "t", t)

    def __getattr__(self, k):
        return getattr(object.__getattribute__(self, "_p"), k)

    def __setattr__(self, k, v):
        if k == "t":
            object.__setattr__(self, k, v)
        else:
            setattr(object.__getattribute__(self, "_p"), k, v)


NKEY = 8448
NQ = 8448
ATT_SCALE = 96 ** -0.5


def h_even_attn(K, io, n_units=2):
    QTd, KTd, Vd, OTd = io["QT"], io["KT"], io["V"], io["OT"]
    onesb = K.sb("a_ones", [128, 1], BF16)
    K.op("dve", lambda e: e.memset(onesb.t[:, :], 1.0), W=[onesb])
    QT = K.sb("a_QT", [128, NQ], BF16)
    KT = K.sb("a_KT", [128, NKEY], BF16)
    V = K.sb("a_V", [128, 66, 128], BF16)
    sq = [K.sb("a_sq%d" % i, [128, 512], BF16) for i in range(2)]
    qsq = K.sb("a_qsq", [1, NQ], F32)
    ksq = K.sb("a_ksq", [1, NKEY], F32)
    kmax = K.sb("a_kmax", [1, 1], F32)
    negm = K.sb("a_negm", [1, NQ], BF16)
    PT = [K.sb("a_PT%d" % i, [128, 512], BF16) for i in range(3)]
    stg = [K.sb("a_stg%d" % i, [128, 512], F32) for i in range(2)]
    for u in range(n_units):
        K.op("dve", lambda e: e.memset(KT.t[:, :], 1.0), W=[KT])
        K.op("pool", lambda e: e.memset(V.t[:, :, :], 1.0), W=[V])
        K.dma("sp", QT.t[0:96, :], QTd.t.ap()[u, 0:96, :], R=[QTd], W=[QT])
        K.dma("act", KT.t[0:96, :], KTd.t.ap()[u, 0:96, :], R=[KTd], W=[KT])
        K.dma("sp", V.t[:, :, 0:64], Vd.t.ap()[u], R=[Vd], W=[V])
        for (src, dst, n) in ((QT, qsq, NQ), (KT, ksq, NKEY)):
            i = 0
            for c0 in range(0, n, 512):
                w = min(512, n - c0)
                s = sq[i % 2]
                i += 1
                K.op("act", lambda e: e.activation(s.t[0:96, 0:w], src.t[0:96, c0:c0 + w], AF.Square), R=[src], W=[s])
                ps = K.ps_get()
                K.op("pe", lambda e: e.matmul(ps.t[0:1, 0:w], onesb.t[0:96, 0:1], s.t[0:96, 0:w], start=True, stop=True), R=[onesb, s], W=[ps])
                K.op("dve", lambda e: e.tensor_copy(dst.t[0:1, c0:c0 + w], ps.t[0:1, 0:w]), R=[ps], W=[dst])
        K.op("dve", lambda e: e.tensor_reduce(kmax.t[0:1, 0:1], ksq.t[0:1, :], axis=AX.X, op=ALU.max), R=[ksq], W=[kmax])
        K.op("dve", lambda e: e.tensor_scalar(qsq.t[0:1, :], qsq.t[0:1, :], kmax.t[0:1, 0:1], None, op0=ALU.mult), R=[qsq, kmax], W=[qsq])
        K.op("act", lambda e: e.activation(qsq.t[0:1, :], qsq.t[0:1, :], AF.Sqrt), R=[qsq], W=[qsq])
        K.op("dve", lambda e: e.tensor_scalar(negm.t[0:1, :], qsq.t[0:1, :], -1.0, None, op0=ALU.mult), R=[qsq], W=[negm])
        K.dma("sp", QT.t[96:97, :], negm.t[0:1, :], R=[negm], W=[QT])
        pi = 0
        slabs = [(512 * i, 512, 66) for i in range(16)] + [(8192, 256, 2)]
        for si, (c0, w, nkt) in enumerate(slabs):
            psO = K.psx[si % 2]
            nxt = None
            for kt in range(nkt):
                if nxt is None:
                    ps1 = K.ps_get()
                    K.op("pe", lambda e: e.matmul(ps1.t[:, 0:w], KT.t[0:97, kt * 128:(kt + 1) * 128], QT.t[0:97, c0:c0 + w], start=True, stop=True), R=[KT, QT], W=[ps1])
                else:
                    ps1 = nxt
                if kt + 1 < nkt:
                    nxt = K.ps_get()
                    K.op("pe", lambda e: e.matmul(nxt.t[:, 0:w], KT.t[0:97, (kt + 1) * 128:(kt + 2) * 128], QT.t[0:97, c0:c0 + w], start=True, stop=True), R=[KT, QT], W=[nxt])
                else:
                    nxt = None
                p = PT[pi % 3]
                pi += 1
                K.op("act", lambda e: e.activation(p.t[:, 0:w], ps1.t[:, 0:w], AF.Exp, scale=ATT_SCALE), R=[ps1], W=[p])
                K.op("pe", lambda e: e.matmul(psO.t[:, 0:w], V.t[:, kt, :], p.t[:, 0:w], start=(kt == 0), stop=(kt == nkt - 1)), R=[V, p], W=[psO])
            s = stg[si % 2]
            K.op("dve", lambda e: e.tensor_copy(s.t[:, 0:w], psO.t[:, 0:w]), R=[psO], W=[s])
            K.dma("sp", OTd.t.ap()[u, :, c0:c0 + w], s.t[:, 0:w], R=[s], W=[OTd])


def decl_ta_even(K):
    io = {}
    io["w_in_e"] = K.dram_in("w_in_e", [1024, 928], F32)
    io["qnT"] = K.dram_in("qnT", [128, 2], F32)
    io["kvnT"] = K.dram_in("kvnT", [128, 1], F32)
    io["w_uq"] = K.dram_in("w_uq", [256, 768], F32)
    io["w_ukv"] = K.dram_in("w_ukv", [128, 1024], F32)
    io["cosq"] = K.dram_in("cosq", [128, NT], F32)
    io["sinq"] = K.dram_in("sinq", [128, NT], F32)
    io["uT"] = K.dram_out("uT", [128, 4, NT], F32)
    io["qT"] = K.dram_out("qT", [128, 6, NT], BF16)
    io["kT"] = K.dram_out("kT", [128, 4, NT], BF16)
    io["vT"] = K.dram_out("vT", [128, 4, NT], BF16)
    io["krT"] = K.dram_out("krT", [16, 2, NT], BF16)
    return io


def decl_mod(K, tag):
    io = {}
    io["cT"] = K.dram_in("cT" + tag, [128, 8, 2], F32)
    io["w_mod"] = K.dram_in("w_mod" + tag, [1024, 6144], F32)
    io["b_modT"] = K.dram_in("b_modT" + tag, [128, 48], F32)
    io["n1gT"] = K.dram_in("n1gT" + tag, [128, 8], F32)
    io["n2gT"] = K.dram_in("n2gT" + tag, [128, 8], F32)
    return io


def build_tok(post, pre, final=False):
    nc = bass.Bass("TRN2", target_bir_lowering=False)
    K = Emit(nc)
    T = Tok(K)
    xin = K.dram_in("xT_in", [128, 8, NT], F32)
    hT = K.sb("hT", [128, 8, NT], BF16)
    mod_a = mod_b = None
    if post is not None:
        iom = decl_mod(K, "_a")
        cTa = T.load("cT_a", iom["cT"], [128, 8, 2])
        bma = T.load("bm_a", iom["b_modT"], [128, 48])
        with nc.named_scope("modvec_a"):
            mod_a = T.modvec(cTa, iom["w_mod"], bma, "a", groups=range(4, 12))
    if pre is not None:
        iom2 = decl_mod(K, "_b")
        cTb = T.load("cT_b", iom2["cT"], [128, 8, 2])
        bmb = T.load("bm_b", iom2["b_modT"], [128, 48])
        with nc.named_scope("modvec_b"):
            mod_b = T.modvec(cTb, iom2["w_mod"], bmb, "b", groups=range(0, 4))
    with K.scope():
        x = K.sb("xT", [128, 8, NT], F32)
        for c in range(8):
            K.dma("sp" if c % 2 == 0 else "act", x.t[:, c, :], xin.t.ap()[:, c, :], R=[xin], W=[x])
        if post is not None:
            iop = decl_tb(K, post)
            with K.scope():
                n2g = T.load("n2g_a", iom["n2gT"], [128, 8])
                tb_phase(K, T, x, hT, mod_a, n2g, iop, post)
            xout = K.dram_out("xT_out", [128, 8, NT], F32)
            for c in range(8):
                K.dma("sp", xout.t.ap()[:, c, :], x.t[:, c, :], R=[x], W=[xout])
            if final:
                fio = K.dram_in("fnT", [128, 8], F32)
                fo = K.dram_out("yT", [128, 8, NT], F32)
                fn = T.load("fn", fio, [128, 8])
                stgs = [K.sb("fstg%d" % i, [128, 512], F32) for i in range(3)]
                i = 0
                for (c0, w, isctx) in SLABS:
                    r = T.rms_rstd(x, 8, c0, w, D)
                    for c in range(8):
                        s = stgs[i % 3]
                        i += 1
                        K.op("dve", lambda e: e.scalar_tensor_tensor(s.t[:, 0:w], x.t[:, c, c0:c0 + w], fn.t[:, c:c + 1], r.t[:, 0:w], op0=ALU.mult, op1=ALU.mult), R=[x, fn, r], W=[s])
                        K.dma("sp", fo.t.ap()[:, c, c0:c0 + w], s.t[:, 0:w], R=[s], W=[fo])
        if pre is not None:
            with K.scope():
                n1g = T.load("n1g_b", iom2["n1gT"], [128, 8])
                with nc.named_scope("norm1"):
                    T.norm_mod(x, hT, mod_b, n1g, 0, 1, "1")
    if pre == "even":
        io = decl_ta_even(K)
        with K.scope(), nc.named_scope("ta_even"):
            ta_even(K, T, hT, io)
    elif pre == "odd":
        io = decl_ta_odd(K)
        with K.scope(), nc.named_scope("ta_odd"):
            ta_odd(K, T, hT, io)
    K.finish()
    return nc


def build_h_even():
    nc = bass.Bass("TRN2", target_bir_lowering=False)
    K = Emit(nc)
    io = {}
    io["QT"] = K.dram_in("QT", [2, 96, NQ], BF16)
    io["KT"] = K.dram_in("KT", [2, 96, NKEY], BF16)
    io["V"] = K.dram_in("V", [2, 128, 66, 64], BF16)
    io["OT"] = K.dram_out("OT", [2, 128, NQ], F32)
    with K.scope():
        h_even_attn(K, io)
    ios = decl_s5(K)
    with K.scope():
        h_even_s5(K, ios)
    K.finish()
    return nc


def to_fm(a):
    n, f = a.shape
    return np.ascontiguousarray(a.reshape(n, f // 128, 128).transpose(2, 1, 0))


def from_fm(t):
    p, c, n = t.shape
    return np.ascontiguousarray(t.transpose(2, 1, 0).reshape(n, c * p))


def vec_fm(v):
    return np.ascontiguousarray(v.reshape(-1, 128).T)


def rope_tables(dim):
    l = np.arange(8192)
    r = (l // 64).astype(np.float32)
    col = (l % 64).astype(np.float32)
    quarter = dim // 4
    inv = (np.float32(10000.0) ** (-np.arange(quarter, dtype=np.float32) / np.float32(quarter))).astype(np.float32)
    ang = np.concatenate([r[:, None] * inv, col[:, None] * inv], axis=-1).astype(np.float32)
    return np.cos(ang).astype(np.float32), np.sin(ang).astype(np.float32)


def core_rope(dim, reps, r):
    cos, sin = rope_tables(dim)
    h = dim // 2
    c = np.ones((reps * h, NT), np.float32)
    s = np.zeros((reps * h, NT), np.float32)
    c[:, 64:] = np.tile(cos[2048 * r:2048 * (r + 1)].T, (reps, 1))
    s[:, 64:] = np.tile(sin[2048 * r:2048 * (r + 1)].T, (reps, 1))
    return c, s


def perm_even(w_in, w_uq, w_ukv):
    ci = np.concatenate([np.arange(896), 896 + 2 * np.arange(16), 897 + 2 * np.arange(16)])
    nope = np.concatenate([96 * h + np.arange(64) for h in range(8)])
    r1 = np.concatenate([96 * h + 64 + 2 * np.arange(16) for h in range(8)])
    r2 = r1 + 1
    kn = np.concatenate([128 * h + np.arange(64) for h in range(8)])
    vv = kn + 64
    return (np.ascontiguousarray(w_in[:, ci]), np.ascontiguousarray(w_uq[:, np.concatenate([nope, r1, r2])]),
            np.ascontiguousarray(w_ukv[:, np.concatenate([kn, vv])]))


def mod_inputs(inp, li, b, tag):
    cT = np.stack([vec_fm(inp["c"][b]), vec_fm(inp["c_ctx"])], axis=-1)
    return {"cT" + tag: np.ascontiguousarray(cT), "w_mod" + tag: inp["w_mod"][li],
            "b_modT" + tag: vec_fm(inp["b_mod"][li]), "n1gT" + tag: vec_fm(inp["norm1_g"][li]),
            "n2gT" + tag: vec_fm(inp["norm2_g"][li])}


def ta_even_inputs(inp, j, r):
    w_in, w_uq, w_ukv = perm_even(inp["w_in_even"][j], inp["mla_w_uq"][j], inp["mla_w_ukv"][j])
    c, s = core_rope(32, 8, r)
    return {"w_in_e": w_in, "w_uq": w_uq, "w_ukv": w_ukv, "qnT": vec_fm(inp["mla_q_norm"][j]),
            "kvnT": vec_fm(inp["mla_kv_norm"][j]), "cosq": c, "sinq": s}


def attn_inputs(res, last=False):
    import ml_dtypes
    bf = ml_dtypes.bfloat16
    maps = []
    for core in range(8):
        QT = np.zeros((2, 96, NQ), bf)
        KT = np.zeros((2, 96, NKEY), bf)
        V = np.zeros((2, 128, 66, 64), bf)
        for ui in range(2):
            unit = core * 2 + ui
            b, h = unit // 8, unit % 8
            ch, ro = h // 2, (h % 2) * 64
            q = np.concatenate([np.concatenate([res[4 * b + r]["qT"][ro:ro + 64, ch, :],
                                                res[4 * b + r]["qT"][h * 16:h * 16 + 16, 4, :],
                                                res[4 * b + r]["qT"][h * 16:h * 16 + 16, 5, :]], axis=0) for r in range(4)], axis=1)
            k = np.concatenate([np.concatenate([res[4 * b + r]["kT"][ro:ro + 64, ch, :],
                                                res[4 * b + r]["krT"][:, 0, :], res[4 * b + r]["krT"][:, 1, :]], axis=0) for r in range(4)], axis=1)
            v = np.concatenate([res[4 * b + r]["vT"][ro:ro + 64, ch, :] for r in range(4)], axis=1)
            q = q.reshape(96, 4, NT)
            k = k.reshape(96, 4, NT)
            v = v.reshape(64, 4, NT)
            QT[ui] = np.concatenate([q[:, :, 64:].reshape(96, 8192), q[:, :, :64].reshape(96, 256)], axis=1)
            KT[ui] = np.concatenate([k[:, :, :64].reshape(96, 256), k[:, :, 64:].reshape(96, 8192)], axis=1)
            vv = np.concatenate([v[:, :, :64].reshape(64, 256), v[:, :, 64:].reshape(64, 8192)], axis=1)
            V[ui] = vv.T.reshape(66, 128, 64).transpose(1, 0, 2)
        maps.append({"QT": QT, "KT": KT, "V": V})
    return maps


NC8 = 1056
TWO_PI = 6.283185307179586


def decl_s5(K):
    io = {}
    for n in ("are", "aim", "ldt"):
        io[n] = K.dram_in("s5_" + n, [64, 8], F32)
    for n in ("bre", "bim", "creT", "cimT"):
        io[n] = K.dram_in("s5_" + n, [64, 8, 16], F32)
    io["U8"] = K.dram_in("s5_U8", [8, 128, 2 * NC8], F32)
    io["mask8"] = K.dram_in("s5_mask8", [128, 128], F32)
    io["ident"] = K.dram_in("s5_ident", [64, 64], F32)
    io["Y8"] = K.dram_out("s5_Y8", [8, 128, 2 * NC8], F32)
    return io


def h_even_s5(K, io):
    NU = 8
    N = 2 * NC8

    def ld(name, shape):
        b = K.sb("s5" + name, shape, F32)
        sl = tuple(slice(None) for _ in shape)
        K.dma("sp", b.t[sl], io[name].t.ap(), R=[io[name]], W=[b])
        return b
    are, aim, ldt = ld("are", [64, 8]), ld("aim", [64, 8]), ld("ldt", [64, 8])
    bre, bim, creT, cimT = ld("bre", [64, 8, 16]), ld("bim", [64, 8, 16]), ld("creT", [64, 8, 16]), ld("cimT", [64, 8, 16])
    mask8, ident = ld("mask8", [128, 128]), ld("ident", [64, 64])
    cnt = {"i": 0}

    def sm(shape=(64, 8), dt=F32):
        cnt["i"] += 1
        return K.sb("s5t%d" % cnt["i"], list(shape), dt)

    def tt(out, a, b, op, R, W, eng="dve"):
        K.op(eng, lambda e: e.tensor_tensor(out, a, b, op=op), R=R, W=W)

    dt_, lr, li, mag = sm(), sm(), sm(), sm()
    K.op("act", lambda e: e.activation(dt_.t[:, :], ldt.t[:, :], AF.Exp), R=[ldt], W=[dt_])
    tt(lr.t[:, :], are.t[:, :], dt_.t[:, :], ALU.mult, [are, dt_], [lr])
    tt(li.t[:, :], aim.t[:, :], dt_.t[:, :], ALU.mult, [aim, dt_], [li])
    K.op("act", lambda e: e.activation(mag.t[:, :], lr.t[:, :], AF.Exp), R=[lr], W=[mag])
    rho, irho = sm(), sm()
    K.op("act", lambda e: e.activation(rho.t[:, :], lr.t[:, :], AF.Exp, scale=8.0), R=[lr], W=[rho])
    K.op("act", lambda e: e.activation(irho.t[:, :], lr.t[:, :], AF.Exp, scale=-8.0), R=[lr], W=[irho])
    kf, ki, r0, rs, rc, s1, c1 = sm(), sm((64, 8), I32), sm(), sm(), sm(), sm(), sm()
    K.op("dve", lambda e: e.tensor_scalar(kf.t[:, :], li.t[:, :], 1.0 / TWO_PI, None, op0=ALU.mult), R=[li], W=[kf])
    K.op("dve", lambda e: e.tensor_copy(ki.t[:, :], kf.t[:, :]), R=[kf], W=[ki])
    K.op("dve", lambda e: e.tensor_copy(kf.t[:, :], ki.t[:, :]), R=[ki], W=[kf])
    K.op("dve", lambda e: e.scalar_tensor_tensor(r0.t[:, :], kf.t[:, :], -TWO_PI, li.t[:, :], op0=ALU.mult, op1=ALU.add), R=[kf, li], W=[r0])
    wa, wb2, wy = sm(), sm(), sm()

    def wrap(dst, shift):
        K.op("dve", lambda e: e.tensor_scalar(wy.t[:, :], r0.t[:, :], float(shift), None, op0=ALU.add), R=[r0], W=[wy])
        K.op("dve", lambda e: e.tensor_scalar(wa.t[:, :], wy.t[:, :], float(np.pi), -TWO_PI, op0=ALU.is_gt, op1=ALU.mult), R=[wy], W=[wa])
        K.op("dve", lambda e: e.tensor_scalar(wb2.t[:, :], wy.t[:, :], -float(np.pi), TWO_PI, op0=ALU.is_lt, op1=ALU.mult), R=[wy], W=[wb2])
        K.op("dve", lambda e: e.tensor_tensor(wa.t[:, :], wa.t[:, :], wb2.t[:, :], op=ALU.add), R=[wa, wb2], W=[wa])
        K.op("dve", lambda e: e.tensor_tensor(dst.t[:, :], wy.t[:, :], wa.t[:, :], op=ALU.add), R=[wy, wa], W=[dst])
    wrap(rs, 0.0)
    wrap(rc, np.pi / 2)
    K.op("act", lambda e: e.activation(s1.t[:, :], rs.t[:, :], AF.Sin), R=[rs], W=[s1])
    K.op("act", lambda e: e.activation(c1.t[:, :], rc.t[:, :], AF.Sin), R=[rc], W=[c1])
    abr, abi = sm(), sm()
    tt(abr.t[:, :], mag.t[:, :], c1.t[:, :], ALU.mult, [mag, c1], [abr])
    tt(abi.t[:, :], mag.t[:, :], s1.t[:, :], ALU.mult, [mag, s1], [abi])
    nr, den, fr, fi, t1, t2 = sm(), sm(), sm(), sm(), sm(), sm()
    K.op("dve", lambda e: e.tensor_scalar(nr.t[:, :], abr.t[:, :], -1.0, None, op0=ALU.add), R=[abr], W=[nr])
    tt(t1.t[:, :], are.t[:, :], are.t[:, :], ALU.mult, [are], [t1])
    tt(t2.t[:, :], aim.t[:, :], aim.t[:, :], ALU.mult, [aim], [t2])
    tt(den.t[:, :], t1.t[:, :], t2.t[:, :], ALU.add, [t1, t2], [den])
    K.op("dve", lambda e: e.reciprocal(den.t[:, :], den.t[:, :]), R=[den], W=[den])
    tt(t1.t[:, :], nr.t[:, :], are.t[:, :], ALU.mult, [nr, are], [t1])
    tt(t2.t[:, :], abi.t[:, :], aim.t[:, :], ALU.mult, [abi, aim], [t2])
    tt(t1.t[:, :], t1.t[:, :], t2.t[:, :], ALU.add, [t1, t2], [t1])
    tt(fr.t[:, :], t1.t[:, :], den.t[:, :], ALU.mult, [t1, den], [fr])
    tt(t1.t[:, :], abi.t[:, :], are.t[:, :], ALU.mult, [abi, are], [t1])
    tt(t2.t[:, :], nr.t[:, :], aim.t[:, :], ALU.mult, [nr, aim], [t2])
    tt(t1.t[:, :], t1.t[:, :], t2.t[:, :], ALU.subtract, [t1, t2], [t1])
    tt(fi.t[:, :], t1.t[:, :], den.t[:, :], ALU.mult, [t1, den], [fi])

    Apr, Api = sm((64, 8, 9)), sm((64, 8, 9))
    Qr, Qi = sm((64, 8, 8)), sm((64, 8, 8))
    K.op("dve", lambda e: e.memset(Apr.t[:, :, 0], 1.0), W=[Apr])
    K.op("dve", lambda e: e.memset(Api.t[:, :, 0], 0.0), W=[Api])
    K.op("dve", lambda e: e.tensor_copy(Qr.t[:, :, 0], fr.t[:, :]), R=[fr], W=[Qr])
    K.op("dve", lambda e: e.tensor_copy(Qi.t[:, :, 0], fi.t[:, :]), R=[fi], W=[Qi])

    def cstep(Tr, Ti, n):
        for tau in range(1, n):
            tt(t1.t[:, :], Tr.t[:, :, tau - 1], abr.t[:, :], ALU.mult, [Tr, abr], [t1])
            tt(t2.t[:, :], Ti.t[:, :, tau - 1], abi.t[:, :], ALU.mult, [Ti, abi], [t2])
            tt(Tr.t[:, :, tau], t1.t[:, :], t2.t[:, :], ALU.subtract, [t1, t2], [Tr])
            tt(t1.t[:, :], Tr.t[:, :, tau - 1], abi.t[:, :], ALU.mult, [Tr, abi], [t1])
            tt(t2.t[:, :], Ti.t[:, :, tau - 1], abr.t[:, :], ALU.mult, [Ti, abr], [t2])
            tt(Ti.t[:, :, tau], t1.t[:, :], t2.t[:, :], ALU.add, [t1, t2], [Ti])
    cstep(Apr, Api, 9)
    cstep(Qr, Qi, 8)
    nApi, nQi = sm((64, 8, 9)), sm((64, 8, 8))
    nApr = sm((64, 8, 9))
    K.op("dve", lambda e: e.tensor_scalar(nApr.t[:, :, :], Apr.t[:, :, :], -1.0, None, op0=ALU.mult), R=[Apr], W=[nApr])
    K.op("dve", lambda e: e.tensor_scalar(nApi.t[:, :, :], Api.t[:, :, :], -1.0, None, op0=ALU.mult), R=[Api], W=[nApi])
    K.op("dve", lambda e: e.tensor_scalar(nQi.t[:, :, :], Qi.t[:, :, :], -1.0, None, op0=ALU.mult), R=[Qi], W=[nQi])
    n2, i8r, i8i, ni8i, phr, phi = sm(), sm(), sm(), sm(), sm(), sm()
    tt(t1.t[:, :], Apr.t[:, :, 8], Apr.t[:, :, 8], ALU.mult, [Apr], [t1])
    tt(t2.t[:, :], Api.t[:, :, 8], Api.t[:, :, 8], ALU.mult, [Api], [t2])
    tt(n2.t[:, :], t1.t[:, :], t2.t[:, :], ALU.add, [t1, t2], [n2])
    K.op("dve", lambda e: e.reciprocal(n2.t[:, :], n2.t[:, :]), R=[n2], W=[n2])
    tt(i8r.t[:, :], Apr.t[:, :, 8], n2.t[:, :], ALU.mult, [Apr, n2], [i8r])
    tt(ni8i.t[:, :], Api.t[:, :, 8], n2.t[:, :], ALU.mult, [Api, n2], [ni8i])
    K.op("dve", lambda e: e.tensor_scalar(i8i.t[:, :], ni8i.t[:, :], -1.0, None, op0=ALU.mult), R=[ni8i], W=[i8i])
    tt(phr.t[:, :], Apr.t[:, :, 8], irho.t[:, :], ALU.mult, [Apr, irho], [phr])
    tt(phi.t[:, :], Api.t[:, :, 8], irho.t[:, :], ALU.mult, [Api, irho], [phi])
    pwr, pwi, npwi = sm((64, 8, 11)), sm((64, 8, 11)), sm((64, 8, 11))
    K.op("dve", lambda e: e.tensor_copy(pwr.t[:, :, 0], phr.t[:, :]), R=[phr], W=[pwr])
    K.op("dve", lambda e: e.tensor_copy(pwi.t[:, :, 0], phi.t[:, :]), R=[phi], W=[pwi])
    for j in range(1, 11):
        tt(t1.t[:, :], pwr.t[:, :, j - 1], pwr.t[:, :, j - 1], ALU.mult, [pwr], [t1])
        tt(t2.t[:, :], pwi.t[:, :, j - 1], pwi.t[:, :, j - 1], ALU.mult, [pwi], [t2])
        tt(pwr.t[:, :, j], t1.t[:, :], t2.t[:, :], ALU.subtract, [t1, t2], [pwr])
        tt(t1.t[:, :], pwr.t[:, :, j - 1], pwi.t[:, :, j - 1], ALU.mult, [pwr, pwi], [t1])
        K.op("dve", lambda e: e.tensor_scalar(pwi.t[:, :, j], t1.t[:, :], 2.0, None, op0=ALU.mult), R=[t1], W=[pwi])
    K.op("dve", lambda e: e.tensor_scalar(npwi.t[:, :, :], pwi.t[:, :, :], -1.0, None, op0=ALU.mult), R=[pwi], W=[npwi])

    if "dbg" in io:
        for i, tb_ in enumerate((dt_, lr, li, mag, r0, rs, rc, s1, c1, abr, abi, fr, fi, i8r, i8i, phr, phi, rho)):
            K.dma("sp", io["dbg"].t.ap()[:, i, :], tb_.t[:, :], R=[tb_], W=[io["dbg"]])
    Wre, Wim = sm((64, 8, 16)), sm((64, 8, 16))
    Wpr, Wpi = sm((64, 128)), sm((64, 128))
    RrT, RiT = sm((64, 8, 16)), sm((64, 8, 16))
    RrTb, RiTb = sm((64, 128), BF16), sm((64, 128), BF16)
    tmp16 = sm((64, 16))
    tmp128 = sm((64, 128))
    MTb = sm((128, 128), BF16)
    WreTb, WimTb = sm((128, 64), BF16), sm((128, 64), BF16)
    Phr, Phi = sm((64, NC8)), sm((64, NC8))
    ptmp = sm((64, 512))
    rmask = sm((64, N))
    U8f = [sm((128, N)) for _ in range(2)]
    U8b = sm((128, N), BF16)
    Sre, Sim = sm((64, N)), sm((64, N))
    Gr, Gi = sm((64, N)), sm((64, N))
    Gr2, Gi2 = sm((64, N)), sm((64, N))
    ta, tb = sm((64, NC8)), sm((64, NC8))
    Hpr, Hpi = sm((64, N), BF16), sm((64, N), BF16)
    ystg = [sm((128, 512)) for _ in range(2)]
    K.op("dve", lambda e: e.memset(Hpr.t[:, :], 0.0), W=[Hpr])
    K.op("dve", lambda e: e.memset(Hpi.t[:, :], 0.0), W=[Hpi])
    slabs = [(c0, min(512, N - c0)) for c0 in range(0, N, 512)]

    for u in range(NU):
        K.dma("act", U8f[u % 2].t[:, :], io["U8"].t.ap()[u], R=[io["U8"]], W=[U8f[u % 2]])
        for s in range(8):
            q = 7 - s
            K.op("dve", lambda e: e.tensor_scalar(tmp16.t[:, :], bre.t[:, u, :], Qr.t[:, u, q:q + 1], None, op0=ALU.mult), R=[bre, Qr], W=[tmp16])
            K.op("dve", lambda e: e.scalar_tensor_tensor(Wre.t[:, s, :], bim.t[:, u, :], nQi.t[:, u, q:q + 1], tmp16.t[:, :], op0=ALU.mult, op1=ALU.add), R=[bim, nQi, tmp16], W=[Wre])
            K.op("dve", lambda e: e.tensor_scalar(tmp16.t[:, :], bim.t[:, u, :], Qr.t[:, u, q:q + 1], None, op0=ALU.mult), R=[bim, Qr], W=[tmp16])
            K.op("dve", lambda e: e.scalar_tensor_tensor(Wim.t[:, s, :], bre.t[:, u, :], Qi.t[:, u, q:q + 1], tmp16.t[:, :], op0=ALU.mult, op1=ALU.add), R=[bre, Qi, tmp16], W=[Wim])
        for t in range(8):
            K.op("dve", lambda e: e.tensor_scalar(tmp16.t[:, :], creT.t[:, u, :], Apr.t[:, u, t + 1:t + 2], None, op0=ALU.mult), R=[creT, Apr], W=[tmp16])
            K.op("dve", lambda e: e.scalar_tensor_tensor(RrT.t[:, t, :], cimT.t[:, u, :], nApi.t[:, u, t + 1:t + 2], tmp16.t[:, :], op0=ALU.mult, op1=ALU.add), R=[cimT, nApi, tmp16], W=[RrT])
            K.op("dve", lambda e: e.tensor_scalar(tmp16.t[:, :], creT.t[:, u, :], nApi.t[:, u, t + 1:t + 2], None, op0=ALU.mult), R=[creT, nApi], W=[tmp16])
            K.op("dve", lambda e: e.scalar_tensor_tensor(RiT.t[:, t, :], cimT.t[:, u, :], nApr.t[:, u, t + 1:t + 2], tmp16.t[:, :], op0=ALU.mult, op1=ALU.add), R=[cimT, nApr, tmp16], W=[RiT])
        Wre2 = Wre.t[:, :, :].rearrange("p s k -> p (s k)")
        Wim2 = Wim.t[:, :, :].rearrange("p s k -> p (s k)")
        Rr2 = RrT.t[:, :, :].rearrange("p s k -> p (s k)")
        Ri2 = RiT.t[:, :, :].rearrange("p s k -> p (s k)")
        K.op("dve", lambda e: e.tensor_scalar(tmp128.t[:, :], Wre2, i8r.t[:, u:u + 1], None, op0=ALU.mult), R=[Wre, i8r], W=[tmp128])
        K.op("dve", lambda e: e.scalar_tensor_tensor(Wpr.t[:, :], Wim2, ni8i.t[:, u:u + 1], tmp128.t[:, :], op0=ALU.mult, op1=ALU.add), R=[Wim, ni8i, tmp128], W=[Wpr])
        K.op("dve", lambda e: e.tensor_scalar(tmp128.t[:, :], Wim2, i8r.t[:, u:u + 1], None, op0=ALU.mult), R=[Wim, i8r], W=[tmp128])
        K.op("dve", lambda e: e.scalar_tensor_tensor(Wpi.t[:, :], Wre2, i8i.t[:, u:u + 1], tmp128.t[:, :], op0=ALU.mult, op1=ALU.add), R=[Wre, i8i, tmp128], W=[Wpi])
        ps = K.ps_get()
        K.op("pe", lambda e: e.matmul(ps.t[:, 0:128], Wpr.t[:, :], Rr2, start=True, stop=False), R=[Wpr, RrT], W=[ps])
        K.op("pe", lambda e: e.matmul(ps.t[:, 0:128], Wpi.t[:, :], Ri2, start=False, stop=True), R=[Wpi, RiT], W=[ps])
        K.op("dve", lambda e: e.tensor_tensor(MTb.t[:, :], ps.t[:, 0:128], mask8.t[:, :], op=ALU.mult), R=[ps, mask8], W=[MTb])
        ps = K.ps_get()
        K.op("pe", lambda e: e.matmul(ps.t[:, 0:64], Wre2, ident.t[:, :], start=True, stop=True), R=[Wre, ident], W=[ps])
        K.op("act", lambda e: e.copy(WreTb.t[:, :], ps.t[:, 0:64]), R=[ps], W=[WreTb])
        ps = K.ps_get()
        K.op("pe", lambda e: e.matmul(ps.t[:, 0:64], Wim2, ident.t[:, :], start=True, stop=True), R=[Wim, ident], W=[ps])
        K.op("act", lambda e: e.copy(WimTb.t[:, :], ps.t[:, 0:64]), R=[ps], W=[WimTb])
        K.op("act", lambda e: e.copy(RrTb.t[:, :], Rr2), R=[RrT], W=[RrTb])
        K.op("act", lambda e: e.copy(RiTb.t[:, :], Ri2), R=[RiT], W=[RiTb])
        K.op("pool", lambda e: e.memset(Phr.t[:, 0:1], 1.0), W=[Phr])
        K.op("pool", lambda e: e.memset(Phi.t[:, 0:1], 0.0), W=[Phi])
        n = 1
        j = 0
        while n < NC8:
            m = min(n, NC8 - n)
            for c0 in range(0, m, 512):
                w = min(512, m - c0)
                K.op("dve", lambda e: e.tensor_scalar(ptmp.t[:, 0:w], Phr.t[:, c0:c0 + w], pwr.t[:, u, j:j + 1], None, op0=ALU.mult), R=[Phr, pwr], W=[ptmp])
                K.op("dve", lambda e: e.scalar_tensor_tensor(Phr.t[:, n + c0:n + c0 + w], Phi.t[:, c0:c0 + w], npwi.t[:, u, j:j + 1], ptmp.t[:, 0:w], op0=ALU.mult, op1=ALU.add), R=[Phi, npwi, ptmp, Phr], W=[Phr])
                K.op("dve", lambda e: e.tensor_scalar(ptmp.t[:, 0:w], Phr.t[:, c0:c0 + w], pwi.t[:, u, j:j + 1], None, op0=ALU.mult), R=[Phr, pwi], W=[ptmp])
                K.op("dve", lambda e: e.scalar_tensor_tensor(Phi.t[:, n + c0:n + c0 + w], Phi.t[:, c0:c0 + w], pwr.t[:, u, j:j + 1], ptmp.t[:, 0:w], op0=ALU.mult, op1=ALU.add), R=[Phi, pwr, ptmp], W=[Phi])
            n *= 2
            j += 1
        K.op("pool", lambda e: e.memset(rmask.t[:, :], 1.0), W=[rmask])
        K.op("pool", lambda e: e.tensor_scalar(rmask.t[:, :], rmask.t[:, :], rho.t[:, u:u + 1], None, op0=ALU.mult), R=[rho, rmask], W=[rmask])
        K.op("pool", lambda e: e.memset(rmask.t[:, 0:1], 0.0), W=[rmask])
        K.op("pool", lambda e: e.memset(rmask.t[:, NC8:NC8 + 1], 0.0), W=[rmask])
        Uf = U8f[u % 2]
        K.op("act", lambda e: e.copy(U8b.t[:, :], Uf.t[:, :]), R=[Uf], W=[U8b])
        for (c0, w) in slabs:
            for (WT, S) in ((WreTb, Sre), (WimTb, Sim)):
                ps = K.ps_get()
                K.op("pe", lambda e: e.matmul(ps.t[0:64, 0:w], WT.t[:, :], U8b.t[:, c0:c0 + w], start=True, stop=True), R=[WT, U8b], W=[ps])
                K.op("act", lambda e: e.copy(S.t[:, c0:c0 + w], ps.t[0:64, 0:w]), R=[ps], W=[S])
        for hb in range(2):
            sl = slice(hb * NC8, (hb + 1) * NC8)
            tt(ta.t[:, :], Phr.t[:, :], Sre.t[:, sl], ALU.mult, [Phr, Sre], [ta])
            tt(tb.t[:, :], Phi.t[:, :], Sim.t[:, sl], ALU.mult, [Phi, Sim], [tb], eng="pool")
            tt(Gr.t[:, sl], ta.t[:, :], tb.t[:, :], ALU.add, [ta, tb], [Gr])
            tt(ta.t[:, :], Phr.t[:, :], Sim.t[:, sl], ALU.mult, [Phr, Sim], [ta])
            tt(tb.t[:, :], Phi.t[:, :], Sre.t[:, sl], ALU.mult, [Phi, Sre], [tb], eng="pool")
            tt(Gi.t[:, sl], ta.t[:, :], tb.t[:, :], ALU.subtract, [ta, tb], [Gi])
        K.op("dve", lambda e: e.tensor_tensor_scan(Gr2.t[:, :], rmask.t[:, :], Gr.t[:, :], 0.0, op0=ALU.mult, op1=ALU.add), R=[rmask, Gr], W=[Gr2])
        K.op("dve", lambda e: e.tensor_tensor_scan(Gi2.t[:, :], rmask.t[:, :], Gi.t[:, :], 0.0, op0=ALU.mult, op1=ALU.add), R=[rmask, Gi], W=[Gi2])
        for hb in range(2):
            o = hb * NC8
            n1 = NC8 - 1
            tt(ta.t[:, 0:n1], Phr.t[:, 0:n1], Gr2.t[:, o:o + n1], ALU.mult, [Phr, Gr2], [ta])
            tt(tb.t[:, 0:n1], Phi.t[:, 0:n1], Gi2.t[:, o:o + n1], ALU.mult, [Phi, Gi2], [tb], eng="pool")
            tt(Hpr.t[:, o + 1:o + 1 + n1], ta.t[:, 0:n1], tb.t[:, 0:n1], ALU.subtract, [ta, tb], [Hpr])
            tt(ta.t[:, 0:n1], Phr.t[:, 0:n1], Gi2.t[:, o:o + n1], ALU.mult, [Phr, Gi2], [ta])
            tt(tb.t[:, 0:n1], Phi.t[:, 0:n1], Gr2.t[:, o:o + n1], ALU.mult, [Phi, Gr2], [tb], eng="pool")
            tt(Hpi.t[:, o + 1:o + 1 + n1], ta.t[:, 0:n1], tb.t[:, 0:n1], ALU.add, [ta, tb], [Hpi])
        for si, (c0, w) in enumerate(slabs):
            ps = K.ps_get()
            K.op("pe", lambda e: e.matmul(ps.t[:, 0:w], MTb.t[:, :], U8b.t[:, c0:c0 + w], start=True, stop=False), R=[MTb, U8b], W=[ps])
            K.op("pe", lambda e: e.matmul(ps.t[:, 0:w], RrTb.t[:, :], Hpr.t[:, c0:c0 + w], start=False, stop=False), R=[RrTb, Hpr], W=[ps])
            K.op("pe", lambda e: e.matmul(ps.t[:, 0:w], RiTb.t[:, :], Hpi.t[:, c0:c0 + w], start=False, stop=True), R=[RiTb, Hpi], W=[ps])
            s = ystg[si % 2]
            K.op("act", lambda e: e.copy(s.t[:, 0:w], ps.t[:, 0:w]), R=[ps], W=[s])
            K.dma("sp", io["Y8"].t.ap()[u, :, c0:c0 + w], s.t[:, 0:w], R=[s], W=[io["Y8"]])


def s5_inputs(inp, j, uT_list):
    useq = []
    for b in range(2):
        parts = [from_fm(uT_list[4 * b + r]) for r in range(4)]
        ctx = np.concatenate([p[:64] for p in parts], 0)
        lat = np.concatenate([p[64:] for p in parts], 0)
        useq.append((ctx, lat))
    mask8 = (np.arange(128)[:, None] // 16 <= np.arange(128)[None, :] // 16).astype(np.float32)
    ident = np.eye(64, dtype=np.float32)
    maps = []
    for core in range(8):
        m = {"s5_mask8": mask8, "s5_ident": ident}
        U8 = np.zeros((8, 128, 2 * NC8), np.float32)
        sel = lambda a: np.stack([a[j, d, 4 * core + gl] for d in range(2) for gl in range(4)], axis=-1)
        m["s5_are"] = np.ascontiguousarray(sel(inp["s5_a_re"]))
        m["s5_aim"] = np.ascontiguousarray(sel(inp["s5_a_im"]))
        m["s5_ldt"] = np.ascontiguousarray(np.broadcast_to(np.stack([inp["s5_log_dt"][j, d, 4 * core + gl] for d in range(2) for gl in range(4)])[None, :], (64, 8)))
        m["s5_bre"] = np.ascontiguousarray(np.stack([inp["s5_b_re"][j, d, 4 * core + gl] for d in range(2) for gl in range(4)], axis=1))
        m["s5_bim"] = np.ascontiguousarray(np.stack([inp["s5_b_im"][j, d, 4 * core + gl] for d in range(2) for gl in range(4)], axis=1))
        m["s5_creT"] = np.ascontiguousarray(np.stack([inp["s5_c_re"][j, d, 4 * core + gl].T for d in range(2) for gl in range(4)], axis=1))
        m["s5_cimT"] = np.ascontiguousarray(np.stack([inp["s5_c_im"][j, d, 4 * core + gl].T for d in range(2) for gl in range(4)], axis=1))
        for d in range(2):
            for gl in range(4):
                g = 4 * core + gl
                for b in range(2):
                    ctx, lat = useq[b]
                    if d == 0:
                        seq = np.concatenate([ctx[:, 16 * g:16 * g + 16], lat[:, 16 * g:16 * g + 16]], 0)
                    else:
                        seq = np.concatenate([ctx[::-1, 16 * g:16 * g + 16], lat[::-1, 16 * g:16 * g + 16]], 0)
                    U8[d * 4 + gl, :, b * NC8:(b + 1) * NC8] = seq.reshape(NC8, 128).T
        m["s5_U8"] = U8
        maps.append(m)
    return maps


def s5_outputs(Y8_list):
    yf = np.zeros((2, 8448, 512), np.float32)
    yb = np.zeros((2, 8448, 512), np.float32)
    for core in range(8):
        Y8 = Y8_list[core]
        for d in range(2):
            for gl in range(4):
                g = 4 * core + gl
                for b in range(2):
                    seq = Y8[d * 4 + gl, :, b * NC8:(b + 1) * NC8].T.reshape(8448, 16)
                    if d == 0:
                        yf[b, :, 16 * g:16 * g + 16] = seq
                    else:
                        yb[b, :256, 16 * g:16 * g + 16] = seq[:256][::-1]
                        yb[b, 256:, 16 * g:16 * g + 16] = seq[256:][::-1]
    return yf, yb


GELU_C = 2.0 * 0.7978845608028654


def decl_tb(K, parity):
    io = {}
    if parity == "even":
        for n in ("uTi", "yfT", "ybT", "OTn", "denT"):
            io[n] = K.dram_in(n, [128, 4, NT], F32)
        io["dT"] = K.dram_in("dT", [128, 4], F32)
        io["w_glu"] = K.dram_in("w_glu", [512, 512], F32)
    else:
        for n in ("of", "ob", "gT"):
            io[n] = K.dram_in(n, [128, 8, NT], F32)
        io["gnT"] = K.dram_in("gnT", [128, 8], F32)
    io["w_out"] = K.dram_in("w_out", [1024, 1024], F32)
    io["w1"] = K.dram_in("ffn_w1", [1024, DFF], F32)
    io["w3"] = K.dram_in("ffn_w3", [1024, DFF], F32)
    io["w2"] = K.dram_in("ffn_w2", [DFF, 1024], F32)
    return io


def tb_phase(K, T, x, hT, mod, n2g, io, parity):
    with K.scope(), K.nc.named_scope("mix_post"):
        stg = [K.sb("tbs%d" % i, [128, 512], F32) for i in range(6)]
        st = {"i": 0}

        def ldslab(d, c, c0, w, q="sp"):
            b = stg[st["i"] % 6]
            st["i"] += 1
            K.dma(q, b.t[:, 0:w], d.t.ap()[:, c, c0:c0 + w], R=[d], W=[b])
            return b

        if parity == "even":
            dT = T.load("dT", io["dT"], [128, 4])
            zb = K.sb("zbT", [128, 4, NT], BF16)
            z = zb
            for (c0, w, isctx) in SLABS:
                for c in range(4):
                    u = ldslab(io["uTi"], c, c0, w)
                    yf = ldslab(io["yfT"], c, c0, w, "act")
                    yb = ldslab(io["ybT"], c, c0, w)
                    t = T.gettmp()
                    t2 = T.gettmp()
                    K.op("dve", lambda e: e.scalar_tensor_tensor(t.t[:, 0:w], u.t[:, 0:w], dT.t[:, c:c + 1], yf.t[:, 0:w], op0=ALU.mult, op1=ALU.add), R=[u, dT, yf], W=[t])
                    K.op("dve", lambda e: e.tensor_tensor(t.t[:, 0:w], t.t[:, 0:w], yb.t[:, 0:w], op=ALU.add), R=[t, yb], W=[t])
                    K.op("dve", lambda e: e.tensor_tensor(t2.t[:, 0:w], t.t[:, 0:w], t.t[:, 0:w], op=ALU.mult), R=[t], W=[t2])
                    K.op("dve", lambda e: e.tensor_scalar(t2.t[:, 0:w], t2.t[:, 0:w], 0.044715, 1.0, op0=ALU.mult, op1=ALU.add), R=[t2], W=[t2])
                    K.op("dve", lambda e: e.tensor_tensor(t2.t[:, 0:w], t2.t[:, 0:w], t.t[:, 0:w], op=ALU.mult), R=[t2, t], W=[t2])
                    K.op("act", lambda e: e.activation(t2.t[:, 0:w], t2.t[:, 0:w], AF.Sigmoid, scale=GELU_C), R=[t2], W=[t2])
                    K.op("dve", lambda e: e.tensor_tensor(zb.t[:, c, c0:c0 + w], t.t[:, 0:w], t2.t[:, 0:w], op=ALU.mult), R=[t, t2], W=[zb])
                    o = ldslab(io["OTn"], c, c0, w, "act")
                    dn = ldslab(io["denT"], c, c0, w)
                    K.op("dve", lambda e: e.reciprocal(dn.t[:, 0:w], dn.t[:, 0:w]), R=[dn], W=[dn])
                    K.op("dve", lambda e: e.tensor_tensor(hT.t[:, 4 + c, c0:c0 + w], o.t[:, 0:w], dn.t[:, 0:w], op=ALU.mult), R=[o, dn], W=[hT])

            def epi_glu(oc, m, si, c0, w, isctx, ps):
                t = T.gettmp()
                K.op("act", lambda e: e.activation(t.t[:, 0:w], ps.t[:, 0:w], AF.Sigmoid), R=[ps], W=[t])
                K.op("dve", lambda e: e.tensor_tensor(hT.t[:, oc, c0:c0 + w], z.t[:, oc, c0:c0 + w], t.t[:, 0:w], op=ALU.mult), R=[z, t], W=[hT])
            T.proj(zb, 4, io["w_glu"], 0, 512, epi_glu, tag="wglu")
        else:
            gn = T.load("gnT", io["gnT"], [128, 8])
            for (c0, w, isctx) in SLABS:
                for c in range(8):
                    a = ldslab(io["of"], c, c0, w)
                    b = ldslab(io["ob"], c, c0, w, "act")
                    g = ldslab(io["gT"], c, c0, w)
                    o = T.gettmp()
                    K.op("dve", lambda e: e.tensor_tensor(o.t[:, 0:w], a.t[:, 0:w], b.t[:, 0:w], op=ALU.add), R=[a, b], W=[o])
                    if c < 4:
                        ps = K.ps_get()
                        K.op("pe", lambda e: e.matmul(ps.t[:, 0:w], T.ones.t[:, :], o.t[:, 0:w], start=True, stop=True), R=[T.ones, o], W=[ps])
                        K.op("dve", lambda e: e.scalar_tensor_tensor(o.t[:, 0:w], ps.t[:, 0:w], -1.0 / 128, o.t[:, 0:w], op0=ALU.mult, op1=ALU.add), R=[ps, o], W=[o])
                    sq = T.gettmp()
                    K.op("act", lambda e: e.activation(sq.t[:, 0:w], o.t[:, 0:w], AF.Square), R=[o], W=[sq])
                    ps = K.ps_get()
                    K.op("pe", lambda e: e.matmul(ps.t[:, 0:w], T.ones.t[:, :], sq.t[:, 0:w], start=True, stop=True), R=[T.ones, sq], W=[ps])
                    K.op("act", lambda e: e.activation(sq.t[:, 0:w], ps.t[:, 0:w], AF.Ln, bias=T.eps.t[:, 0:1], scale=1.0 / 128), R=[ps, T.eps], W=[sq])
                    K.op("act", lambda e: e.activation(sq.t[:, 0:w], sq.t[:, 0:w], AF.Exp, scale=-0.5), R=[sq], W=[sq])
                    K.op("dve", lambda e: e.scalar_tensor_tensor(o.t[:, 0:w], o.t[:, 0:w], gn.t[:, c:c + 1], sq.t[:, 0:w], op0=ALU.mult, op1=ALU.mult), R=[o, gn, sq], W=[o])
                    K.op("act", lambda e: e.activation(g.t[:, 0:w], g.t[:, 0:w], AF.Silu), R=[g], W=[g])
                    K.op("dve", lambda e: e.tensor_tensor(hT.t[:, c, c0:c0 + w], o.t[:, 0:w], g.t[:, 0:w], op=ALU.mult), R=[o, g], W=[hT])

    def epi_out(oc, m, si, c0, w, isctx, ps):
        K.op("dve", lambda e: e.scalar_tensor_tensor(x.t[:, oc, c0:c0 + w], ps.t[:, 0:w], mod.t[:, isctx, 16 + oc:17 + oc], x.t[:, oc, c0:c0 + w], op0=ALU.mult, op1=ALU.add), R=[ps, mod, x], W=[x])
    with K.nc.named_scope("w_out"):
        T.proj(hT, 8, io["w_out"], 0, 1024, epi_out, tag="wout")
    with K.nc.named_scope("norm2"):
        T.norm_mod(x, hT, mod, n2g, 3, 4, "2")
    _ffn_sid = K.nc.enter_named_scope("ffn", False)[0]
    FG = 2
    w1v = io["w1"].t.ap().rearrange("(c p) n -> p c n", p=128)
    w3v = io["w3"].t.ap().rearrange("(c p) n -> p c n", p=128)
    w2v = io["w2"].t.ap().rearrange("(c p) n -> p c n", p=128)
    wb1 = [K.sb("f1_%d" % i, [128, 8, 128 * FG], BF16) for i in range(2)]
    wb3 = [K.sb("f3_%d" % i, [128, 8, 128 * FG], BF16) for i in range(2)]
    wb2 = [K.sb("f2_%d" % i, [128, FG, 1024], BF16) for i in range(2)]
    aT = [K.sb("faT%d" % i, [128, FG, 512], BF16) for i in range(2)]
    sl = [K.sb("fsl%d" % i, [128, 512], F32) for i in range(2)]
    ai = 0
    for fg in range(22 // FG):
        b1, b3, b2 = wb1[fg % 2], wb3[fg % 2], wb2[fg % 2]
        K.dma("pool", b1.t[:, :, :], w1v[:, :, fg * 128 * FG:(fg + 1) * 128 * FG], R=[io["w1"]], W=[b1])
        K.dma("pool", b3.t[:, :, :], w3v[:, :, fg * 128 * FG:(fg + 1) * 128 * FG], R=[io["w3"]], W=[b3])
        K.dma("pool", b2.t[:, :, :], w2v[:, fg * FG:(fg + 1) * FG, :], R=[io["w2"]], W=[b2])
        for (c0, w, isctx) in SLABS:
            a = aT[ai % 2]
            ai += 1
            for fc in range(FG):
                ps1 = K.ps_get()
                for kc in range(8):
                    K.op("pe", lambda e: e.matmul(ps1.t[:, 0:w], b1.t[:, kc, fc * 128:(fc + 1) * 128], hT.t[:, kc, c0:c0 + w], start=(kc == 0), stop=(kc == 7)), R=[b1, hT], W=[ps1])
                ps3 = K.ps_get()
                for kc in range(8):
                    K.op("pe", lambda e: e.matmul(ps3.t[:, 0:w], b3.t[:, kc, fc * 128:(fc + 1) * 128], hT.t[:, kc, c0:c0 + w], start=(kc == 0), stop=(kc == 7)), R=[b3, hT], W=[ps3])
                s_ = sl[fc % 2]
                K.op("act", lambda e: e.activation(s_.t[:, 0:w], ps1.t[:, 0:w], AF.Silu), R=[ps1], W=[s_])
                K.op("dve", lambda e: e.tensor_tensor(a.t[:, fc, 0:w], s_.t[:, 0:w], ps3.t[:, 0:w], op=ALU.mult), R=[s_, ps3], W=[a])
            for oc in range(8):
                ps = K.ps_get()
                for fc in range(FG):
                    K.op("pe", lambda e: e.matmul(ps.t[:, 0:w], b2.t[:, fc, oc * 128:(oc + 1) * 128], a.t[:, fc, 0:w], start=(fc == 0), stop=(fc == FG - 1)), R=[b2, a], W=[ps])
                K.op("dve", lambda e: e.scalar_tensor_tensor(x.t[:, oc, c0:c0 + w], ps.t[:, 0:w], mod.t[:, isctx, 40 + oc:41 + oc], x.t[:, oc, c0:c0 + w], op0=ALU.mult, op1=ALU.add), R=[ps, mod, x], W=[x])
    K.nc.leave_named_scope("ffn", _ffn_sid, False)


def decl_ta_odd(K):
    io = {}
    io["w_in_o"] = K.dram_in("w_in_o", [1024, 4608], F32)
    io["cosr"] = K.dram_in("cosr", [128, NT], F32)
    io["sinr"] = K.dram_in("sinr", [128, NT], F32)
    io["pb"] = K.dram_out("pb", [128, 20, NT], BF16)
    io["pf"] = K.dram_out("pf", [128, 16, NT], F32)
    return io


def ta_odd(K, T, hT, io):
    pb, pf = io["pb"], io["pf"]
    put_b = stage_out(K, T, pb, None, BF16)
    put_f = stage_out(K, T, pf, None, F32)
    x1 = [K.sb("ox1_%d" % i, [128, 512], F32) for i in range(2)]
    x2 = [K.sb("ox2_%d" % i, [128, 512], F32) for i in range(2)]
    RSC = 128 ** -0.5

    def epi(oc, m, si, c0, w, isctx, ps):
        if oc < 8:
            sc = 1.0 if oc < 4 else RSC
            if oc % 2 == 0:
                K.op("act", lambda e: e.mul(x1[si % 2].t[:, 0:w], ps.t[:, 0:w], sc), R=[ps], W=[x1[si % 2]])
            else:
                K.op("act", lambda e: e.mul(x2[si % 2].t[:, 0:w], ps.t[:, 0:w], sc), R=[ps], W=[x2[si % 2]])
                T.rope(si, c0, w, x1[si % 2], x2[si % 2], 0, 128, io["cosr"], io["sinr"], put_b, pb.t.ap()[:, oc - 1, c0:c0 + w], pb.t.ap()[:, oc, c0:c0 + w])
        else:
            sec = (oc - 8) // 4
            j = (oc - 8) % 4
            if sec in (0, 2, 5):
                di = {0: 8, 2: 12, 5: 16}[sec] + j
                put_b("act", 128, w, lambda b: K.op("act", lambda e: e.copy(b.t[:, 0:w], ps.t[:, 0:w]), R=[ps], W=[b]), pb.t.ap()[:, di, c0:c0 + w])
            else:
                di = {1: 0, 3: 4, 4: 8, 6: 12}[sec] + j
                put_f("act", 128, w, lambda b: K.op("act", lambda e: e.copy(b.t[:, 0:w], ps.t[:, 0:w]), R=[ps], W=[b]), pf.t.ap()[:, di, c0:c0 + w])
    T.proj(hT, 8, io["w_in_o"], 0, 4608, epi, tag="wino")


def perm_odd(w_in):
    cols = []
    for sec in range(2):
        base = sec * 512
        for pair in range(2):
            hs = (2 * pair, 2 * pair + 1)
            cols.append(np.concatenate([base + h * 128 + 2 * np.arange(64) for h in hs]))
            cols.append(np.concatenate([base + h * 128 + 2 * np.arange(64) + 1 for h in hs]))
    cols.append(np.arange(1024, 4608))
    return np.ascontiguousarray(w_in[:, np.concatenate(cols)])


def core_rows(seq, r):
    return np.concatenate([seq[64 * r:64 * r + 64], seq[256 + 2048 * r:256 + 2048 * (r + 1)]], axis=0)


def tb_common_inputs(inp, li):
    return {"ffn_w1": inp["ffn_w1"][li], "ffn_w3": inp["ffn_w3"][li], "ffn_w2": inp["ffn_w2"][li]}


def tb_even_inputs(inp, li, uT_list, OT_list, yf, yb):
    j = li // 2
    maps = []
    for core in range(8):
        b, r = core // 4, core % 4
        m = tb_common_inputs(inp, li)
        m["uTi"] = uT_list[core]
        m["yfT"] = to_fm(core_rows(yf[b], r))
        m["ybT"] = to_fm(core_rows(yb[b], r))
        On = np.zeros((128, 4, NT), np.float32)
        Dn = np.zeros((128, 4, NT), np.float32)
        for h in range(8):
            unit = b * 8 + h
            OT = OT_list[unit // 2][unit % 2]
            cols = np.concatenate([8192 + 64 * r + np.arange(64), 2048 * r + np.arange(2048)])
            ro = (h % 2) * 64
            On[ro:ro + 64, h // 2, :] = OT[0:64][:, cols]
            Dn[ro:ro + 64, h // 2, :] = OT[64:128][:, cols]
        m["OTn"] = On
        m["denT"] = Dn
        m["dT"] = vec_fm(inp["s5_d"][j])
        m["w_glu"] = inp["s5_w_glu"][j]
        m["w_out"] = inp["w_out_even"][j]
        maps.append(m)
    return maps


def ta_odd_inputs(inp, j, r):
    c, s = core_rope(128, 2, r)
    return {"w_in_o": perm_odd(inp["w_in_odd"][j]), "cosr": c, "sinr": s}


LSEQ = 8448
NCK = 132


def decl_h_odd(K):
    io = {}
    io["GqT"] = K.dram_in("GqT", [4, 128, LSEQ], BF16)
    io["GkT"] = K.dram_in("GkT", [2, 128, LSEQ], BF16)
    io["Gf"] = K.dram_in("Gf", [2, 128, LSEQ], F32)
    io["Gv"] = K.dram_in("Gv", [4, 64, NCK, 128], BF16)
    io["lgam"] = K.dram_in("lgam", [128, 2], F32)
    io["lbl"] = K.dram_in("lbl", [128, 3], F32)
    io["lbsel"] = K.dram_in("lbsel", [128, 3], F32)
    io["cmask"] = K.dram_in("cmask", [64, 64], F32)
    io["cmask2"] = K.dram_in("cmask2", [128, 128], F32)
    io["identb"] = K.dram_in("identb", [128, 128], BF16)
    io["Go"] = K.dram_out("Go", [4, 64, NCK, 128], F32)
    return io


def h_odd(K, io):
    def ld(name, shape, dt=F32):
        b = K.sb("g_" + name, shape, dt)
        sl = tuple(slice(None) for _ in shape)
        K.dma("sp", b.t[sl], io[name].t.ap(), R=[io[name]], W=[b])
        return b
    lgam, lbl, lbsel = ld("lgam", [128, 2]), ld("lbl", [128, 3]), ld("lbsel", [128, 3])
    cmask, identb = ld("cmask", [64, 64]), ld("identb", [128, 128], BF16)
    e3, lb, oml, tot = K.sb("g_e3", [128, 3], F32), K.sb("g_lb", [128, 1], F32), K.sb("g_oml", [128, 1], F32), K.sb("g_tot", [128, 1], F32)
    K.op("act", lambda e: e.activation(e3.t[:, :], lbl.t[:, :], AF.Exp), R=[lbl], W=[e3])
    K.op("dve", lambda e: e.tensor_reduce(tot.t[:, :], e3.t[:, :], axis=AX.X, op=ALU.add), R=[e3], W=[tot])
    K.op("dve", lambda e: e.tensor_tensor(e3.t[:, :], e3.t[:, :], lbsel.t[:, :], op=ALU.mult), R=[e3, lbsel], W=[e3])
    K.op("dve", lambda e: e.tensor_reduce(lb.t[:, :], e3.t[:, :], axis=AX.X, op=ALU.add), R=[e3], W=[lb])
    K.op("dve", lambda e: e.reciprocal(tot.t[:, :], tot.t[:, :]), R=[tot], W=[tot])
    K.op("dve", lambda e: e.tensor_tensor(lb.t[:, :], lb.t[:, :], tot.t[:, :], op=ALU.mult), R=[lb, tot], W=[lb])
    K.op("dve", lambda e: e.tensor_scalar(oml.t[:, :], lb.t[:, :], -1.0, 1.0, op0=ALU.mult, op1=ALU.add), R=[lb], W=[oml])

    HW = LSEQ // 2
    A = K.sb("g_A", [128, HW], F32)
    Bc = K.sb("g_B", [128, HW], F32)
    cmask2 = ld("cmask2", [128, 128])
    for pair in range(2):
        hg = pair == 1
        CS = 64 if hg else 128
        NCH_ = LSEQ // CS
        NH = NCH_ // 2
        cm = cmask if hg else cmask2
        with K.scope():
            rmask = K.sb("g_rm%d" % pair, [128, HW], BF16)
            K.op("pool", lambda e: e.memset(rmask.t[:, :], 1.0), W=[rmask])
            K.op("pool", lambda e: e.memset(rmask.t[:, :].rearrange("p (c t) -> p c t", t=CS)[:, :, 0], 0.0), W=[rmask])
            Av = A.t[:, :].rearrange("p (c t) -> p c t", t=CS)
            U = []
            for i in range(2):
                t_ = "%d_%d" % (pair, i)
                U.append(dict(
                    qt=K.sb("g_q" + t_, [128, LSEQ], BF16), kt=K.sb("g_k" + t_, [128, LSEQ], BF16),
                    v=K.sb("g_v" + t_, [CS, NCH_, 128], BF16), dec=K.sb("g_dec" + t_, [128, NCH_], F32),
                    S=K.sb("g_S" + t_, [128, 128], F32), Sb=K.sb("g_Sb" + t_, [128, 128], BF16),
                    tmpS=K.sb("g_tS" + t_, [128, 128], F32),
                    ktok=[K.sb("g_kt%s_%d" % (t_, j), [CS, 128], BF16) for j in range(2)],
                    attm=[K.sb("g_at%s_%d" % (t_, j), [CS, CS], BF16) for j in range(2)],
                    ostg=[K.sb("g_os%s_%d" % (t_, j), [CS, 4, 128], F32) for j in range(2)]))
            for i in range(2):
                u = pair * 2 + i
                d = U[i]
                qt, kt, v = d["qt"], d["kt"], d["v"]
                K.dma("sp", qt.t[:, :], io["GqT"].t.ap()[u], R=[io["GqT"]], W=[qt])
                if hg:
                    K.dma("act", v.t[:, :, :], io["Gv"].t.ap()[u], R=[io["Gv"]], W=[v])
                else:
                    gv2 = io["Gv"].t.ap()[u].rearrange("t (c two) e -> t c two e", two=2)
                    K.dma("act", v.t[0:64, :, :], gv2[:, :, 0, :], R=[io["Gv"]], W=[v])
                    K.dma("act", v.t[64:128, :, :], gv2[:, :, 1, :], R=[io["Gv"]], W=[v])
                    K.dma("sp", kt.t[:, :], io["GkT"].t.ap()[u], R=[io["GkT"]], W=[kt])
                for hf in range(2):
                    sl = slice(hf * HW, (hf + 1) * HW)
                    if not hg:
                        K.op("pool", lambda e: e.memset(A.t[:, :], 1.0), W=[A])
                        K.op("dve", lambda e: e.tensor_scalar(A.t[:, :], A.t[:, :], lgam.t[:, u:u + 1], None, op0=ALU.mult), R=[A, lgam], W=[A])
                    else:
                        K.dma("sp", A.t[:, :], io["Gf"].t.ap()[u - 2, :, sl], R=[io["Gf"]], W=[A])
                        K.op("act", lambda e: e.activation(A.t[:, :], A.t[:, :], AF.Sigmoid), R=[A], W=[A])
                        K.op("dve", lambda e: e.tensor_scalar(A.t[:, :], A.t[:, :], oml.t[:, 0:1], lb.t[:, 0:1], op0=ALU.mult, op1=ALU.add), R=[A, oml, lb], W=[A])
                        K.op("dve", lambda e: e.tensor_scalar(kt.t[:, sl], A.t[:, :], -1.0, 1.0, op0=ALU.mult, op1=ALU.add), R=[A], W=[kt])
                        K.op("act", lambda e: e.activation(A.t[:, :], A.t[:, :], AF.Ln), R=[A], W=[A])
                    K.op("dve", lambda e: e.tensor_tensor_scan(Bc.t[:, :], rmask.t[:, :], A.t[:, :], 0.0, op0=ALU.mult, op1=ALU.add), R=[rmask, A], W=[Bc])
                    K.op("act", lambda e: e.activation(A.t[:, :], Bc.t[:, :], AF.Exp), R=[Bc], W=[A])
                    K.op("dve", lambda e: e.tensor_tensor(qt.t[:, sl], qt.t[:, sl], A.t[:, :], op=ALU.mult), R=[qt, A], W=[qt])
                    K.op("pool", lambda e: e.tensor_copy(d["dec"].t[:, hf * NH:(hf + 1) * NH], Av[:, :, CS - 1]), R=[A], W=[d["dec"]])
                    K.op("act", lambda e: e.activation(Bc.t[:, :], Bc.t[:, :], AF.Exp, scale=-1.0), R=[Bc], W=[Bc])
                    K.op("pool", lambda e: e.tensor_tensor(kt.t[:, sl], kt.t[:, sl], Bc.t[:, :], op=ALU.mult), R=[kt, Bc], W=[kt])
                K.op("dve", lambda e: e.memset(d["S"].t[:, :], 0.0), W=[d["S"]])
                K.op("dve", lambda e: e.memset(d["Sb"].t[:, :], 0.0), W=[d["Sb"]])
            for c in range(NCH_):
                cs = slice(c * CS, (c + 1) * CS)
                for i in range(2):
                    u = pair * 2 + i
                    d = U[i]
                    qt, kt, v, S, Sb, tmpS = d["qt"], d["kt"], d["v"], d["S"], d["Sb"], d["tmpS"]
                    pst = K.ps_get()
                    K.op("pe", lambda e: e.matmul(pst.t[0:CS, 0:128], kt.t[:, cs], identb.t[:, :], start=True, stop=True), R=[kt, identb], W=[pst])
                    pst2 = K.ps_get()
                    K.op("pe", lambda e: e.matmul(pst2.t[0:CS, 0:CS], kt.t[:, cs], qt.t[:, cs], start=True, stop=True), R=[kt, qt], W=[pst2])
                    kk_ = d["ktok"][c % 2]
                    am = d["attm"][c % 2]
                    K.op("act", lambda e: e.copy(kk_.t[:, :], pst.t[0:CS, 0:128]), R=[pst], W=[kk_])
                    K.op("dve", lambda e: e.tensor_tensor(am.t[:, :], pst2.t[0:CS, 0:CS], cm.t[:, :], op=ALU.mult), R=[pst2, cm], W=[am])
                    pso = K.ps_get()
                    K.op("pe", lambda e: e.matmul(pso.t[0:CS, 0:128], am.t[:, :], v.t[:, c, :], start=True, stop=False), R=[am, v], W=[pso])
                    K.op("pe", lambda e: e.matmul(pso.t[0:CS, 0:128], qt.t[:, cs], Sb.t[:, :], start=False, stop=True), R=[qt, Sb], W=[pso])
                    psk = K.ps_get()
                    K.op("pe", lambda e: e.matmul(psk.t[:, 0:128], kk_.t[:, :], v.t[:, c, :], start=True, stop=True), R=[kk_, v], W=[psk])
                    K.op("dve", lambda e: e.tensor_tensor(tmpS.t[:, :], psk.t[:, 0:128], S.t[:, :], op=ALU.add), R=[psk, S], W=[tmpS])
                    K.op("dve", lambda e: e.tensor_scalar(S.t[:, :], tmpS.t[:, :], d["dec"].t[:, c:c + 1], None, op0=ALU.mult), R=[tmpS, d["dec"]], W=[S])
                    K.op("act", lambda e: e.copy(Sb.t[:, :], S.t[:, :]), R=[S], W=[Sb])
                    og = d["ostg"][(c // 4) % 2]
                    K.op("act", lambda e: e.copy(og.t[:, c % 4, :], pso.t[0:CS, 0:128]), R=[pso], W=[og])
                    if c % 4 == 3 or c == NCH_ - 1:
                        c0 = (c // 4) * 4
                        n = c - c0 + 1
                        if hg:
                            K.dma("sp", io["Go"].t.ap()[u, :, c0:c0 + n, :], og.t[:, 0:n, :], R=[og], W=[io["Go"]])
                        else:
                            go2 = io["Go"].t.ap()[u].rearrange("t (c two) e -> t c two e", two=2)
                            K.dma("sp", go2[:, c0:c0 + n, 0, :], og.t[0:64, 0:n, :], R=[og], W=[io["Go"]])
                            K.dma("sp", go2[:, c0:c0 + n, 1, :], og.t[64:128, 0:n, :], R=[og], W=[io["Go"]])


def build_h_odd():
    nc = bass.Bass("TRN2", target_bir_lowering=False)
    K = Emit(nc)
    io = decl_h_odd(K)
    h_odd(K, io)
    K.finish()
    return nc


def flipseq(a, rev):
    if not rev:
        return a
    return np.concatenate([a[:256][::-1], a[256:][::-1]], axis=0)


def gather_seq(core_arrays, b):
    parts = [core_arrays[4 * b + r] for r in range(4)]
    ctx = np.concatenate([p[:, :64] for p in parts], axis=1)
    lat = np.concatenate([p[:, 64:] for p in parts], axis=1)
    return np.concatenate([ctx, lat], axis=1).T


def h_odd_inputs(inp, j, pb_list, pf_list):
    import ml_dtypes
    bf = ml_dtypes.bfloat16
    cmask = (np.arange(64)[:, None] <= np.arange(64)[None, :]).astype(np.float32)
    cmask2 = (np.arange(128)[:, None] <= np.arange(128)[None, :]).astype(np.float32)
    identb = np.eye(128, dtype=np.float32).astype(bf)
    sel = np.zeros((128, 3), np.float32)
    sel[:, :j + 1] = 1.0
    maps = []
    for core in range(8):
        b, h = core // 4, core % 4
        ch, ro = 2 * (h // 2), (h % 2) * 64
        rq = gather_seq([np.concatenate([p[ro:ro + 64, ch], p[ro:ro + 64, ch + 1]], 0) for p in pb_list], b)
        rk = gather_seq([np.concatenate([p[ro:ro + 64, 4 + ch], p[ro:ro + 64, 5 + ch]], 0) for p in pb_list], b)
        rv = gather_seq([p[:, 8 + h] for p in pb_list], b)
        hq = gather_seq([p[:, 12 + h] for p in pb_list], b)
        hi = gather_seq([p[:, 16 + h] for p in pb_list], b)
        hff = gather_seq([p[:, 4 + h] for p in pf_list], b)
        hfb = gather_seq([p[:, 8 + h] for p in pf_list], b)
        GqT = np.stack([flipseq(rq, 0).T, flipseq(rq, 1).T, flipseq(hq, 0).T, flipseq(hq, 1).T])
        GkT = np.stack([flipseq(rk, 0).T, flipseq(rk, 1).T])
        Gf = np.stack([flipseq(hff, 0).T, flipseq(hfb, 1).T])
        tok = lambda a: a.reshape(NCK, 64, 128).transpose(1, 0, 2)
        Gv = np.stack([tok(flipseq(rv, 0)), tok(flipseq(rv, 1)), tok(flipseq(hi, 0)), tok(flipseq(hi, 1))])
        lg = np.array([np.log1p(-np.exp2(np.float32(-(5.0 + off) - h))) for off in (0.0, 0.5)], np.float32)
        maps.append({"GqT": np.ascontiguousarray(GqT), "GkT": np.ascontiguousarray(GkT), "Gf": np.ascontiguousarray(Gf),
                     "Gv": np.ascontiguousarray(Gv), "lgam": np.ascontiguousarray(np.broadcast_to(lg[None, :], (128, 2))),
                     "lbl": np.ascontiguousarray(inp["hg_lb_logits"][:, h * 128:(h + 1) * 128].T),
                     "lbsel": sel, "cmask": cmask, "cmask2": cmask2, "identb": identb})
    return maps


def tb_odd_inputs(inp, li, Go_list, pf_list):
    j = li // 2
    nat = {}
    for core in range(8):
        b, h = core // 4, core % 4
        Go = Go_list[core]
        for u in range(4):
            seq = Go[u].transpose(1, 0, 2).reshape(LSEQ, 128)
            nat[(b, h, u)] = flipseq(seq, u % 2)
    maps = []
    for core in range(8):
        b, r = core // 4, core % 4
        m = tb_common_inputs(inp, li)
        of = np.concatenate([core_rows(nat[(b, h, 0)], r) for h in range(4)] + [core_rows(nat[(b, h, 2)], r) for h in range(4)], axis=1)
        ob = np.concatenate([core_rows(nat[(b, h, 1)], r) for h in range(4)] + [core_rows(nat[(b, h, 3)], r) for h in range(4)], axis=1)
        m["of"] = to_fm(of)
        m["ob"] = to_fm(ob)
        pf = pf_list[core]
        m["gT"] = np.ascontiguousarray(np.concatenate([pf[:, 0:4], pf[:, 12:16]], axis=1))
        m["gnT"] = vec_fm(np.concatenate([inp["ret_gn"][j], inp["hg_gn"][j]]))
        m["w_out"] = inp["w_out_odd"][j]
        maps.append(m)
    return maps


_PROGS = {}


def _prog(key, fn):
    if key not in _PROGS:
        _PROGS[key] = fn()
    return _PROGS[key]


def _run(nc, maps):
    import sys, time
    t0 = time.time()
    res = run_bass_kernel_spmd(nc, maps, core_ids=list(range(8))).results
    out = [{k: np.asarray(v) for k, v in r.items()} for r in res]
    print("[kernel] launch done in %.1fs" % (time.time() - t0), file=sys.stderr, flush=True)
    return out


def kernel(**inp):
    inp = {k: np.asarray(v) for k, v in inp.items()}
    maps = []
    for core in range(8):
        b, r = core // 4, core % 4
        X = np.concatenate([inp["ctx"][b, 64 * r:64 * r + 64], inp["x"][b, 2048 * r:2048 * (r + 1)]], axis=0)
        m = {"xT_in": to_fm(X)}
        m.update(mod_inputs(inp, 0, b, "_b"))
        m.update(ta_even_inputs(inp, 0, r))
        maps.append(m)
    ta = _run(_prog("A", lambda: build_tok(None, "even")), maps)
    xT = None
    for li in range(4):
        j = li // 2
        last = li == 3
        if li % 2 == 0:
            am = attn_inputs(ta)
            sm_ = s5_inputs(inp, j, [r["uT"] for r in ta])
            for core in range(8):
                am[core].update(sm_[core])
            ho = _run(_prog("HE", build_h_even), am)
            yf, yb = s5_outputs([o["s5_Y8"] for o in ho])
            maps = tb_even_inputs(inp, li, [r["uT"] for r in ta], [o["OT"] for o in ho], yf, yb)
        else:
            hm = h_odd_inputs(inp, j, [r["pb"] for r in ta], [r["pf"] for r in ta])
            ho = _run(_prog("HO", build_h_odd), hm)
            maps = tb_odd_inputs(inp, li, [o["Go"] for o in ho], [r["pf"] for r in ta])
        for core in range(8):
            b, r = core // 4, core % 4
            if li == 0:
                X = np.concatenate([inp["ctx"][b, 64 * r:64 * r + 64], inp["x"][b, 2048 * r:2048 * (r + 1)]], axis=0)
                maps[core]["xT_in"] = to_fm(X)
            else:
                maps[core]["xT_in"] = xT[core]
            maps[core].update(mod_inputs(inp, li, b, "_a"))
            if not last:
                maps[core].update(mod_inputs(inp, li + 1, b, "_b"))
                if li % 2 == 0:
                    maps[core].update(ta_odd_inputs(inp, (li + 1) // 2, r))
                else:
                    maps[core].update(ta_even_inputs(inp, (li + 1) // 2, r))
            else:
                maps[core]["fnT"] = vec_fm(inp["final_norm"])
        if last:
            ta = _run(_prog("D", lambda: build_tok("odd", None, final=True)), maps)
        elif li % 2 == 0:
            ta = _run(_prog("B", lambda: build_tok("even", "odd")), maps)
        else:
            ta = _run(_prog("C", lambda: build_tok("odd", "even")), maps)
        xT = [r["xT_out"] for r in ta]
    out = np.zeros((2, 8192, 1024), np.float32)
    for core in range(8):
        b, r = core // 4, core % 4
        out[b, 2048 * r:2048 * (r + 1)] = from_fm(ta[core]["yT"])[64:]
    return out
```

```python
import numpy as np
from contextlib import ExitStack
import concourse.bass as bass
import concourse.mybir as mybir
from concourse.bass_utils import run_bass_kernel_spmd

F32 = mybir.dt.float32
BF16 = mybir.dt.bfloat16
I32 = mybir.dt.int32
AF = mybir.ActivationFunctionType
ALU = mybir.AluOpType
AX = mybir.AxisListType


class Buf:
    __slots__ = ("t", "lw", "rd", "name")

    def __init__(self, t, name=""):
        self.t = t
        self.lw = None
        self.rd = []
        self.name = name


class Emit:
    NDS = 24

    def __init__(self, nc):
        self.nc = nc
        self.es = ExitStack()
        self.eng = {"pe": nc.tensor, "act": nc.scalar, "dve": nc.vector, "pool": nc.gpsimd, "sp": nc.sync}
        self.sem = {e: self.es.enter_context(nc.semaphore("s_" + e)) for e in ("pe", "act", "dve", "pool")}
        self.cnt = {e: 0 for e in self.sem}
        self.dsem = [self.es.enter_context(nc.semaphore("d%d" % i)) for i in range(self.NDS)]
        self.dcnt = [0] * self.NDS
        self.dnext = 0
        self.seen = {e: {} for e in self.eng}
        self.outs = []
        self.psb = [self.ps("psb%d" % i, [128, 512], F32) for i in range(6)]
        self.psx = [self.ps("psx%d" % i, [128, 512], F32) for i in range(2)]
        self.psi = 0
        self.n_inst = 0

    def dram_in(self, name, shape, dt):
        return Buf(self.nc.dram_tensor(name, list(shape), dt, kind="ExternalInput"), name)

    def dram_out(self, name, shape, dt):
        return Buf(self.nc.dram_tensor(name, list(shape), dt, kind="ExternalOutput"), name)

    def dram_tmp(self, name, shape, dt):
        return Buf(self.nc.dram_tensor(name, list(shape), dt, kind="Internal"), name)

    def sb(self, name, shape, dt):
        self.n_sb = getattr(self, "n_sb", 0) + 1
        return Buf(self.es.enter_context(self.nc.sbuf_tensor("sb%d_%s" % (self.n_sb, name), list(shape), dt)), name)

    def ps(self, name, shape, dt):
        return Buf(self.es.enter_context(self.nc.psum_tensor(name, list(shape), dt)), name)

    def ps_get(self):
        b = self.psb[self.psi % len(self.psb)]
        self.psi += 1
        return b

    def _wait(self, e, key, val):
        if self.seen[e].get(key, 0) >= val:
            return
        self.seen[e][key] = val
        s = self.sem[key] if isinstance(key, str) else self.dsem[key]
        self.eng[e].wait_ge(s, val)

    def _deps(self, e, R, W):
        deps = {}
        for r in R:
            if r.lw is not None:
                k, v = r.lw
                deps[k] = max(deps.get(k, 0), v)
        for w in W:
            if w.lw is not None:
                k, v = w.lw
                deps[k] = max(deps.get(k, 0), v)
            for k, v in w.rd:
                deps[k] = max(deps.get(k, 0), v)
        for k, v in deps.items():
            if k == "pe" and e == "pe":
                continue
            self._wait(e, k, v)

    def _record(self, ev, R, W):
        for w in W:
            w.lw = ev
            w.rd = []
        for r in R:
            if r in W:
                continue
            r.rd.append(ev)
            if len(r.rd) > 12:
                m = {}
                for k, v in r.rd:
                    m[k] = max(m.get(k, 0), v)
                r.rd = list(m.items())

    def op(self, e, fn, R=(), W=()):
        self._deps(e, R, W)
        inst = fn(self.eng[e])
        self.cnt[e] += 1
        inst.then_inc(self.sem[e], 1)
        self._record((e, self.cnt[e]), R, W)
        self.n_inst += 1
        return inst

    def dma(self, q, out, in_, R=(), W=(), **kw):
        i = self.dnext
        self.dnext = (self.dnext + 1) % self.NDS
        if self.dcnt[i] > 0:
            self._wait(q, i, 16 * self.dcnt[i])
        self._deps(q, R, W)
        inst = self.eng[q].dma_start(out=out, in_=in_, **kw)
        self.dcnt[i] += 1
        inst.then_inc(self.dsem[i], 16)
        ev = (i, 16 * self.dcnt[i])
        self._record(ev, R, W)
        self.n_inst += 1
        return ev

    def barrier(self):
        for e in ("pe", "act", "dve", "pool", "sp"):
            for i in range(self.NDS):
                if self.dcnt[i] > 0:
                    self._wait(e, i, 16 * self.dcnt[i])
            for e2 in ("pe", "act", "dve", "pool"):
                if e2 != e and self.cnt[e2] > 0:
                    self._wait(e, e2, self.cnt[e2])

    def scope(self):
        K = self

        class _S:
            def __enter__(s2):
                s2.old = K.es
                K.es = ExitStack()
                return s2

            def __exit__(s2, *a):
                K.barrier()
                K.es.close()
                K.es = s2.old
                return False
        return _S()

    def finish(self):
        for i in range(self.NDS):
            if self.dcnt[i] > 0:
                self._wait("sp", i, 16 * self.dcnt[i])
        for e in ("pe", "act", "dve", "pool"):
            if self.cnt[e] > 0:
                self._wait("sp", e, self.cnt[e])
        self.es.close()


D = 1024
NCH = 8
NT = 2112
SLABS = [(0, 64, 1)] + [(64 + 512 * i, 512, 0) for i in range(4)]
DFF = 2816
EPS = 1e-6


def pslice(ap, lo, hi):
    return ap[lo:hi]


class Tok:
    def __init__(self, K):
        self.K = K
        nc = K.nc
        self.ones = K.sb("ones_f", [128, 128], F32)
        K.op("dve", lambda e: e.memset(self.ones.t[:, :], 1.0), W=[self.ones])
        self.onesb = K.sb("ones_b", [128, 128], BF16)
        K.op("dve", lambda e: e.memset(self.onesb.t[:, :], 1.0), W=[self.onesb])
        self.eps = K.sb("eps_c", [128, 1], F32)
        K.op("dve", lambda e: e.memset(self.eps.t[:, :], EPS), W=[self.eps])
        self.sq = [K.sb("sq%d" % i, [128, 512], BF16) for i in range(3)]
        self.sqi = 0
        self.rstd = [K.sb("rstd%d" % i, [128, 512], F32) for i in range(2)]
        self.rsi = 0
        self.tmp = [K.sb("ttmp%d" % i, [128, 512], F32) for i in range(3)]
        self.tmi = 0

    def gettmp(self):
        self.tmi += 1
        return self.tmp[self.tmi % len(self.tmp)]

    def load(self, name, dram, shape, dt=F32, q="sp", src=None):
        b = self.K.sb(name, shape, dt)
        sl = tuple(slice(None) for _ in shape)
        self.K.dma(q, b.t[sl], src if src is not None else dram.t.ap(), R=[dram], W=[b])
        return b

    def rms_rstd(self, src, nchunks, c0, w, nfeat, srcchunk0=0):
        K = self.K
        ps = K.ps_get()
        for c in range(nchunks):
            sq = self.sq[self.sqi % 3]
            self.sqi += 1
            K.op("act", lambda e: e.activation(sq.t[:, 0:w], src.t[:, srcchunk0 + c, c0:c0 + w], AF.Square), R=[src], W=[sq])
            K.op("pe", lambda e: e.matmul(ps.t[:, 0:w], self.onesb.t[:, :], sq.t[:, 0:w], start=(c == 0), stop=(c == nchunks - 1)), R=[sq, self.onesb], W=[ps])
        r = self.rstd[self.rsi % 2]
        self.rsi += 1
        K.op("act", lambda e: e.activation(r.t[:, 0:w], ps.t[:, 0:w], AF.Ln, bias=self.eps.t[:, 0:1], scale=1.0 / nfeat), R=[ps, self.eps], W=[r])
        K.op("act", lambda e: e.activation(r.t[:, 0:w], r.t[:, 0:w], AF.Exp, scale=-0.5), R=[r], W=[r])
        return r

    def modvec(self, cT, w_mod, b_modT, tag="", groups=range(12)):
        K = self.K
        sc = K.sb("silc" + tag, [128, 8, 2], F32)
        K.op("act", lambda e: e.activation(sc.t[:, :, :], cT.t[:, :, :], AF.Silu), R=[cT], W=[sc])
        mod = K.sb("modv" + tag, [128, 2, 48], F32)
        wv = w_mod.t.ap().rearrange("(c p) n -> p c n", p=128)
        if not hasattr(self, "wmodb"):
            self.wmodb = [K.sb("wmod%d" % i, [128, 8, 512], F32) for i in range(2)]
        wbufs = self.wmodb
        K.op("dve", lambda e: e.memset(mod.t[:, :, :], 0.0), W=[mod])
        for gi, g in enumerate(groups):
            wb = wbufs[gi % 2]
            K.dma("sp" if gi % 2 == 0 else "act", wb.t[:, :, :], wv[:, :, g * 512:(g + 1) * 512], R=[w_mod], W=[wb])
            ps = K.ps_get()
            for oc in range(4):
                for kc in range(8):
                    K.op("pe", lambda e: e.matmul(ps.t[:, oc * 2:oc * 2 + 2], wb.t[:, kc, oc * 128:(oc + 1) * 128], sc.t[:, kc, :], start=(kc == 0), stop=(kc == 7)), R=[wb, sc], W=[ps])
            for oc in range(4):
                o = g * 4 + oc
                K.op("dve", lambda e: e.tensor_scalar(mod.t[:, :, o], ps.t[:, oc * 2:oc * 2 + 2], b_modT.t[:, o:o + 1], None, op0=ALU.add), R=[ps, b_modT], W=[mod])
        return mod

    def norm_mod(self, x, dst, mod, gT, jshift, jscale, tag):
        K = self.K
        sc = K.sb("nsc" + tag, [128, 2, 8], F32)
        for w in range(2):
            K.op("dve", lambda e: e.scalar_tensor_tensor(sc.t[:, w, :], mod.t[:, w, jscale * 8:jscale * 8 + 8], 1.0, gT.t[:, :], op0=ALU.add, op1=ALU.mult), R=[mod, gT], W=[sc])
        for (c0, w, isctx) in SLABS:
            r = self.rms_rstd(x, 8, c0, w, D)
            for c in range(8):
                t = self.gettmp()
                K.op("dve", lambda e: e.tensor_tensor(t.t[:, 0:w], x.t[:, c, c0:c0 + w], r.t[:, 0:w], op=ALU.mult), R=[x, r], W=[t])
                K.op("act", lambda e: e.activation(dst.t[:, c, c0:c0 + w], t.t[:, 0:w], AF.Identity, bias=mod.t[:, isctx, jshift * 8 + c:jshift * 8 + c + 1], scale=sc.t[:, isctx, c:c + 1]), R=[t, mod, sc], W=[dst])

    def rope(self, si, c0, w, x1, x2, col, P, cosd, sind, put, dst1, dst2):
        K = self.K
        if not hasattr(self, "rc"):
            self.rc = [K.sb("ropec%d" % i, [128, 2, 512], F32) for i in range(2)]
            self.rta = K.sb("ropea", [128, 512], F32)
            self.rtb = K.sb("ropeb", [128, 512], F32)
            self.rci = 0
        c = self.rc[self.rci % 2]
        self.rci += 1
        ta, tb = self.rta, self.rtb
        K.dma("act", c.t[0:P, 0, 0:w], cosd.t.ap()[0:P, c0:c0 + w], R=[cosd], W=[c])
        K.dma("act", c.t[0:P, 1, 0:w], sind.t.ap()[0:P, c0:c0 + w], R=[sind], W=[c])
        K.op("dve", lambda e: e.tensor_tensor(ta.t[0:P, 0:w], x1.t[0:P, col:col + w], c.t[0:P, 0, 0:w], op=ALU.mult), R=[c, x1], W=[ta])
        K.op("dve", lambda e: e.tensor_tensor(tb.t[0:P, 0:w], x2.t[0:P, col:col + w], c.t[0:P, 1, 0:w], op=ALU.mult), R=[c, x2], W=[tb])
        put("dve", P, w, lambda b: K.op("dve", lambda e: e.tensor_tensor(b.t[0:P, 0:w], ta.t[0:P, 0:w], tb.t[0:P, 0:w], op=ALU.subtract), R=[ta, tb], W=[b]), dst1)
        K.op("dve", lambda e: e.tensor_tensor(ta.t[0:P, 0:w], x1.t[0:P, col:col + w], c.t[0:P, 1, 0:w], op=ALU.mult), R=[c, x1], W=[ta])
        K.op("dve", lambda e: e.tensor_tensor(tb.t[0:P, 0:w], x2.t[0:P, col:col + w], c.t[0:P, 0, 0:w], op=ALU.mult), R=[c, x2], W=[tb])
        put("dve", P, w, lambda b: K.op("dve", lambda e: e.tensor_tensor(b.t[0:P, 0:w], ta.t[0:P, 0:w], tb.t[0:P, 0:w], op=ALU.add), R=[ta, tb], W=[b]), dst2)

    def proj(self, src, nk, w_dram, col0, ncols, epi, group=512, tag="w", krows=128):
        K = self.K
        wv = w_dram.t.ap().rearrange("(c p) n -> p c n", p=krows)
        ngroups = (ncols + group - 1) // group
        if not hasattr(self, "wb_" + tag):
            setattr(self, "wb_" + tag, [K.sb("wb_%s%d" % (tag, i), [128, nk, group], BF16) for i in range(2)])
            setattr(self, "wbi_" + tag, 0)
        bufs = getattr(self, "wb_" + tag)
        oc = 0
        for g in range(ngroups):
            gi = getattr(self, "wbi_" + tag)
            setattr(self, "wbi_" + tag, gi + 1)
            wb = bufs[gi % 2]
            gc0 = col0 + g * group
            gw = min(group, ncols - g * group)
            K.dma("pool", wb.t[0:krows, :, 0:gw], wv[:, :, gc0:gc0 + gw], R=[w_dram], W=[wb])
            nm = (gw + 127) // 128
            for si, (c0, w, isctx) in enumerate(SLABS):
                for mi in range(nm):
                    m = min(128, gw - mi * 128)
                    ps = K.ps_get()
                    for kc in range(nk):
                        K.op("pe", lambda e: e.matmul(ps.t[0:m, 0:w], wb.t[0:krows, kc, mi * 128:mi * 128 + m], src.t[0:krows, kc, c0:c0 + w], start=(kc == 0), stop=(kc == nk - 1)), R=[wb, src], W=[ps])
                    epi(oc + mi, m, si, c0, w, isctx, ps)
            oc += nm


def stage_out(K, T, dram, dst_ap_fn, dt):
    bufs = [K.sb("stg_%s%d" % (dram.name, i), [128, 512], dt) for i in range(3)]
    st = {"i": 0}

    def put(eng, m, w, ps_or_fn, dst_ap):
        b = bufs[st["i"] % 3]
        st["i"] += 1
        ps_or_fn(b)
        K.dma("sp", dst_ap, b.t[0:m, 0:w], R=[b], W=[dram])
    return put


def ta_even(K, T, hT, io):
    uT, qT, kT, vT, krT = io["uT"], io["qT"], io["kT"], io["vT"], io["krT"]
    w_in, w_uq, w_ukv = io["w_in_e"], io["w_uq"], io["w_ukv"]
    qn = T.load("qn", io["qnT"], [128, 2])
    kvn = T.load("kvn", io["kvnT"], [128, 1])
    cq = K.sb("cq", [128, 3, NT], F32)
    kr = K.sb("kr12", [16, 2, NT], F32)
    put_u = stage_out(K, T, uT, None, F32)
    put_b = stage_out(K, T, qT, None, BF16)

    def epi_in(oc, m, si, c0, w, isctx, ps):
        if oc < 4:
            put_u("act", 128, w, lambda b: K.op("act", lambda e: e.copy(b.t[:, 0:w], ps.t[:, 0:w]), R=[ps], W=[b]), uT.t.ap()[:, oc, c0:c0 + w])
        else:
            K.op("dve", lambda e: e.tensor_copy(cq.t[:, oc - 4, c0:c0 + w], ps.t[:, 0:w]), R=[ps], W=[cq])
    T.proj(hT, 8, w_in, 0, 896, epi_in, tag="win")

    def epi_kr(j):
        def f(oc, m, si, c0, w, isctx, ps):
            K.op("dve", lambda e: e.tensor_copy(kr.t[:, j, c0:c0 + w], ps.t[0:16, 0:w]), R=[ps], W=[kr])
        return f
    T.proj(hT, 8, w_in, 896, 16, epi_kr(0), tag="win")
    T.proj(hT, 8, w_in, 912, 16, epi_kr(1), tag="win")

    cqn = K.sb("cqn", [128, 3, NT], BF16)
    for (c0, w, isctx) in SLABS:
        r = T.rms_rstd(cq, 2, c0, w, 256)
        for c in range(2):
            K.op("dve", lambda e: e.scalar_tensor_tensor(cqn.t[:, c, c0:c0 + w], cq.t[:, c, c0:c0 + w], qn.t[:, c:c + 1], r.t[:, 0:w], op0=ALU.mult, op1=ALU.mult), R=[cq, qn, r], W=[cqn])
        r = T.rms_rstd(cq, 1, c0, w, 128, srcchunk0=2)
        K.op("dve", lambda e: e.scalar_tensor_tensor(cqn.t[:, 2, c0:c0 + w], cq.t[:, 2, c0:c0 + w], kvn.t[:, 0:1], r.t[:, 0:w], op0=ALU.mult, op1=ALU.mult), R=[cq, kvn, r], W=[cqn])

    q1 = [K.sb("q1t%d" % i, [128, 512], F32) for i in range(2)]
    q2 = [K.sb("q2t%d" % i, [128, 512], F32) for i in range(2)]

    def epi_q(oc, m, si, c0, w, isctx, ps):
        if oc < 4:
            put_b("act", 128, w, lambda b: K.op("act", lambda e: e.copy(b.t[:, 0:w], ps.t[:, 0:w]), R=[ps], W=[b]), qT.t.ap()[:, oc, c0:c0 + w])
        elif oc == 4:
            K.op("act", lambda e: e.copy(q1[si % 2].t[:, 0:w], ps.t[:, 0:w]), R=[ps], W=[q1[si % 2]])
        else:
            K.op("act", lambda e: e.copy(q2[si % 2].t[:, 0:w], ps.t[:, 0:w]), R=[ps], W=[q2[si % 2]])
            T.rope(si, c0, w, q1[si % 2], q2[si % 2], 0, 128, io["cosq"], io["sinq"], put_b, qT.t.ap()[:, 4, c0:c0 + w], qT.t.ap()[:, 5, c0:c0 + w])
    T.proj(cqn, 2, w_uq, 0, 768, epi_q, tag="wuq")

    def epi_kv(oc, m, si, c0, w, isctx, ps):
        dst = kT if oc < 4 else vT
        put_b("act", 128, w, lambda b: K.op("act", lambda e: e.copy(b.t[:, 0:w], ps.t[:, 0:w]), R=[ps], W=[b]), dst.t.ap()[:, oc % 4, c0:c0 + w])
    T.proj(Buf3(cqn, 2), 1, w_ukv, 0, 1024, epi_kv, tag="wukv")
    k1 = K.sb("kr1s", [16, 512], F32)
    k2 = K.sb("kr2s", [16, 512], F32)
    for si, (c0, w, isctx) in enumerate(SLABS):
        K.op("act", lambda e: e.copy(k1.t[:, 0:w], kr.t[:, 0, c0:c0 + w]), R=[kr], W=[k1])
        K.op("act", lambda e: e.copy(k2.t[:, 0:w], kr.t[:, 1, c0:c0 + w]), R=[kr], W=[k2])
        T.rope(si, c0, w, k1, k2, 0, 16, io["cosq"], io["sinq"], put_b, krT.t.ap()[:, 0, c0:c0 + w], krT.t.ap()[:, 1, c0:c0 + w])


class _ChunkView:
    def __init__(self, t, off):
        self._t = t
        self._off = off

    def __getitem__(self, idx):
        p, c, n = idx
        return self._t[p, c + self._off, n]


def Buf3(buf, off):
    b = Buf(_ChunkView(buf.t, off), buf.name)
    return _Alias(buf, b.t)


class _Alias:
    def __init__(self, parent, t):
        object.__setattr__(self, "_p", parent)
        object.__setattr__(self, "t", t)

    def __getattr__(self, k):
        return getattr(object.__getattribute__(self, "_p"), k)

    def __setattr__(self, k, v):
        if k == "t":
            object.__setattr__(self, k, v)
        else:
            setattr(object.__getattribute__(self, "_p"), k, v)


NKEY = 8448
NQ = 8448
ATT_SCALE = 96 ** -0.5


def h_even_attn(K, io, n_units=2):
    QTd, KTd, Vd, OTd = io["QT"], io["KT"], io["V"], io["OT"]
    onesb = K.sb("a_ones", [128, 1], BF16)
    K.op("dve", lambda e: e.memset(onesb.t[:, :], 1.0), W=[onesb])
    QT = K.sb("a_QT", [128, NQ], BF16)
    KT = K.sb("a_KT", [128, NKEY], BF16)
    V = K.sb("a_V", [128, 66, 128], BF16)
    sq = [K.sb("a_sq%d" % i, [128, 512], BF16) for i in range(2)]
    qsq = K.sb("a_qsq", [1, NQ], F32)
    ksq = K.sb("a_ksq", [1, NKEY], F32)
    kmax = K.sb("a_kmax", [1, 1], F32)
    negm = K.sb("a_negm", [1, NQ], BF16)
    PT = [K.sb("a_PT%d" % i, [128, 512], BF16) for i in range(3)]
    stg = [K.sb("a_stg%d" % i, [128, 512], F32) for i in range(2)]
    for u in range(n_units):
        K.op("dve", lambda e: e.memset(KT.t[:, :], 1.0), W=[KT])
        K.op("pool", lambda e: e.memset(V.t[:, :, :], 1.0), W=[V])
        K.dma("sp", QT.t[0:96, :], QTd.t.ap()[u, 0:96, :], R=[QTd], W=[QT])
        K.dma("act", KT.t[0:96, :], KTd.t.ap()[u, 0:96, :], R=[KTd], W=[KT])
        K.dma("sp", V.t[:, :, 0:64], Vd.t.ap()[u], R=[Vd], W=[V])
        for (src, dst, n) in ((QT, qsq, NQ), (KT, ksq, NKEY)):
            i = 0
            for c0 in range(0, n, 512):
                w = min(512, n - c0)
                s = sq[i % 2]
                i += 1
                K.op("act", lambda e: e.activation(s.t[0:96, 0:w], src.t[0:96, c0:c0 + w], AF.Square), R=[src], W=[s])
                ps = K.ps_get()
                K.op("pe", lambda e: e.matmul(ps.t[0:1, 0:w], onesb.t[0:96, 0:1], s.t[0:96, 0:w], start=True, stop=True), R=[onesb, s], W=[ps])
                K.op("dve", lambda e: e.tensor_copy(dst.t[0:1, c0:c0 + w], ps.t[0:1, 0:w]), R=[ps], W=[dst])
        K.op("dve", lambda e: e.tensor_reduce(kmax.t[0:1, 0:1], ksq.t[0:1, :], axis=AX.X, op=ALU.max), R=[ksq], W=[kmax])
        K.op("dve", lambda e: e.tensor_scalar(qsq.t[0:1, :], qsq.t[0:1, :], kmax.t[0:1, 0:1], None, op0=ALU.mult), R=[qsq, kmax], W=[qsq])
        K.op("act", lambda e: e.activation(qsq.t[0:1, :], qsq.t[0:1, :], AF.Sqrt), R=[qsq], W=[qsq])
        K.op("dve", lambda e: e.tensor_scalar(negm.t[0:1, :], qsq.t[0:1, :], -1.0, None, op0=ALU.mult), R=[qsq], W=[negm])
        K.dma("sp", QT.t[96:97, :], negm.t[0:1, :], R=[negm], W=[QT])
        pi = 0
        slabs = [(512 * i, 512, 66) for i in range(16)] + [(8192, 256, 2)]
        for si, (c0, w, nkt) in enumerate(slabs):
            psO = K.psx[si % 2]
            nxt = None
            for kt in range(nkt):
                if nxt is None:
                    ps1 = K.ps_get()
                    K.op("pe", lambda e: e.matmul(ps1.t[:, 0:w], KT.t[0:97, kt * 128:(kt + 1) * 128], QT.t[0:97, c0:c0 + w], start=True, stop=True), R=[KT, QT], W=[ps1])
                else:
                    ps1 = nxt
                if kt + 1 < nkt:
                    nxt = K.ps_get()
                    K.op("pe", lambda e: e.matmul(nxt.t[:, 0:w], KT.t[0:97, (kt + 1) * 128:(kt + 2) * 128], QT.t[0:97, c0:c0 + w], start=True, stop=True), R=[KT, QT], W=[nxt])
                else:
                    nxt = None
                p = PT[pi % 3]
                pi += 1
                K.op("act", lambda e: e.activation(p.t[:, 0:w], ps1.t[:, 0:w], AF.Exp, scale=ATT_SCALE), R=[ps1], W=[p])
                K.op("pe", lambda e: e.matmul(psO.t[:, 0:w], V.t[:, kt, :], p.t[:, 0:w], start=(kt == 0), stop=(kt == nkt - 1)), R=[V, p], W=[psO])
            s = stg[si % 2]
            K.op("dve", lambda e: e.tensor_copy(s.t[:, 0:w], psO.t[:, 0:w]), R=[psO], W=[s])
            K.dma("sp", OTd.t.ap()[u, :, c0:c0 + w], s.t[:, 0:w], R=[s], W=[OTd])


def decl_ta_even(K):
    io = {}
    io["w_in_e"] = K.dram_in("w_in_e", [1024, 928], F32)
    io["qnT"] = K.dram_in("qnT", [128, 2], F32)
    io["kvnT"] = K.dram_in("kvnT", [128, 1], F32)
    io["w_uq"] = K.dram_in("w_uq", [256, 768], F32)
    io["w_ukv"] = K.dram_in("w_ukv", [128, 1024], F32)
    io["cosq"] = K.dram_in("cosq", [128, NT], F32)
    io["sinq"] = K.dram_in("sinq", [128, NT], F32)
    io["uT"] = K.dram_out("uT", [128, 4, NT], F32)
    io["qT"] = K.dram_out("qT", [128, 6, NT], BF16)
    io["kT"] = K.dram_out("kT", [128, 4, NT], BF16)
    io["vT"] = K.dram_out("vT", [128, 4, NT], BF16)
    io["krT"] = K.dram_out("krT", [16, 2, NT], BF16)
    return io


def decl_mod(K, tag):
    io = {}
    io["cT"] = K.dram_in("cT" + tag, [128, 8, 2], F32)
    io["w_mod"] = K.dram_in("w_mod" + tag, [1024, 6144], F32)
    io["b_modT"] = K.dram_in("b_modT" + tag, [128, 48], F32)
    io["n1gT"] = K.dram_in("n1gT" + tag, [128, 8], F32)
    io["n2gT"] = K.dram_in("n2gT" + tag, [128, 8], F32)
    return io


def build_tok(post, pre, final=False):
    nc = bass.Bass("TRN2", target_bir_lowering=False)
    K = Emit(nc)
    T = Tok(K)
    xin = K.dram_in("xT_in", [128, 8, NT], F32)
    hT = K.sb("hT", [128, 8, NT], BF16)
    mod_a = mod_b = None
    if post is not None:
        iom = decl_mod(K, "_a")
        cTa = T.load("cT_a", iom["cT"], [128, 8, 2])
        bma = T.load("bm_a", iom["b_modT"], [128, 48])
        with nc.named_scope("modvec_a"):
            mod_a = T.modvec(cTa, iom["w_mod"], bma, "a", groups=range(4, 12))
    if pre is not None:
        iom2 = decl_mod(K, "_b")
        cTb = T.load("cT_b", iom2["cT"], [128, 8, 2])
        bmb = T.load("bm_b", iom2["b_modT"], [128, 48])
        with nc.named_scope("modvec_b"):
            mod_b = T.modvec(cTb, iom2["w_mod"], bmb, "b", groups=range(0, 4))
    with K.scope():
        x = K.sb("xT", [128, 8, NT], F32)
        for c in range(8):
            K.dma("sp" if c % 2 == 0 else "act", x.t[:, c, :], xin.t.ap()[:, c, :], R=[xin], W=[x])
        if post is not None:
            iop = decl_tb(K, post)
            with K.scope():
                n2g = T.load("n2g_a", iom["n2gT"], [128, 8])
                tb_phase(K, T, x, hT, mod_a, n2g, iop, post)
            xout = K.dram_out("xT_out", [128, 8, NT], F32)
            for c in range(8):
                K.dma("sp", xout.t.ap()[:, c, :], x.t[:, c, :], R=[x], W=[xout])
            if final:
                fio = K.dram_in("fnT", [128, 8], F32)
                fo = K.dram_out("yT", [128, 8, NT], F32)
                fn = T.load("fn", fio, [128, 8])
                stgs = [K.sb("fstg%d" % i, [128, 512], F32) for i in range(3)]
                i = 0
                for (c0, w, isctx) in SLABS:
                    r = T.rms_rstd(x, 8, c0, w, D)
                    for c in range(8):
                        s = stgs[i % 3]
                        i += 1
                        K.op("dve", lambda e: e.scalar_tensor_tensor(s.t[:, 0:w], x.t[:, c, c0:c0 + w], fn.t[:, c:c + 1], r.t[:, 0:w], op0=ALU.mult, op1=ALU.mult), R=[x, fn, r], W=[s])
                        K.dma("sp", fo.t.ap()[:, c, c0:c0 + w], s.t[:, 0:w], R=[s], W=[fo])
        if pre is not None:
            with K.scope():
                n1g = T.load("n1g_b", iom2["n1gT"], [128, 8])
                with nc.named_scope("norm1"):
                    T.norm_mod(x, hT, mod_b, n1g, 0, 1, "1")
    if pre == "even":
        io = decl_ta_even(K)
        with K.scope(), nc.named_scope("ta_even"):
            ta_even(K, T, hT, io)
    elif pre == "odd":
        io = decl_ta_odd(K)
        with K.scope(), nc.named_scope("ta_odd"):
            ta_odd(K, T, hT, io)
    K.finish()
    return nc


def build_h_even():
    nc = bass.Bass("TRN2", target_bir_lowering=False)
    K = Emit(nc)
    io = {}
    io["QT"] = K.dram_in("QT", [2, 96, NQ], BF16)
    io["KT"] = K.dram_in("KT", [2, 96, NKEY], BF16)
    io["V"] = K.dram_in("V", [2, 128, 66, 64], BF16)
    io["OT"] = K.dram_out("OT", [2, 128, NQ], F32)
    with K.scope():
        h_even_attn(K, io)
    ios = decl_s5(K)
    with K.scope():
        h_even_s5(K, ios)
    K.finish()
    return nc


def to_fm(a):
    n, f = a.shape
    return np.ascontiguousarray(a.reshape(n, f // 128, 128).transpose(2, 1, 0))


def from_fm(t):
    p, c, n = t.shape
    return np.ascontiguousarray(t.transpose(2, 1, 0).reshape(n, c * p))


def vec_fm(v):
    return np.ascontiguousarray(v.reshape(-1, 128).T)


def rope_tables(dim):
    l = np.arange(8192)
    r = (l // 64).astype(np.float32)
    col = (l % 64).astype(np.float32)
    quarter = dim // 4
    inv = (np.float32(10000.0) ** (-np.arange(quarter, dtype=np.float32) / np.float32(quarter))).astype(np.float32)
    ang = np.concatenate([r[:, None] * inv, col[:, None] * inv], axis=-1).astype(np.float32)
    return np.cos(ang).astype(np.float32), np.sin(ang).astype(np.float32)


def core_rope(dim, reps, r):
    cos, sin = rope_tables(dim)
    h = dim // 2
    c = np.ones((reps * h, NT), np.float32)
    s = np.zeros((reps * h, NT), np.float32)
    c[:, 64:] = np.tile(cos[2048 * r:2048 * (r + 1)].T, (reps, 1))
    s[:, 64:] = np.tile(sin[2048 * r:2048 * (r + 1)].T, (reps, 1))
    return c, s


def perm_even(w_in, w_uq, w_ukv):
    ci = np.concatenate([np.arange(896), 896 + 2 * np.arange(16), 897 + 2 * np.arange(16)])
    nope = np.concatenate([96 * h + np.arange(64) for h in range(8)])
    r1 = np.concatenate([96 * h + 64 + 2 * np.arange(16) for h in range(8)])
    r2 = r1 + 1
    kn = np.concatenate([128 * h + np.arange(64) for h in range(8)])
    vv = kn + 64
    return (np.ascontiguousarray(w_in[:, ci]), np.ascontiguousarray(w_uq[:, np.concatenate([nope, r1, r2])]),
            np.ascontiguousarray(w_ukv[:, np.concatenate([kn, vv])]))


def mod_inputs(inp, li, b, tag):
    cT = np.stack([vec_fm(inp["c"][b]), vec_fm(inp["c_ctx"])], axis=-1)
    return {"cT" + tag: np.ascontiguousarray(cT), "w_mod" + tag: inp["w_mod"][li],
            "b_modT" + tag: vec_fm(inp["b_mod"][li]), "n1gT" + tag: vec_fm(inp["norm1_g"][li]),
            "n2gT" + tag: vec_fm(inp["norm2_g"][li])}


def ta_even_inputs(inp, j, r):
    w_in, w_uq, w_ukv = perm_even(inp["w_in_even"][j], inp["mla_w_uq"][j], inp["mla_w_ukv"][j])
    c, s = core_rope(32, 8, r)
    return {"w_in_e": w_in, "w_uq": w_uq, "w_ukv": w_ukv, "qnT": vec_fm(inp["mla_q_norm"][j]),
            "kvnT": vec_fm(inp["mla_kv_norm"][j]), "cosq": c, "sinq": s}


def attn_inputs(res, last=False):
    import ml_dtypes
    bf = ml_dtypes.bfloat16
    maps = []
    for core in range(8):
        QT = np.zeros((2, 96, NQ), bf)
        KT = np.zeros((2, 96, NKEY), bf)
        V = np.zeros((2, 128, 66, 64), bf)
        for ui in range(2):
            unit = core * 2 + ui
            b, h = unit // 8, unit % 8
            ch, ro = h // 2, (h % 2) * 64
            q = np.concatenate([np.concatenate([res[4 * b + r]["qT"][ro:ro + 64, ch, :],
                                                res[4 * b + r]["qT"][h * 16:h * 16 + 16, 4, :],
                                                res[4 * b + r]["qT"][h * 16:h * 16 + 16, 5, :]], axis=0) for r in range(4)], axis=1)
            k = np.concatenate([np.concatenate([res[4 * b + r]["kT"][ro:ro + 64, ch, :],
                                                res[4 * b + r]["krT"][:, 0, :], res[4 * b + r]["krT"][:, 1, :]], axis=0) for r in range(4)], axis=1)
            v = np.concatenate([res[4 * b + r]["vT"][ro:ro + 64, ch, :] for r in range(4)], axis=1)
            q = q.reshape(96, 4, NT)
            k = k.reshape(96, 4, NT)
            v = v.reshape(64, 4, NT)
            QT[ui] = np.concatenate([q[:, :, 64:].reshape(96, 8192), q[:, :, :64].reshape(96, 256)], axis=1)
            KT[ui] = np.concatenate([k[:, :, :64].reshape(96, 256), k[:, :, 64:].reshape(96, 8192)], axis=1)
            vv = np.concatenate([v[:, :, :64].reshape(64, 256), v[:, :, 64:].reshape(64, 8192)], axis=1)
            V[ui] = vv.T.reshape(66, 128, 64).transpose(1, 0, 2)
        maps.append({"QT": QT, "KT": KT, "V": V})
    return maps


NC8 = 1056
TWO_PI = 6.283185307179586


def decl_s5(K):
    io = {}
    for n in ("are", "aim", "ldt"):
        io[n] = K.dram_in("s5_" + n, [64, 8], F32)
    for n in ("bre", "bim", "creT", "cimT"):
        io[n] = K.dram_in("s5_" + n, [64, 8, 16], F32)
    io["U8"] = K.dram_in("s5_U8", [8, 128, 2 * NC8], F32)
    io["mask8"] = K.dram_in("s5_mask8", [128, 128], F32)
    io["ident"] = K.dram_in("s5_ident", [64, 64], F32)
    io["Y8"] = K.dram_out("s5_Y8", [8, 128, 2 * NC8], F32)
    return io


def h_even_s5(K, io):
    NU = 8
    N = 2 * NC8

    def ld(name, shape):
        b = K.sb("s5" + name, shape, F32)
        sl = tuple(slice(None) for _ in shape)
        K.dma("sp", b.t[sl], io[name].t.ap(), R=[io[name]], W=[b])
        return b
    are, aim, ldt = ld("are", [64, 8]), ld("aim", [64, 8]), ld("ldt", [64, 8])
    bre, bim, creT, cimT = ld("bre", [64, 8, 16]), ld("bim", [64, 8, 16]), ld("creT", [64, 8, 16]), ld("cimT", [64, 8, 16])
    mask8, ident = ld("mask8", [128, 128]), ld("ident", [64, 64])
    cnt = {"i": 0}

    def sm(shape=(64, 8), dt=F32):
        cnt["i"] += 1
        return K.sb("s5t%d" % cnt["i"], list(shape), dt)

    def tt(out, a, b, op, R, W, eng="dve"):
        K.op(eng, lambda e: e.tensor_tensor(out, a, b, op=op), R=R, W=W)

    dt_, lr, li, mag = sm(), sm(), sm(), sm()
    K.op("act", lambda e: e.activation(dt_.t[:, :], ldt.t[:, :], AF.Exp), R=[ldt], W=[dt_])
    tt(lr.t[:, :], are.t[:, :], dt_.t[:, :], ALU.mult, [are, dt_], [lr])
    tt(li.t[:, :], aim.t[:, :], dt_.t[:, :], ALU.mult, [aim, dt_], [li])
    K.op("act", lambda e: e.activation(mag.t[:, :], lr.t[:, :], AF.Exp), R=[lr], W=[mag])
    rho, irho = sm(), sm()
    K.op("act", lambda e: e.activation(rho.t[:, :], lr.t[:, :], AF.Exp, scale=8.0), R=[lr], W=[rho])
    K.op("act", lambda e: e.activation(irho.t[:, :], lr.t[:, :], AF.Exp, scale=-8.0), R=[lr], W=[irho])
    kf, ki, r0, rs, rc, s1, c1 = sm(), sm((64, 8), I32), sm(), sm(), sm(), sm(), sm()
    K.op("dve", lambda e: e.tensor_scalar(kf.t[:, :], li.t[:, :], 1.0 / TWO_PI, None, op0=ALU.mult), R=[li], W=[kf])
    K.op("dve", lambda e: e.tensor_copy(ki.t[:, :], kf.t[:, :]), R=[kf], W=[ki])
    K.op("dve", lambda e: e.tensor_copy(kf.t[:, :], ki.t[:, :]), R=[ki], W=[kf])
    K.op("dve", lambda e: e.scalar_tensor_tensor(r0.t[:, :], kf.t[:, :], -TWO_PI, li.t[:, :], op0=ALU.mult, op1=ALU.add), R=[kf, li], W=[r0])
    wa, wb2, wy = sm(), sm(), sm()

    def wrap(dst, shift):
        K.op("dve", lambda e: e.tensor_scalar(wy.t[:, :], r0.t[:, :], float(shift), None, op0=ALU.add), R=[r0], W=[wy])
        K.op("dve", lambda e: e.tensor_scalar(wa.t[:, :], wy.t[:, :], float(np.pi), -TWO_PI, op0=ALU.is_gt, op1=ALU.mult), R=[wy], W=[wa])
        K.op("dve", lambda e: e.tensor_scalar(wb2.t[:, :], wy.t[:, :], -float(np.pi), TWO_PI, op0=ALU.is_lt, op1=ALU.mult), R=[wy], W=[wb2])
        K.op("dve", lambda e: e.tensor_tensor(wa.t[:, :], wa.t[:, :], wb2.t[:, :], op=ALU.add), R=[wa, wb2], W=[wa])
        K.op("dve", lambda e: e.tensor_tensor(dst.t[:, :], wy.t[:, :], wa.t[:, :], op=ALU.add), R=[wy, wa], W=[dst])
    wrap(rs, 0.0)
    wrap(rc, np.pi / 2)
    K.op("act", lambda e: e.activation(s1.t[:, :], rs.t[:, :], AF.Sin), R=[rs], W=[s1])
    K.op("act", lambda e: e.activation(c1.t[:, :], rc.t[:, :], AF.Sin), R=[rc], W=[c1])
    abr, abi = sm(), sm()
    tt(abr.t[:, :], mag.t[:, :], c1.t[:, :], ALU.mult, [mag, c1], [abr])
    tt(abi.t[:, :], mag.t[:, :], s1.t[:, :], ALU.mult, [mag, s1], [abi])
    nr, den, fr, fi, t1, t2 = sm(), sm(), sm(), sm(), sm(), sm()
    K.op("dve", lambda e: e.tensor_scalar(nr.t[:, :], abr.t[:, :], -1.0, None, op0=ALU.add), R=[abr], W=[nr])
    tt(t1.t[:, :], are.t[:, :], are.t[:, :], ALU.mult, [are], [t1])
    tt(t2.t[:, :], aim.t[:, :], aim.t[:, :], ALU.mult, [aim], [t2])
    tt(den.t[:, :], t1.t[:, :], t2.t[:, :], ALU.add, [t1, t2], [den])
    K.op("dve", lambda e: e.reciprocal(den.t[:, :], den.t[:, :]), R=[den], W=[den])
    tt(t1.t[:, :], nr.t[:, :], are.t[:, :], ALU.mult, [nr, are], [t1])
    tt(t2.t[:, :], abi.t[:, :], aim.t[:, :], ALU.mult, [abi, aim], [t2])
    tt(t1.t[:, :], t1.t[:, :], t2.t[:, :], ALU.add, [t1, t2], [t1])
    tt(fr.t[:, :], t1.t[:, :], den.t[:, :], ALU.mult, [t1, den], [fr])
    tt(t1.t[:, :], abi.t[:, :], are.t[:, :], ALU.mult, [abi, are], [t1])
    tt(t2.t[:, :], nr.t[:, :], aim.t[:, :], ALU.mult, [nr, aim], [t2])
    tt(t1.t[:, :], t1.t[:, :], t2.t[:, :], ALU.subtract, [t1, t2], [t1])
    tt(fi.t[:, :], t1.t[:, :], den.t[:, :], ALU.mult, [t1, den], [fi])

    Apr, Api = sm((64, 8, 9)), sm((64, 8, 9))
    Qr, Qi = sm((64, 8, 8)), sm((64, 8, 8))
    K.op("dve", lambda e: e.memset(Apr.t[:, :, 0], 1.0), W=[Apr])
    K.op("dve", lambda e: e.memset(Api.t[:, :, 0], 0.0), W=[Api])
    K.op("dve", lambda e: e.tensor_copy(Qr.t[:, :, 0], fr.t[:, :]), R=[fr], W=[Qr])
    K.op("dve", lambda e: e.tensor_copy(Qi.t[:, :, 0], fi.t[:, :]), R=[fi], W=[Qi])

    def cstep(Tr, Ti, n):
        for tau in range(1, n):
            tt(t1.t[:, :], Tr.t[:, :, tau - 1], abr.t[:, :], ALU.mult, [Tr, abr], [t1])
            tt(t2.t[:, :], Ti.t[:, :, tau - 1], abi.t[:, :], ALU.mult, [Ti, abi], [t2])
            tt(Tr.t[:, :, tau], t1.t[:, :], t2.t[:, :], ALU.subtract, [t1, t2], [Tr])
            tt(t1.t[:, :], Tr.t[:, :, tau - 1], abi.t[:, :], ALU.mult, [Tr, abi], [t1])
            tt(t2.t[:, :], Ti.t[:, :, tau - 1], abr.t[:, :], ALU.mult, [Ti, abr], [t2])
            tt(Ti.t[:, :, tau], t1.t[:, :], t2.t[:, :], ALU.add, [t1, t2], [Ti])
    cstep(Apr, Api, 9)
    cstep(Qr, Qi, 8)
    nApi, nQi = sm((64, 8, 9)), sm((64, 8, 8))
    nApr = sm((64, 8, 9))
    K.op("dve", lambda e: e.tensor_scalar(nApr.t[:, :, :], Apr.t[:, :, :], -1.0, None, op0=ALU.mult), R=[Apr], W=[nApr])
    K.op("dve", lambda e: e.tensor_scalar(nApi.t[:, :, :], Api.t[:, :, :], -1.0, None, op0=ALU.mult), R=[Api], W=[nApi])
    K.op("dve", lambda e: e.tensor_scalar(nQi.t[:, :, :], Qi.t[:, :, :], -1.0, None, op0=ALU.mult), R=[Qi], W=[nQi])
    n2, i8r, i8i, ni8i, phr, phi = sm(), sm(), sm(), sm(), sm(), sm()
    tt(t1.t[:, :], Apr.t[:, :, 8], Apr.t[:, :, 8], ALU.mult, [Apr], [t1])
    tt(t2.t[:, :], Api.t[:, :, 8], Api.t[:, :, 8], ALU.mult, [Api], [t2])
    tt(n2.t[:, :], t1.t[:, :], t2.t[:, :], ALU.add, [t1, t2], [n2])
    K.op("dve", lambda e: e.reciprocal(n2.t[:, :], n2.t[:, :]), R=[n2], W=[n2])
    tt(i8r.t[:, :], Apr.t[:, :, 8], n2.t[:, :], ALU.mult, [Apr, n2], [i8r])
    tt(ni8i.t[:, :], Api.t[:, :, 8], n2.t[:, :], ALU.mult, [Api, n2], [ni8i])
    K.op("dve", lambda e: e.tensor_scalar(i8i.t[:, :], ni8i.t[:, :], -1.0, None, op0=ALU.mult), R=[ni8i], W=[i8i])
    tt(phr.t[:, :], Apr.t[:, :, 8], irho.t[:, :], ALU.mult, [Apr, irho], [phr])
    tt(phi.t[:, :], Api.t[:, :, 8], irho.t[:, :], ALU.mult, [Api, irho], [phi])
    pwr, pwi, npwi = sm((64, 8, 11)), sm((64, 8, 11)), sm((64, 8, 11))
    K.op("dve", lambda e: e.tensor_copy(pwr.t[:, :, 0], phr.t[:, :]), R=[phr], W=[pwr])
    K.op("dve", lambda e: e.tensor_copy(pwi.t[:, :, 0], phi.t[:, :]), R=[phi], W=[pwi])
    for j in range(1, 11):
        tt(t1.t[:, :], pwr.t[:, :, j - 1], pwr.t[:, :, j - 1], ALU.mult, [pwr], [t1])
        tt(t2.t[:, :], pwi.t[:, :, j - 1], pwi.t[:, :, j - 1], ALU.mult, [pwi], [t2])
        tt(pwr.t[:, :, j], t1.t[:, :], t2.t[:, :], ALU.subtract, [t1, t2], [pwr])
        tt(t1.t[:, :], pwr.t[:, :, j - 1], pwi.t[:, :, j - 1], ALU.mult, [pwr, pwi], [t1])
        K.op("dve", lambda e: e.tensor_scalar(pwi.t[:, :, j], t1.t[:, :], 2.0, None, op0=ALU.mult), R=[t1], W=[pwi])
    K.op("dve", lambda e: e.tensor_scalar(npwi.t[:, :, :], pwi.t[:, :, :], -1.0, None, op0=ALU.mult), R=[pwi], W=[npwi])

    if "dbg" in io:
        for i, tb_ in enumerate((dt_, lr, li, mag, r0, rs, rc, s1, c1, abr, abi, fr, fi, i8r, i8i, phr, phi, rho)):
            K.dma("sp", io["dbg"].t.ap()[:, i, :], tb_.t[:, :], R=[tb_], W=[io["dbg"]])
    Wre, Wim = sm((64, 8, 16)), sm((64, 8, 16))
    Wpr, Wpi = sm((64, 128)), sm((64, 128))
    RrT, RiT = sm((64, 8, 16)), sm((64, 8, 16))
    RrTb, RiTb = sm((64, 128), BF16), sm((64, 128), BF16)
    tmp16 = sm((64, 16))
    tmp128 = sm((64, 128))
    MTb = sm((128, 128), BF16)
    WreTb, WimTb = sm((128, 64), BF16), sm((128, 64), BF16)
    Phr, Phi = sm((64, NC8)), sm((64, NC8))
    ptmp = sm((64, 512))
    rmask = sm((64, N))
    U8f = [sm((128, N)) for _ in range(2)]
    U8b = sm((128, N), BF16)
    Sre, Sim = sm((64, N)), sm((64, N))
    Gr, Gi = sm((64, N)), sm((64, N))
    Gr2, Gi2 = sm((64, N)), sm((64, N))
    ta, tb = sm((64, NC8)), sm((64, NC8))
    Hpr, Hpi = sm((64, N), BF16), sm((64, N), BF16)
    ystg = [sm((128, 512)) for _ in range(2)]
    K.op("dve", lambda e: e.memset(Hpr.t[:, :], 0.0), W=[Hpr])
    K.op("dve", lambda e: e.memset(Hpi.t[:, :], 0.0), W=[Hpi])
    slabs = [(c0, min(512, N - c0)) for c0 in range(0, N, 512)]

    for u in range(NU):
        K.dma("act", U8f[u % 2].t[:, :], io["U8"].t.ap()[u], R=[io["U8"]], W=[U8f[u % 2]])
        for s in range(8):
            q = 7 - s
            K.op("dve", lambda e: e.tensor_scalar(tmp16.t[:, :], bre.t[:, u, :], Qr.t[:, u, q:q + 1], None, op0=ALU.mult), R=[bre, Qr], W=[tmp16])
            K.op("dve", lambda e: e.scalar_tensor_tensor(Wre.t[:, s, :], bim.t[:, u, :], nQi.t[:, u, q:q + 1], tmp16.t[:, :], op0=ALU.mult, op1=ALU.add), R=[bim, nQi, tmp16], W=[Wre])
            K.op("dve", lambda e: e.tensor_scalar(tmp16.t[:, :], bim.t[:, u, :], Qr.t[:, u, q:q + 1], None, op0=ALU.mult), R=[bim, Qr], W=[tmp16])
            K.op("dve", lambda e: e.scalar_tensor_tensor(Wim.t[:, s, :], bre.t[:, u, :], Qi.t[:, u, q:q + 1], tmp16.t[:, :], op0=ALU.mult, op1=ALU.add), R=[bre, Qi, tmp16], W=[Wim])
        for t in range(8):
            K.op("dve", lambda e: e.tensor_scalar(tmp16.t[:, :], creT.t[:, u, :], Apr.t[:, u, t + 1:t + 2], None, op0=ALU.mult), R=[creT, Apr], W=[tmp16])
            K.op("dve", lambda e: e.scalar_tensor_tensor(RrT.t[:, t, :], cimT.t[:, u, :], nApi.t[:, u, t + 1:t + 2], tmp16.t[:, :], op0=ALU.mult, op1=ALU.add), R=[cimT, nApi, tmp16], W=[RrT])
            K.op("dve", lambda e: e.tensor_scalar(tmp16.t[:, :], creT.t[:, u, :], nApi.t[:, u, t + 1:t + 2], None, op0=ALU.mult), R=[creT, nApi], W=[tmp16])
            K.op("dve", lambda e: e.scalar_tensor_tensor(RiT.t[:, t, :], cimT.t[:, u, :], nApr.t[:, u, t + 1:t + 2], tmp16.t[:, :], op0=ALU.mult, op1=ALU.add), R=[cimT, nApr, tmp16], W=[RiT])
        Wre2 = Wre.t[:, :, :].rearrange("p s k -> p (s k)")
        Wim2 = Wim.t[:, :, :].rearrange("p s k -> p (s k)")
        Rr2 = RrT.t[:, :, :].rearrange("p s k -> p (s k)")
        Ri2 = RiT.t[:, :, :].rearrange("p s k -> p (s k)")
        K.op("dve", lambda e: e.tensor_scalar(tmp128.t[:, :], Wre2, i8r.t[:, u:u + 1], None, op0=ALU.mult), R=[Wre, i8r], W=[tmp128])
        K.op("dve", lambda e: e.scalar_tensor_tensor(Wpr.t[:, :], Wim2, ni8i.t[:, u:u + 1], tmp128.t[:, :], op0=ALU.mult, op1=ALU.add), R=[Wim, ni8i, tmp128], W=[Wpr])
        K.op("dve", lambda e: e.tensor_scalar(tmp128.t[:, :], Wim2, i8r.t[:, u:u + 1], None, op0=ALU.mult), R=[Wim, i8r], W=[tmp128])
        K.op("dve", lambda e: e.scalar_tensor_tensor(Wpi.t[:, :], Wre2, i8i.t[:, u:u + 1], tmp128.t[:, :], op0=ALU.mult, op1=ALU.add), R=[Wre, i8i, tmp128], W=[Wpi])
        ps = K.ps_get()
        K.op("pe", lambda e: e.matmul(ps.t[:, 0:128], Wpr.t[:, :], Rr2, start=True, stop=False), R=[Wpr, RrT], W=[ps])
        K.op("pe", lambda e: e.matmul(ps.t[:, 0:128], Wpi.t[:, :], Ri2, start=False, stop=True), R=[Wpi, RiT], W=[ps])
        K.op("dve", lambda e: e.tensor_tensor(MTb.t[:, :], ps.t[:, 0:128], mask8.t[:, :], op=ALU.mult), R=[ps, mask8], W=[MTb])
        ps = K.ps_get()
        K.op("pe", lambda e: e.matmul(ps.t[:, 0:64], Wre2, ident.t[:, :], start=True, stop=True), R=[Wre, ident], W=[ps])
        K.op("act", lambda e: e.copy(WreTb.t[:, :], ps.t[:, 0:64]), R=[ps], W=[WreTb])
        ps = K.ps_get()
        K.op("pe", lambda e: e.matmul(ps.t[:, 0:64], Wim2, ident.t[:, :], start=True, stop=True), R=[Wim, ident], W=[ps])
        K.op("act", lambda e: e.copy(WimTb.t[:, :], ps.t[:, 0:64]), R=[ps], W=[WimTb])
        K.op("act", lambda e: e.copy(RrTb.t[:, :], Rr2), R=[RrT], W=[RrTb])
        K.op("act", lambda e: e.copy(RiTb.t[:, :], Ri2), R=[RiT], W=[RiTb])
        K.op("pool", lambda e: e.memset(Phr.t[:, 0:1], 1.0), W=[Phr])
        K.op("pool", lambda e: e.memset(Phi.t[:, 0:1], 0.0), W=[Phi])
        n = 1
        j = 0
        while n < NC8:
            m = min(n, NC8 - n)
            for c0 in range(0, m, 512):
                w = min(512, m - c0)
                K.op("dve", lambda e: e.tensor_scalar(ptmp.t[:, 0:w], Phr.t[:, c0:c0 + w], pwr.t[:, u, j:j + 1], None, op0=ALU.mult), R=[Phr, pwr], W=[ptmp])
                K.op("dve", lambda e: e.scalar_tensor_tensor(Phr.t[:, n + c0:n + c0 + w], Phi.t[:, c0:c0 + w], npwi.t[:, u, j:j + 1], ptmp.t[:, 0:w], op0=ALU.mult, op1=ALU.add), R=[Phi, npwi, ptmp, Phr], W=[Phr])
                K.op("dve", lambda e: e.tensor_scalar(ptmp.t[:, 0:w], Phr.t[:, c0:c0 + w], pwi.t[:, u, j:j + 1], None, op0=ALU.mult), R=[Phr, pwi], W=[ptmp])
                K.op("dve", lambda e: e.scalar_tensor_tensor(Phi.t[:, n + c0:n + c0 + w], Phi.t[:, c0:c0 + w], pwr.t[:, u, j:j + 1], ptmp.t[:, 0:w], op0=ALU.mult, op1=ALU.add), R=[Phi, pwr, ptmp], W=[Phi])
            n *= 2
            j += 1
        K.op("pool", lambda e: e.memset(rmask.t[:, :], 1.0), W=[rmask])
        K.op("pool", lambda e: e.tensor_scalar(rmask.t[:, :], rmask.t[:, :], rho.t[:, u:u + 1], None, op0=ALU.mult), R=[rho, rmask], W=[rmask])
        K.op("pool", lambda e: e.memset(rmask.t[:, 0:1], 0.0), W=[rmask])
        K.op("pool", lambda e: e.memset(rmask.t[:, NC8:NC8 + 1], 0.0), W=[rmask])
        Uf = U8f[u % 2]
        K.op("act", lambda e: e.copy(U8b.t[:, :], Uf.t[:, :]), R=[Uf], W=[U8b])
        for (c0, w) in slabs:
            for (WT, S) in ((WreTb, Sre), (WimTb, Sim)):
                ps = K.ps_get()
                K.op("pe", lambda e: e.matmul(ps.t[0:64, 0:w], WT.t[:, :], U8b.t[:, c0:c0 + w], start=True, stop=True), R=[WT, U8b], W=[ps])
                K.op("act", lambda e: e.copy(S.t[:, c0:c0 + w], ps.t[0:64, 0:w]), R=[ps], W=[S])
        for hb in range(2):
            sl = slice(hb * NC8, (hb + 1) * NC8)
            tt(ta.t[:, :], Phr.t[:, :], Sre.t[:, sl], ALU.mult, [Phr, Sre], [ta])
            tt(tb.t[:, :], Phi.t[:, :], Sim.t[:, sl], ALU.mult, [Phi, Sim], [tb], eng="pool")
            tt(Gr.t[:, sl], ta.t[:, :], tb.t[:, :], ALU.add, [ta, tb], [Gr])
            tt(ta.t[:, :], Phr.t[:, :], Sim.t[:, sl], ALU.mult, [Phr, Sim], [ta])
            tt(tb.t[:, :], Phi.t[:, :], Sre.t[:, sl], ALU.mult, [Phi, Sre], [tb], eng="pool")
            tt(Gi.t[:, sl], ta.t[:, :], tb.t[:, :], ALU.subtract, [ta, tb], [Gi])
        K.op("dve", lambda e: e.tensor_tensor_scan(Gr2.t[:, :], rmask.t[:, :], Gr.t[:, :], 0.0, op0=ALU.mult, op1=ALU.add), R=[rmask, Gr], W=[Gr2])
        K.op("dve", lambda e: e.tensor_tensor_scan(Gi2.t[:, :], rmask.t[:, :], Gi.t[:, :], 0.0, op0=ALU.mult, op1=ALU.add), R=[rmask, Gi], W=[Gi2])
        for hb in range(2):
            o = hb * NC8
            n1 = NC8 - 1
            tt(ta.t[:, 0:n1], Phr.t[:, 0:n1], Gr2.t[:, o:o + n1], ALU.mult, [Phr, Gr2], [ta])
            tt(tb.t[:, 0:n1], Phi.t[:, 0:n1], Gi2.t[:, o:o + n1], ALU.mult, [Phi, Gi2], [tb], eng="pool")
            tt(Hpr.t[:, o + 1:o + 1 + n1], ta.t[:, 0:n1], tb.t[:, 0:n1], ALU.subtract, [ta, tb], [Hpr])
            tt(ta.t[:, 0:n1], Phr.t[:, 0:n1], Gi2.t[:, o:o + n1], ALU.mult, [Phr, Gi2], [ta])
            tt(tb.t[:, 0:n1], Phi.t[:, 0:n1], Gr2.t[:, o:o + n1], ALU.mult, [Phi, Gr2], [tb], eng="pool")
            tt(Hpi.t[:, o + 1:o + 1 + n1], ta.t[:, 0:n1], tb.t[:, 0:n1], ALU.add, [ta, tb], [Hpi])
        for si, (c0, w) in enumerate(slabs):
            ps = K.ps_get()
            K.op("pe", lambda e: e.matmul(ps.t[:, 0:w], MTb.t[:, :], U8b.t[:, c0:c0 + w], start=True, stop=False), R=[MTb, U8b], W=[ps])
            K.op("pe", lambda e: e.matmul(ps.t[:, 0:w], RrTb.t[:, :], Hpr.t[:, c0:c0 + w], start=False, stop=False), R=[RrTb, Hpr], W=[ps])
            K.op("pe", lambda e: e.matmul(ps.t[:, 0:w], RiTb.t[:, :], Hpi.t[:, c0:c0 + w], start=False, stop=True), R=[RiTb, Hpi], W=[ps])
            s = ystg[si % 2]
            K.op("act", lambda e: e.copy(s.t[:, 0:w], ps.t[:, 0:w]), R=[ps], W=[s])
            K.dma("sp", io["Y8"].t.ap()[u, :, c0:c0 + w], s.t[:, 0:w], R=[s], W=[io["Y8"]])


def s5_inputs(inp, j, uT_list):
    useq = []
    for b in range(2):
        parts = [from_fm(uT_list[4 * b + r]) for r in range(4)]
        ctx = np.concatenate([p[:64] for p in parts], 0)
        lat = np.concatenate([p[64:] for p in parts], 0)
        useq.append((ctx, lat))
    mask8 = (np.arange(128)[:, None] // 16 <= np.arange(128)[None, :] // 16).astype(np.float32)
    ident = np.eye(64, dtype=np.float32)
    maps = []
    for core in range(8):
        m = {"s5_mask8": mask8, "s5_ident": ident}
        U8 = np.zeros((8, 128, 2 * NC8), np.float32)
        sel = lambda a: np.stack([a[j, d, 4 * core + gl] for d in range(2) for gl in range(4)], axis=-1)
        m["s5_are"] = np.ascontiguousarray(sel(inp["s5_a_re"]))
        m["s5_aim"] = np.ascontiguousarray(sel(inp["s5_a_im"]))
        m["s5_ldt"] = np.ascontiguousarray(np.broadcast_to(np.stack([inp["s5_log_dt"][j, d, 4 * core + gl] for d in range(2) for gl in range(4)])[None, :], (64, 8)))
        m["s5_bre"] = np.ascontiguousarray(np.stack([inp["s5_b_re"][j, d, 4 * core + gl] for d in range(2) for gl in range(4)], axis=1))
        m["s5_bim"] = np.ascontiguousarray(np.stack([inp["s5_b_im"][j, d, 4 * core + gl] for d in range(2) for gl in range(4)], axis=1))
        m["s5_creT"] = np.ascontiguousarray(np.stack([inp["s5_c_re"][j, d, 4 * core + gl].T for d in range(2) for gl in range(4)], axis=1))
        m["s5_cimT"] = np.ascontiguousarray(np.stack([inp["s5_c_im"][j, d, 4 * core + gl].T for d in range(2) for gl in range(4)], axis=1))
        for d in range(2):
            for gl in range(4):
                g = 4 * core + gl
                for b in range(2):
                    ctx, lat = useq[b]
                    if d == 0:
                        seq = np.concatenate([ctx[:, 16 * g:16 * g + 16], lat[:, 16 * g:16 * g + 16]], 0)
                    else:
                        seq = np.concatenate([ctx[::-1, 16 * g:16 * g + 16], lat[::-1, 16 * g:16 * g + 16]], 0)
                    U8[d * 4 + gl, :, b * NC8:(b + 1) * NC8] = seq.reshape(NC8, 128).T
        m["s5_U8"] = U8
        maps.append(m)
    return maps


def s5_outputs(Y8_list):
    yf = np.zeros((2, 8448, 512), np.float32)
    yb = np.zeros((2, 8448, 512), np.float32)
    for core in range(8):
        Y8 = Y8_list[core]
        for d in range(2):
            for gl in range(4):
                g = 4 * core + gl
                for b in range(2):
                    seq = Y8[d * 4 + gl, :, b * NC8:(b + 1) * NC8].T.reshape(8448, 16)
                    if d == 0:
                        yf[b, :, 16 * g:16 * g + 16] = seq
                    else:
                        yb[b, :256, 16 * g:16 * g + 16] = seq[:256][::-1]
                        yb[b, 256:, 16 * g:16 * g + 16] = seq[256:][::-1]
    return yf, yb


GELU_C = 2.0 * 0.7978845608028654


def decl_tb(K, parity):
    io = {}
    if parity == "even":
        for n in ("uTi", "yfT", "ybT", "OTn", "denT"):
            io[n] = K.dram_in(n, [128, 4, NT], F32)
        io["dT"] = K.dram_in("dT", [128, 4], F32)
        io["w_glu"] = K.dram_in("w_glu", [512, 512], F32)
    else:
        for n in ("of", "ob", "gT"):
            io[n] = K.dram_in(n, [128, 8, NT], F32)
        io["gnT"] = K.dram_in("gnT", [128, 8], F32)
    io["w_out"] = K.dram_in("w_out", [1024, 1024], F32)
    io["w1"] = K.dram_in("ffn_w1", [1024, DFF], F32)
    io["w3"] = K.dram_in("ffn_w3", [1024, DFF], F32)
    io["w2"] = K.dram_in("ffn_w2", [DFF, 1024], F32)
    return io


def tb_phase(K, T, x, hT, mod, n2g, io, parity):
    with K.scope(), K.nc.named_scope("mix_post"):
        stg = [K.sb("tbs%d" % i, [128, 512], F32) for i in range(6)]
        st = {"i": 0}

        def ldslab(d, c, c0, w, q="sp"):
            b = stg[st["i"] % 6]
            st["i"] += 1
            K.dma(q, b.t[:, 0:w], d.t.ap()[:, c, c0:c0 + w], R=[d], W=[b])
            return b

        if parity == "even":
            dT = T.load("dT", io["dT"], [128, 4])
            zb = K.sb("zbT", [128, 4, NT], BF16)
            z = zb
            for (c0, w, isctx) in SLABS:
                for c in range(4):
                    u = ldslab(io["uTi"], c, c0, w)
                    yf = ldslab(io["yfT"], c, c0, w, "act")
                    yb = ldslab(io["ybT"], c, c0, w)
                    t = T.gettmp()
                    t2 = T.gettmp()
                    K.op("dve", lambda e: e.scalar_tensor_tensor(t.t[:, 0:w], u.t[:, 0:w], dT.t[:, c:c + 1], yf.t[:, 0:w], op0=ALU.mult, op1=ALU.add), R=[u, dT, yf], W=[t])
                    K.op("dve", lambda e: e.tensor_tensor(t.t[:, 0:w], t.t[:, 0:w], yb.t[:, 0:w], op=ALU.add), R=[t, yb], W=[t])
                    K.op("dve", lambda e: e.tensor_tensor(t2.t[:, 0:w], t.t[:, 0:w], t.t[:, 0:w], op=ALU.mult), R=[t], W=[t2])
                    K.op("dve", lambda e: e.tensor_scalar(t2.t[:, 0:w], t2.t[:, 0:w], 0.044715, 1.0, op0=ALU.mult, op1=ALU.add), R=[t2], W=[t2])
                    K.op("dve", lambda e: e.tensor_tensor(t2.t[:, 0:w], t2.t[:, 0:w], t.t[:, 0:w], op=ALU.mult), R=[t2, t], W=[t2])
                    K.op("act", lambda e: e.activation(t2.t[:, 0:w], t2.t[:, 0:w], AF.Sigmoid, scale=GELU_C), R=[t2], W=[t2])
                    K.op("dve", lambda e: e.tensor_tensor(zb.t[:, c, c0:c0 + w], t.t[:, 0:w], t2.t[:, 0:w], op=ALU.mult), R=[t, t2], W=[zb])
                    o = ldslab(io["OTn"], c, c0, w, "act")
                    dn = ldslab(io["denT"], c, c0, w)
                    K.op("dve", lambda e: e.reciprocal(dn.t[:, 0:w], dn.t[:, 0:w]), R=[dn], W=[dn])
                    K.op("dve", lambda e: e.tensor_tensor(hT.t[:, 4 + c, c0:c0 + w], o.t[:, 0:w], dn.t[:, 0:w], op=ALU.mult), R=[o, dn], W=[hT])

            def epi_glu(oc, m, si, c0, w, isctx, ps):
                t = T.gettmp()
                K.op("act", lambda e: e.activation(t.t[:, 0:w], ps.t[:, 0:w], AF.Sigmoid), R=[ps], W=[t])
                K.op("dve", lambda e: e.tensor_tensor(hT.t[:, oc, c0:c0 + w], z.t[:, oc, c0:c0 + w], t.t[:, 0:w], op=ALU.mult), R=[z, t], W=[hT])
            T.proj(zb, 4, io["w_glu"], 0, 512, epi_glu, tag="wglu")
        else:
            gn = T.load("gnT", io["gnT"], [128, 8])
            for (c0, w, isctx) in SLABS:
                for c in range(8):
                    a = ldslab(io["of"], c, c0, w)
                    b = ldslab(io["ob"], c, c0, w, "act")
                    g = ldslab(io["gT"], c, c0, w)
                    o = T.gettmp()
                    K.op("dve", lambda e: e.tensor_tensor(o.t[:, 0:w], a.t[:, 0:w], b.t[:, 0:w], op=ALU.add), R=[a, b], W=[o])
                    if c < 4:
                        ps = K.ps_get()
                        K.op("pe", lambda e: e.matmul(ps.t[:, 0:w], T.ones.t[:, :], o.t[:, 0:w], start=True, stop=True), R=[T.ones, o], W=[ps])
                        K.op("dve", lambda e: e.scalar_tensor_tensor(o.t[:, 0:w], ps.t[:, 0:w], -1.0 / 128, o.t[:, 0:w], op0=ALU.mult, op1=ALU.add), R=[ps, o], W=[o])
                    sq = T.gettmp()
                    K.op("act", lambda e: e.activation(sq.t[:, 0:w], o.t[:, 0:w], AF.Square), R=[o], W=[sq])
                    ps = K.ps_get()
                    K.op("pe", lambda e: e.matmul(ps.t[:, 0:w], T.ones.t[:, :], sq.t[:, 0:w], start=True, stop=True), R=[T.ones, sq], W=[ps])
                    K.op("act", lambda e: e.activation(sq.t[:, 0:w], ps.t[:, 0:w], AF.Ln, bias=T.eps.t[:, 0:1], scale=1.0 / 128), R=[ps, T.eps], W=[sq])
                    K.op("act", lambda e: e.activation(sq.t[:, 0:w], sq.t[:, 0:w], AF.Exp, scale=-0.5), R=[sq], W=[sq])
                    K.op("dve", lambda e: e.scalar_tensor_tensor(o.t[:, 0:w], o.t[:, 0:w], gn.t[:, c:c + 1], sq.t[:, 0:w], op0=ALU.mult, op1=ALU.mult), R=[o, gn, sq], W=[o])
                    K.op("act", lambda e: e.activation(g.t[:, 0:w], g.t[:, 0:w], AF.Silu), R=[g], W=[g])
                    K.op("dve", lambda e: e.tensor_tensor(hT.t[:, c, c0:c0 + w], o.t[:, 0:w], g.t[:, 0:w], op=ALU.mult), R=[o, g], W=[hT])

    def epi_out(oc, m, si, c0, w, isctx, ps):
        K.op("dve", lambda e: e.scalar_tensor_tensor(x.t[:, oc, c0:c0 + w], ps.t[:, 0:w], mod.t[:, isctx, 16 + oc:17 + oc], x.t[:, oc, c0:c0 + w], op0=ALU.mult, op1=ALU.add), R=[ps, mod, x], W=[x])
    with K.nc.named_scope("w_out"):
        T.proj(hT, 8, io["w_out"], 0, 1024, epi_out, tag="wout")
    with K.nc.named_scope("norm2"):
        T.norm_mod(x, hT, mod, n2g, 3, 4, "2")
    _ffn_sid = K.nc.enter_named_scope("ffn", False)[0]
    FG = 2
    w1v = io["w1"].t.ap().rearrange("(c p) n -> p c n", p=128)
    w3v = io["w3"].t.ap().rearrange("(c p) n -> p c n", p=128)
    w2v = io["w2"].t.ap().rearrange("(c p) n -> p c n", p=128)
    wb1 = [K.sb("f1_%d" % i, [128, 8, 128 * FG], BF16) for i in range(2)]
    wb3 = [K.sb("f3_%d" % i, [128, 8, 128 * FG], BF16) for i in range(2)]
    wb2 = [K.sb("f2_%d" % i, [128, FG, 1024], BF16) for i in range(2)]
    aT = [K.sb("faT%d" % i, [128, FG, 512], BF16) for i in range(2)]
    sl = [K.sb("fsl%d" % i, [128, 512], F32) for i in range(2)]
    ai = 0
    for fg in range(22 // FG):
        b1, b3, b2 = wb1[fg % 2], wb3[fg % 2], wb2[fg % 2]
        K.dma("pool", b1.t[:, :, :], w1v[:, :, fg * 128 * FG:(fg + 1) * 128 * FG], R=[io["w1"]], W=[b1])
        K.dma("pool", b3.t[:, :, :], w3v[:, :, fg * 128 * FG:(fg + 1) * 128 * FG], R=[io["w3"]], W=[b3])
        K.dma("pool", b2.t[:, :, :], w2v[:, fg * FG:(fg + 1) * FG, :], R=[io["w2"]], W=[b2])
        for (c0, w, isctx) in SLABS:
            a = aT[ai % 2]
            ai += 1
            for fc in range(FG):
                ps1 = K.ps_get()
                for kc in range(8):
                    K.op("pe", lambda e: e.matmul(ps1.t[:, 0:w], b1.t[:, kc, fc * 128:(fc + 1) * 128], hT.t[:, kc, c0:c0 + w], start=(kc == 0), stop=(kc == 7)), R=[b1, hT], W=[ps1])
                ps3 = K.ps_get()
                for kc in range(8):
                    K.op("pe", lambda e: e.matmul(ps3.t[:, 0:w], b3.t[:, kc, fc * 128:(fc + 1) * 128], hT.t[:, kc, c0:c0 + w], start=(kc == 0), stop=(kc == 7)), R=[b3, hT], W=[ps3])
                s_ = sl[fc % 2]
                K.op("act", lambda e: e.activation(s_.t[:, 0:w], ps1.t[:, 0:w], AF.Silu), R=[ps1], W=[s_])
                K.op("dve", lambda e: e.tensor_tensor(a.t[:, fc, 0:w], s_.t[:, 0:w], ps3.t[:, 0:w], op=ALU.mult), R=[s_, ps3], W=[a])
            for oc in range(8):
                ps = K.ps_get()
                for fc in range(FG):
                    K.op("pe", lambda e: e.matmul(ps.t[:, 0:w], b2.t[:, fc, oc * 128:(oc + 1) * 128], a.t[:, fc, 0:w], start=(fc == 0), stop=(fc == FG - 1)), R=[b2, a], W=[ps])
                K.op("dve", lambda e: e.scalar_tensor_tensor(x.t[:, oc, c0:c0 + w], ps.t[:, 0:w], mod.t[:, isctx, 40 + oc:41 + oc], x.t[:, oc, c0:c0 + w], op0=ALU.mult, op1=ALU.add), R=[ps, mod, x], W=[x])
    K.nc.leave_named_scope("ffn", _ffn_sid, False)


def decl_ta_odd(K):
    io = {}
    io["w_in_o"] = K.dram_in("w_in_o", [1024, 4608], F32)
    io["cosr"] = K.dram_in("cosr", [128, NT], F32)
    io["sinr"] = K.dram_in("sinr", [128, NT], F32)
    io["pb"] = K.dram_out("pb", [128, 20, NT], BF16)
    io["pf"] = K.dram_out("pf", [128, 16, NT], F32)
    return io


def ta_odd(K, T, hT, io):
    pb, pf = io["pb"], io["pf"]
    put_b = stage_out(K, T, pb, None, BF16)
    put_f = stage_out(K, T, pf, None, F32)
    x1 = [K.sb("ox1_%d" % i, [128, 512], F32) for i in range(2)]
    x2 = [K.sb("ox2_%d" % i, [128, 512], F32) for i in range(2)]
    RSC = 128 ** -0.5

    def epi(oc, m, si, c0, w, isctx, ps):
        if oc < 8:
            sc = 1.0 if oc < 4 else RSC
            if oc % 2 == 0:
                K.op("act", lambda e: e.mul(x1[si % 2].t[:, 0:w], ps.t[:, 0:w], sc), R=[ps], W=[x1[si % 2]])
            else:
                K.op("act", lambda e: e.mul(x2[si % 2].t[:, 0:w], ps.t[:, 0:w], sc), R=[ps], W=[x2[si % 2]])
                T.rope(si, c0, w, x1[si % 2], x2[si % 2], 0, 128, io["cosr"], io["sinr"], put_b, pb.t.ap()[:, oc - 1, c0:c0 + w], pb.t.ap()[:, oc, c0:c0 + w])
        else:
            sec = (oc - 8) // 4
            j = (oc - 8) % 4
            if sec in (0, 2, 5):
                di = {0: 8, 2: 12, 5: 16}[sec] + j
                put_b("act", 128, w, lambda b: K.op("act", lambda e: e.copy(b.t[:, 0:w], ps.t[:, 0:w]), R=[ps], W=[b]), pb.t.ap()[:, di, c0:c0 + w])
            else:
                di = {1: 0, 3: 4, 4: 8, 6: 12}[sec] + j
                put_f("act", 128, w, lambda b: K.op("act", lambda e: e.copy(b.t[:, 0:w], ps.t[:, 0:w]), R=[ps], W=[b]), pf.t.ap()[:, di, c0:c0 + w])
    T.proj(hT, 8, io["w_in_o"], 0, 4608, epi, tag="wino")


def perm_odd(w_in):
    cols = []
    for sec in range(2):
        base = sec * 512
        for pair in range(2):
            hs = (2 * pair, 2 * pair + 1)
            cols.append(np.concatenate([base + h * 128 + 2 * np.arange(64) for h in hs]))
            cols.append(np.concatenate([base + h * 128 + 2 * np.arange(64) + 1 for h in hs]))
    cols.append(np.arange(1024, 4608))
    return np.ascontiguousarray(w_in[:, np.concatenate(cols)])


def core_rows(seq, r):
    return np.concatenate([seq[64 * r:64 * r + 64], seq[256 + 2048 * r:256 + 2048 * (r + 1)]], axis=0)


def tb_common_inputs(inp, li):
    return {"ffn_w1": inp["ffn_w1"][li], "ffn_w3": inp["ffn_w3"][li], "ffn_w2": inp["ffn_w2"][li]}


def tb_even_inputs(inp, li, uT_list, OT_list, yf, yb):
    j = li // 2
    maps = []
    for core in range(8):
        b, r = core // 4, core % 4
        m = tb_common_inputs(inp, li)
        m["uTi"] = uT_list[core]
        m["yfT"] = to_fm(core_rows(yf[b], r))
        m["ybT"] = to_fm(core_rows(yb[b], r))
        On = np.zeros((128, 4, NT), np.float32)
        Dn = np.zeros((128, 4, NT), np.float32)
        for h in range(8):
            unit = b * 8 + h
            OT = OT_list[unit // 2][unit % 2]
            cols = np.concatenate([8192 + 64 * r + np.arange(64), 2048 * r + np.arange(2048)])
            ro = (h % 2) * 64
            On[ro:ro + 64, h // 2, :] = OT[0:64][:, cols]
            Dn[ro:ro + 64, h // 2, :] = OT[64:128][:, cols]
        m["OTn"] = On
        m["denT"] = Dn
        m["dT"] = vec_fm(inp["s5_d"][j])
        m["w_glu"] = inp["s5_w_glu"][j]
        m["w_out"] = inp["w_out_even"][j]
        maps.append(m)
    return maps


def ta_odd_inputs(inp, j, r):
    c, s = core_rope(128, 2, r)
    return {"w_in_o": perm_odd(inp["w_in_odd"][j]), "cosr": c, "sinr": s}


LSEQ = 8448
NCK = 132


def decl_h_odd(K):
    io = {}
    io["GqT"] = K.dram_in("GqT", [4, 128, LSEQ], BF16)
    io["GkT"] = K.dram_in("GkT", [2, 128, LSEQ], BF16)
    io["Gf"] = K.dram_in("Gf", [2, 128, LSEQ], F32)
    io["Gv"] = K.dram_in("Gv", [4, 64, NCK, 128], BF16)
    io["lgam"] = K.dram_in("lgam", [128, 2], F32)
    io["lbl"] = K.dram_in("lbl", [128, 3], F32)
    io["lbsel"] = K.dram_in("lbsel", [128, 3], F32)
    io["cmask"] = K.dram_in("cmask", [64, 64], F32)
    io["cmask2"] = K.dram_in("cmask2", [128, 128], F32)
    io["identb"] = K.dram_in("identb", [128, 128], BF16)
    io["Go"] = K.dram_out("Go", [4, 64, NCK, 128], F32)
    return io


def h_odd(K, io):
    def ld(name, shape, dt=F32):
        b = K.sb("g_" + name, shape, dt)
        sl = tuple(slice(None) for _ in shape)
        K.dma("sp", b.t[sl], io[name].t.ap(), R=[io[name]], W=[b])
        return b
    lgam, lbl, lbsel = ld("lgam", [128, 2]), ld("lbl", [128, 3]), ld("lbsel", [128, 3])
    cmask, identb = ld("cmask", [64, 64]), ld("identb", [128, 128], BF16)
    e3, lb, oml, tot = K.sb("g_e3", [128, 3], F32), K.sb("g_lb", [128, 1], F32), K.sb("g_oml", [128, 1], F32), K.sb("g_tot", [128, 1], F32)
    K.op("act", lambda e: e.activation(e3.t[:, :], lbl.t[:, :], AF.Exp), R=[lbl], W=[e3])
    K.op("dve", lambda e: e.tensor_reduce(tot.t[:, :], e3.t[:, :], axis=AX.X, op=ALU.add), R=[e3], W=[tot])
    K.op("dve", lambda e: e.tensor_tensor(e3.t[:, :], e3.t[:, :], lbsel.t[:, :], op=ALU.mult), R=[e3, lbsel], W=[e3])
    K.op("dve", lambda e: e.tensor_reduce(lb.t[:, :], e3.t[:, :], axis=AX.X, op=ALU.add), R=[e3], W=[lb])
    K.op("dve", lambda e: e.reciprocal(tot.t[:, :], tot.t[:, :]), R=[tot], W=[tot])
    K.op("dve", lambda e: e.tensor_tensor(lb.t[:, :], lb.t[:, :], tot.t[:, :], op=ALU.mult), R=[lb, tot], W=[lb])
    K.op("dve", lambda e: e.tensor_scalar(oml.t[:, :], lb.t[:, :], -1.0, 1.0, op0=ALU.mult, op1=ALU.add), R=[lb], W=[oml])

    HW = LSEQ // 2
    A = K.sb("g_A", [128, HW], F32)
    Bc = K.sb("g_B", [128, HW], F32)
    cmask2 = ld("cmask2", [128, 128])
    for pair in range(2):
        hg = pair == 1
        CS = 128
        NCH_ = LSEQ // CS
        NH = NCH_ // 2
        cm = cmask2
        with K.scope():
            rmask = K.sb("g_rm%d" % pair, [128, HW], BF16)
            K.op("pool", lambda e: e.memset(rmask.t[:, :], 1.0), W=[rmask])
            K.op("pool", lambda e: e.memset(rmask.t[:, :].rearrange("p (c t) -> p c t", t=CS)[:, :, 0], 0.0), W=[rmask])
            Av = A.t[:, :].rearrange("p (c t) -> p c t", t=CS)
            U = []
            for i in range(2):
                t_ = "%d_%d" % (pair, i)
                U.append(dict(
                    qt=K.sb("g_q" + t_, [128, LSEQ], BF16), kt=K.sb("g_k" + t_, [128, LSEQ], BF16),
                    v=K.sb("g_v" + t_, [CS, NCH_, 128], BF16), dec=K.sb("g_dec" + t_, [128, NCH_], F32),
                    cmid=K.sb("g_cmid" + t_, [128, NCH_], F32), ecm=K.sb("g_ecm" + t_, [128, NCH_], F32),
                    dec2=K.sb("g_dec2" + t_, [128, NCH_], F32), clp=K.sb("g_clp" + t_, [CS, CS], F32),
                    S=K.sb("g_S" + t_, [128, 128], F32), Sb=K.sb("g_Sb" + t_, [128, 128], BF16),
                    tmpS=K.sb("g_tS" + t_, [128, 128], F32),
                    ktok=[K.sb("g_kt%s_%d" % (t_, j), [CS, 128], BF16) for j in range(2)],
                    attm=[K.sb("g_at%s_%d" % (t_, j), [CS, CS], BF16) for j in range(2)],
                    ostg=[K.sb("g_os%s_%d" % (t_, j), [CS, 4, 128], F32) for j in range(2)]))
            for i in range(2):
                u = pair * 2 + i
                d = U[i]
                qt, kt, v = d["qt"], d["kt"], d["v"]
                K.dma("sp", qt.t[:, :], io["GqT"].t.ap()[u], R=[io["GqT"]], W=[qt])
                gv2 = io["Gv"].t.ap()[u].rearrange("t (c two) e -> t c two e", two=2)
                K.dma("act", v.t[0:64, :, :], gv2[:, :, 0, :], R=[io["Gv"]], W=[v])
                K.dma("act", v.t[64:128, :, :], gv2[:, :, 1, :], R=[io["Gv"]], W=[v])
                if not hg:
                    K.dma("sp", kt.t[:, :], io["GkT"].t.ap()[u], R=[io["GkT"]], W=[kt])
                for hf in range(2):
                    sl = slice(hf * HW, (hf + 1) * HW)
                    if not hg:
                        K.op("pool", lambda e: e.memset(A.t[:, :], 1.0), W=[A])
                        K.op("dve", lambda e: e.tensor_scalar(A.t[:, :], A.t[:, :], lgam.t[:, u:u + 1], None, op0=ALU.mult), R=[A, lgam], W=[A])
                    else:
                        K.dma("sp", A.t[:, :], io["Gf"].t.ap()[u - 2, :, sl], R=[io["Gf"]], W=[A])
                        K.op("act", lambda e: e.activation(A.t[:, :], A.t[:, :], AF.Sigmoid), R=[A], W=[A])
                        K.op("dve", lambda e: e.tensor_scalar(A.t[:, :], A.t[:, :], oml.t[:, 0:1], lb.t[:, 0:1], op0=ALU.mult, op1=ALU.add), R=[A, oml, lb], W=[A])
                        K.op("dve", lambda e: e.tensor_scalar(kt.t[:, sl], A.t[:, :], -1.0, 1.0, op0=ALU.mult, op1=ALU.add), R=[A], W=[kt])
                        K.op("act", lambda e: e.activation(A.t[:, :], A.t[:, :], AF.Ln), R=[A], W=[A])
                    K.op("dve", lambda e: e.tensor_tensor_scan(Bc.t[:, :], rmask.t[:, :], A.t[:, :], 0.0, op0=ALU.mult, op1=ALU.add), R=[rmask, A], W=[Bc])
                    Bv = Bc.t[:, :].rearrange("p (c t) -> p c t", t=CS)
                    K.op("pool", lambda e: e.tensor_copy(d["cmid"].t[:, hf * NH:(hf + 1) * NH], Bv[:, :, CS // 2 - 1]), R=[Bc], W=[d["cmid"]])
                    K.op("pool", lambda e: e.tensor_copy(d["dec"].t[:, hf * NH:(hf + 1) * NH], Bv[:, :, CS - 1]), R=[Bc], W=[d["dec"]])
                    for cc in range(NH):
                        K.op("dve", lambda e: e.tensor_scalar(Bv[:, cc, :], Bv[:, cc, :], d["cmid"].t[:, hf * NH + cc:hf * NH + cc + 1], None, op0=ALU.subtract), R=[Bc, d["cmid"]], W=[Bc])
                    K.op("act", lambda e: e.activation(A.t[:, :], Bc.t[:, :], AF.Exp), R=[Bc], W=[A])
                    K.op("dve", lambda e: e.tensor_tensor(qt.t[:, sl], qt.t[:, sl], A.t[:, :], op=ALU.mult), R=[qt, A], W=[qt])
                    K.op("act", lambda e: e.activation(Bc.t[:, :], Bc.t[:, :], AF.Exp, scale=-1.0), R=[Bc], W=[Bc])
                    K.op("pool", lambda e: e.tensor_tensor(kt.t[:, sl], kt.t[:, sl], Bc.t[:, :], op=ALU.mult), R=[kt, Bc], W=[kt])
                K.op("dve", lambda e: e.tensor_tensor(d["dec2"].t[:, :], d["dec"].t[:, :], d["cmid"].t[:, :], op=ALU.subtract), R=[d["dec"], d["cmid"]], W=[d["dec2"]])
                K.op("act", lambda e: e.activation(d["dec2"].t[:, :], d["dec2"].t[:, :], AF.Exp), R=[d["dec2"]], W=[d["dec2"]])
                K.op("act", lambda e: e.activation(d["ecm"].t[:, :], d["cmid"].t[:, :], AF.Exp), R=[d["cmid"]], W=[d["ecm"]])
                K.op("act", lambda e: e.activation(d["dec"].t[:, :], d["dec"].t[:, :], AF.Exp), R=[d["dec"]], W=[d["dec"]])
                K.op("dve", lambda e: e.memset(d["S"].t[:, :], 0.0), W=[d["S"]])
                K.op("dve", lambda e: e.memset(d["Sb"].t[:, :], 0.0), W=[d["Sb"]])
            for c in range(NCH_):
                cs = slice(c * CS, (c + 1) * CS)
                for i in range(2):
                    u = pair * 2 + i
                    d = U[i]
                    qt, kt, v, S, Sb, tmpS = d["qt"], d["kt"], d["v"], d["S"], d["Sb"], d["tmpS"]
                    pst = K.ps_get()
                    K.op("pe", lambda e: e.matmul(pst.t[0:CS, 0:128], kt.t[:, cs], identb.t[:, :], start=True, stop=True), R=[kt, identb], W=[pst])
                    pst2 = K.ps_get()
                    K.op("pe", lambda e: e.matmul(pst2.t[0:CS, 0:CS], kt.t[:, cs], qt.t[:, cs], start=True, stop=True), R=[kt, qt], W=[pst2])
                    kk_ = d["ktok"][c % 2]
                    am = d["attm"][c % 2]
                    K.op("act", lambda e: e.copy(kk_.t[:, :], pst.t[0:CS, 0:128]), R=[pst], W=[kk_])
                    K.op("dve", lambda e: e.tensor_scalar(d["clp"].t[:, :], pst2.t[0:CS, 0:CS], 1.0e30, -1.0e30, op0=ALU.min, op1=ALU.max), R=[pst2], W=[d["clp"]])
                    K.op("dve", lambda e: e.tensor_tensor(am.t[:, :], d["clp"].t[:, :], cm.t[:, :], op=ALU.mult), R=[d["clp"], cm], W=[am])
                    pso = K.ps_get()
                    K.op("pe", lambda e: e.matmul(pso.t[0:CS, 0:128], am.t[:, :], v.t[:, c, :], start=True, stop=False), R=[am, v], W=[pso])
                    K.op("pe", lambda e: e.matmul(pso.t[0:CS, 0:128], qt.t[:, cs], Sb.t[:, :], start=False, stop=True), R=[qt, Sb], W=[pso])
                    psk = K.ps_get()
                    K.op("pe", lambda e: e.matmul(psk.t[:, 0:128], kk_.t[:, :], v.t[:, c, :], start=True, stop=True), R=[kk_, v], W=[psk])
                    K.op("dve", lambda e: e.tensor_scalar(tmpS.t[:, :], S.t[:, :], d["dec"].t[:, c:c + 1], None, op0=ALU.mult), R=[S, d["dec"]], W=[tmpS])
                    K.op("dve", lambda e: e.scalar_tensor_tensor(S.t[:, :], psk.t[:, 0:128], d["dec2"].t[:, c:c + 1], tmpS.t[:, :], op0=ALU.mult, op1=ALU.add), R=[psk, d["dec2"], tmpS], W=[S])
                    if c + 1 < NCH_:
                        K.op("act", lambda e: e.activation(Sb.t[:, :], S.t[:, :], AF.Identity, scale=d["ecm"].t[:, c + 1:c + 2]), R=[S, d["ecm"]], W=[Sb])
                    og = d["ostg"][(c // 4) % 2]
                    K.op("act", lambda e: e.copy(og.t[:, c % 4, :], pso.t[0:CS, 0:128]), R=[pso], W=[og])
                    if c % 4 == 3 or c == NCH_ - 1:
                        c0 = (c // 4) * 4
                        n = c - c0 + 1
                        go2 = io["Go"].t.ap()[u].rearrange("t (c two) e -> t c two e", two=2)
                        K.dma("sp", go2[:, c0:c0 + n, 0, :], og.t[0:64, 0:n, :], R=[og], W=[io["Go"]])
                        K.dma("sp", go2[:, c0:c0 + n, 1, :], og.t[64:128, 0:n, :], R=[og], W=[io["Go"]])


def build_h_odd():
    nc = bass.Bass("TRN2", target_bir_lowering=False)
    K = Emit(nc)
    io = decl_h_odd(K)
    h_odd(K, io)
    K.finish()
    return nc


def flipseq(a, rev):
    if not rev:
        return a
    return np.concatenate([a[:256][::-1], a[256:][::-1]], axis=0)


def gather_seq(core_arrays, b):
    parts = [core_arrays[4 * b + r] for r in range(4)]
    ctx = np.concatenate([p[:, :64] for p in parts], axis=1)
    lat = np.concatenate([p[:, 64:] for p in parts], axis=1)
    return np.concatenate([ctx, lat], axis=1).T


def h_odd_inputs(inp, j, pb_list, pf_list):
    import ml_dtypes
    bf = ml_dtypes.bfloat16
    cmask = (np.arange(64)[:, None] <= np.arange(64)[None, :]).astype(np.float32)
    cmask2 = (np.arange(128)[:, None] <= np.arange(128)[None, :]).astype(np.float32)
    identb = np.eye(128, dtype=np.float32).astype(bf)
    sel = np.zeros((128, 3), np.float32)
    sel[:, :j + 1] = 1.0
    maps = []
    for core in range(8):
        b, h = core // 4, core % 4
        ch, ro = 2 * (h // 2), (h % 2) * 64
        rq = gather_seq([np.concatenate([p[ro:ro + 64, ch], p[ro:ro + 64, ch + 1]], 0) for p in pb_list], b)
        rk = gather_seq([np.concatenate([p[ro:ro + 64, 4 + ch], p[ro:ro + 64, 5 + ch]], 0) for p in pb_list], b)
        rv = gather_seq([p[:, 8 + h] for p in pb_list], b)
        hq = gather_seq([p[:, 12 + h] for p in pb_list], b)
        hi = gather_seq([p[:, 16 + h] for p in pb_list], b)
        hff = gather_seq([p[:, 4 + h] for p in pf_list], b)
        hfb = gather_seq([p[:, 8 + h] for p in pf_list], b)
        GqT = np.stack([flipseq(rq, 0).T, flipseq(rq, 1).T, flipseq(hq, 0).T, flipseq(hq, 1).T])
        GkT = np.stack([flipseq(rk, 0).T, flipseq(rk, 1).T])
        Gf = np.stack([flipseq(hff, 0).T, flipseq(hfb, 1).T])
        tok = lambda a: a.reshape(NCK, 64, 128).transpose(1, 0, 2)
        Gv = np.stack([tok(flipseq(rv, 0)), tok(flipseq(rv, 1)), tok(flipseq(hi, 0)), tok(flipseq(hi, 1))])
        lg = np.array([np.log1p(-np.exp2(np.float32(-(5.0 + off) - h))) for off in (0.0, 0.5)], np.float32)
        maps.append({"GqT": np.ascontiguousarray(GqT), "GkT": np.ascontiguousarray(GkT), "Gf": np.ascontiguousarray(Gf),
                     "Gv": np.ascontiguousarray(Gv), "lgam": np.ascontiguousarray(np.broadcast_to(lg[None, :], (128, 2))),
                     "lbl": np.ascontiguousarray(inp["hg_lb_logits"][:, h * 128:(h + 1) * 128].T),
                     "lbsel": sel, "cmask": cmask, "cmask2": cmask2, "identb": identb})
    return maps


def tb_odd_inputs(inp, li, Go_list, pf_list):
    j = li // 2
    nat = {}
    for core in range(8):
        b, h = core // 4, core % 4
        Go = Go_list[core]
        for u in range(4):
            seq = Go[u].transpose(1, 0, 2).reshape(LSEQ, 128)
            nat[(b, h, u)] = flipseq(seq, u % 2)
    maps = []
    for core in range(8):
        b, r = core // 4, core % 4
        m = tb_common_inputs(inp, li)
        of = np.concatenate([core_rows(nat[(b, h, 0)], r) for h in range(4)] + [core_rows(nat[(b, h, 2)], r) for h in range(4)], axis=1)
        ob = np.concatenate([core_rows(nat[(b, h, 1)], r) for h in range(4)] + [core_rows(nat[(b, h, 3)], r) for h in range(4)], axis=1)
        m["of"] = to_fm(of)
        m["ob"] = to_fm(ob)
        pf = pf_list[core]
        m["gT"] = np.ascontiguousarray(np.concatenate([pf[:, 0:4], pf[:, 12:16]], axis=1))
        m["gnT"] = vec_fm(np.concatenate([inp["ret_gn"][j], inp["hg_gn"][j]]))
        m["w_out"] = inp["w_out_odd"][j]
        maps.append(m)
    return maps


_PROGS = {}


def _prog(key, fn):
    if key not in _PROGS:
        _PROGS[key] = fn()
    return _PROGS[key]


def _run(nc, maps):
    import sys, time
    t0 = time.time()
    res = run_bass_kernel_spmd(nc, maps, core_ids=list(range(8))).results
    out = [{k: np.asarray(v) for k, v in r.items()} for r in res]
    print("[kernel] launch done in %.1fs" % (time.time() - t0), file=sys.stderr, flush=True)
    return out


def kernel(**inp):
    inp = {k: np.asarray(v) for k, v in inp.items()}
    maps = []
    for core in range(8):
        b, r = core // 4, core % 4
        X = np.concatenate([inp["ctx"][b, 64 * r:64 * r + 64], inp["x"][b, 2048 * r:2048 * (r + 1)]], axis=0)
        m = {"xT_in": to_fm(X)}
        m.update(mod_inputs(inp, 0, b, "_b"))
        m.update(ta_even_inputs(inp, 0, r))
        maps.append(m)
    ta = _run(_prog("A", lambda: build_tok(None, "even")), maps)
    xT = None
    for li in range(4):
        j = li // 2
        last = li == 3
        if li % 2 == 0:
            am = attn_inputs(ta)
            sm_ = s5_inputs(inp, j, [r["uT"] for r in ta])
            for core in range(8):
                am[core].update(sm_[core])
            ho = _run(_prog("HE", build_h_even), am)
            yf, yb = s5_outputs([o["s5_Y8"] for o in ho])
            maps = tb_even_inputs(inp, li, [r["uT"] for r in ta], [o["OT"] for o in ho], yf, yb)
        else:
            hm = h_odd_inputs(inp, j, [r["pb"] for r in ta], [r["pf"] for r in ta])
            ho = _run(_prog("HO", build_h_odd), hm)
            maps = tb_odd_inputs(inp, li, [o["Go"] for o in ho], [r["pf"] for r in ta])
        for core in range(8):
            b, r = core // 4, core % 4
            if li == 0:
                X = np.concatenate([inp["ctx"][b, 64 * r:64 * r + 64], inp["x"][b, 2048 * r:2048 * (r + 1)]], axis=0)
                maps[core]["xT_in"] = to_fm(X)
            else:
                maps[core]["xT_in"] = xT[core]
            maps[core].update(mod_inputs(inp, li, b, "_a"))
            if not last:
                maps[core].update(mod_inputs(inp, li + 1, b, "_b"))
                if li % 2 == 0:
                    maps[core].update(ta_odd_inputs(inp, (li + 1) // 2, r))
                else:
                    maps[core].update(ta_even_inputs(inp, (li + 1) // 2, r))
            else:
                maps[core]["fnT"] = vec_fm(inp["final_norm"])
        if last:
            ta = _run(_prog("D", lambda: build_tok("odd", None, final=True)), maps)
        elif li % 2 == 0:
            ta = _run(_prog("B", lambda: build_tok("even", "odd")), maps)
        else:
            ta = _run(_prog("C", lambda: build_tok("odd", "even")), maps)
        xT = [r["xT_out"] for r in ta]
    out = np.zeros((2, 8192, 1024), np.float32)
    for core in range(8):
        b, r = core // 4, core % 4
        out[b, 2048 * r:2048 * (r + 1)] = from_fm(ta[core]["yT"])[64:]
    return out
```
